# Optimizing a Trainium2 kernel written in Bass

```python
import numpy as np
import jax, jax.numpy as jnp
from jax import lax

D_MODEL = 2048
BATCH = 4
SEQ = 2048
DEPTH = 2

D_FF = 5632
EPS = 1e-6
CONV_CH = D_MODEL // 2
CONV_GROUPS = 8
CONV_WIDTH = 31
GMLP_CH = D_MODEL // 2
GMLP_GROUPS = 8
GMLP_GDIM = GMLP_CH // GMLP_GROUPS
GMLP_CHUNK = 128
HEAD_DIM = 128
N_HEADS = D_MODEL // HEAD_DIM
N_KV = 4
HPG = N_HEADS // N_KV
KV_W = N_KV * HEAD_DIM
ROT_DIM = HEAD_DIM // 4
ROPE_THETA = 500000.0
CMP_LEN = 32
CMP_STRIDE = 16
CMP_HIDDEN = 256
SEL_BLOCK = 64
SEL_TOPK = 16
WINDOW = 512
Q_BLOCK = 128
SEL_Q_CHUNK = 16
FORCE_BONUS = 1e3
NEG = -1e30
NSA_IN = D_MODEL + 6 * KV_W + 3 * N_HEADS
AB_IN = 2 * CONV_CH + 2 * GMLP_CH

kernel_name = 'hybrid_conv_gmlp_nsa_macaron'


def rmsnorm(x, g):
    xf = x.astype(jnp.float32)
    y = xf * lax.rsqrt(jnp.mean(xf * xf, axis=-1, keepdims=True) + EPS)
    return y.astype(x.dtype) * g


def layernorm(x, g, b):
    xf = x.astype(jnp.float32)
    mu = jnp.mean(xf, axis=-1, keepdims=True)
    var = jnp.mean(jnp.square(xf - mu), axis=-1, keepdims=True)
    return ((xf - mu) * lax.rsqrt(var + EPS)).astype(x.dtype) * g + b


def swiglu(h, w_gate, w_up, w_down):
    return (jax.nn.silu(h @ w_gate) * (h @ w_up)) @ w_down


def rope_tables(positions):
    inv = ROPE_THETA ** (-jnp.arange(0, ROT_DIM, 2, dtype=jnp.float32) / ROT_DIM)
    ang = positions.astype(jnp.float32)[..., None] * inv
    return jnp.cos(ang), jnp.sin(ang)


def apply_rope(x, cos, sin):
    half = ROT_DIM // 2
    cos = cos.astype(x.dtype)
    sin = sin.astype(x.dtype)
    x1, x2, rest = x[..., :half], x[..., half:ROT_DIM], x[..., ROT_DIM:]
    return jnp.concatenate([x1 * cos - x2 * sin, x2 * cos + x1 * sin, rest], axis=-1)


def conv_gmlp_mixer(h, w_in, conv_w, conv_b, conv_ln_g, conv_ln_b,
                    gmlp_ln_g, gmlp_ln_b, gmlp_ws, gmlp_bs, w_out):
    B_, S, _ = h.shape
    z = h @ w_in
    a_val, a_gate, b_u, b_v = jnp.split(z, [CONV_CH, 2 * CONV_CH, 2 * CONV_CH + GMLP_CH], axis=-1)
    a = a_val * jax.nn.sigmoid(a_gate)
    a = lax.conv_general_dilated(a, conv_w, (1,), [(CONV_WIDTH - 1, 0)],
                                 dimension_numbers=('NWC', 'WIO', 'NWC'),
                                 feature_group_count=CONV_CH) + conv_b
    a = jax.nn.silu(layernorm(a, conv_ln_g, conv_ln_b))
    nc = S // GMLP_CHUNK
    u = jax.nn.gelu(b_u).reshape(B_, nc, GMLP_CHUNK, GMLP_GROUPS, GMLP_GDIM)
    v = jax.nn.gelu(b_v).reshape(B_, nc, GMLP_CHUNK, GMLP_GROUPS, GMLP_GDIM)
    v = layernorm(v, gmlp_ln_g, gmlp_ln_b)
    tri = jnp.tril(jnp.ones((GMLP_CHUNK, GMLP_CHUNK), dtype=bool))
    ws = jnp.where(tri[None], gmlp_ws, 0.0).astype(v.dtype)
    sv = jnp.einsum('gts,bcsgd->bctgd', ws, v) + gmlp_bs.T[:, :, None]
    bo = (u * sv).reshape(B_, S, GMLP_CH)
    return jnp.concatenate([a, bo], axis=-1) @ w_out


def nsa_mixer(h, cos, sin, w_in, cmp_pe_k, cmp_w1_k, cmp_w2_k,
              cmp_pe_v, cmp_w1_v, cmp_w2_v, w_out):
    B_, S, _ = h.shape
    f32 = jnp.float32
    scale = HEAD_DIM ** -0.5
    z = h @ w_in
    offs = [D_MODEL + i * KV_W for i in range(7)]
    q, kc, vc, ks, vs, kw, vw, g = jnp.split(z, offs, axis=-1)
    q = q.reshape(B_, S, N_KV, HPG, HEAD_DIM)
    kc, vc, ks, vs, kw, vw = [t.reshape(B_, S, N_KV, HEAD_DIM) for t in (kc, vc, ks, vs, kw, vw)]
    gates = jax.nn.sigmoid(g.reshape(B_, S, N_KV, HPG, 3))
    q_r = apply_rope(q, cos[:, :, None, None], sin[:, :, None, None])
    ks_r = apply_rope(ks, cos[:, :, None], sin[:, :, None])
    kw_r = apply_rope(kw, cos[:, :, None], sin[:, :, None])
    t_np = np.arange(S)

    n_cmp = (S - CMP_LEN) // CMP_STRIDE + 1
    cidx = np.arange(n_cmp)[:, None] * CMP_STRIDE + np.arange(CMP_LEN)[None]

    def compress(k, pe, w1, w2):
        blk = k[:, cidx] + pe[:, None, :]
        blk = blk.transpose(0, 1, 3, 2, 4).reshape(B_, n_cmp, N_KV, CMP_LEN * HEAD_DIM)
        return jax.nn.gelu(blk @ w1) @ w2

    kcmp = compress(kc, cmp_pe_k, cmp_w1_k, cmp_w2_k)
    vcmp = compress(vc, cmp_pe_v, cmp_w1_v, cmp_w2_v)
    cend = cidx[:, -1]
    cmask = jnp.asarray(cend[None, :] <= t_np[:, None])
    s_c = jnp.einsum('btghd,bngd->bghtn', q, kcmp, preferred_element_type=f32) * scale
    p_c = jax.nn.softmax(jnp.where(cmask, s_c, NEG), axis=-1) * cmask
    o_c = jnp.einsum('bghtn,bngd->btghd', p_c.astype(vcmp.dtype), vcmp)

    n_blk = S // SEL_BLOCK
    n_sel = min(SEL_TOPK, n_blk)
    sb = np.arange(n_blk) * SEL_BLOCK
    overlap = (cidx[:, 0][:, None] < sb[None] + SEL_BLOCK) & (cend[:, None] >= sb[None])
    imp = jnp.einsum('bghtn,nj->btgj', p_c, jnp.asarray(overlap, f32))
    jb = np.arange(n_blk)[None]
    cur = (t_np // SEL_BLOCK)[:, None]
    forced = (jb == 0) | (jb == cur) | (jb == cur - 1)
    valid = jb * SEL_BLOCK <= t_np[:, None]
    imp = imp + FORCE_BONUS * jnp.asarray(forced, f32)[None, :, None, :]
    imp = jnp.where(jnp.asarray(valid)[None, :, None, :], imp, NEG)
    _, sel_idx = lax.top_k(imp, n_sel)

    ksb = ks_r.reshape(B_, n_blk, SEL_BLOCK, N_KV, HEAD_DIM).transpose(0, 3, 1, 2, 4)
    vsb = vs.reshape(B_, n_blk, SEL_BLOCK, N_KV, HEAD_DIM).transpose(0, 3, 1, 2, 4)
    bi = jnp.arange(B_)[:, None, None, None]
    gi = jnp.arange(N_KV)[None, None, :, None]

    def sel_chunk(args):
        qc, idx, tc = args
        kg = ksb[bi, gi, idx]
        vg = vsb[bi, gi, idx]
        kpos = idx[..., None] * SEL_BLOCK + jnp.arange(SEL_BLOCK)
        m = kpos <= tc[None, :, None, None, None]
        s = jnp.einsum('bqghd,bqgnld->bqghnl', qc, kg, preferred_element_type=f32) * scale
        s = jnp.where(m[:, :, :, None], s, NEG)
        p = jax.nn.softmax(s, axis=(-2, -1))
        return jnp.einsum('bqghnl,bqgnld->bqghd', p.astype(vg.dtype), vg)

    nq = S // SEL_Q_CHUNK
    qc_all = q_r.reshape(B_, nq, SEL_Q_CHUNK, N_KV, HPG, HEAD_DIM).transpose(1, 0, 2, 3, 4, 5)
    idx_all = sel_idx.reshape(B_, nq, SEL_Q_CHUNK, N_KV, n_sel).transpose(1, 0, 2, 3, 4)
    tc_all = jnp.arange(S, dtype=jnp.int32).reshape(nq, SEL_Q_CHUNK)
    o_s = lax.map(sel_chunk, (qc_all, idx_all, tc_all))
    o_s = o_s.transpose(1, 0, 2, 3, 4, 5).reshape(B_, S, N_KV, HPG, HEAD_DIM)

    nqb = S // Q_BLOCK
    span = WINDOW + Q_BLOCK
    widx = np.arange(nqb)[:, None] * Q_BLOCK + np.arange(span)[None]
    kpad = jnp.pad(kw_r, ((0, 0), (WINDOW, 0), (0, 0), (0, 0)))
    vpad = jnp.pad(vw, ((0, 0), (WINDOW, 0), (0, 0), (0, 0)))
    kwin = kpad[:, widx]
    vwin = vpad[:, widx]
    n_ = np.arange(nqb)[:, None, None]
    q_ = np.arange(Q_BLOCK)[None, :, None]
    k_ = np.arange(span)[None, None, :]
    kpos = n_ * Q_BLOCK + k_ - WINDOW
    qpos = n_ * Q_BLOCK + q_
    wmask = jnp.asarray((kpos <= qpos) & (kpos > qpos - WINDOW) & (kpos >= 0))
    qb = q_r.reshape(B_, nqb, Q_BLOCK, N_KV, HPG, HEAD_DIM)
    s_w = jnp.einsum('bnqghd,bnkgd->bnghqk', qb, kwin, preferred_element_type=f32) * scale
    s_w = jnp.where(wmask[None, :, None, None], s_w, NEG)
    p_w = jax.nn.softmax(s_w, axis=-1)
    o_w = jnp.einsum('bnghqk,bnkgd->bnqghd', p_w.astype(vwin.dtype), vwin)
    o_w = o_w.reshape(B_, S, N_KV, HPG, HEAD_DIM)

    o = gates[..., 0:1] * o_c + gates[..., 1:2] * o_s + gates[..., 2:3] * o_w
    return o.reshape(B_, S, D_MODEL) @ w_out


def setup_inputs(seed: int = 0) -> dict:
    key = jax.random.key(seed)
    ks = iter(jax.random.split(key, 64))
    f32 = jnp.float32

    def dense(shape, fan_in):
        return jax.random.normal(next(ks), shape, f32) * fan_in ** -0.5

    def gain(shape):
        return 1.0 + 0.02 * jax.random.normal(next(ks), shape, f32)

    def bias(shape, s=0.02):
        return s * jax.random.normal(next(ks), shape, f32)

    x = jax.random.normal(next(ks), (BATCH, SEQ, D_MODEL), f32)
    offset = jax.random.randint(next(ks), (BATCH, 1), 0, 1024, dtype=jnp.int32)
    positions = offset + jnp.arange(SEQ, dtype=jnp.int32)[None, :]

    def ffn(prefix, d):
        d[prefix + '_norm'] = gain((D_MODEL,))
        d[prefix + '_w_gate'] = dense((D_MODEL, D_FF), D_MODEL)
        d[prefix + '_w_up'] = dense((D_MODEL, D_FF), D_MODEL)
        d[prefix + '_w_down'] = dense((D_FF, D_MODEL), D_FF)

    d = {'x': x, 'positions': positions}
    ffn('l0_ffn1', d)
    d['l0_mix_norm'] = gain((D_MODEL,))
    d['l0_w_in'] = dense((D_MODEL, AB_IN), D_MODEL)
    d['l0_conv_w'] = dense((CONV_WIDTH, 1, CONV_CH), CONV_WIDTH)
    d['l0_conv_b'] = bias((CONV_CH,))
    d['l0_conv_ln_g'] = gain((CONV_CH,))
    d['l0_conv_ln_b'] = bias((CONV_CH,))
    d['l0_gmlp_ln_g'] = gain((GMLP_GROUPS, GMLP_GDIM))
    d['l0_gmlp_ln_b'] = bias((GMLP_GROUPS, GMLP_GDIM))
    d['l0_gmlp_ws'] = dense((GMLP_GROUPS, GMLP_CHUNK, GMLP_CHUNK), GMLP_CHUNK)
    d['l0_gmlp_bs'] = gain((GMLP_GROUPS, GMLP_CHUNK))
    d['l0_w_out'] = dense((CONV_CH + GMLP_CH, D_MODEL), CONV_CH + GMLP_CH)
    ffn('l0_ffn2', d)
    ffn('l1_ffn1', d)
    d['l1_mix_norm'] = gain((D_MODEL,))
    d['l1_w_in'] = dense((D_MODEL, NSA_IN), D_MODEL)
    d['l1_cmp_pe_k'] = bias((CMP_LEN, HEAD_DIM), 0.1)
    d['l1_cmp_w1_k'] = dense((CMP_LEN * HEAD_DIM, CMP_HIDDEN), CMP_LEN * HEAD_DIM)
    d['l1_cmp_w2_k'] = dense((CMP_HIDDEN, HEAD_DIM), CMP_HIDDEN)
    d['l1_cmp_pe_v'] = bias((CMP_LEN, HEAD_DIM), 0.1)
    d['l1_cmp_w1_v'] = dense((CMP_LEN * HEAD_DIM, CMP_HIDDEN), CMP_LEN * HEAD_DIM)
    d['l1_cmp_w2_v'] = dense((CMP_HIDDEN, HEAD_DIM), CMP_HIDDEN)
    d['l1_w_out'] = dense((D_MODEL, D_MODEL), D_MODEL)
    ffn('l1_ffn2', d)
    d['final_norm'] = gain((D_MODEL,))
    return d


def reference(x, positions,
              l0_ffn1_norm, l0_ffn1_w_gate, l0_ffn1_w_up, l0_ffn1_w_down,
              l0_mix_norm, l0_w_in, l0_conv_w, l0_conv_b, l0_conv_ln_g, l0_conv_ln_b,
              l0_gmlp_ln_g, l0_gmlp_ln_b, l0_gmlp_ws, l0_gmlp_bs, l0_w_out,
              l0_ffn2_norm, l0_ffn2_w_gate, l0_ffn2_w_up, l0_ffn2_w_down,
              l1_ffn1_norm, l1_ffn1_w_gate, l1_ffn1_w_up, l1_ffn1_w_down,
              l1_mix_norm, l1_w_in, l1_cmp_pe_k, l1_cmp_w1_k, l1_cmp_w2_k,
              l1_cmp_pe_v, l1_cmp_w1_v, l1_cmp_w2_v, l1_w_out,
              l1_ffn2_norm, l1_ffn2_w_gate, l1_ffn2_w_up, l1_ffn2_w_down,
              final_norm):
    cos, sin = rope_tables(positions)
    layers = (
        dict(ffn1=(l0_ffn1_norm, l0_ffn1_w_gate, l0_ffn1_w_up, l0_ffn1_w_down),
             mix_norm=l0_mix_norm,
             mix=(l0_w_in, l0_conv_w, l0_conv_b, l0_conv_ln_g, l0_conv_ln_b,
                  l0_gmlp_ln_g, l0_gmlp_ln_b, l0_gmlp_ws, l0_gmlp_bs, l0_w_out),
             ffn2=(l0_ffn2_norm, l0_ffn2_w_gate, l0_ffn2_w_up, l0_ffn2_w_down)),
        dict(ffn1=(l1_ffn1_norm, l1_ffn1_w_gate, l1_ffn1_w_up, l1_ffn1_w_down),
             mix_norm=l1_mix_norm,
             mix=(l1_w_in, l1_cmp_pe_k, l1_cmp_w1_k, l1_cmp_w2_k,
                  l1_cmp_pe_v, l1_cmp_w1_v, l1_cmp_w2_v, l1_w_out),
             ffn2=(l1_ffn2_norm, l1_ffn2_w_gate, l1_ffn2_w_up, l1_ffn2_w_down)),
    )
    for i in range(DEPTH):
        L = layers[i]
        n1, g1, u1, d1 = L['ffn1']
        x = x + 0.5 * swiglu(rmsnorm(x, n1), g1, u1, d1)
        hn = rmsnorm(x, L['mix_norm'])
        if i % 2 == 0:
            x = x + conv_gmlp_mixer(hn, *L['mix'])
        else:
            x = x + nsa_mixer(hn, cos, sin, *L['mix'])
        n2, g2, u2, d2 = L['ffn2']
        x = x + 0.5 * swiglu(rmsnorm(x, n2), g2, u2, d2)
    return rmsnorm(x, final_norm)
```

```python
import os
import numpy as np
import concourse.bass as bass
import concourse.mybir as mybir
from concourse.bass_utils import run_bass_kernel_spmd

F32 = mybir.dt.float32
BF16 = mybir.dt.bfloat16
I32 = mybir.dt.int32
AF = mybir.ActivationFunctionType
ALU = mybir.AluOpType

_ES = [None]
_UID = [0]


def SB(nc, name, shape, dtype=None):
    dtype = F32 if dtype is None else dtype
    _UID[0] += 1
    nm = '%s_%d' % (name, _UID[0])
    if _ES[0] is None:
        return nc.alloc_sbuf_tensor(nm, list(shape), dtype)
    return _ES[0].enter_context(nc.sbuf_tensor(nm, list(shape), dtype))


def mk_dt(nc, over, pre=''):
    def dt(name, shape, kind="ExternalInput", dtype=F32):
        if over is not None and name in over:
            return over[name]
        return nc.dram_tensor(pre + name, shape, dtype, kind=kind).ap()
    return dt


D = 2048
DFF = 5632
NCORES = 8
EPS = 1e-6


class KB:
    NS = 6

    def __init__(self, nc):
        self.nc = nc
        self.eng = dict(pe=nc.tensor, dve=nc.vector, act=nc.scalar, pool=nc.gpsimd, sp=nc.sync)
        self.sem = {}
        self.cnt = {}
        for e in ('pe', 'dve', 'act', 'pool'):
            self.sem[e] = nc.alloc_semaphore('c_' + e)
            self.cnt[e] = 0
        self.dsem = {q: [nc.alloc_semaphore('d_%s%d' % (q, i)) for i in range(self.NS)]
                     for q in ('sp', 'pool', 'act')}
        self.dcnt = {q: 0 for q in self.dsem}
        self.seen = {e: {} for e in self.eng}
        self.st = {}
        self.semobj = {}
        for s in list(self.sem.values()) + [x for v in self.dsem.values() for x in v]:
            self.semobj[s.num] = s
        self.nwait = 0

    def _deps(self, r, w):
        deps = {}

        def add(tok):
            if tok is None:
                return
            s, v = tok
            if deps.get(s, 0) < v:
                deps[s] = v
        for k in r:
            st = self.st.get(k)
            if st:
                add(st[0])
        for k in w:
            st = self.st.get(k)
            if st:
                add(st[0])
                for s, v in st[1].items():
                    add((s, v))
        return deps

    def _emit_waits(self, e, deps, skip_sem=None):
        eng = self.eng[e]
        seen = self.seen[e]
        for s, v in deps.items():
            if skip_sem is not None and s == skip_sem:
                continue
            if seen.get(s, 0) >= v:
                continue
            eng.wait_ge(self.semobj[s], v)
            seen[s] = v
            self.nwait += 1

    def _commit(self, tok, r, w):
        for k in r:
            st = self.st.setdefault(k, [None, {}])
            if st[1].get(tok[0], 0) < tok[1]:
                st[1][tok[0]] = tok[1]
        for k in w:
            self.st[k] = [tok, {}]

    def op(self, e, fn, r=(), w=()):
        deps = self._deps(r, w)
        self._emit_waits(e, deps, skip_sem=(self.sem['pe'].num if e == 'pe' else None))
        inst = fn(self.eng[e])
        self.cnt[e] += 1
        inst.then_inc(self.sem[e], 1)
        tok = (self.sem[e].num, self.cnt[e])
        self._commit(tok, r, w)
        return tok

    def mm(self, out, pairs, r=(), w=()):
        deps = self._deps(r, w)
        self._emit_waits('pe', deps, skip_sem=self.sem['pe'].num)
        n = len(pairs)
        inst = None
        for i, (lhsT, rhs) in enumerate(pairs):
            inst = self.nc.tensor.matmul(out, lhsT, rhs, start=(i == 0), stop=(i == n - 1))
        self.cnt['pe'] += 1
        inst.then_inc(self.sem['pe'], 1)
        tok = (self.sem['pe'].num, self.cnt['pe'])
        self._commit(tok, r, w)
        return tok

    def mm1(self, out, lhsT, rhs, start, stop, r=(), w=()):
        return self.op('pe', lambda e: e.matmul(out, lhsT, rhs, start=start, stop=stop), r=r, w=w)

    def dma(self, q, out, in_, r=(), w=(), **kw):
        deps = self._deps(r, w)
        i = self.dcnt[q]
        self.dcnt[q] += 1
        s = self.dsem[q][i % self.NS]
        rnd = i // self.NS
        if rnd > 0:
            deps[s.num] = max(deps.get(s.num, 0), 16 * rnd)
        self._emit_waits(q, deps)
        inst = self.eng[q].dma_start(out=out, in_=in_, **kw)
        inst.then_inc(s, 16)
        tok = (s.num, 16 * (rnd + 1))
        self._commit(tok, r, w)
        return tok

    def barrier(self):
        deps = {}
        for e, sm in self.sem.items():
            if self.cnt[e] > 0:
                deps[sm.num] = self.cnt[e]
        for q, sl in self.dsem.items():
            n = self.dcnt[q]
            for i, sm in enumerate(sl):
                k = (n - 1 - i) // self.NS + 1 if n > i else 0
                if k > 0:
                    deps[sm.num] = 16 * k
        for e in self.eng:
            self._emit_waits(e, dict(deps))

    def finish(self, e='sp'):
        deps = {}
        for k, st in self.st.items():
            if st[0] is not None:
                s, v = st[0]
                if deps.get(s, 0) < v:
                    deps[s] = v
        self._emit_waits(e, deps)


class Core:
    def __init__(self, nc, T, kb=None, ps=None):
        self.nc = nc
        self.kb = KB(nc) if kb is None else kb
        self.T = T
        self.TB = [(i, min(512, T - i)) for i in range(0, T, 512)]
        self.xT = SB(nc, 'xT', [128, 16, T], F32)
        self.hT = SB(nc, 'hT', [128, 16, T], BF16)
        self.wg = [SB(nc, 'wg%d' % i, [128, 16, 256], BF16) for i in range(2)]
        self.wu = [SB(nc, 'wu%d' % i, [128, 16, 256], BF16) for i in range(2)]
        self.wd = [SB(nc, 'wd%d' % i, [128, 2, 2048], BF16) for i in range(4)]
        self.actT = [SB(nc, 'actT%d' % i, [128, 4, T], BF16) for i in range(2)]
        self.tmp = [SB(nc, 'tmp%d' % i, [128, 512], F32) for i in range(2)]
        self.sq = [SB(nc, 'sq%d' % i, [128, 512], BF16) for i in range(2)]
        self.rstd = SB(nc, 'rstd', [128, 512], F32)
        self.ones = SB(nc, 'ones', [128, 128], BF16)
        self.ps = nc.alloc_psum_tensor('ps', [128, 8 * 512], F32) if ps is None else ps
        self.ntmp = 0
        self.nsq = 0
        self.nwt = 0
        self.nwd = 0
        self.nact = 0
        self.kb.op('dve', lambda e: e.memset(self.ones[:], 1.0), w=['ones'])
        self.half_sb = SB(nc, 'half', [128, 1], F32)
        self.kb.op('dve', lambda e: e.memset(self.half_sb[:], 0.5), w=['half'])
        self.eps_sb = SB(nc, 'eps', [128, 1], F32)
        self.kb.op('dve', lambda e: e.memset(self.eps_sb[:], EPS), w=['eps'])

    def bank(self, b, n=512):
        return self.ps[:, b * 512:b * 512 + n]


def rmsnorm_T(c, g_sb, gkey, out=None, okey='hT', src=None, skey='xT', bank=6):
    kb = c.kb
    out = c.hT if out is None else out
    src = c.xT if src is None else src
    for (t0, tn) in c.TB:
        for kc in range(16):
            i = c.nsq % 2
            c.nsq += 1
            sq = c.sq[i]
            kb.op('act', lambda e, kc=kc, sq=sq: e.activation(out=sq[:, :tn], in_=src[:, kc, t0:t0 + tn],
                                                             func=AF.Square),
                  r=[(skey, kc, t0)], w=[('sq', i)])
            kb.mm1(c.bank(bank, tn), c.ones[:], sq[:, :tn], kc == 0, kc == 15,
                   r=['ones', ('sq', i)], w=([('ps', bank)] if kc in (0, 15) else []))
        kb.op('act', lambda e: e.activation(out=c.rstd[:, :tn], in_=c.bank(bank, tn), func=AF.Sqrt,
                                            scale=1.0 / D, bias=c.eps_sb[:, 0:1]),
              r=[('ps', bank), 'eps'], w=['rstd'])
        kb.op('dve', lambda e: e.reciprocal(out=c.rstd[:, :tn], in_=c.rstd[:, :tn]), r=['rstd'], w=['rstd'])
        for kc in range(16):
            kb.op('dve', lambda e, kc=kc: e.scalar_tensor_tensor(
                out=out[:, kc, t0:t0 + tn], in0=src[:, kc, t0:t0 + tn], scalar=g_sb[:, kc:kc + 1],
                in1=c.rstd[:, :tn], op0=ALU.mult, op1=ALU.mult),
                r=[(skey, kc, t0), 'rstd', gkey], w=[(okey, kc, t0)])


def ffn_T(c, wg_d, wu_d, wd_d, dff=DFF):
    kb = c.kb
    nc = c.nc
    wg_v = wg_d.rearrange("(kc p) f -> p kc f", p=128)
    wu_v = wu_d.rearrange("(kc p) f -> p kc f", p=128)
    NFB = dff // 512
    gbank = 0
    dbank = 0
    for fb in range(NFB):
        ab = c.nact % 2
        c.nact += 1
        actT = c.actT[ab]
        wd_tiles = []
        for half in range(2):
            wt = fb * 2 + half
            wb = c.nwt % 2
            c.nwt += 1
            kb.dma('pool', c.wg[wb][:], wg_v[:, :, wt * 256:(wt + 1) * 256], w=[('wg', wb)])
            kb.dma('pool', c.wu[wb][:], wu_v[:, :, wt * 256:(wt + 1) * 256], w=[('wu', wb)])
            db = c.nwd % 4
            c.nwd += 1
            kb.dma('pool', c.wd[db][:],
                   wd_d[wt * 256:(wt + 1) * 256, :].rearrange("(fc p) d -> p fc d", p=128), w=[('wd', db)])
            wd_tiles.append(db)
            for j in range(2):
                fcl = half * 2 + j
                for (t0, tn) in c.TB:
                    bg = gbank % 4
                    bu = (gbank + 1) % 4
                    gbank += 2
                    kb.mm(c.bank(bg, tn), [(c.wg[wb][:, kc, j * 128:(j + 1) * 128], c.hT[:, kc, t0:t0 + tn])
                                           for kc in range(16)],
                          r=[('wg', wb)] + [('hT', kc, t0) for kc in range(16)], w=[('ps', bg)])
                    kb.mm(c.bank(bu, tn), [(c.wu[wb][:, kc, j * 128:(j + 1) * 128], c.hT[:, kc, t0:t0 + tn])
                                           for kc in range(16)],
                          r=[('wu', wb)] + [('hT', kc, t0) for kc in range(16)], w=[('ps', bu)])
                    ti = c.ntmp % 2
                    c.ntmp += 1
                    tmp = c.tmp[ti]
                    kb.op('act', lambda e: e.activation(out=tmp[:, :tn], in_=c.bank(bg, tn), func=AF.Silu),
                          r=[('ps', bg)], w=[('tmp', ti)])
                    kb.op('dve', lambda e: e.tensor_tensor(out=actT[:, fcl, t0:t0 + tn], in0=c.bank(bu, tn),
                                                           in1=tmp[:, :tn], op=ALU.mult),
                          r=[('ps', bu), ('tmp', ti)], w=[('actT', ab, fcl, t0)])
        for dc in range(16):
            for (t0, tn) in c.TB:
                bd = 4 + dbank % 2
                dbank += 1
                kb.mm(c.bank(bd, tn),
                      [(c.wd[wd_tiles[fcl // 2]][:, fcl % 2, dc * 128:(dc + 1) * 128], actT[:, fcl, t0:t0 + tn])
                       for fcl in range(4)],
                      r=[('wd', wd_tiles[0]), ('wd', wd_tiles[1])] + [('actT', ab, fcl, t0) for fcl in range(4)],
                      w=[('ps', bd)])
                if True:
                    kb.op('dve', lambda e: e.scalar_tensor_tensor(
                        out=c.xT[:, dc, t0:t0 + tn], in0=c.bank(bd, tn), scalar=c.half_sb[:, 0:1], in1=c.xT[:, dc, t0:t0 + tn],
                        op0=ALU.mult, op1=ALU.add),
                        r=[('ps', bd), ('xT', dc, t0), 'half'], w=[('xT', dc, t0)])


def load_small(c, name, d_ap, shape, dtype=F32, q='sp'):
    t = SB(c.nc, name, list(shape), dtype)
    c.kb.dma(q, t[:], d_ap, w=[name])
    return t


def load_xT(c, x_d):
    v = x_d.rearrange("(kc p) t -> p kc t", p=128)
    for kc in range(0, 16, 4):
        c.kb.dma('sp', c.xT[:, kc:kc + 4, :], v[:, kc:kc + 4, :],
                 w=[('xT', k, t0) for k in range(kc, kc + 4) for (t0, _) in c.TB])


def store_T(c, out_d, src, skey):
    v = out_d.rearrange("(kc p) t -> p kc t", p=128)
    for kc in range(0, 16, 4):
        c.kb.dma('sp', v[:, kc:kc + 4, :], src[:, kc:kc + 4, :],
                 r=[(skey, k, t0) for k in range(kc, kc + 4) for (t0, _) in c.TB], w=[('out', kc)])


def build_ffn_test(T, dff=DFF, stage=2):
    nc = bass.Bass("TRN2", target_bir_lowering=False)
    x_d = nc.dram_tensor("xTd", [D, T], F32, kind="ExternalInput").ap()
    g1 = nc.dram_tensor("g1", [128, 16], F32, kind="ExternalInput").ap()
    g2 = nc.dram_tensor("g2", [128, 16], F32, kind="ExternalInput").ap()
    wg = nc.dram_tensor("wgd", [D, dff], F32, kind="ExternalInput").ap()
    wu = nc.dram_tensor("wud", [D, dff], F32, kind="ExternalInput").ap()
    wd = nc.dram_tensor("wdd", [dff, D], F32, kind="ExternalInput").ap()
    y_d = nc.dram_tensor("yTd", [D, T], BF16, kind="ExternalOutput").ap()
    c = Core(nc, T)
    g1s = load_small(c, 'g1s', g1, [128, 16])
    g2s = load_small(c, 'g2s', g2, [128, 16])
    load_xT(c, x_d)
    rmsnorm_T(c, g1s, 'g1s')
    if stage >= 1:
        ffn_T(c, wg, wu, wd, dff)
    if stage >= 2:
        rmsnorm_T(c, g2s, 'g2s')
    store_T(c, y_d, c.hT, 'hT')
    c.kb.finish('sp')
    return nc


GELU_C = 0.7978845608028654
RING = [('wg', 0), ('wu', 0), ('wg', 1), ('wu', 1)]


def load_wt(c, w_v, col0, ncols, nkc=16):
    i = getattr(c, 'nring', 0)
    c.nring = i + 1
    name, b = RING[i % 4]
    buf = c.wg[b] if name == 'wg' else c.wu[b]
    c.kb.dma('pool', buf[:, :nkc, :ncols], w_v[:, :, col0:col0 + ncols], w=[(name, b)])
    return buf, (name, b)


def extra_tiles(c):
    nc = c.nc
    c.stg = [SB(nc, 'stg%d' % i, [128, 512], F32) for i in range(2)]
    c.ga = c.tmp[0]
    c.gb = c.tmp[1]
    c.nstg = 0
    c.c1 = SB(nc, 'c1', [128, 1], F32)
    c.kb.op('dve', lambda e: e.memset(c.c1[:], 1.0), w=['c1'])
    c.cg = SB(nc, 'cg', [128, 1], F32)
    c.kb.op('dve', lambda e: e.memset(c.cg[:], 0.044715), w=['cg'])


def gelu_from(c, src, skey, out, okey, P, n):
    kb = c.kb
    kb.op('act', lambda e: e.activation(out=c.ga[:P, :n], in_=src, func=AF.Square), r=[skey], w=[('tmp', 0)])
    kb.op('dve', lambda e: e.tensor_scalar(out=c.ga[:P, :n], in0=c.ga[:P, :n], scalar1=c.cg[:P, 0:1],
                                           scalar2=c.c1[:P, 0:1], op0=ALU.mult, op1=ALU.add),
          r=[('tmp', 0), 'cg', 'c1'], w=[('tmp', 0)])
    kb.op('dve', lambda e: e.tensor_tensor(out=c.ga[:P, :n], in0=src, in1=c.ga[:P, :n], op=ALU.mult),
          r=[skey, ('tmp', 0)], w=[('tmp', 0)])
    kb.op('act', lambda e: e.activation(out=c.gb[:P, :n], in_=c.ga[:P, :n], func=AF.Sigmoid, scale=2.0 * GELU_C),
          r=[('tmp', 0)], w=[('tmp', 1)])
    kb.op('dve', lambda e: e.tensor_tensor(out=out, in0=src, in1=c.gb[:P, :n], op=ALU.mult),
          r=[skey, ('tmp', 1)], w=[okey])


def build_L1(T=1024, dff=DFF, nc=None, over=None, core=None):
    fused = nc is not None
    nc = bass.Bass("TRN2", target_bir_lowering=False) if nc is None else nc
    dt = mk_dt(nc, over)
    x_d = dt("xTd", [D, T])
    g1 = dt("g1", [128, 16])
    g2 = dt("g2", [128, 16])
    wg = dt("wgd", [D, dff])
    wu = dt("wud", [D, dff])
    wd = dt("wdd", [dff, D])
    w_in = dt("w_in", [D, 4096])
    lng = dt("lng", [1, 1024])
    lnb = dt("lnb", [1, 1024])
    wsT_d = dt("wsT", [128, 8, 128])
    tri_d = dt("tri", [128, 128])
    bs_d = dt("bs", [1, 1024])
    x1_o = dt("x1T", [D, T], "ExternalOutput")
    a_o = dt("aT", [1024, T], "ExternalOutput")
    bo_o = dt("boT", [1024, T], "ExternalOutput")
    c = Core(nc, T) if core is None else core
    kb = c.kb
    extra_tiles(c)
    g1s = load_small(c, 'g1s', g1, [128, 16])
    g2s = load_small(c, 'g2s', g2, [128, 16])
    lng_s = load_small(c, 'lng_s', lng.partition_broadcast(128), [128, 1024])
    lnb_s = load_small(c, 'lnb_s', lnb.partition_broadcast(128), [128, 1024])
    bs_s = load_small(c, 'bs_s', bs_d.partition_broadcast(128), [128, 1024])
    wsT_f = load_small(c, 'wsT_f', wsT_d, [128, 8, 128], BF16, q='pool')
    tri_s = load_small(c, 'tri_s', tri_d, [128, 128], BF16, q='pool')
    wsT_m = SB(nc, 'wsT_m', [128, 8, 128], BF16)
    for g in range(8):
        kb.op('dve', lambda e, g=g: e.tensor_tensor(out=wsT_m[:, g, :], in0=wsT_f[:, g, :], in1=tri_s[:],
                                                    op=ALU.mult), r=['wsT_f', 'tri_s'], w=[('wsT_m', g)])
    load_xT(c, x_d)
    rmsnorm_T(c, g1s, 'g1s')
    ffn_T(c, wg, wu, wd, dff)
    store_T(c, x1_o, c.xT, 'xT')
    rmsnorm_T(c, g2s, 'g2s')
    w_v = w_in.rearrange("(kc p) f -> p kc f", p=128)
    hkeys = lambda t0: [('hT', kc, t0) for kc in range(16)]
    gbank = 0
    for ip in range(4):
        bv, kv = load_wt(c, w_v, ip * 256, 256)
        bg_, kg = load_wt(c, w_v, 1024 + ip * 256, 256)
        for j in range(2):
            ch = ip * 2 + j
            for (t0, tn) in c.TB:
                b0 = gbank % 4
                b1 = (gbank + 1) % 4
                gbank += 2
                kb.mm(c.bank(b0, tn), [(bv[:, kc, j * 128:(j + 1) * 128], c.hT[:, kc, t0:t0 + tn]) for kc in range(16)],
                      r=[kv] + hkeys(t0), w=[('ps', b0)])
                kb.mm(c.bank(b1, tn), [(bg_[:, kc, j * 128:(j + 1) * 128], c.hT[:, kc, t0:t0 + tn]) for kc in range(16)],
                      r=[kg] + hkeys(t0), w=[('ps', b1)])
                ti = c.ntmp % 2
                c.ntmp += 1
                si = c.nstg % 2
                c.nstg += 1
                kb.op('act', lambda e: e.activation(out=c.tmp[ti][:, :tn], in_=c.bank(b1, tn), func=AF.Sigmoid),
                      r=[('ps', b1)], w=[('tmp', ti)])
                kb.op('dve', lambda e: e.tensor_tensor(out=c.stg[si][:, :tn], in0=c.bank(b0, tn), in1=c.tmp[ti][:, :tn],
                                                       op=ALU.mult), r=[('ps', b0), ('tmp', ti)], w=[('stg', si)])
                kb.dma('sp', a_o[ch * 128:(ch + 1) * 128, t0:t0 + tn], c.stg[si][:, :tn], r=[('stg', si)],
                       w=[('a_o', ch, t0)])
    for ip in range(4):
        bu_, ku = load_wt(c, w_v, 2048 + ip * 256, 256)
        for j in range(2):
            g = ip * 2 + j
            for (t0, tn) in c.TB:
                b0 = gbank % 4
                gbank += 1
                kb.mm(c.bank(b0, tn), [(bu_[:, kc, j * 128:(j + 1) * 128], c.hT[:, kc, t0:t0 + tn]) for kc in range(16)],
                      r=[ku] + hkeys(t0), w=[('ps', b0)])
                gelu_from(c, c.bank(b0, tn), ('ps', b0), c.actT[g // 4][:, g % 4, t0:t0 + tn], ('uT', g, t0), 128, tn)
    vg = SB(nc, 'vg', [128, 256], F32)
    vln = SB(nc, 'vln', [128, 256], BF16)
    stats = SB(nc, 'stats', [128, 6], F32)
    mv = SB(nc, 'mv', [128, 2], F32)
    NT = T // 128
    for ip in range(4):
        bw, kw_ = load_wt(c, w_v, 3072 + ip * 256, 256)
        for tt in range(NT):
            t0b = (tt * 128 // 512) * 512
            b0 = gbank % 4
            gbank += 1
            kb.mm(c.bank(b0, 256), [(c.hT[:, kc, tt * 128:(tt + 1) * 128], bw[:, kc, 0:256]) for kc in range(16)],
                  r=[kw_] + hkeys(t0b), w=[('ps', b0)])
            gelu_from(c, c.bank(b0, 256), ('ps', b0), vg[:, :], 'vg', 128, 256)
            for gg in range(2):
                g = ip * 2 + gg
                sl = slice(gg * 128, (gg + 1) * 128)
                gsl = slice(g * 128, (g + 1) * 128)
                kb.op('dve', lambda e: e.bn_stats(out=stats[:], in_=vg[:, sl]), r=['vg'], w=['stats'])
                kb.op('dve', lambda e: e.bn_aggr(out=mv[:], in_=stats[:]), r=['stats'], w=['mv'])
                kb.op('act', lambda e: e.activation(out=mv[:, 1:2], in_=mv[:, 1:2], func=AF.Sqrt, bias=c.eps_sb[:, 0:1]),
                      r=['mv', 'eps'], w=['mv'])
                kb.op('dve', lambda e: e.reciprocal(out=mv[:, 1:2], in_=mv[:, 1:2]), r=['mv'], w=['mv'])
                kb.op('dve', lambda e: e.tensor_scalar(out=vg[:, sl], in0=vg[:, sl], scalar1=mv[:, 0:1],
                                                       scalar2=mv[:, 1:2], op0=ALU.subtract, op1=ALU.mult),
                      r=['vg', 'mv'], w=['vg'])
                kb.op('dve', lambda e: e.tensor_tensor(out=vg[:, sl], in0=vg[:, sl], in1=lng_s[:, gsl], op=ALU.mult),
                      r=['vg', 'lng_s'], w=['vg'])
                kb.op('dve', lambda e: e.tensor_tensor(out=vln[:, sl], in0=vg[:, sl], in1=lnb_s[:, gsl], op=ALU.add),
                      r=['vg', 'lnb_s'], w=[('vln', gg)])
                b1 = 4 + (gbank % 2)
                gbank += 1
                kb.mm(c.bank(b1, 128), [(vln[:, sl], wsT_m[:, g, :])], r=[('vln', gg), ('wsT_m', g)], w=[('ps', b1)])
                si = c.nstg % 2
                c.nstg += 1
                kb.op('dve', lambda e: e.tensor_tensor(out=c.stg[si][:, :128], in0=c.bank(b1, 128), in1=bs_s[:, gsl],
                                                       op=ALU.add), r=[('ps', b1), 'bs_s'], w=[('stg', si)])
                kb.op('dve', lambda e: e.tensor_tensor(out=c.stg[si][:, :128], in0=c.stg[si][:, :128],
                                                       in1=c.actT[g // 4][:, g % 4, tt * 128:(tt + 1) * 128], op=ALU.mult),
                      r=[('stg', si), ('uT', g, t0b)], w=[('stg', si)])
                kb.dma('sp', bo_o[g * 128:(g + 1) * 128, tt * 128:(tt + 1) * 128], c.stg[si][:, :128], r=[('stg', si)],
                       w=[('bo_o', g, tt)])
    if fused:
        kb.barrier()
        return nc
    kb.finish('sp')
    return nc


def proj_out_T(c, w_d, ncols_total, out_d, gbank0=0):
    kb = c.kb
    w_v = w_d.rearrange("(kc p) f -> p kc f", p=128)
    gbank = gbank0
    col = 0
    while col < ncols_total:
        ncol_t = min(256, ncols_total - col)
        buf, key = load_wt(c, w_v, col, ncol_t)
        j0 = 0
        while j0 < ncol_t:
            m = min(128, ncol_t - j0)
            for (t0, tn) in c.TB:
                b0 = gbank % 4
                gbank += 1
                kb.mm(c.ps[:m, b0 * 512:b0 * 512 + tn],
                      [(buf[:, kc, j0:j0 + m], c.hT[:, kc, t0:t0 + tn]) for kc in range(16)],
                      r=[key] + [('hT', kc, t0) for kc in range(16)], w=[('ps', b0)])
                si = c.nstg % 2
                c.nstg += 1
                if si == 0:
                    kb.op('act', lambda e: e.activation(out=c.stg[si][:m, :tn], in_=c.ps[:m, b0 * 512:b0 * 512 + tn],
                                                        func=AF.Copy), r=[('ps', b0)], w=[('stg', si)])
                else:
                    kb.op('dve', lambda e: e.tensor_copy(out=c.stg[si][:m, :tn], in_=c.ps[:m, b0 * 512:b0 * 512 + tn]),
                          r=[('ps', b0)], w=[('stg', si)])
                kb.dma('sp', out_d[col + j0:col + j0 + m, t0:t0 + tn], c.stg[si][:m, :tn], r=[('stg', si)],
                       w=[('z_o', col + j0, t0)])
            j0 += m
        col += ncol_t
    return gbank


def proj_resid_T(c, w_d, gbank0=0):
    kb = c.kb
    w_v = w_d.rearrange("(kc p) f -> p kc f", p=128)
    gbank = gbank0
    for dcp in range(8):
        buf, key = load_wt(c, w_v, dcp * 256, 256)
        for j in range(2):
            dc = dcp * 2 + j
            for (t0, tn) in c.TB:
                b0 = gbank % 4
                gbank += 1
                kb.mm(c.bank(b0, tn), [(buf[:, kc, j * 128:(j + 1) * 128], c.hT[:, kc, t0:t0 + tn]) for kc in range(16)],
                      r=[key] + [('hT', kc, t0) for kc in range(16)], w=[('ps', b0)])
                kb.op('dve', lambda e: e.tensor_tensor(out=c.xT[:, dc, t0:t0 + tn], in0=c.bank(b0, tn),
                                                       in1=c.xT[:, dc, t0:t0 + tn], op=ALU.add),
                      r=[('ps', b0), ('xT', dc, t0)], w=[('xT', dc, t0)])
    return gbank


def build_L2(T=1024, dff=DFF, nz=5168, nc=None, over=None, core=None):
    fused = nc is not None
    nc = bass.Bass("TRN2", target_bir_lowering=False) if nc is None else nc
    dt = mk_dt(nc, over)
    HALO = 32
    x_d = dt("x1Td", [D, T])
    a_d = None if fused else dt("aTh", [1024, HALO + T])
    bo_d = dt("boTd", [1024, T])
    cw_d = dt("conv_w", [128, 8, 31])
    cb_d = dt("conv_b", [128, 8])
    cg_d = dt("cln_g", [128, 8])
    cbb_d = dt("cln_b", [128, 8])
    wout_d = dt("w_out", [D, D])
    gA = dt("gA", [128, 16]); wgA = dt("wgA", [D, dff]); wuA = dt("wuA", [D, dff]); wdA = dt("wdA", [dff, D])
    gB = dt("gB", [128, 16]); wgB = dt("wgB", [D, dff]); wuB = dt("wuB", [D, dff]); wdB = dt("wdB", [dff, D])
    gM = dt("gM", [128, 16])
    win_d = dt("w_in1", [D, nz])
    x4_o = dt("x4T", [D, T], "ExternalOutput")
    z_o = dt("zT", [nz, T], "ExternalOutput")
    c = Core(nc, T) if core is None else core
    kb = c.kb
    extra_tiles(c)
    cw = load_small(c, 'cw', cw_d, [128, 8, 31])
    cb = load_small(c, 'cb', cb_d, [128, 8])
    cg = load_small(c, 'cgn', cg_d, [128, 8])
    cbb = load_small(c, 'cbb', cbb_d, [128, 8])
    gAs = load_small(c, 'gAs', gA, [128, 16])
    gBs = load_small(c, 'gBs', gB, [128, 16])
    gMs = load_small(c, 'gMs', gM, [128, 16])
    kinv = SB(nc, 'kinv', [128, 1], F32)
    kb.op('dve', lambda e: e.memset(kinv[:], 1.0 / 1024), w=['kinv'])
    load_xT(c, x_d)
    kb.dma('pool', c.hT[:, 8:16, :], bo_d.rearrange("(g p) t -> p g t", p=128),
           w=[('hT', k, t0) for k in range(8, 16) for (t0, _) in c.TB])
    abuf = [c.wg[i][:].rearrange("p a b -> p (a b)").bitcast(F32) for i in range(2)]
    ybuf = [c.wd[i][:].rearrange("p a b -> p (a b)").bitcast(F32) for i in range(4)]
    actkeys = lambda i: [('actT', i, fcl, t0) for fcl in range(4) for (t0, _) in c.TB]
    mr = c.actT[0][:].rearrange("p a b -> p (a b)").bitcast(F32)
    mean = mr[:, 0:T]
    rstd = mr[:, T:2 * T]
    S1 = [6, 7]
    S2 = [4, 5]
    for ch in range(8):
        ab = ch % 2
        a_sb = abuf[ab]
        if not fused:
            kb.dma('sp', a_sb[:, 0:HALO + T], a_d[ch * 128:(ch + 1) * 128, :], w=[('wg', ab)])
        else:
            if over.get('aT_prev') is None:
                kb.op('dve', lambda e: e.memset(a_sb[:, 0:HALO], 0.0), w=[('wg', ab)])
            else:
                kb.dma('sp', a_sb[:, 0:HALO], over['aT_prev'][ch * 128:(ch + 1) * 128, T - HALO:T], w=[('wg', ab)])
            kb.dma('sp', a_sb[:, HALO:HALO + T], over['aT_cur'][ch * 128:(ch + 1) * 128, :], r=[('wg', ab)],
                   w=[('wg', ab)])
        y = ybuf[ch // 2][:, (ch % 2) * T:(ch % 2) * T + T]
        ykey = ('wd', ch // 2)
        for k in range(31):
            if k == 0:
                kb.op('dve', lambda e: e.tensor_scalar(out=y, in0=a_sb[:, 2:2 + T], scalar1=cw[:, ch, 0:1], scalar2=None,
                                                       op0=ALU.mult), r=[('wg', ab), 'cw'], w=[ykey])
            else:
                kb.op('dve', lambda e, k=k: e.scalar_tensor_tensor(out=y, in0=a_sb[:, 2 + k:2 + k + T],
                                                                  scalar=cw[:, ch, k:k + 1], in1=y,
                                                                  op0=ALU.mult, op1=ALU.add),
                      r=[('wg', ab), 'cw', ykey], w=[ykey])
        kb.op('dve', lambda e: e.tensor_scalar(out=y, in0=y, scalar1=cb[:, ch:ch + 1], scalar2=None, op0=ALU.add),
              r=[ykey, 'cb'], w=[ykey])
        for bi, (t0, tn) in enumerate(c.TB):
            i = c.nsq % 2
            c.nsq += 1
            kb.op('act', lambda e: e.activation(out=c.sq[i][:, :tn], in_=y[:, t0:t0 + tn], func=AF.Copy),
                  r=[ykey], w=[('sq', i)])
            kb.mm1(c.bank(S1[bi], tn), c.ones[:], c.sq[i][:, :tn], ch == 0, ch == 7,
                   r=['ones', ('sq', i)], w=([('ps', S1[bi])] if ch in (0, 7) else []))
            i = c.nsq % 2
            c.nsq += 1
            kb.op('act', lambda e: e.activation(out=c.sq[i][:, :tn], in_=y[:, t0:t0 + tn], func=AF.Square),
                  r=[ykey], w=[('sq', i)])
            kb.mm1(c.bank(S2[bi], tn), c.ones[:], c.sq[i][:, :tn], ch == 0, ch == 7,
                   r=['ones', ('sq', i)], w=([('ps', S2[bi])] if ch in (0, 7) else []))
    for bi, (t0, tn) in enumerate(c.TB):
        kb.op('act', lambda e: e.activation(out=mean[:, t0:t0 + tn], in_=c.bank(S1[bi], tn), func=AF.Copy,
                                            scale=1.0 / 1024), r=[('ps', S1[bi])], w=actkeys(0))
        kb.op('dve', lambda e: e.tensor_tensor(out=c.tmp[0][:, :tn], in0=mean[:, t0:t0 + tn], in1=mean[:, t0:t0 + tn],
                                               op=ALU.mult), r=actkeys(0), w=[('tmp', 0)])
        kb.op('dve', lambda e: e.scalar_tensor_tensor(out=rstd[:, t0:t0 + tn], in0=c.bank(S2[bi], tn), scalar=kinv[:, 0:1],
                                                      in1=c.tmp[0][:, :tn], op0=ALU.mult, op1=ALU.subtract),
              r=[('ps', S2[bi]), 'kinv', ('tmp', 0)], w=actkeys(0))
        kb.op('act', lambda e: e.activation(out=rstd[:, t0:t0 + tn], in_=rstd[:, t0:t0 + tn], func=AF.Sqrt,
                                            bias=c.eps_sb[:, 0:1]), r=actkeys(0) + ['eps'], w=actkeys(0))
        kb.op('dve', lambda e: e.reciprocal(out=rstd[:, t0:t0 + tn], in_=rstd[:, t0:t0 + tn]), r=actkeys(0), w=actkeys(0))
    for ch in range(8):
        y = ybuf[ch // 2][:, (ch % 2) * T:(ch % 2) * T + T]
        ykey = ('wd', ch // 2)
        for (t0, tn) in c.TB:
            kb.op('dve', lambda e: e.tensor_tensor(out=y[:, t0:t0 + tn], in0=y[:, t0:t0 + tn], in1=mean[:, t0:t0 + tn],
                                                   op=ALU.subtract), r=[ykey] + actkeys(0), w=[ykey])
            kb.op('dve', lambda e: e.tensor_tensor(out=y[:, t0:t0 + tn], in0=y[:, t0:t0 + tn], in1=rstd[:, t0:t0 + tn],
                                                   op=ALU.mult), r=[ykey] + actkeys(0), w=[ykey])
            kb.op('act', lambda e: e.activation(out=c.hT[:, ch, t0:t0 + tn], in_=y[:, t0:t0 + tn], func=AF.Silu,
                                                scale=cg[:, ch:ch + 1], bias=cbb[:, ch:ch + 1]),
                  r=[ykey, 'cgn', 'cbb'], w=[('hT', ch, t0)])
    gb_ = proj_resid_T(c, wout_d)
    rmsnorm_T(c, gAs, 'gAs')
    ffn_T(c, wgA, wuA, wdA, dff)
    rmsnorm_T(c, gBs, 'gBs')
    ffn_T(c, wgB, wuB, wdB, dff)
    store_T(c, x4_o, c.xT, 'xT')
    rmsnorm_T(c, gMs, 'gMs')
    proj_out_T(c, win_d, nz, z_o)
    if fused:
        kb.barrier()
        return nc
    kb.finish('sp')
    return nc


def build_L4(T=1024, dff=DFF, nc=None, over=None, core=None):
    fused = nc is not None
    nc = bass.Bass("TRN2", target_bir_lowering=False) if nc is None else nc
    dt = mk_dt(nc, over)
    x_d = dt("x4Td", [D, T])
    o_d = None if fused else dt("oTd", [D, T])
    wout_d = dt("w_out1", [D, D])
    gA = dt("gA", [128, 16]); wgA = dt("wgA", [D, dff]); wuA = dt("wuA", [D, dff]); wdA = dt("wdA", [dff, D])
    gF = dt("gF", [128, 16])
    y_o = dt("yT", [D, T], "ExternalOutput")
    c = Core(nc, T) if core is None else core
    kb = c.kb
    gAs = load_small(c, 'gAs', gA, [128, 16])
    gFs = load_small(c, 'gFs', gF, [128, 16])
    load_xT(c, x_d)
    if not fused:
        ov = o_d.rearrange("(kc p) t -> p kc t", p=128)
        for k0 in range(0, 16, 8):
            kb.dma('pool', c.hT[:, k0:k0 + 8, :], ov[:, k0:k0 + 8, :],
                   w=[('hT', k, t0) for k in range(k0, k0 + 8) for (t0, _) in c.TB])
    else:
        extra_tiles(c)
        fl = load_small(c, 'flag', over['flag'], [128, 2])
        x1v = over['x4T_1'].rearrange("(kc p) t -> p kc t", p=128)
        for kc in range(16):
            for (t0, tn) in c.TB:
                si = c.nstg % 2
                c.nstg += 1
                kb.dma('sp', c.stg[si][:, :tn], x1v[:, kc, t0:t0 + tn], w=[('stg', si)])
                kb.op('dve', lambda e: e.tensor_scalar(out=c.xT[:, kc, t0:t0 + tn], in0=c.xT[:, kc, t0:t0 + tn],
                                                       scalar1=fl[:, 0:1], scalar2=None, op0=ALU.mult),
                      r=[('xT', kc, t0), 'flag'], w=[('xT', kc, t0)])
                kb.op('dve', lambda e: e.scalar_tensor_tensor(out=c.xT[:, kc, t0:t0 + tn], in0=c.stg[si][:, :tn],
                                                              scalar=fl[:, 1:2], in1=c.xT[:, kc, t0:t0 + tn],
                                                              op0=ALU.mult, op1=ALU.add),
                      r=[('stg', si), ('xT', kc, t0), 'flag'], w=[('xT', kc, t0)])
        otm = [SB(nc, 'otm%d' % i, [128, 2048], BF16) for i in range(2)]
        otn = [SB(nc, 'otn%d' % i, [128, 2048], BF16) for i in range(2)]
        idb = load_small(c, 'idb4', over['ident'], [128, 128], BF16, q='pool')
        ntr = 0
        for tt in range(T // 128):
            i = tt % 2
            t0b = (tt * 128 // 512) * 512
            kb.dma('pool', otm[i][:], over['o_tok'][tt * 128:(tt + 1) * 128, :], w=[('otm', i)])
            kb.dma('pool', otn[i][:], over['o_tok1'][tt * 128:(tt + 1) * 128, :], w=[('otn', i)])
            kb.op('dve', lambda e: e.tensor_scalar(out=otm[i][:], in0=otm[i][:], scalar1=fl[:, 0:1], scalar2=None,
                                                   op0=ALU.mult), r=[('otm', i), 'flag'], w=[('otm', i)])
            kb.op('dve', lambda e: e.scalar_tensor_tensor(out=otm[i][:], in0=otn[i][:], scalar=fl[:, 1:2], in1=otm[i][:],
                                                          op0=ALU.mult, op1=ALU.add),
                  r=[('otn', i), ('otm', i), 'flag'], w=[('otm', i)])
            for k4 in range(4):
                tb = 2 + ntr % 2
                ntr += 1
                tbk = c.ps[:, tb * 512:(tb + 1) * 512].bitcast(BF16)
                pe_multi(kb, [(lambda e, j=j: e.transpose(out=tbk[:, j * 128:(j + 1) * 128],
                                                          in_=otm[i][:, (k4 * 4 + j) * 128:(k4 * 4 + j + 1) * 128],
                                                          identity=idb[:])) for j in range(4)],
                         r=[('otm', i), 'idb4'], w=[('ps', tb)])
                kb.op('dve', lambda e: e.tensor_copy(out=c.hT[:, k4 * 4:k4 * 4 + 4, tt * 128:(tt + 1) * 128],
                                                     in_=tbk[:, 0:512].rearrange("p (a b) -> p a b", b=128)),
                      r=[('ps', tb)], w=[('hT', k, t0b) for k in range(k4 * 4, k4 * 4 + 4)])
    proj_resid_T(c, wout_d)
    rmsnorm_T(c, gAs, 'gAs')
    ffn_T(c, wgA, wuA, wdA, dff)
    rmsnorm_T(c, gFs, 'gFs', out=c.xT, okey='xT')
    store_T(c, y_o, c.xT, 'xT')
    if fused:
        kb.barrier()
        return nc
    kb.finish('sp')
    return nc


S = 2048
NQT = 16
SCALE = 128 ** -0.5
MAGIC = 12582912.0
PI = 3.141592653589793


def pe_multi(kb, fns, r=(), w=()):
    deps = kb._deps(r, w)
    kb._emit_waits('pe', deps, skip_sem=kb.sem['pe'].num)
    inst = None
    for f in fns:
        inst = f(kb.nc.tensor)
    kb.cnt['pe'] += 1
    inst.then_inc(kb.sem['pe'], 1)
    tok = (kb.sem['pe'].num, kb.cnt['pe'])
    kb._commit(tok, r, w)
    return tok


def build_L3(nc=None, over=None, kb=None, ps=None, gp=0):
    fused = nc is not None
    nc = bass.Bass("TRN2", target_bir_lowering=False) if nc is None else nc
    dt = mk_dt(nc, over)
    if not fused:
        q_d = dt("qT", [8, 128, S]); qp_d = dt("qPT", [8, 128, S])
        ks_d = dt("ksT", [2, 128, S]); ksp_d = dt("ksPT", [2, 128, S])
        kw_d = dt("kwT", [2, 128, S]); kwp_d = dt("kwPT", [2, 128, S])
        kc_d = dt("kcT", [128, 2, S]); vc_d = dt("vcT", [128, 2, S])
        vs_d = dt("vs", [2, S, 128]); vw_d = dt("vw", [2, S, 128])
        gl_d = dt("gl", [S, 24])
    else:
        zT = over['zT']
        rows = lambda base, i: zT[base + i * 128:base + (i + 1) * 128, :]
        q_d = [rows(0, gp * 8 + hh) for hh in range(8)]
        qp_d = [None] * 8
        ks_d = [rows(3072, 2 * gp + g) for g in range(2)]; ksp_d = [None] * 2
        kw_d = [rows(4096, 2 * gp + g) for g in range(2)]; kwp_d = [None] * 2
        kc_d = zT[2048 + 2 * gp * 128:2048 + (2 * gp + 2) * 128, :].rearrange("(g p) s -> p g s", p=128)
        vc_d = zT[2560 + 2 * gp * 128:2560 + (2 * gp + 2) * 128, :].rearrange("(g p) s -> p g s", p=128)
        vsT_d = [rows(3584, 2 * gp + g) for g in range(2)]
        vwT_d = [rows(4608, 2 * gp + g) for g in range(2)]
        glT_d = zT[5120 + gp * 24:5120 + (gp + 1) * 24, :]
    pos_d = dt("pos", [1, S], dtype=I32)
    inv_d = dt("inv", [128, 1]); sgn_d = dt("sgn", [128, 1])
    pek_d = dt("pekT", [128, 32]); w1k_d = dt("w1k", [4096, 256]); w2k_d = dt("w2k", [256, 128])
    pev_d = dt("pevT", [128, 32]); w1v_d = dt("w1v", [4096, 256]); w2v_d = dt("w2v", [256, 128])
    ovl_d = dt("ovl", [128, 32]); cm_d = dt("cm", [128, 16, 128]); fbv_d = dt("fbv", [128, 16, 32])
    tri_d = dt("tri", [128, 128]); tri2_d = dt("tri2", [128, 128]); id_d = dt("ident", [128, 128])
    o_o = dt("o", [S, 1024], "ExternalOutput")
    kb = KB(nc) if kb is None else kb
    A = lambda name, shape, dtype=F32: SB(nc, 's_' + name, list(shape), dtype)

    def small(name, d_ap, shape, dtype=F32, q='sp'):
        t = A(name, shape, dtype)
        kb.dma(q, t[:], d_ap, w=[name])
        return t

    def const(name, val):
        t = A(name, [128, 1])
        kb.op('dve', lambda e: e.memset(t[:], val), w=[name])
        return t
    ps = nc.alloc_psum_tensor('ps', [128, 8 * 512], F32) if ps is None else ps
    bank = lambda b, n=512, p=128: ps[:p, b * 512:b * 512 + n]
    bankbf = lambda b: ps[:, b * 512:(b + 1) * 512].bitcast(BF16)
    H = S // 2
    xs = [A('xs%d' % i, [128, H]) for i in range(2)]
    xp = [A('xp%d' % i, [128, H]) for i in range(2)]
    inv = small('inv', inv_d, [128, 1]); sgn = small('sgn', sgn_d, [128, 1])
    ovl = small('ovl', ovl_d, [128, 32]); cm = small('cm', cm_d, [128, 16, 128]); fbv = small('fbv', fbv_d, [128, 16, 32])
    tri = small('tri', tri_d, [128, 128], BF16, 'pool'); tri2 = small('tri2', tri2_d, [128, 128], BF16, 'pool')
    idf = small('idf', id_d, [128, 128]); idb = small('idb', id_d, [128, 128], BF16, 'pool')
    if not fused:
        gl = small('gl', gl_d.rearrange("(n p) c -> p n c", p=128), [128, 16, 24])
        kb.op('act', lambda e: e.activation(out=gl[:], in_=gl[:], func=AF.Sigmoid), r=['gl'], w=['gl'])
    else:
        gl = A('gl', [128, 16, 24])
        for hf in range(2):
            kb.dma('sp', xs[hf][:24, :], glT_d[:, hf * H:(hf + 1) * H], w=[('xs', hf)])
        for kt in range(16):
            pe_multi(kb, [lambda e: e.transpose(out=bank(7, 24), in_=xs[kt // 8][:24, (kt % 8) * 128:(kt % 8 + 1) * 128],
                                                identity=idf[:24, :24])], r=[('xs', kt // 8), 'idf'], w=[('ps', 7)])
            kb.op('act', lambda e: e.activation(out=gl[:, kt, :], in_=bank(7, 24), func=AF.Sigmoid),
                  r=[('ps', 7)], w=['gl'])
    c_i2p = const('c_i2p', 1.0 / (2 * PI)); c_mag = const('c_mag', MAGIC); c_nmag = const('c_nmag', -MAGIC)
    c_n2p = const('c_n2p', -2 * PI); c_pi = const('c_pi', PI); c_npi = const('c_npi', -PI); c_hpi = const('c_hpi', PI / 2)
    c_tiny = const('c_tiny', 1e-30); c_one = const('c_one', 1.0); c_cg = const('c_cg', 0.044715)
    c_nsc = const('c_nsc', -SCALE)
    posi = A('posi', [128, S], I32)
    kb.dma('sp', posi[:], pos_d.partition_broadcast(128), w=['posi'])
    ang = A('ang', [128, S]); cosT = A('cosT', [128, S]); sinT = A('sinT', [128, S])
    kb.op('dve', lambda e: e.tensor_copy(out=ang[:], in_=posi[:]), r=['posi'], w=['ang'])
    kb.op('dve', lambda e: e.tensor_scalar(out=ang[:], in0=ang[:], scalar1=inv[:, 0:1], scalar2=None, op0=ALU.mult),
          r=['ang', 'inv'], w=['ang'])

    def sin_of(dst, dkey, shift):
        wk = posi[:].bitcast(F32)
        src = ang
        if shift is not None:
            kb.op('dve', lambda e: e.tensor_scalar(out=dst[:], in0=ang[:], scalar1=shift[:, 0:1], scalar2=None, op0=ALU.add),
                  r=['ang'], w=[dkey])
            src = dst
        kb.op('dve', lambda e: e.tensor_scalar(out=wk, in0=src[:], scalar1=c_i2p[:, 0:1], scalar2=c_mag[:, 0:1],
                                               op0=ALU.mult, op1=ALU.add), r=[dkey, 'ang', 'posi'], w=['posi'])
        kb.op('dve', lambda e: e.tensor_scalar(out=wk, in0=wk, scalar1=c_nmag[:, 0:1], scalar2=None, op0=ALU.add),
              r=['posi'], w=['posi'])
        kb.op('dve', lambda e: e.scalar_tensor_tensor(out=dst[:], in0=wk, scalar=c_n2p[:, 0:1], in1=src[:],
                                                      op0=ALU.mult, op1=ALU.add), r=['posi', 'ang', dkey], w=[dkey])
        kb.op('dve', lambda e: e.tensor_scalar(out=dst[:], in0=dst[:], scalar1=c_pi[:, 0:1], scalar2=c_npi[:, 0:1],
                                               op0=ALU.min, op1=ALU.max), r=[dkey], w=[dkey])
        kb.op('act', lambda e: e.activation(out=dst[:], in_=dst[:], func=AF.Sin), r=[dkey], w=[dkey])
    sin_of(sinT, 'sinT', None)
    kb.op('dve', lambda e: e.tensor_scalar(out=sinT[:], in0=sinT[:], scalar1=sgn[:, 0:1], scalar2=None, op0=ALU.mult),
          r=['sinT', 'sgn'], w=['sinT'])
    sin_of(cosT, 'cosT', c_hpi)
    qT = A('qTb', [128, 8, S], BF16); qr = A('qr', [128, 8, S], BF16)
    ksr = A('ksr', [128, 2, S], BF16); kwr = A('kwr', [128, 2, S], BF16)
    kcT = A('kcTb', [128, 2, S], BF16); vcT = A('vcTb', [128, 2, S], BF16)
    vs = A('vsb', [128, 2, 16, 132], BF16); vw = A('vwb', [128, 2, 16, 132], BF16)
    kb.dma('pool', kcT[:], kc_d, w=['kcT'])
    kb.dma('pool', vcT[:], vc_d, w=['vcT'])
    kb.op('dve', lambda e: e.memset(vs[:, :, :, 128:129], 1.0), w=['vs1'])
    kb.op('dve', lambda e: e.memset(vw[:, :, :, 128:129], 1.0), w=['vw1'])
    if not fused:
        for g in range(2):
            kb.dma('pool', vs[:, g, :, 0:128], vs_d[g].rearrange("(n p) d -> p n d", p=128), w=[('vs', g)])
            kb.dma('pool', vw[:, g, :, 0:128], vw_d[g].rearrange("(n p) d -> p n d", p=128), w=[('vw', g)])
    else:
        vtmp = [xp[i][:].bitcast(BF16) for i in range(2)]
        nv = 0
        for g in range(2):
            for (srcs, dstt, key) in ((vsT_d, vs, 'vs'), (vwT_d, vw, 'vw')):
                vi = nv % 2
                nv += 1
                kb.dma('pool', vtmp[vi], srcs[g], w=[('xp', vi)])
                for k4 in range(4):
                    tb = 2 + k4 % 2
                    tbk = bankbf(tb)
                    pe_multi(kb, [(lambda e, j=j: e.transpose(out=tbk[:, j * 128:(j + 1) * 128],
                                                              in_=vtmp[vi][:, (k4 * 4 + j) * 128:(k4 * 4 + j + 1) * 128],
                                                              identity=idb[:])) for j in range(4)],
                             r=[('xp', vi), 'idb'], w=[('ps', tb)])
                    kb.op('dve', lambda e: e.tensor_copy(out=dstt[:, g, k4 * 4:k4 * 4 + 4, 0:128],
                                                         in_=tbk[:, 0:512].rearrange("p (a b) -> p a b", b=128)),
                          r=[('ps', tb)], w=[(key, g)])
    nst = [0]

    def rope(src_d, srcp_d, dst, dkey, plain=None):
        for hf in range(2):
            i = nst[0] % 2
            nst[0] += 1
            sl = slice(hf * H, (hf + 1) * H)
            kb.dma('sp', xs[i][:], src_d[:, sl], w=[('xs', i)])
            if srcp_d is not None:
                kb.dma('sp', xp[i][:], srcp_d[:, sl], w=[('xp', i)])
            else:
                kb.dma('sp', xp[i][0:16, :], src_d[16:32, sl], w=[('xp', i)])
                kb.dma('sp', xp[i][16:32, :], src_d[0:16, sl], r=[('xp', i)], w=[('xp', i)])
                kb.dma('sp', xp[i][32:128, :], src_d[32:128, sl], r=[('xp', i)], w=[('xp', i)])
            if plain is not None:
                kb.op('act', lambda e: e.activation(out=plain[:, sl], in_=xs[i][:], func=AF.Copy), r=[('xs', i)],
                      w=[(dkey, 'plain', hf)])
            kb.op('dve', lambda e: e.tensor_tensor(out=xs[i][:], in0=xs[i][:], in1=cosT[:, sl], op=ALU.mult),
                  r=[('xs', i), 'cosT'], w=[('xs', i)])
            kb.op('dve', lambda e: e.tensor_tensor(out=xp[i][:], in0=xp[i][:], in1=sinT[:, sl], op=ALU.mult),
                  r=[('xp', i), 'sinT'], w=[('xp', i)])
            kb.op('dve', lambda e: e.tensor_tensor(out=dst[:, sl], in0=xs[i][:], in1=xp[i][:], op=ALU.add),
                  r=[('xs', i), ('xp', i)], w=[(dkey, hf)])
    for hh in range(8):
        rope(q_d[hh], qp_d[hh], qr[:, hh, :], ('qr', hh), plain=qT[:, hh, :])
    for g in range(2):
        rope(ks_d[g], ksp_d[g], ksr[:, g, :], ('ksr', g))
        rope(kw_d[g], kwp_d[g], kwr[:, g, :], ('kwr', g))
    qkeys = lambda hh: [(('qr', hh), 0), (('qr', hh), 1), (('qr', hh), 'plain', 0), (('qr', hh), 'plain', 1)]
    w1 = A('w1', [128, 32, 256], BF16)
    w2 = A('w2', [128, 2, 128], BF16)
    hid = A('hid', [128, 2, 128], BF16)
    hx = A('hx', [128, 128]); ha = A('ha', [128, 128]); hb = A('hb', [128, 128])
    cpe = A('cpe', [128, 2])
    kcmpT = A('kcmpT', [128, 2, 128], BF16)
    vcmp = A('vcmp', [128, 2, 128])
    kb.op('dve', lambda e: e.memset(hid[:], 0.0), w=['hid'])
    kb.op('dve', lambda e: e.memset(kcmpT[:], 0.0), w=['kcmpT'])
    kb.op('dve', lambda e: e.memset(vcmp[:], 0.0), w=['vcmp'])
    for which, (pe_d, w1_d, w2_d, srcT, skey) in enumerate([(pek_d, w1k_d, w2k_d, kcT, 'kcT'), (pev_d, w1v_d, w2v_d, vcT, 'vcT')]):
        pe_b = small('pe_b%d' % which, pe_d, [128, 32], BF16, 'pool')
        for l0 in range(0, 32, 8):
            kb.dma('pool', w1[:, l0:l0 + 8, :], w1_d[l0 * 128:(l0 + 8) * 128, :].rearrange("(l p) h -> p l h", p=128),
                   w=[('w1', l0)])
        kb.dma('pool', w2[:], w2_d.rearrange("(c p) d -> p c d", p=128), w=['w2'])
        w1keys = [('w1', l0) for l0 in range(0, 32, 8)]
        src4 = srcT[:].rearrange("p g (n r) -> p g n r", r=16)
        for hc in range(2):
            kb.mm(bank(7, 1), [(w1[:, l, hc * 128:(hc + 1) * 128], pe_b[:, l:l + 1]) for l in range(32)],
                  r=w1keys + ['pe_b%d' % which], w=[('ps', 7)])
            kb.op('dve', lambda e: e.tensor_copy(out=cpe[:, hc:hc + 1], in_=bank(7, 1)), r=[('ps', 7)], w=[('cpe', hc)])
        for g in range(2):
            for hc in range(2):
                kb.mm(bank(6, 127), [(w1[:, l, hc * 128:(hc + 1) * 128], src4[:, g, (l // 16):(l // 16) + 127, l % 16])
                                     for l in range(32)], r=w1keys + [skey], w=[('ps', 6)])
                kb.op('dve', lambda e: e.tensor_scalar(out=hx[:, :127], in0=bank(6, 127), scalar1=cpe[:, hc:hc + 1],
                                                       scalar2=None, op0=ALU.add), r=[('ps', 6), ('cpe', hc)], w=['hx'])
                kb.op('act', lambda e: e.activation(out=ha[:, :127], in_=hx[:, :127], func=AF.Square), r=['hx'], w=['ha'])
                kb.op('dve', lambda e: e.tensor_scalar(out=ha[:, :127], in0=ha[:, :127], scalar1=c_cg[:, 0:1],
                                                       scalar2=c_one[:, 0:1], op0=ALU.mult, op1=ALU.add), r=['ha'], w=['ha'])
                kb.op('dve', lambda e: e.tensor_tensor(out=ha[:, :127], in0=hx[:, :127], in1=ha[:, :127], op=ALU.mult),
                      r=['hx', 'ha'], w=['ha'])
                kb.op('act', lambda e: e.activation(out=hb[:, :127], in_=ha[:, :127], func=AF.Sigmoid, scale=2.0 * GELU_C),
                      r=['ha'], w=['hb'])
                kb.op('dve', lambda e: e.tensor_tensor(out=hid[:, hc, :127], in0=hx[:, :127], in1=hb[:, :127], op=ALU.mult),
                      r=['hx', 'hb'], w=[('hid', hc)])
            if which == 0:
                kb.mm(bank(6, 127), [(w2[:, hc, :], hid[:, hc, :127]) for hc in range(2)],
                      r=['w2', ('hid', 0), ('hid', 1)], w=[('ps', 6)])
                kb.op('dve', lambda e: e.tensor_copy(out=kcmpT[:, g, :127], in_=bank(6, 127)), r=[('ps', 6)],
                      w=[('kcmpT', g)])
            else:
                kb.mm(bank(6, 128, 127), [(hid[:, hc, :127], w2[:, hc, :]) for hc in range(2)],
                      r=['w2', ('hid', 0), ('hid', 1)], w=[('ps', 6)])
                kb.op('dve', lambda e: e.tensor_copy(out=vcmp[:127, g, :], in_=bank(6, 128, 127)), r=[('ps', 6)],
                      w=[('vcmp', g)])
    sc4 = A('sc4', [128, 4, 128]); mx = A('mx', [128, 1]); rs4 = A('rs4', [128, 4])
    pcT4 = A('pcT4', [128, 4, 128])
    impm = A('impm', [128, 32]); wk32 = A('wk32', [128, 32]); m8 = A('m8', [128, 8]); m8b = A('m8b', [128, 8])
    sel = A('sel', [128, 32], BF16)
    pbuf = [A('pbuf%d' % i, [128, 512], BF16) for i in range(2)]
    pT = [A('pT%d' % i, [128, 512], BF16) for i in range(2)]
    oacc = [A('oacc%d' % i, [128, 4, 128]) for i in range(2)]
    gsc = [A('gsc%d' % i, [128, 1]) for i in range(2)]
    cnt = dict(o=0, c=0, j=0)
    X = mybir.AxisListType.X

    for g in range(2):
        for qt in range(NQT):
            oi = cnt['o'] % 2
            cnt['o'] += 1
            oa = oacc[oi]
            oakeys = [('oacc', oi, h) for h in range(4)]
            qsl = slice(qt * 128, (qt + 1) * 128)
            pe_multi(kb, [(lambda e, h=h: e.matmul(bank(6, 512)[:, h * 128:(h + 1) * 128], qT[:, g * 4 + h, qsl],
                                                   kcmpT[:, g, :], start=True, stop=True)) for h in range(4)],
                     r=[(('qr', g * 4 + h), 'plain', qt // 8) for h in range(4)] + [('kcmpT', g)], w=[('ps', 6)])
            kb.op('dve', lambda e: e.reduce_max(out=mx[:], in_=bank(6, 512), axis=X), r=[('ps', 6)], w=['mx'])
            kb.op('dve', lambda e: e.tensor_scalar(out=mx[:], in0=mx[:], scalar1=c_nsc[:, 0:1], scalar2=None,
                                                   op0=ALU.mult), r=['mx'], w=['mx'])
            kb.op('act', lambda e: e.activation(out=sc4[:].rearrange("p h n -> p (h n)"), in_=bank(6, 512), func=AF.Exp,
                                                scale=SCALE, bias=mx[:, 0:1]), r=[('ps', 6), 'mx'], w=['sc4'])
            kb.op('dve', lambda e: e.tensor_tensor(out=sc4[:], in0=sc4[:],
                                                   in1=cm[:, qt, :].unsqueeze(1).broadcast_to([128, 4, 128]),
                                                   op=ALU.mult), r=['sc4', 'cm'], w=['sc4'])
            kb.op('dve', lambda e: e.reduce_sum(out=rs4[:], in_=sc4[:], axis=X), r=['sc4'], w=['rs4'])
            kb.op('dve', lambda e: e.tensor_scalar(out=rs4[:], in0=rs4[:], scalar1=c_tiny[:, 0:1], scalar2=None,
                                                   op0=ALU.max), r=['rs4'], w=['rs4'])
            kb.op('dve', lambda e: e.reciprocal(out=rs4[:], in_=rs4[:]), r=['rs4'], w=['rs4'])
            kb.op('dve', lambda e: e.tensor_tensor(out=sc4[:], in0=sc4[:],
                                                   in1=rs4[:, :].unsqueeze(2).broadcast_to([128, 4, 128]),
                                                   op=ALU.mult), r=['sc4', 'rs4'], w=['sc4'])
            pe_multi(kb, [(lambda e, h=h: e.transpose(out=bank(6, 512)[:, h * 128:(h + 1) * 128], in_=sc4[:, h, :],
                                                      identity=idf[:])) for h in range(4)],
                     r=['sc4', 'idf'], w=[('ps', 6)])
            kb.op('act', lambda e: e.activation(out=pcT4[:].rearrange("p h n -> p (h n)"), in_=bank(6, 512), func=AF.Copy),
                  r=[('ps', 6)], w=['pcT4'])
            kb.mm(bank(7, 32), [(pcT4[:, h, :], ovl[:, :]) for h in range(4)], r=['pcT4', 'ovl'], w=[('ps', 7)])
            pe_multi(kb, [(lambda e, h=h: e.matmul(bank(6, 512)[:, h * 128:(h + 1) * 128], pcT4[:, h, :], vcmp[:, g, :],
                                                   start=True, stop=True)) for h in range(4)],
                     r=['pcT4', ('vcmp', g)], w=[('ps', 6)])
            gview = lambda br: gl[:, qt, g * 12:(g + 1) * 12].rearrange("p (h c) -> p h c", c=3)[:, :, br:br + 1]
            kb.op('dve', lambda e: e.tensor_tensor(out=oa[:], in0=bank(6, 512).rearrange("p (h d) -> p h d", d=128),
                                                   in1=gview(0).broadcast_to([128, 4, 128]), op=ALU.mult),
                  r=[('ps', 6), 'gl'], w=oakeys)
            kb.op('dve', lambda e: e.tensor_tensor(out=impm[:], in0=bank(7, 32), in1=fbv[:, qt, :], op=ALU.add),
                  r=[('ps', 7), 'fbv'], w=['impm'])
            kb.op('dve', lambda e: e.max(out=m8[:], in_=impm[:]), r=['impm'], w=['m8'])
            kb.op('dve', lambda e: e.match_replace(out=wk32[:], in_to_replace=m8[:], in_values=impm[:], imm_value=-3.0e38),
                  r=['impm', 'm8'], w=['wk32'])
            kb.op('dve', lambda e: e.max(out=m8b[:], in_=wk32[:]), r=['wk32'], w=['m8b'])
            kb.op('dve', lambda e: e.tensor_scalar(out=sel[:], in0=impm[:], scalar1=m8b[:, 7:8], scalar2=None,
                                                   op0=ALU.is_ge), r=['impm', 'm8b'], w=['sel'])
            chunks = []
            for h in range(4):
                hh = g * 4 + h
                for br in (1, 2):
                    kts = list(range(qt + 1)) if br == 1 else list(range(max(0, qt - 4), qt + 1))
                    parts = [kts[i:i + 4] for i in range(0, len(kts), 4)]
                    accb = 4 + cnt['j'] % 2
                    ji = cnt['j'] % 2
                    cnt['j'] += 1
                    npv = len(kts)
                    ipv = 0
                    for pi_, ch in enumerate(parts):
                        chunks.append(dict(h=h, hh=hh, br=br, ch=ch, accb=accb, ji=ji, ipv0=ipv, npv=npv,
                                           last=(pi_ == len(parts) - 1)))
                        ipv += len(ch)

            def stage1(c_):
                i = c_['idx']
                ch = c_['ch']
                n = 128 * len(ch)
                k0 = ch[0] * 128
                sb = i % 2
                kT, kkey = (ksr[:, g, :], ('ksr', g)) if c_['br'] == 1 else (kwr[:, g, :], ('kwr', g))
                kb.mm(bank(sb, n), [(qr[:, c_['hh'], qsl], kT[:, k0:k0 + n])],
                      r=[(('qr', c_['hh']), qt // 8), (kkey, 0), (kkey, 1)], w=[('ps', sb)])
                pb = pbuf[i % 2]
                pk = ('pbuf', i % 2)
                kb.op('act', lambda e: e.activation(out=pb[:, :n], in_=bank(sb, n), func=AF.Exp, scale=SCALE),
                      r=[('ps', sb)], w=[pk])
                if c_['br'] == 1:
                    nb = 2 * len(ch)
                    kb.op('dve', lambda e: e.tensor_tensor(
                        out=pb[:, :n].rearrange("p (b k) -> p b k", k=64), in0=pb[:, :n].rearrange("p (b k) -> p b k", k=64),
                        in1=sel[:, 2 * ch[0]:2 * ch[0] + nb].unsqueeze(2).broadcast_to([128, nb, 64]), op=ALU.mult),
                        r=[pk, 'sel'], w=[pk])
                if c_['br'] == 2 and qt >= 4 and ch[0] == qt - 4:
                    kb.op('dve', lambda e: e.tensor_tensor(out=pb[:, 0:128], in0=pb[:, 0:128], in1=tri2[:], op=ALU.mult),
                          r=[pk, 'tri2'], w=[pk])
                if ch[-1] == qt:
                    off = 128 * (len(ch) - 1)
                    kb.op('dve', lambda e: e.tensor_tensor(out=pb[:, off:off + 128], in0=pb[:, off:off + 128], in1=tri[:],
                                                           op=ALU.mult), r=[pk, 'tri'], w=[pk])

            def stage2(c_):
                i = c_['idx']
                ch = c_['ch']
                n = 128 * len(ch)
                pb = pbuf[i % 2]
                tb = 2 + i % 2
                tbk = bankbf(tb)
                pe_multi(kb, [(lambda e, j=j: e.transpose(out=tbk[:, j * 128:(j + 1) * 128], in_=pb[:, j * 128:(j + 1) * 128],
                                                          identity=idb[:])) for j in range(len(ch))],
                         r=[('pbuf', i % 2), 'idb'], w=[('ps', tb)])
                if i % 2 == 0:
                    kb.op('act', lambda e: e.activation(out=pT[0][:, :n], in_=tbk[:, :n], func=AF.Copy), r=[('ps', tb)],
                          w=[('pT', 0)])
                else:
                    kb.op('dve', lambda e: e.tensor_copy(out=pT[1][:, :n], in_=tbk[:, :n]), r=[('ps', tb)], w=[('pT', 1)])

            def stage3(c_):
                i = c_['idx']
                ch = c_['ch']
                accb = c_['accb']
                vaug, vkeys = (vs, [('vs', g), 'vs1']) if c_['br'] == 1 else (vw, [('vw', g), 'vw1'])
                fns = []
                for j, kt in enumerate(ch):
                    ip = c_['ipv0'] + j
                    fns.append(lambda e, j=j, kt=kt, first=(ip == 0), lastm=(ip == c_['npv'] - 1): e.matmul(
                        bank(accb, 129), pT[i % 2][:, j * 128:(j + 1) * 128], vaug[:, g, kt, 0:129], start=first, stop=lastm))
                pe_multi(kb, fns, r=[('pT', i % 2)] + vkeys, w=[('ps', accb)])
                if c_['last']:
                    h = c_['h']
                    gs_ = gsc[c_['ji']]
                    gk = ('gsc', c_['ji'])
                    col = h_col(c_['hh'], c_['br'])
                    kb.op('dve', lambda e: e.reciprocal(out=gs_[:], in_=bank(accb, 129)[:, 128:129]), r=[('ps', accb)], w=[gk])
                    kb.op('dve', lambda e: e.tensor_tensor(out=gs_[:], in0=gs_[:], in1=gl[:, qt, col:col + 1], op=ALU.mult),
                          r=[gk, 'gl'], w=[gk])
                    kb.op('dve', lambda e: e.scalar_tensor_tensor(out=oa[:, h, :], in0=bank(accb, 128), scalar=gs_[:, 0:1],
                                                                  in1=oa[:, h, :], op0=ALU.mult, op1=ALU.add),
                          r=[('ps', accb), gk, ('oacc', oi, h)], w=[('oacc', oi, h)])

            nchk = len(chunks)
            for i, c_ in enumerate(chunks):
                c_['idx'] = cnt['c'] + i
            for i in range(nchk + 2):
                if i < nchk:
                    stage1(chunks[i])
                if 0 <= i - 1 < nchk:
                    stage2(chunks[i - 1])
                if 0 <= i - 2 < nchk:
                    stage3(chunks[i - 2])
            cnt['c'] += nchk
            kb.dma('sp', o_o[qt * 128:(qt + 1) * 128, g * 512:(g + 1) * 512], oa[:].rearrange("p h d -> p (h d)"),
                   r=oakeys, w=[('o_o', g, qt)])
    if fused:
        kb.barrier()
        return nc
    kb.finish('sp')
    return nc


def h_col(hh, br):
    return hh * 3 + br


def l3_consts():
    i = np.arange(128)[:, None]
    j = np.arange(128)[None, :]
    tri = (j <= i).astype(np.float32)
    tri2 = (j > i).astype(np.float32)
    n = np.arange(128)
    cm = np.zeros((128, 16, 128), np.float32)
    fbv = np.zeros((128, 16, 32), np.float32)
    jb = np.arange(32)
    for qt in range(16):
        t = qt * 128 + np.arange(128)
        cm[:, qt, :] = ((16 * n[None, :] + 31 <= t[:, None]) & (n[None, :] < 127)).astype(np.float32)
        cur = (t // 64)[:, None]
        forced = (jb[None] == 0) | (jb[None] == cur) | (jb[None] == cur - 1)
        valid = jb[None] * 64 <= t[:, None]
        fbv[:, qt, :] = np.where(valid, np.where(forced, 1000.0, 0.0), -1e30)
    ovl = np.zeros((128, 32), np.float32)
    for nn in range(127):
        for jj in range(32):
            if 16 * nn < 64 * jj + 64 and 16 * nn + 31 >= 64 * jj:
                ovl[nn, jj] = 1.0
    d = np.arange(128)
    inv = np.where(d < 32, 500000.0 ** (-(2.0 * (d % 16)) / 32.0), 0.0).astype(np.float32)[:, None]
    sgn = np.where(d < 16, -1.0, np.where(d < 32, 1.0, 0.0)).astype(np.float32)[:, None]
    return dict(tri=tri, tri2=tri2, cm=cm, fbv=fbv, ovl=ovl, inv=inv, sgn=sgn, ident=np.eye(128, dtype=np.float32))


def swap_rot(xT):
    y = xT.copy()
    y[..., 0:16, :] = xT[..., 16:32, :]
    y[..., 16:32, :] = xT[..., 0:16, :]
    return y


def prep_L3(zT_b, pos_b, half, W, consts):
    c = np.ascontiguousarray
    gs = [2 * half, 2 * half + 1]
    qT = c(zT_b[half * 1024:(half + 1) * 1024].reshape(8, 128, S))

    def grp(base):
        return c(np.stack([zT_b[base + g * 128:base + (g + 1) * 128] for g in gs]))
    kc, vc, ks, vs_, kw, vw_ = [grp(2048 + i * 512) for i in range(6)]
    gl = c(zT_b[5120 + half * 24:5120 + (half + 1) * 24].T)
    m = dict(qT=qT, qPT=swap_rot(qT), ksT=ks, ksPT=swap_rot(ks), kwT=kw, kwPT=swap_rot(kw),
             kcT=c(kc.transpose(1, 0, 2)), vcT=c(vc.transpose(1, 0, 2)),
             vs=c(vs_.transpose(0, 2, 1)), vw=c(vw_.transpose(0, 2, 1)), gl=gl,
             pos=c(pos_b.reshape(1, S).astype(np.int32)))
    m.update(W)
    m.update(consts)
    return m


def prep_L3_weights(pe_k, w1_k, w2_k, pe_v, w1_v, w2_v):
    c = np.ascontiguousarray
    return dict(pekT=c(pe_k.T), w1k=c(w1_k), w2k=c(w2_k), pevT=c(pe_v.T), w1v=c(w1_v), w2v=c(w2_v))


_PROGS = {}


def _prog(name, fn):
    if name not in _PROGS:
        _PROGS[name] = fn()
    return _PROGS[name]


def _lay(g, n=16):
    return np.ascontiguousarray(np.asarray(g, np.float32).reshape(n, 128).T)


def kernel_unfused(**inp):
    c = np.ascontiguousarray
    f32 = lambda a: np.asarray(a, dtype=np.float32)
    x = f32(inp['x'])
    pos = np.asarray(inp['positions'])
    T = 1024
    cores = list(range(NCORES))
    tok = lambda ci: (ci // 2, slice((ci % 2) * T, (ci % 2 + 1) * T))
    tri_st = np.triu(np.ones((128, 128), np.float32))
    common = dict(g1=_lay(inp['l0_ffn1_norm']), g2=_lay(inp['l0_mix_norm']), wgd=f32(inp['l0_ffn1_w_gate']),
                  wud=f32(inp['l0_ffn1_w_up']), wdd=f32(inp['l0_ffn1_w_down']), w_in=f32(inp['l0_w_in']),
                  lng=c(f32(inp['l0_gmlp_ln_g']).reshape(1, 1024)), lnb=c(f32(inp['l0_gmlp_ln_b']).reshape(1, 1024)),
                  wsT=c(f32(inp['l0_gmlp_ws']).transpose(2, 0, 1)), tri=tri_st,
                  bs=c(f32(inp['l0_gmlp_bs']).reshape(1, 1024)))
    maps = []
    for ci in cores:
        b, sl = tok(ci)
        m = dict(common)
        m['xTd'] = c(x[b, sl].T)
        maps.append(m)
    r1 = run_bass_kernel_spmd(_prog('L1', build_L1), maps, core_ids=cores).results
    common = dict(conv_w=c(f32(inp['l0_conv_w'])[:, 0, :].reshape(31, 8, 128).transpose(2, 1, 0)),
                  conv_b=_lay(inp['l0_conv_b'], 8), cln_g=_lay(inp['l0_conv_ln_g'], 8), cln_b=_lay(inp['l0_conv_ln_b'], 8),
                  w_out=f32(inp['l0_w_out']),
                  gA=_lay(inp['l0_ffn2_norm']), wgA=f32(inp['l0_ffn2_w_gate']), wuA=f32(inp['l0_ffn2_w_up']),
                  wdA=f32(inp['l0_ffn2_w_down']),
                  gB=_lay(inp['l1_ffn1_norm']), wgB=f32(inp['l1_ffn1_w_gate']), wuB=f32(inp['l1_ffn1_w_up']),
                  wdB=f32(inp['l1_ffn1_w_down']),
                  gM=_lay(inp['l1_mix_norm']), w_in1=f32(inp['l1_w_in']))
    maps = []
    for ci in cores:
        m = dict(common)
        aT = r1[ci]['aT']
        halo = np.zeros((1024, 32), np.float32)
        if ci % 2 == 1:
            halo = r1[ci - 1]['aT'][:, T - 32:]
        m['aTh'] = c(np.concatenate([halo, aT], axis=1))
        m['boTd'] = r1[ci]['boT']
        m['x1Td'] = r1[ci]['x1T']
        maps.append(m)
    r2 = run_bass_kernel_spmd(_prog('L2', build_L2), maps, core_ids=cores).results
    W = prep_L3_weights(*[f32(inp[k]) for k in ('l1_cmp_pe_k', 'l1_cmp_w1_k', 'l1_cmp_w2_k',
                                                'l1_cmp_pe_v', 'l1_cmp_w1_v', 'l1_cmp_w2_v')])
    consts = l3_consts()
    maps = []
    for ci in cores:
        b, half = ci // 2, ci % 2
        zT_b = np.concatenate([r2[2 * b]['zT'], r2[2 * b + 1]['zT']], axis=1)
        maps.append(prep_L3(zT_b, pos[b], half, W, consts))
    r3 = run_bass_kernel_spmd(_prog('L3', build_L3), maps, core_ids=cores).results
    common = dict(w_out1=f32(inp['l1_w_out']), gA=_lay(inp['l1_ffn2_norm']), wgA=f32(inp['l1_ffn2_w_gate']),
                  wuA=f32(inp['l1_ffn2_w_up']), wdA=f32(inp['l1_ffn2_w_down']), gF=_lay(inp['final_norm']))
    maps = []
    for ci in cores:
        b, sl = tok(ci)
        o_b = np.concatenate([r3[2 * b]['o'], r3[2 * b + 1]['o']], axis=1)
        m = dict(common)
        m['oTd'] = c(o_b[sl].T)
        m['x4Td'] = r2[ci]['x4T']
        maps.append(m)
    r4 = run_bass_kernel_spmd(_prog('L4', build_L4), maps, core_ids=cores).results
    out = np.zeros((4, 2048, 2048), np.float32)
    for ci in cores:
        b, sl = tok(ci)
        out[b, sl] = r4[ci]['yT'].T
    return out


from contextlib import ExitStack

W_NAMES = [('l0_ffn1', 'f1'), ('l0_ffn2', 'f2'), ('l1_ffn1', 'f3'), ('l1_ffn2', 'f4')]


def build_fused(dff=DFF, nz=5168):
    nc = bass.Bass("TRN2", target_bir_lowering=False)
    T = 1024
    ext = lambda name, shape, dtype=F32: nc.dram_tensor(name, shape, dtype, kind="ExternalInput").ap()
    scr = lambda name, shape: nc.dram_tensor(name, shape, F32, kind="Internal").ap()
    I = {}
    I['xT'] = ext('xT', [2, D, T])
    for _, s in W_NAMES:
        I[s + '_g'] = ext(s + '_g', [128, 16])
        I[s + '_wg'] = ext(s + '_wg', [D, dff])
        I[s + '_wu'] = ext(s + '_wu', [D, dff])
        I[s + '_wd'] = ext(s + '_wd', [dff, D])
    for name, shape in [('g_mix0', [128, 16]), ('w_in0', [D, 4096]), ('lng', [1, 1024]), ('lnb', [1, 1024]),
                        ('wsT', [128, 8, 128]), ('tri_st', [128, 128]), ('bs', [1, 1024]),
                        ('conv_w', [128, 8, 31]), ('conv_b', [128, 8]), ('cln_g', [128, 8]), ('cln_b', [128, 8]),
                        ('w_out0', [D, D]), ('g_mix1', [128, 16]), ('w_in1', [D, nz]),
                        ('inv', [128, 1]), ('sgn', [128, 1]), ('pekT', [128, 32]), ('w1k', [4096, 256]),
                        ('w2k', [256, 128]), ('pevT', [128, 32]), ('w1v', [4096, 256]), ('w2v', [256, 128]),
                        ('ovl', [128, 32]), ('cm', [128, 16, 128]), ('fbv', [128, 16, 32]), ('tri', [128, 128]),
                        ('tri2', [128, 128]), ('ident', [128, 128]), ('w_out1', [D, D]), ('g_fin', [128, 16])]:
        I[name] = ext(name, shape)
    I['pos'] = ext('pos', [1, S], I32)
    yT = nc.dram_tensor('yT', [D, T], F32, kind="ExternalOutput").ap()
    I['flag'] = ext('flag', [128, 2])
    x1T = scr('x1T_s', [2, D, T]); aT = scr('aT_s', [2, 1024, T]); boT = scr('boT_s', [2, 1024, T])
    x4T = scr('x4T_s', [2, D, T]); zT = scr('zT_s', [nz, 2 * T]); o_s = scr('o_s', [2 * T, 2048])
    kb = KB(nc)
    ps = nc.alloc_psum_tensor('ps', [128, 8 * 512], F32)
    with ExitStack() as es_core:
        _ES[0] = es_core
        core = Core(nc, T, kb, ps)
        for h in range(2):
            with ExitStack() as es:
                _ES[0] = es
                build_L1(T, dff, nc, dict(xTd=I['xT'][h], g1=I['f1_g'], g2=I['g_mix0'], wgd=I['f1_wg'], wud=I['f1_wu'],
                                          wdd=I['f1_wd'], w_in=I['w_in0'], lng=I['lng'], lnb=I['lnb'], wsT=I['wsT'],
                                          tri=I['tri_st'], bs=I['bs'], x1T=x1T[h], aT=aT[h], boT=boT[h]), core)
            with ExitStack() as es:
                _ES[0] = es
                build_L2(T, dff, nz, nc, dict(x1Td=x1T[h], aT_cur=aT[h], aT_prev=(aT[0] if h == 1 else None),
                                              boTd=boT[h], conv_w=I['conv_w'], conv_b=I['conv_b'], cln_g=I['cln_g'],
                                              cln_b=I['cln_b'], w_out=I['w_out0'],
                                              gA=I['f2_g'], wgA=I['f2_wg'], wuA=I['f2_wu'], wdA=I['f2_wd'],
                                              gB=I['f3_g'], wgB=I['f3_wg'], wuB=I['f3_wu'], wdB=I['f3_wd'],
                                              gM=I['g_mix1'], w_in1=I['w_in1'], x4T=x4T[h],
                                              zT=zT[:, h * T:(h + 1) * T]), core)
            _ES[0] = es_core
    for gp in range(2):
        with ExitStack() as es:
            _ES[0] = es
            ov = {k: I[k] for k in ('inv', 'sgn', 'pekT', 'w1k', 'w2k', 'pevT', 'w1v', 'w2v', 'ovl', 'cm', 'fbv',
                                    'tri', 'tri2', 'ident', 'pos')}
            ov['zT'] = zT
            ov['o'] = o_s[:, gp * 1024:(gp + 1) * 1024]
            build_L3(nc, ov, kb, ps, gp)
    with ExitStack() as es_core:
        _ES[0] = es_core
        core = Core(nc, T, kb, ps)
        with ExitStack() as es:
            _ES[0] = es
            build_L4(T, dff, nc, dict(x4Td=x4T[0], x4T_1=x4T[1], o_tok=o_s[0:T, :], o_tok1=o_s[T:2 * T, :],
                                      flag=I['flag'], ident=I['ident'], w_out1=I['w_out1'], gA=I['f4_g'],
                                      wgA=I['f4_wg'], wuA=I['f4_wu'], wdA=I['f4_wd'], gF=I['g_fin'], yT=yT), core)
        _ES[0] = es_core
    _ES[0] = None
    kb.finish('sp')
    return nc


def fused_inputs(inp, b, r=0):
    c = np.ascontiguousarray
    f32 = lambda a: np.asarray(a, dtype=np.float32)
    x = f32(inp['x'])
    m = dict(xT=c(np.stack([x[b, 0:1024].T, x[b, 1024:2048].T])))
    for pre, s in W_NAMES:
        m[s + '_g'] = _lay(inp[pre + '_norm'])
        m[s + '_wg'] = f32(inp[pre + '_w_gate'])
        m[s + '_wu'] = f32(inp[pre + '_w_up'])
        m[s + '_wd'] = f32(inp[pre + '_w_down'])
    m.update(g_mix0=_lay(inp['l0_mix_norm']), w_in0=f32(inp['l0_w_in']),
             lng=c(f32(inp['l0_gmlp_ln_g']).reshape(1, 1024)), lnb=c(f32(inp['l0_gmlp_ln_b']).reshape(1, 1024)),
             wsT=c(f32(inp['l0_gmlp_ws']).transpose(2, 0, 1)), tri_st=np.triu(np.ones((128, 128), np.float32)),
             bs=c(f32(inp['l0_gmlp_bs']).reshape(1, 1024)),
             conv_w=c(f32(inp['l0_conv_w'])[:, 0, :].reshape(31, 8, 128).transpose(2, 1, 0)),
             conv_b=_lay(inp['l0_conv_b'], 8), cln_g=_lay(inp['l0_conv_ln_g'], 8), cln_b=_lay(inp['l0_conv_ln_b'], 8),
             w_out0=f32(inp['l0_w_out']), g_mix1=_lay(inp['l1_mix_norm']), w_in1=f32(inp['l1_w_in']),
             w_out1=f32(inp['l1_w_out']), g_fin=_lay(inp['final_norm']),
             pos=c(np.asarray(inp['positions'])[b].reshape(1, S).astype(np.int32)))
    m.update(prep_L3_weights(*[f32(inp[k]) for k in ('l1_cmp_pe_k', 'l1_cmp_w1_k', 'l1_cmp_w2_k',
                                                     'l1_cmp_pe_v', 'l1_cmp_w1_v', 'l1_cmp_w2_v')]))
    m.update(l3_consts())
    fl = np.zeros((128, 2), np.float32)
    fl[:, r] = 1.0
    m['flag'] = fl
    return m


def kernel(**inp):
    nc = _prog('fused', build_fused)
    maps = [fused_inputs(inp, ci // 2, ci % 2) for ci in range(NCORES)]
    res = run_bass_kernel_spmd(nc, maps, core_ids=list(range(NCORES))).results
    out = np.zeros((4, 2048, 2048), np.float32)
    for ci in range(NCORES):
        b, h = ci // 2, ci % 2
        out[b, h * 1024:(h + 1) * 1024] = res[ci]['yT'].T
    return out
```

```python
import os
import numpy as np
import concourse.bass as bass
import concourse.mybir as mybir
from concourse.bass_utils import run_bass_kernel_spmd

F32 = mybir.dt.float32
BF16 = mybir.dt.bfloat16
I32 = mybir.dt.int32
AF = mybir.ActivationFunctionType
ALU = mybir.AluOpType

_ES = [None]
_UID = [0]


def SB(nc, name, shape, dtype=None):
    dtype = F32 if dtype is None else dtype
    _UID[0] += 1
    nm = '%s_%d' % (name, _UID[0])
    if _ES[0] is None:
        return nc.alloc_sbuf_tensor(nm, list(shape), dtype)
    return _ES[0].enter_context(nc.sbuf_tensor(nm, list(shape), dtype))


def mk_dt(nc, over, pre=''):
    def dt(name, shape, kind="ExternalInput", dtype=F32):
        if over is not None and name in over:
            return over[name]
        return nc.dram_tensor(pre + name, shape, dtype, kind=kind).ap()
    return dt


D = 2048
DFF = 5632
NCORES = 8
EPS = 1e-6


class KB:
    NS = 6

    def __init__(self, nc):
        self.nc = nc
        self.eng = dict(pe=nc.tensor, dve=nc.vector, act=nc.scalar, pool=nc.gpsimd, sp=nc.sync)
        self.sem = {}
        self.cnt = {}
        for e in ('pe', 'dve', 'act', 'pool'):
            self.sem[e] = nc.alloc_semaphore('c_' + e)
            self.cnt[e] = 0
        self.nsq = {'sp': 6, 'pool': 4, 'act': 2}
        self.dsem = {q: [nc.alloc_semaphore('d_%s%d' % (q, i)) for i in range(self.nsq[q])]
                     for q in ('sp', 'pool', 'act')}
        self.dcnt = {q: 0 for q in self.dsem}
        self.seen = {e: {} for e in self.eng}
        self.st = {}
        self.semobj = {}
        for s in list(self.sem.values()) + [x for v in self.dsem.values() for x in v]:
            self.semobj[s.num] = s
        self.nwait = 0

    def _deps(self, r, w):
        deps = {}

        def add(tok):
            if tok is None:
                return
            s, v = tok
            if deps.get(s, 0) < v:
                deps[s] = v
        for k in r:
            st = self.st.get(k)
            if st:
                add(st[0])
        for k in w:
            st = self.st.get(k)
            if st:
                add(st[0])
                for s, v in st[1].items():
                    add((s, v))
        return deps

    def _emit_waits(self, e, deps, skip_sem=None):
        eng = self.eng[e]
        seen = self.seen[e]
        for s, v in deps.items():
            if skip_sem is not None and s == skip_sem:
                continue
            if seen.get(s, 0) >= v:
                continue
            eng.wait_ge(self.semobj[s], v)
            seen[s] = v
            self.nwait += 1

    def _commit(self, tok, r, w):
        for k in r:
            st = self.st.setdefault(k, [None, {}])
            if st[1].get(tok[0], 0) < tok[1]:
                st[1][tok[0]] = tok[1]
        for k in w:
            self.st[k] = [tok, {}]

    def op(self, e, fn, r=(), w=()):
        deps = self._deps(r, w)
        self._emit_waits(e, deps, skip_sem=(self.sem['pe'].num if e == 'pe' else None))
        inst = fn(self.eng[e])
        self.cnt[e] += 1
        inst.then_inc(self.sem[e], 1)
        tok = (self.sem[e].num, self.cnt[e])
        self._commit(tok, r, w)
        return tok

    def mm(self, out, pairs, r=(), w=()):
        deps = self._deps(r, w)
        self._emit_waits('pe', deps, skip_sem=self.sem['pe'].num)
        n = len(pairs)
        inst = None
        for i, (lhsT, rhs) in enumerate(pairs):
            inst = self.nc.tensor.matmul(out, lhsT, rhs, start=(i == 0), stop=(i == n - 1))
        self.cnt['pe'] += 1
        inst.then_inc(self.sem['pe'], 1)
        tok = (self.sem['pe'].num, self.cnt['pe'])
        self._commit(tok, r, w)
        return tok

    def mm1(self, out, lhsT, rhs, start, stop, r=(), w=()):
        return self.op('pe', lambda e: e.matmul(out, lhsT, rhs, start=start, stop=stop), r=r, w=w)

    def dma(self, q, out, in_, r=(), w=(), **kw):
        deps = self._deps(r, w)
        i = self.dcnt[q]
        self.dcnt[q] += 1
        s = self.dsem[q][i % self.nsq[q]]
        rnd = i // self.nsq[q]
        if rnd > 0:
            deps[s.num] = max(deps.get(s.num, 0), 16 * rnd)
        self._emit_waits(q, deps)
        inst = self.eng[q].dma_start(out=out, in_=in_, **kw)
        inst.then_inc(s, 16)
        tok = (s.num, 16 * (rnd + 1))
        self._commit(tok, r, w)
        return tok

    def barrier(self):
        deps = {}
        for e, sm in self.sem.items():
            if self.cnt[e] > 0:
                deps[sm.num] = self.cnt[e]
        for q, sl in self.dsem.items():
            n = self.dcnt[q]
            for i, sm in enumerate(sl):
                k = (n - 1 - i) // self.nsq[q] + 1 if n > i else 0
                if k > 0:
                    deps[sm.num] = 16 * k
        for e in self.eng:
            self._emit_waits(e, dict(deps))

    def finish(self, e='sp'):
        deps = {}
        for k, st in self.st.items():
            if st[0] is not None:
                s, v = st[0]
                if deps.get(s, 0) < v:
                    deps[s] = v
        self._emit_waits(e, deps)


class Core:
    def __init__(self, nc, T, kb=None, ps=None):
        self.nc = nc
        self.kb = KB(nc) if kb is None else kb
        self.T = T
        self.TB = [(i, min(512, T - i)) for i in range(0, T, 512)]
        self.xT = SB(nc, 'xT', [128, 16, T], F32)
        self.hT = SB(nc, 'hT', [128, 16, T], BF16)
        self.wg = [SB(nc, 'wg%d' % i, [128, 16, 256], BF16) for i in range(2)]
        self.wu = [SB(nc, 'wu%d' % i, [128, 16, 256], BF16) for i in range(2)]
        self.wd = [SB(nc, 'wd%d' % i, [128, 2, 2048], BF16) for i in range(4)]
        self.actT = [SB(nc, 'actT%d' % i, [128, 4, T], BF16) for i in range(2)]
        self.tmp = [SB(nc, 'tmp%d' % i, [128, 512], F32) for i in range(2)]
        self.sq = [SB(nc, 'sq%d' % i, [128, 512], BF16) for i in range(2)]
        self.rstd = SB(nc, 'rstd', [128, 512], F32)
        self.ones = SB(nc, 'ones', [128, 128], BF16)
        self.ps = nc.alloc_psum_tensor('ps', [128, 8 * 512], F32) if ps is None else ps
        self.ntmp = 0
        self.nsq = 0
        self.nwt = 0
        self.nwd = 0
        self.nact = 0
        self.kb.op('dve', lambda e: e.memset(self.ones[:], 1.0), w=['ones'])
        self.half_sb = SB(nc, 'half', [128, 1], F32)
        self.kb.op('dve', lambda e: e.memset(self.half_sb[:], 0.5), w=['half'])
        self.eps_sb = SB(nc, 'eps', [128, 1], F32)
        self.kb.op('dve', lambda e: e.memset(self.eps_sb[:], EPS), w=['eps'])

    def bank(self, b, n=512):
        return self.ps[:, b * 512:b * 512 + n]


def rmsnorm_T(c, g_sb, gkey, out=None, okey='hT', src=None, skey='xT', bank=6):
    kb = c.kb
    out = c.hT if out is None else out
    src = c.xT if src is None else src
    for (t0, tn) in c.TB:
        for kc in range(16):
            i = c.nsq % 2
            c.nsq += 1
            sq = c.sq[i]
            kb.op('act', lambda e, kc=kc, sq=sq: e.activation(out=sq[:, :tn], in_=src[:, kc, t0:t0 + tn],
                                                             func=AF.Square),
                  r=[(skey, kc, t0)], w=[('sq', i)])
            kb.mm1(c.bank(bank, tn), c.ones[:], sq[:, :tn], kc == 0, kc == 15,
                   r=['ones', ('sq', i)], w=([('ps', bank)] if kc in (0, 15) else []))
        kb.op('act', lambda e: e.activation(out=c.rstd[:, :tn], in_=c.bank(bank, tn), func=AF.Sqrt,
                                            scale=1.0 / D, bias=c.eps_sb[:, 0:1]),
              r=[('ps', bank), 'eps'], w=['rstd'])
        kb.op('dve', lambda e: e.reciprocal(out=c.rstd[:, :tn], in_=c.rstd[:, :tn]), r=['rstd'], w=['rstd'])
        for kc in range(16):
            kb.op('dve', lambda e, kc=kc: e.scalar_tensor_tensor(
                out=out[:, kc, t0:t0 + tn], in0=src[:, kc, t0:t0 + tn], scalar=g_sb[:, kc:kc + 1],
                in1=c.rstd[:, :tn], op0=ALU.mult, op1=ALU.mult),
                r=[(skey, kc, t0), 'rstd', gkey], w=[(okey, kc, t0)])


def ffn_T(c, wg_d, wu_d, wd_d, dff=DFF):
    kb = c.kb
    nc = c.nc
    wg_v = wg_d.rearrange("(kc p) f -> p kc f", p=128)
    wu_v = wu_d.rearrange("(kc p) f -> p kc f", p=128)
    NFB = dff // 512
    gbank = 0
    dbank = 0
    for fb in range(NFB):
        ab = c.nact % 2
        c.nact += 1
        actT = c.actT[ab]
        wd_tiles = []
        for half in range(2):
            wt = fb * 2 + half
            wb = c.nwt % 2
            c.nwt += 1
            kb.dma('pool', c.wg[wb][:], wg_v[:, :, wt * 256:(wt + 1) * 256], w=[('wg', wb)])
            kb.dma('pool', c.wu[wb][:], wu_v[:, :, wt * 256:(wt + 1) * 256], w=[('wu', wb)])
            db = c.nwd % 4
            c.nwd += 1
            kb.dma('pool', c.wd[db][:],
                   wd_d[wt * 256:(wt + 1) * 256, :].rearrange("(fc p) d -> p fc d", p=128), w=[('wd', db)])
            wd_tiles.append(db)
            for j in range(2):
                fcl = half * 2 + j
                for (t0, tn) in c.TB:
                    bg = gbank % 4
                    bu = (gbank + 1) % 4
                    gbank += 2
                    kb.mm(c.bank(bg, tn), [(c.wg[wb][:, kc, j * 128:(j + 1) * 128], c.hT[:, kc, t0:t0 + tn])
                                           for kc in range(16)],
                          r=[('wg', wb)] + [('hT', kc, t0) for kc in range(16)], w=[('ps', bg)])
                    kb.mm(c.bank(bu, tn), [(c.wu[wb][:, kc, j * 128:(j + 1) * 128], c.hT[:, kc, t0:t0 + tn])
                                           for kc in range(16)],
                          r=[('wu', wb)] + [('hT', kc, t0) for kc in range(16)], w=[('ps', bu)])
                    ti = c.ntmp % 2
                    c.ntmp += 1
                    tmp = c.tmp[ti]
                    kb.op('act', lambda e: e.activation(out=tmp[:, :tn], in_=c.bank(bg, tn), func=AF.Silu),
                          r=[('ps', bg)], w=[('tmp', ti)])
                    kb.op('dve', lambda e: e.tensor_tensor(out=actT[:, fcl, t0:t0 + tn], in0=c.bank(bu, tn),
                                                           in1=tmp[:, :tn], op=ALU.mult),
                          r=[('ps', bu), ('tmp', ti)], w=[('actT', ab, fcl, t0)])
        for dc in range(16):
            for (t0, tn) in c.TB:
                bd = 4 + dbank % 4
                dbank += 1
                kb.mm(c.bank(bd, tn),
                      [(c.wd[wd_tiles[fcl // 2]][:, fcl % 2, dc * 128:(dc + 1) * 128], actT[:, fcl, t0:t0 + tn])
                       for fcl in range(4)],
                      r=[('wd', wd_tiles[0]), ('wd', wd_tiles[1])] + [('actT', ab, fcl, t0) for fcl in range(4)],
                      w=[('ps', bd)])
                if True:
                    kb.op('dve', lambda e: e.scalar_tensor_tensor(
                        out=c.xT[:, dc, t0:t0 + tn], in0=c.bank(bd, tn), scalar=c.half_sb[:, 0:1], in1=c.xT[:, dc, t0:t0 + tn],
                        op0=ALU.mult, op1=ALU.add),
                        r=[('ps', bd), ('xT', dc, t0), 'half'], w=[('xT', dc, t0)])


def load_small(c, name, d_ap, shape, dtype=F32, q='sp'):
    t = SB(c.nc, name, list(shape), dtype)
    c.kb.dma(q, t[:], d_ap, w=[name])
    return t


def load_xT(c, x_d):
    v = x_d.rearrange("(kc p) t -> p kc t", p=128)
    for kc in range(0, 16, 4):
        c.kb.dma('sp', c.xT[:, kc:kc + 4, :], v[:, kc:kc + 4, :],
                 w=[('xT', k, t0) for k in range(kc, kc + 4) for (t0, _) in c.TB])


def store_T(c, out_d, src, skey):
    v = out_d.rearrange("(kc p) t -> p kc t", p=128)
    for kc in range(0, 16, 4):
        c.kb.dma('sp', v[:, kc:kc + 4, :], src[:, kc:kc + 4, :],
                 r=[(skey, k, t0) for k in range(kc, kc + 4) for (t0, _) in c.TB], w=[('out', kc)])


def build_ffn_test(T, dff=DFF, stage=2):
    nc = bass.Bass("TRN2", target_bir_lowering=False)
    x_d = nc.dram_tensor("xTd", [D, T], F32, kind="ExternalInput").ap()
    g1 = nc.dram_tensor("g1", [128, 16], F32, kind="ExternalInput").ap()
    g2 = nc.dram_tensor("g2", [128, 16], F32, kind="ExternalInput").ap()
    wg = nc.dram_tensor("wgd", [D, dff], F32, kind="ExternalInput").ap()
    wu = nc.dram_tensor("wud", [D, dff], F32, kind="ExternalInput").ap()
    wd = nc.dram_tensor("wdd", [dff, D], F32, kind="ExternalInput").ap()
    y_d = nc.dram_tensor("yTd", [D, T], BF16, kind="ExternalOutput").ap()
    c = Core(nc, T)
    g1s = load_small(c, 'g1s', g1, [128, 16])
    g2s = load_small(c, 'g2s', g2, [128, 16])
    load_xT(c, x_d)
    rmsnorm_T(c, g1s, 'g1s')
    if stage >= 1:
        ffn_T(c, wg, wu, wd, dff)
    if stage >= 2:
        rmsnorm_T(c, g2s, 'g2s')
    store_T(c, y_d, c.hT, 'hT')
    c.kb.finish('sp')
    return nc


GELU_C = 0.7978845608028654
RING = [('wg', 0), ('wu', 0), ('wg', 1), ('wu', 1)]


def load_wt(c, w_v, col0, ncols, nkc=16):
    i = getattr(c, 'nring', 0)
    c.nring = i + 1
    name, b = RING[i % 4]
    buf = c.wg[b] if name == 'wg' else c.wu[b]
    c.kb.dma('pool', buf[:, :nkc, :ncols], w_v[:, :, col0:col0 + ncols], w=[(name, b)])
    return buf, (name, b)


def extra_tiles(c):
    nc = c.nc
    c.stg = [SB(nc, 'stg%d' % i, [128, 512], F32) for i in range(2)]
    c.ga = c.tmp[0]
    c.gb = c.tmp[1]
    c.nstg = 0
    c.c1 = SB(nc, 'c1', [128, 1], F32)
    c.kb.op('dve', lambda e: e.memset(c.c1[:], 1.0), w=['c1'])
    c.cg = SB(nc, 'cg', [128, 1], F32)
    c.kb.op('dve', lambda e: e.memset(c.cg[:], 0.044715), w=['cg'])


def gelu_from(c, src, skey, out, okey, P, n):
    kb = c.kb
    kb.op('act', lambda e: e.activation(out=c.ga[:P, :n], in_=src, func=AF.Square), r=[skey], w=[('tmp', 0)])
    kb.op('dve', lambda e: e.tensor_scalar(out=c.ga[:P, :n], in0=c.ga[:P, :n], scalar1=c.cg[:P, 0:1],
                                           scalar2=c.c1[:P, 0:1], op0=ALU.mult, op1=ALU.add),
          r=[('tmp', 0), 'cg', 'c1'], w=[('tmp', 0)])
    kb.op('dve', lambda e: e.tensor_tensor(out=c.ga[:P, :n], in0=src, in1=c.ga[:P, :n], op=ALU.mult),
          r=[skey, ('tmp', 0)], w=[('tmp', 0)])
    kb.op('act', lambda e: e.activation(out=c.gb[:P, :n], in_=c.ga[:P, :n], func=AF.Sigmoid, scale=2.0 * GELU_C),
          r=[('tmp', 0)], w=[('tmp', 1)])
    kb.op('dve', lambda e: e.tensor_tensor(out=out, in0=src, in1=c.gb[:P, :n], op=ALU.mult),
          r=[skey, ('tmp', 1)], w=[okey])


def build_L1(T=1024, dff=DFF, nc=None, over=None, core=None):
    fused = nc is not None
    nc = bass.Bass("TRN2", target_bir_lowering=False) if nc is None else nc
    dt = mk_dt(nc, over)
    x_d = dt("xTd", [D, T])
    g1 = dt("g1", [128, 16])
    g2 = dt("g2", [128, 16])
    wg = dt("wgd", [D, dff])
    wu = dt("wud", [D, dff])
    wd = dt("wdd", [dff, D])
    w_in = dt("w_in", [D, 4096])
    lng = dt("lng", [1, 1024])
    lnb = dt("lnb", [1, 1024])
    wsT_d = dt("wsT", [128, 8, 128])
    tri_d = dt("tri", [128, 128])
    bs_d = dt("bs", [1, 1024])
    x1_o = dt("x1T", [D, T], "ExternalOutput")
    a_o = dt("aT", [1024, T], "ExternalOutput")
    bo_o = dt("boT", [1024, T], "ExternalOutput")
    c = Core(nc, T) if core is None else core
    kb = c.kb
    extra_tiles(c)
    g1s = load_small(c, 'g1s', g1, [128, 16])
    g2s = load_small(c, 'g2s', g2, [128, 16])
    lng_s = load_small(c, 'lng_s', lng.partition_broadcast(128), [128, 1024])
    lnb_s = load_small(c, 'lnb_s', lnb.partition_broadcast(128), [128, 1024])
    bs_s = load_small(c, 'bs_s', bs_d.partition_broadcast(128), [128, 1024])
    wsT_f = load_small(c, 'wsT_f', wsT_d, [128, 8, 128], BF16, q='pool')
    tri_s = load_small(c, 'tri_s', tri_d, [128, 128], BF16, q='pool')
    wsT_m = SB(nc, 'wsT_m', [128, 8, 128], BF16)
    for g in range(8):
        kb.op('dve', lambda e, g=g: e.tensor_tensor(out=wsT_m[:, g, :], in0=wsT_f[:, g, :], in1=tri_s[:],
                                                    op=ALU.mult), r=['wsT_f', 'tri_s'], w=[('wsT_m', g)])
    load_xT(c, x_d)
    rmsnorm_T(c, g1s, 'g1s')
    ffn_T(c, wg, wu, wd, dff)
    store_T(c, x1_o, c.xT, 'xT')
    rmsnorm_T(c, g2s, 'g2s')
    w_v = w_in.rearrange("(kc p) f -> p kc f", p=128)
    hkeys = lambda t0: [('hT', kc, t0) for kc in range(16)]
    gbank = 0
    for ip in range(4):
        bv, kv = load_wt(c, w_v, ip * 256, 256)
        bg_, kg = load_wt(c, w_v, 1024 + ip * 256, 256)
        for j in range(2):
            ch = ip * 2 + j
            for (t0, tn) in c.TB:
                b0 = gbank % 4
                b1 = (gbank + 1) % 4
                gbank += 2
                kb.mm(c.bank(b0, tn), [(bv[:, kc, j * 128:(j + 1) * 128], c.hT[:, kc, t0:t0 + tn]) for kc in range(16)],
                      r=[kv] + hkeys(t0), w=[('ps', b0)])
                kb.mm(c.bank(b1, tn), [(bg_[:, kc, j * 128:(j + 1) * 128], c.hT[:, kc, t0:t0 + tn]) for kc in range(16)],
                      r=[kg] + hkeys(t0), w=[('ps', b1)])
                ti = c.ntmp % 2
                c.ntmp += 1
                si = c.nstg % 2
                c.nstg += 1
                kb.op('act', lambda e: e.activation(out=c.tmp[ti][:, :tn], in_=c.bank(b1, tn), func=AF.Sigmoid),
                      r=[('ps', b1)], w=[('tmp', ti)])
                kb.op('dve', lambda e: e.tensor_tensor(out=c.stg[si][:, :tn], in0=c.bank(b0, tn), in1=c.tmp[ti][:, :tn],
                                                       op=ALU.mult), r=[('ps', b0), ('tmp', ti)], w=[('stg', si)])
                kb.dma('sp', a_o[ch * 128:(ch + 1) * 128, t0:t0 + tn], c.stg[si][:, :tn], r=[('stg', si)],
                       w=[('a_o', ch, t0)])
    for ip in range(4):
        bu_, ku = load_wt(c, w_v, 2048 + ip * 256, 256)
        for j in range(2):
            g = ip * 2 + j
            for (t0, tn) in c.TB:
                b0 = gbank % 4
                gbank += 1
                kb.mm(c.bank(b0, tn), [(bu_[:, kc, j * 128:(j + 1) * 128], c.hT[:, kc, t0:t0 + tn]) for kc in range(16)],
                      r=[ku] + hkeys(t0), w=[('ps', b0)])
                gelu_from(c, c.bank(b0, tn), ('ps', b0), c.actT[g // 4][:, g % 4, t0:t0 + tn], ('uT', g, t0), 128, tn)
    vg = SB(nc, 'vg', [128, 256], F32)
    vln = SB(nc, 'vln', [128, 256], BF16)
    stats = SB(nc, 'stats', [128, 6], F32)
    mv = SB(nc, 'mv', [128, 2], F32)
    NT = T // 128
    for ip in range(4):
        bw, kw_ = load_wt(c, w_v, 3072 + ip * 256, 256)
        for tt in range(NT):
            t0b = (tt * 128 // 512) * 512
            b0 = gbank % 4
            gbank += 1
            kb.mm(c.bank(b0, 256), [(c.hT[:, kc, tt * 128:(tt + 1) * 128], bw[:, kc, 0:256]) for kc in range(16)],
                  r=[kw_] + hkeys(t0b), w=[('ps', b0)])
            gelu_from(c, c.bank(b0, 256), ('ps', b0), vg[:, :], 'vg', 128, 256)
            for gg in range(2):
                g = ip * 2 + gg
                sl = slice(gg * 128, (gg + 1) * 128)
                gsl = slice(g * 128, (g + 1) * 128)
                kb.op('dve', lambda e: e.bn_stats(out=stats[:], in_=vg[:, sl]), r=['vg'], w=['stats'])
                kb.op('dve', lambda e: e.bn_aggr(out=mv[:], in_=stats[:]), r=['stats'], w=['mv'])
                kb.op('act', lambda e: e.activation(out=mv[:, 1:2], in_=mv[:, 1:2], func=AF.Sqrt, bias=c.eps_sb[:, 0:1]),
                      r=['mv', 'eps'], w=['mv'])
                kb.op('dve', lambda e: e.reciprocal(out=mv[:, 1:2], in_=mv[:, 1:2]), r=['mv'], w=['mv'])
                kb.op('dve', lambda e: e.tensor_scalar(out=vg[:, sl], in0=vg[:, sl], scalar1=mv[:, 0:1],
                                                       scalar2=mv[:, 1:2], op0=ALU.subtract, op1=ALU.mult),
                      r=['vg', 'mv'], w=['vg'])
                kb.op('dve', lambda e: e.tensor_tensor(out=vg[:, sl], in0=vg[:, sl], in1=lng_s[:, gsl], op=ALU.mult),
                      r=['vg', 'lng_s'], w=['vg'])
                kb.op('dve', lambda e: e.tensor_tensor(out=vln[:, sl], in0=vg[:, sl], in1=lnb_s[:, gsl], op=ALU.add),
                      r=['vg', 'lnb_s'], w=[('vln', gg)])
                b1 = 4 + (gbank % 2)
                gbank += 1
                kb.mm(c.bank(b1, 128), [(vln[:, sl], wsT_m[:, g, :])], r=[('vln', gg), ('wsT_m', g)], w=[('ps', b1)])
                si = c.nstg % 2
                c.nstg += 1
                kb.op('dve', lambda e: e.tensor_tensor(out=c.stg[si][:, :128], in0=c.bank(b1, 128), in1=bs_s[:, gsl],
                                                       op=ALU.add), r=[('ps', b1), 'bs_s'], w=[('stg', si)])
                kb.op('dve', lambda e: e.tensor_tensor(out=c.stg[si][:, :128], in0=c.stg[si][:, :128],
                                                       in1=c.actT[g // 4][:, g % 4, tt * 128:(tt + 1) * 128], op=ALU.mult),
                      r=[('stg', si), ('uT', g, t0b)], w=[('stg', si)])
                kb.dma('sp', bo_o[g * 128:(g + 1) * 128, tt * 128:(tt + 1) * 128], c.stg[si][:, :128], r=[('stg', si)],
                       w=[('bo_o', g, tt)])
    if fused:
        kb.barrier()
        return nc
    kb.finish('sp')
    return nc


def proj_out_T(c, w_d, ncols_total, out_d, gbank0=0):
    kb = c.kb
    w_v = w_d.rearrange("(kc p) f -> p kc f", p=128)
    gbank = gbank0
    col = 0
    while col < ncols_total:
        ncol_t = min(256, ncols_total - col)
        buf, key = load_wt(c, w_v, col, ncol_t)
        j0 = 0
        while j0 < ncol_t:
            m = min(128, ncol_t - j0)
            for (t0, tn) in c.TB:
                b0 = gbank % 4
                gbank += 1
                kb.mm(c.ps[:m, b0 * 512:b0 * 512 + tn],
                      [(buf[:, kc, j0:j0 + m], c.hT[:, kc, t0:t0 + tn]) for kc in range(16)],
                      r=[key] + [('hT', kc, t0) for kc in range(16)], w=[('ps', b0)])
                si = c.nstg % 2
                c.nstg += 1
                if si == 0:
                    kb.op('act', lambda e: e.activation(out=c.stg[si][:m, :tn], in_=c.ps[:m, b0 * 512:b0 * 512 + tn],
                                                        func=AF.Copy), r=[('ps', b0)], w=[('stg', si)])
                else:
                    kb.op('dve', lambda e: e.tensor_copy(out=c.stg[si][:m, :tn], in_=c.ps[:m, b0 * 512:b0 * 512 + tn]),
                          r=[('ps', b0)], w=[('stg', si)])
                kb.dma('sp', out_d[col + j0:col + j0 + m, t0:t0 + tn], c.stg[si][:m, :tn], r=[('stg', si)],
                       w=[('z_o', col + j0, t0)])
            j0 += m
        col += ncol_t
    return gbank


def proj_resid_T(c, w_d, gbank0=0):
    kb = c.kb
    w_v = w_d.rearrange("(kc p) f -> p kc f", p=128)
    gbank = gbank0
    for dcp in range(8):
        buf, key = load_wt(c, w_v, dcp * 256, 256)
        for j in range(2):
            dc = dcp * 2 + j
            for (t0, tn) in c.TB:
                b0 = gbank % 4
                gbank += 1
                kb.mm(c.bank(b0, tn), [(buf[:, kc, j * 128:(j + 1) * 128], c.hT[:, kc, t0:t0 + tn]) for kc in range(16)],
                      r=[key] + [('hT', kc, t0) for kc in range(16)], w=[('ps', b0)])
                kb.op('dve', lambda e: e.tensor_tensor(out=c.xT[:, dc, t0:t0 + tn], in0=c.bank(b0, tn),
                                                       in1=c.xT[:, dc, t0:t0 + tn], op=ALU.add),
                      r=[('ps', b0), ('xT', dc, t0)], w=[('xT', dc, t0)])
    return gbank


def build_L2(T=1024, dff=DFF, nz=5168, nc=None, over=None, core=None):
    fused = nc is not None
    nc = bass.Bass("TRN2", target_bir_lowering=False) if nc is None else nc
    dt = mk_dt(nc, over)
    HALO = 32
    x_d = dt("x1Td", [D, T])
    a_d = None if fused else dt("aTh", [1024, HALO + T])
    bo_d = dt("boTd", [1024, T])
    cw_d = dt("conv_w", [128, 8, 31])
    cb_d = dt("conv_b", [128, 8])
    cg_d = dt("cln_g", [128, 8])
    cbb_d = dt("cln_b", [128, 8])
    wout_d = dt("w_out", [D, D])
    gA = dt("gA", [128, 16]); wgA = dt("wgA", [D, dff]); wuA = dt("wuA", [D, dff]); wdA = dt("wdA", [dff, D])
    gB = dt("gB", [128, 16]); wgB = dt("wgB", [D, dff]); wuB = dt("wuB", [D, dff]); wdB = dt("wdB", [dff, D])
    gM = dt("gM", [128, 16])
    win_d = dt("w_in1", [D, nz])
    x4_o = dt("x4T", [D, T], "ExternalOutput")
    z_o = dt("zT", [nz, T], "ExternalOutput")
    c = Core(nc, T) if core is None else core
    kb = c.kb
    extra_tiles(c)
    cw = load_small(c, 'cw', cw_d, [128, 8, 31])
    cb = load_small(c, 'cb', cb_d, [128, 8])
    cg = load_small(c, 'cgn', cg_d, [128, 8])
    cbb = load_small(c, 'cbb', cbb_d, [128, 8])
    gAs = load_small(c, 'gAs', gA, [128, 16])
    gBs = load_small(c, 'gBs', gB, [128, 16])
    gMs = load_small(c, 'gMs', gM, [128, 16])
    kinv = SB(nc, 'kinv', [128, 1], F32)
    kb.op('dve', lambda e: e.memset(kinv[:], 1.0 / 1024), w=['kinv'])
    load_xT(c, x_d)
    kb.dma('pool', c.hT[:, 8:16, :], bo_d.rearrange("(g p) t -> p g t", p=128),
           w=[('hT', k, t0) for k in range(8, 16) for (t0, _) in c.TB])
    abuf = [c.wg[i][:].rearrange("p a b -> p (a b)").bitcast(F32) for i in range(2)]
    ybuf = [c.wd[i][:].rearrange("p a b -> p (a b)").bitcast(F32) for i in range(4)]
    actkeys = lambda i: [('actT', i, fcl, t0) for fcl in range(4) for (t0, _) in c.TB]
    mr = c.actT[0][:].rearrange("p a b -> p (a b)").bitcast(F32)
    mean = mr[:, 0:T]
    rstd = mr[:, T:2 * T]
    S1 = [6, 7]
    S2 = [4, 5]
    for ch in range(8):
        ab = ch % 2
        a_sb = abuf[ab]
        if not fused:
            kb.dma('sp', a_sb[:, 0:HALO + T], a_d[ch * 128:(ch + 1) * 128, :], w=[('wg', ab)])
        else:
            if over.get('aT_prev') is None:
                kb.op('dve', lambda e: e.memset(a_sb[:, 0:HALO], 0.0), w=[('wg', ab)])
            else:
                kb.dma('sp', a_sb[:, 0:HALO], over['aT_prev'][ch * 128:(ch + 1) * 128, T - HALO:T], w=[('wg', ab)])
            kb.dma('sp', a_sb[:, HALO:HALO + T], over['aT_cur'][ch * 128:(ch + 1) * 128, :], r=[('wg', ab)],
                   w=[('wg', ab)])
        y = ybuf[ch // 2][:, (ch % 2) * T:(ch % 2) * T + T]
        ykey = ('wd', ch // 2)
        for k in range(31):
            if k == 0:
                kb.op('dve', lambda e: e.tensor_scalar(out=y, in0=a_sb[:, 2:2 + T], scalar1=cw[:, ch, 0:1], scalar2=None,
                                                       op0=ALU.mult), r=[('wg', ab), 'cw'], w=[ykey])
            else:
                kb.op('dve', lambda e, k=k: e.scalar_tensor_tensor(out=y, in0=a_sb[:, 2 + k:2 + k + T],
                                                                  scalar=cw[:, ch, k:k + 1], in1=y,
                                                                  op0=ALU.mult, op1=ALU.add),
                      r=[('wg', ab), 'cw', ykey], w=[ykey])
        kb.op('dve', lambda e: e.tensor_scalar(out=y, in0=y, scalar1=cb[:, ch:ch + 1], scalar2=None, op0=ALU.add),
              r=[ykey, 'cb'], w=[ykey])
        for bi, (t0, tn) in enumerate(c.TB):
            i = c.nsq % 2
            c.nsq += 1
            kb.op('act', lambda e: e.activation(out=c.sq[i][:, :tn], in_=y[:, t0:t0 + tn], func=AF.Copy),
                  r=[ykey], w=[('sq', i)])
            kb.mm1(c.bank(S1[bi], tn), c.ones[:], c.sq[i][:, :tn], ch == 0, ch == 7,
                   r=['ones', ('sq', i)], w=([('ps', S1[bi])] if ch in (0, 7) else []))
            i = c.nsq % 2
            c.nsq += 1
            kb.op('act', lambda e: e.activation(out=c.sq[i][:, :tn], in_=y[:, t0:t0 + tn], func=AF.Square),
                  r=[ykey], w=[('sq', i)])
            kb.mm1(c.bank(S2[bi], tn), c.ones[:], c.sq[i][:, :tn], ch == 0, ch == 7,
                   r=['ones', ('sq', i)], w=([('ps', S2[bi])] if ch in (0, 7) else []))
    for bi, (t0, tn) in enumerate(c.TB):
        kb.op('act', lambda e: e.activation(out=mean[:, t0:t0 + tn], in_=c.bank(S1[bi], tn), func=AF.Copy,
                                            scale=1.0 / 1024), r=[('ps', S1[bi])], w=actkeys(0))
        kb.op('dve', lambda e: e.tensor_tensor(out=c.tmp[0][:, :tn], in0=mean[:, t0:t0 + tn], in1=mean[:, t0:t0 + tn],
                                               op=ALU.mult), r=actkeys(0), w=[('tmp', 0)])
        kb.op('dve', lambda e: e.scalar_tensor_tensor(out=rstd[:, t0:t0 + tn], in0=c.bank(S2[bi], tn), scalar=kinv[:, 0:1],
                                                      in1=c.tmp[0][:, :tn], op0=ALU.mult, op1=ALU.subtract),
              r=[('ps', S2[bi]), 'kinv', ('tmp', 0)], w=actkeys(0))
        kb.op('act', lambda e: e.activation(out=rstd[:, t0:t0 + tn], in_=rstd[:, t0:t0 + tn], func=AF.Sqrt,
                                            bias=c.eps_sb[:, 0:1]), r=actkeys(0) + ['eps'], w=actkeys(0))
        kb.op('dve', lambda e: e.reciprocal(out=rstd[:, t0:t0 + tn], in_=rstd[:, t0:t0 + tn]), r=actkeys(0), w=actkeys(0))
    for ch in range(8):
        y = ybuf[ch // 2][:, (ch % 2) * T:(ch % 2) * T + T]
        ykey = ('wd', ch // 2)
        for (t0, tn) in c.TB:
            kb.op('dve', lambda e: e.tensor_tensor(out=y[:, t0:t0 + tn], in0=y[:, t0:t0 + tn], in1=mean[:, t0:t0 + tn],
                                                   op=ALU.subtract), r=[ykey] + actkeys(0), w=[ykey])
            kb.op('dve', lambda e: e.tensor_tensor(out=y[:, t0:t0 + tn], in0=y[:, t0:t0 + tn], in1=rstd[:, t0:t0 + tn],
                                                   op=ALU.mult), r=[ykey] + actkeys(0), w=[ykey])
            kb.op('act', lambda e: e.activation(out=c.hT[:, ch, t0:t0 + tn], in_=y[:, t0:t0 + tn], func=AF.Silu,
                                                scale=cg[:, ch:ch + 1], bias=cbb[:, ch:ch + 1]),
                  r=[ykey, 'cgn', 'cbb'], w=[('hT', ch, t0)])
    gb_ = proj_resid_T(c, wout_d)
    rmsnorm_T(c, gAs, 'gAs')
    ffn_T(c, wgA, wuA, wdA, dff)
    rmsnorm_T(c, gBs, 'gBs')
    ffn_T(c, wgB, wuB, wdB, dff)
    store_T(c, x4_o, c.xT, 'xT')
    rmsnorm_T(c, gMs, 'gMs')
    proj_out_T(c, win_d, nz, z_o)
    if fused:
        kb.barrier()
        return nc
    kb.finish('sp')
    return nc


def build_L4(T=1024, dff=DFF, nc=None, over=None, core=None):
    fused = nc is not None
    nc = bass.Bass("TRN2", target_bir_lowering=False) if nc is None else nc
    dt = mk_dt(nc, over)
    x_d = dt("x4Td", [D, T])
    o_d = None if fused else dt("oTd", [D, T])
    wout_d = dt("w_out1", [D, D])
    gA = dt("gA", [128, 16]); wgA = dt("wgA", [D, dff]); wuA = dt("wuA", [D, dff]); wdA = dt("wdA", [dff, D])
    gF = dt("gF", [128, 16])
    y_o = dt("yT", [D, T], "ExternalOutput")
    c = Core(nc, T) if core is None else core
    kb = c.kb
    gAs = load_small(c, 'gAs', gA, [128, 16])
    gFs = load_small(c, 'gFs', gF, [128, 16])
    load_xT(c, x_d)
    if not fused:
        ov = o_d.rearrange("(kc p) t -> p kc t", p=128)
        for k0 in range(0, 16, 8):
            kb.dma('pool', c.hT[:, k0:k0 + 8, :], ov[:, k0:k0 + 8, :],
                   w=[('hT', k, t0) for k in range(k0, k0 + 8) for (t0, _) in c.TB])
    else:
        extra_tiles(c)
        fl = load_small(c, 'flag', over['flag'], [128, 2])
        x1v = over['x4T_1'].rearrange("(kc p) t -> p kc t", p=128)
        for kc in range(16):
            for (t0, tn) in c.TB:
                si = c.nstg % 2
                c.nstg += 1
                kb.dma('sp', c.stg[si][:, :tn], x1v[:, kc, t0:t0 + tn], w=[('stg', si)])
                kb.op('dve', lambda e: e.tensor_scalar(out=c.xT[:, kc, t0:t0 + tn], in0=c.xT[:, kc, t0:t0 + tn],
                                                       scalar1=fl[:, 0:1], scalar2=None, op0=ALU.mult),
                      r=[('xT', kc, t0), 'flag'], w=[('xT', kc, t0)])
                kb.op('dve', lambda e: e.scalar_tensor_tensor(out=c.xT[:, kc, t0:t0 + tn], in0=c.stg[si][:, :tn],
                                                              scalar=fl[:, 1:2], in1=c.xT[:, kc, t0:t0 + tn],
                                                              op0=ALU.mult, op1=ALU.add),
                      r=[('stg', si), ('xT', kc, t0), 'flag'], w=[('xT', kc, t0)])
        otm = [SB(nc, 'otm%d' % i, [128, 2048], BF16) for i in range(2)]
        otn = [SB(nc, 'otn%d' % i, [128, 2048], BF16) for i in range(2)]
        idb = load_small(c, 'idb4', over['ident'], [128, 128], BF16, q='pool')
        ntr = 0
        for tt in range(T // 128):
            i = tt % 2
            t0b = (tt * 128 // 512) * 512
            kb.dma('pool', otm[i][:], over['o_tok'][tt * 128:(tt + 1) * 128, :], w=[('otm', i)])
            kb.dma('pool', otn[i][:], over['o_tok1'][tt * 128:(tt + 1) * 128, :], w=[('otn', i)])
            kb.op('dve', lambda e: e.tensor_scalar(out=otm[i][:], in0=otm[i][:], scalar1=fl[:, 0:1], scalar2=None,
                                                   op0=ALU.mult), r=[('otm', i), 'flag'], w=[('otm', i)])
            kb.op('dve', lambda e: e.scalar_tensor_tensor(out=otm[i][:], in0=otn[i][:], scalar=fl[:, 1:2], in1=otm[i][:],
                                                          op0=ALU.mult, op1=ALU.add),
                  r=[('otn', i), ('otm', i), 'flag'], w=[('otm', i)])
            for k4 in range(4):
                tb = 2 + ntr % 2
                ntr += 1
                tbk = c.ps[:, tb * 512:(tb + 1) * 512].bitcast(BF16)
                pe_multi(kb, [(lambda e, j=j: e.transpose(out=tbk[:, j * 128:(j + 1) * 128],
                                                          in_=otm[i][:, (k4 * 4 + j) * 128:(k4 * 4 + j + 1) * 128],
                                                          identity=idb[:])) for j in range(4)],
                         r=[('otm', i), 'idb4'], w=[('ps', tb)])
                kb.op('dve', lambda e: e.tensor_copy(out=c.hT[:, k4 * 4:k4 * 4 + 4, tt * 128:(tt + 1) * 128],
                                                     in_=tbk[:, 0:512].rearrange("p (a b) -> p a b", b=128)),
                      r=[('ps', tb)], w=[('hT', k, t0b) for k in range(k4 * 4, k4 * 4 + 4)])
    proj_resid_T(c, wout_d)
    rmsnorm_T(c, gAs, 'gAs')
    ffn_T(c, wgA, wuA, wdA, dff)
    rmsnorm_T(c, gFs, 'gFs', out=c.xT, okey='xT')
    store_T(c, y_o, c.xT, 'xT')
    if fused:
        kb.barrier()
        return nc
    kb.finish('sp')
    return nc


S = 2048
NQT = 16
SCALE = 128 ** -0.5
MAGIC = 12582912.0
PI = 3.141592653589793


def pe_multi(kb, fns, r=(), w=()):
    deps = kb._deps(r, w)
    kb._emit_waits('pe', deps, skip_sem=kb.sem['pe'].num)
    inst = None
    for f in fns:
        inst = f(kb.nc.tensor)
    kb.cnt['pe'] += 1
    inst.then_inc(kb.sem['pe'], 1)
    tok = (kb.sem['pe'].num, kb.cnt['pe'])
    kb._commit(tok, r, w)
    return tok


def build_L3(nc=None, over=None, kb=None, ps=None, gp=0):
    fused = nc is not None
    nc = bass.Bass("TRN2", target_bir_lowering=False) if nc is None else nc
    dt = mk_dt(nc, over)
    if not fused:
        q_d = dt("qT", [8, 128, S]); qp_d = dt("qPT", [8, 128, S])
        ks_d = dt("ksT", [2, 128, S]); ksp_d = dt("ksPT", [2, 128, S])
        kw_d = dt("kwT", [2, 128, S]); kwp_d = dt("kwPT", [2, 128, S])
        kc_d = dt("kcT", [128, 2, S]); vc_d = dt("vcT", [128, 2, S])
        vs_d = dt("vs", [2, S, 128]); vw_d = dt("vw", [2, S, 128])
        gl_d = dt("gl", [S, 24])
    else:
        zT = over['zT']
        rows = lambda base, i: zT[base + i * 128:base + (i + 1) * 128, :]
        q_d = [rows(0, gp * 8 + hh) for hh in range(8)]
        qp_d = [None] * 8
        ks_d = [rows(3072, 2 * gp + g) for g in range(2)]; ksp_d = [None] * 2
        kw_d = [rows(4096, 2 * gp + g) for g in range(2)]; kwp_d = [None] * 2
        kc_d = zT[2048 + 2 * gp * 128:2048 + (2 * gp + 2) * 128, :].rearrange("(g p) s -> p g s", p=128)
        vc_d = zT[2560 + 2 * gp * 128:2560 + (2 * gp + 2) * 128, :].rearrange("(g p) s -> p g s", p=128)
        vsT_d = [rows(3584, 2 * gp + g) for g in range(2)]
        vwT_d = [rows(4608, 2 * gp + g) for g in range(2)]
        glT_d = zT[5120 + gp * 24:5120 + (gp + 1) * 24, :]
    pos_d = dt("pos", [1, S], dtype=I32)
    inv_d = dt("inv", [128, 1]); sgn_d = dt("sgn", [128, 1])
    pek_d = dt("pekT", [128, 32]); w1k_d = dt("w1k", [4096, 256]); w2k_d = dt("w2k", [256, 128])
    pev_d = dt("pevT", [128, 32]); w1v_d = dt("w1v", [4096, 256]); w2v_d = dt("w2v", [256, 128])
    ovl_d = dt("ovl", [128, 32]); cm_d = dt("cm", [128, 16, 128]); fbv_d = dt("fbv", [128, 16, 32])
    tri_d = dt("tri", [128, 128]); tri2_d = dt("tri2", [128, 128]); id_d = dt("ident", [128, 128])
    o_o = dt("o", [S, 1024], "ExternalOutput")
    kb = KB(nc) if kb is None else kb
    A = lambda name, shape, dtype=F32: SB(nc, 's_' + name, list(shape), dtype)

    def small(name, d_ap, shape, dtype=F32, q='sp'):
        t = A(name, shape, dtype)
        kb.dma(q, t[:], d_ap, w=[name])
        return t

    def const(name, val):
        t = A(name, [128, 1])
        kb.op('dve', lambda e: e.memset(t[:], val), w=[name])
        return t
    ps = nc.alloc_psum_tensor('ps', [128, 8 * 512], F32) if ps is None else ps
    bank = lambda b, n=512, p=128: ps[:p, b * 512:b * 512 + n]
    bankbf = lambda b: ps[:, b * 512:(b + 1) * 512].bitcast(BF16)
    H = S // 2
    xs = [A('xs%d' % i, [128, H]) for i in range(2)]
    xp = [A('xp%d' % i, [128, H]) for i in range(2)]
    inv = small('inv', inv_d, [128, 1]); sgn = small('sgn', sgn_d, [128, 1])
    ovl = small('ovl', ovl_d, [128, 32]); cm = small('cm', cm_d, [128, 16, 128]); fbv = small('fbv', fbv_d, [128, 16, 32])
    tri = small('tri', tri_d, [128, 128], BF16, 'pool'); tri2 = small('tri2', tri2_d, [128, 128], BF16, 'pool')
    idf = small('idf', id_d, [128, 128]); idb = small('idb', id_d, [128, 128], BF16, 'pool')
    if not fused:
        gl = small('gl', gl_d.rearrange("(n p) c -> p n c", p=128), [128, 16, 24])
        kb.op('act', lambda e: e.activation(out=gl[:], in_=gl[:], func=AF.Sigmoid), r=['gl'], w=['gl'])
    else:
        gl = A('gl', [128, 16, 24])
        for hf in range(2):
            kb.dma('sp', xs[hf][:24, :], glT_d[:, hf * H:(hf + 1) * H], w=[('xs', hf)])
        for kt in range(16):
            pe_multi(kb, [lambda e: e.transpose(out=bank(7, 24), in_=xs[kt // 8][:24, (kt % 8) * 128:(kt % 8 + 1) * 128],
                                                identity=idf[:24, :24])], r=[('xs', kt // 8), 'idf'], w=[('ps', 7)])
            kb.op('act', lambda e: e.activation(out=gl[:, kt, :], in_=bank(7, 24), func=AF.Sigmoid),
                  r=[('ps', 7)], w=['gl'])
    c_i2p = const('c_i2p', 1.0 / (2 * PI)); c_mag = const('c_mag', MAGIC); c_nmag = const('c_nmag', -MAGIC)
    c_n2p = const('c_n2p', -2 * PI); c_pi = const('c_pi', PI); c_npi = const('c_npi', -PI); c_hpi = const('c_hpi', PI / 2)
    c_tiny = const('c_tiny', 1e-30); c_one = const('c_one', 1.0); c_cg = const('c_cg', 0.044715)
    c_nsc = const('c_nsc', -SCALE)
    posi = A('posi', [128, S], I32)
    kb.dma('sp', posi[:], pos_d.partition_broadcast(128), w=['posi'])
    ang = A('ang', [128, S]); cosT = A('cosT', [128, S]); sinT = A('sinT', [128, S])
    kb.op('dve', lambda e: e.tensor_copy(out=ang[:], in_=posi[:]), r=['posi'], w=['ang'])
    kb.op('dve', lambda e: e.tensor_scalar(out=ang[:], in0=ang[:], scalar1=inv[:, 0:1], scalar2=None, op0=ALU.mult),
          r=['ang', 'inv'], w=['ang'])

    def sin_of(dst, dkey, shift):
        wk = posi[:].bitcast(F32)
        src = ang
        if shift is not None:
            kb.op('dve', lambda e: e.tensor_scalar(out=dst[:], in0=ang[:], scalar1=shift[:, 0:1], scalar2=None, op0=ALU.add),
                  r=['ang'], w=[dkey])
            src = dst
        kb.op('dve', lambda e: e.tensor_scalar(out=wk, in0=src[:], scalar1=c_i2p[:, 0:1], scalar2=c_mag[:, 0:1],
                                               op0=ALU.mult, op1=ALU.add), r=[dkey, 'ang', 'posi'], w=['posi'])
        kb.op('dve', lambda e: e.tensor_scalar(out=wk, in0=wk, scalar1=c_nmag[:, 0:1], scalar2=None, op0=ALU.add),
              r=['posi'], w=['posi'])
        kb.op('dve', lambda e: e.scalar_tensor_tensor(out=dst[:], in0=wk, scalar=c_n2p[:, 0:1], in1=src[:],
                                                      op0=ALU.mult, op1=ALU.add), r=['posi', 'ang', dkey], w=[dkey])
        kb.op('dve', lambda e: e.tensor_scalar(out=dst[:], in0=dst[:], scalar1=c_pi[:, 0:1], scalar2=c_npi[:, 0:1],
                                               op0=ALU.min, op1=ALU.max), r=[dkey], w=[dkey])
        kb.op('act', lambda e: e.activation(out=dst[:], in_=dst[:], func=AF.Sin), r=[dkey], w=[dkey])
    sin_of(sinT, 'sinT', None)
    kb.op('dve', lambda e: e.tensor_scalar(out=sinT[:], in0=sinT[:], scalar1=sgn[:, 0:1], scalar2=None, op0=ALU.mult),
          r=['sinT', 'sgn'], w=['sinT'])
    sin_of(cosT, 'cosT', c_hpi)
    qT = A('qTb', [128, 8, S], BF16); qr = A('qr', [128, 8, S], BF16)
    ksr = A('ksr', [128, 2, S], BF16); kwr = A('kwr', [128, 2, S], BF16)
    kcT = A('kcTb', [128, 2, S], BF16); vcT = A('vcTb', [128, 2, S], BF16)
    vs = A('vsb', [128, 2, 16, 132], BF16); vw = A('vwb', [128, 2, 16, 132], BF16)
    kb.dma('pool', kcT[:], kc_d, w=['kcT'])
    kb.dma('pool', vcT[:], vc_d, w=['vcT'])
    kb.op('dve', lambda e: e.memset(vs[:, :, :, 128:129], 1.0), w=['vs1'])
    kb.op('dve', lambda e: e.memset(vw[:, :, :, 128:129], 1.0), w=['vw1'])
    if not fused:
        for g in range(2):
            kb.dma('pool', vs[:, g, :, 0:128], vs_d[g].rearrange("(n p) d -> p n d", p=128), w=[('vs', g)])
            kb.dma('pool', vw[:, g, :, 0:128], vw_d[g].rearrange("(n p) d -> p n d", p=128), w=[('vw', g)])
    else:
        vtmp = [xp[i][:].bitcast(BF16) for i in range(2)]
        nv = 0
        for g in range(2):
            for (srcs, dstt, key) in ((vsT_d, vs, 'vs'), (vwT_d, vw, 'vw')):
                vi = nv % 2
                nv += 1
                kb.dma('pool', vtmp[vi], srcs[g], w=[('xp', vi)])
                for k4 in range(4):
                    tb = 2 + k4 % 2
                    tbk = bankbf(tb)
                    pe_multi(kb, [(lambda e, j=j: e.transpose(out=tbk[:, j * 128:(j + 1) * 128],
                                                              in_=vtmp[vi][:, (k4 * 4 + j) * 128:(k4 * 4 + j + 1) * 128],
                                                              identity=idb[:])) for j in range(4)],
                             r=[('xp', vi), 'idb'], w=[('ps', tb)])
                    kb.op('dve', lambda e: e.tensor_copy(out=dstt[:, g, k4 * 4:k4 * 4 + 4, 0:128],
                                                         in_=tbk[:, 0:512].rearrange("p (a b) -> p a b", b=128)),
                          r=[('ps', tb)], w=[(key, g)])
    nst = [0]

    def rope(src_d, srcp_d, dst, dkey, plain=None):
        for hf in range(2):
            i = nst[0] % 2
            nst[0] += 1
            sl = slice(hf * H, (hf + 1) * H)
            kb.dma('sp', xs[i][:], src_d[:, sl], w=[('xs', i)])
            if srcp_d is not None:
                kb.dma('sp', xp[i][:], srcp_d[:, sl], w=[('xp', i)])
            else:
                kb.dma('sp', xp[i][0:16, :], src_d[16:32, sl], w=[('xp', i)])
                kb.dma('sp', xp[i][16:32, :], src_d[0:16, sl], r=[('xp', i)], w=[('xp', i)])
                kb.dma('sp', xp[i][32:128, :], src_d[32:128, sl], r=[('xp', i)], w=[('xp', i)])
            if plain is not None:
                kb.op('act', lambda e: e.activation(out=plain[:, sl], in_=xs[i][:], func=AF.Copy), r=[('xs', i)],
                      w=[(dkey, 'plain', hf)])
            kb.op('dve', lambda e: e.tensor_tensor(out=xs[i][:], in0=xs[i][:], in1=cosT[:, sl], op=ALU.mult),
                  r=[('xs', i), 'cosT'], w=[('xs', i)])
            kb.op('dve', lambda e: e.tensor_tensor(out=xp[i][:], in0=xp[i][:], in1=sinT[:, sl], op=ALU.mult),
                  r=[('xp', i), 'sinT'], w=[('xp', i)])
            kb.op('dve', lambda e: e.tensor_tensor(out=dst[:, sl], in0=xs[i][:], in1=xp[i][:], op=ALU.add),
                  r=[('xs', i), ('xp', i)], w=[(dkey, hf)])
    for hh in range(8):
        rope(q_d[hh], qp_d[hh], qr[:, hh, :], ('qr', hh), plain=qT[:, hh, :])
    for g in range(2):
        rope(ks_d[g], ksp_d[g], ksr[:, g, :], ('ksr', g))
        rope(kw_d[g], kwp_d[g], kwr[:, g, :], ('kwr', g))
    qkeys = lambda hh: [(('qr', hh), 0), (('qr', hh), 1), (('qr', hh), 'plain', 0), (('qr', hh), 'plain', 1)]
    w1 = A('w1', [128, 32, 256], BF16)
    w2 = A('w2', [128, 2, 128], BF16)
    hid = A('hid', [128, 2, 128], BF16)
    hx = A('hx', [128, 128]); ha = A('ha', [128, 128]); hb = A('hb', [128, 128])
    cpe = A('cpe', [128, 2])
    kcmpT = A('kcmpT', [128, 2, 128], BF16)
    vcmp = A('vcmp', [128, 2, 128])
    kb.op('dve', lambda e: e.memset(hid[:], 0.0), w=['hid'])
    kb.op('dve', lambda e: e.memset(kcmpT[:], 0.0), w=['kcmpT'])
    kb.op('dve', lambda e: e.memset(vcmp[:], 0.0), w=['vcmp'])
    for which, (pe_d, w1_d, w2_d, srcT, skey) in enumerate([(pek_d, w1k_d, w2k_d, kcT, 'kcT'), (pev_d, w1v_d, w2v_d, vcT, 'vcT')]):
        pe_b = small('pe_b%d' % which, pe_d, [128, 32], BF16, 'pool')
        for l0 in range(0, 32, 8):
            kb.dma('pool', w1[:, l0:l0 + 8, :], w1_d[l0 * 128:(l0 + 8) * 128, :].rearrange("(l p) h -> p l h", p=128),
                   w=[('w1', l0)])
        kb.dma('pool', w2[:], w2_d.rearrange("(c p) d -> p c d", p=128), w=['w2'])
        w1keys = [('w1', l0) for l0 in range(0, 32, 8)]
        src4 = srcT[:].rearrange("p g (n r) -> p g n r", r=16)
        for hc in range(2):
            kb.mm(bank(7, 1), [(w1[:, l, hc * 128:(hc + 1) * 128], pe_b[:, l:l + 1]) for l in range(32)],
                  r=w1keys + ['pe_b%d' % which], w=[('ps', 7)])
            kb.op('dve', lambda e: e.tensor_copy(out=cpe[:, hc:hc + 1], in_=bank(7, 1)), r=[('ps', 7)], w=[('cpe', hc)])
        for g in range(2):
            for hc in range(2):
                kb.mm(bank(6, 127), [(w1[:, l, hc * 128:(hc + 1) * 128], src4[:, g, (l // 16):(l // 16) + 127, l % 16])
                                     for l in range(32)], r=w1keys + [skey], w=[('ps', 6)])
                kb.op('dve', lambda e: e.tensor_scalar(out=hx[:, :127], in0=bank(6, 127), scalar1=cpe[:, hc:hc + 1],
                                                       scalar2=None, op0=ALU.add), r=[('ps', 6), ('cpe', hc)], w=['hx'])
                kb.op('act', lambda e: e.activation(out=ha[:, :127], in_=hx[:, :127], func=AF.Square), r=['hx'], w=['ha'])
                kb.op('dve', lambda e: e.tensor_scalar(out=ha[:, :127], in0=ha[:, :127], scalar1=c_cg[:, 0:1],
                                                       scalar2=c_one[:, 0:1], op0=ALU.mult, op1=ALU.add), r=['ha'], w=['ha'])
                kb.op('dve', lambda e: e.tensor_tensor(out=ha[:, :127], in0=hx[:, :127], in1=ha[:, :127], op=ALU.mult),
                      r=['hx', 'ha'], w=['ha'])
                kb.op('act', lambda e: e.activation(out=hb[:, :127], in_=ha[:, :127], func=AF.Sigmoid, scale=2.0 * GELU_C),
                      r=['ha'], w=['hb'])
                kb.op('dve', lambda e: e.tensor_tensor(out=hid[:, hc, :127], in0=hx[:, :127], in1=hb[:, :127], op=ALU.mult),
                      r=['hx', 'hb'], w=[('hid', hc)])
            if which == 0:
                kb.mm(bank(6, 127), [(w2[:, hc, :], hid[:, hc, :127]) for hc in range(2)],
                      r=['w2', ('hid', 0), ('hid', 1)], w=[('ps', 6)])
                kb.op('dve', lambda e: e.tensor_copy(out=kcmpT[:, g, :127], in_=bank(6, 127)), r=[('ps', 6)],
                      w=[('kcmpT', g)])
            else:
                kb.mm(bank(6, 128, 127), [(hid[:, hc, :127], w2[:, hc, :]) for hc in range(2)],
                      r=['w2', ('hid', 0), ('hid', 1)], w=[('ps', 6)])
                kb.op('dve', lambda e: e.tensor_copy(out=vcmp[:127, g, :], in_=bank(6, 128, 127)), r=[('ps', 6)],
                      w=[('vcmp', g)])
    sc4 = A('sc4', [128, 4, 128]); mx = A('mx', [128, 1]); rs4 = A('rs4', [128, 4])
    pcT4 = A('pcT4', [128, 4, 128])
    impm = A('impm', [128, 32]); wk32 = A('wk32', [128, 32]); m8 = A('m8', [128, 8]); m8b = A('m8b', [128, 8])
    sel = A('sel', [128, 32], BF16)
    pbuf = [A('pbuf%d' % i, [128, 512], BF16) for i in range(2)]
    pT = [A('pT%d' % i, [128, 512], BF16) for i in range(2)]
    oacc = [A('oacc%d' % i, [128, 4, 128]) for i in range(2)]
    gsc = [A('gsc%d' % i, [128, 1]) for i in range(2)]
    cnt = dict(o=0, c=0, j=0)
    X = mybir.AxisListType.X

    for g in range(2):
        for qt in range(NQT):
            oi = cnt['o'] % 2
            cnt['o'] += 1
            oa = oacc[oi]
            oakeys = [('oacc', oi, h) for h in range(4)]
            qsl = slice(qt * 128, (qt + 1) * 128)
            pe_multi(kb, [(lambda e, h=h: e.matmul(bank(6, 512)[:, h * 128:(h + 1) * 128], qT[:, g * 4 + h, qsl],
                                                   kcmpT[:, g, :], start=True, stop=True)) for h in range(4)],
                     r=[(('qr', g * 4 + h), 'plain', qt // 8) for h in range(4)] + [('kcmpT', g)], w=[('ps', 6)])
            kb.op('dve', lambda e: e.reduce_max(out=mx[:], in_=bank(6, 512), axis=X), r=[('ps', 6)], w=['mx'])
            kb.op('dve', lambda e: e.tensor_scalar(out=mx[:], in0=mx[:], scalar1=c_nsc[:, 0:1], scalar2=None,
                                                   op0=ALU.mult), r=['mx'], w=['mx'])
            kb.op('act', lambda e: e.activation(out=sc4[:].rearrange("p h n -> p (h n)"), in_=bank(6, 512), func=AF.Exp,
                                                scale=SCALE, bias=mx[:, 0:1]), r=[('ps', 6), 'mx'], w=['sc4'])
            kb.op('dve', lambda e: e.tensor_tensor(out=sc4[:], in0=sc4[:],
                                                   in1=cm[:, qt, :].unsqueeze(1).broadcast_to([128, 4, 128]),
                                                   op=ALU.mult), r=['sc4', 'cm'], w=['sc4'])
            kb.op('dve', lambda e: e.reduce_sum(out=rs4[:], in_=sc4[:], axis=X), r=['sc4'], w=['rs4'])
            kb.op('dve', lambda e: e.tensor_scalar(out=rs4[:], in0=rs4[:], scalar1=c_tiny[:, 0:1], scalar2=None,
                                                   op0=ALU.max), r=['rs4'], w=['rs4'])
            kb.op('dve', lambda e: e.reciprocal(out=rs4[:], in_=rs4[:]), r=['rs4'], w=['rs4'])
            kb.op('dve', lambda e: e.tensor_tensor(out=sc4[:], in0=sc4[:],
                                                   in1=rs4[:, :].unsqueeze(2).broadcast_to([128, 4, 128]),
                                                   op=ALU.mult), r=['sc4', 'rs4'], w=['sc4'])
            pe_multi(kb, [(lambda e, h=h: e.transpose(out=bank(6, 512)[:, h * 128:(h + 1) * 128], in_=sc4[:, h, :],
                                                      identity=idf[:])) for h in range(4)],
                     r=['sc4', 'idf'], w=[('ps', 6)])
            kb.op('act', lambda e: e.activation(out=pcT4[:].rearrange("p h n -> p (h n)"), in_=bank(6, 512), func=AF.Copy),
                  r=[('ps', 6)], w=['pcT4'])
            kb.mm(bank(7, 32), [(pcT4[:, h, :], ovl[:, :]) for h in range(4)], r=['pcT4', 'ovl'], w=[('ps', 7)])
            pe_multi(kb, [(lambda e, h=h: e.matmul(bank(6, 512)[:, h * 128:(h + 1) * 128], pcT4[:, h, :], vcmp[:, g, :],
                                                   start=True, stop=True)) for h in range(4)],
                     r=['pcT4', ('vcmp', g)], w=[('ps', 6)])
            gview = lambda br: gl[:, qt, g * 12:(g + 1) * 12].rearrange("p (h c) -> p h c", c=3)[:, :, br:br + 1]
            kb.op('dve', lambda e: e.tensor_tensor(out=oa[:], in0=bank(6, 512).rearrange("p (h d) -> p h d", d=128),
                                                   in1=gview(0).broadcast_to([128, 4, 128]), op=ALU.mult),
                  r=[('ps', 6), 'gl'], w=oakeys)
            kb.op('dve', lambda e: e.tensor_tensor(out=impm[:], in0=bank(7, 32), in1=fbv[:, qt, :], op=ALU.add),
                  r=[('ps', 7), 'fbv'], w=['impm'])
            kb.op('dve', lambda e: e.max(out=m8[:], in_=impm[:]), r=['impm'], w=['m8'])
            kb.op('dve', lambda e: e.match_replace(out=wk32[:], in_to_replace=m8[:], in_values=impm[:], imm_value=-3.0e38),
                  r=['impm', 'm8'], w=['wk32'])
            kb.op('dve', lambda e: e.max(out=m8b[:], in_=wk32[:]), r=['wk32'], w=['m8b'])
            kb.op('dve', lambda e: e.tensor_scalar(out=sel[:], in0=impm[:], scalar1=m8b[:, 7:8], scalar2=None,
                                                   op0=ALU.is_ge), r=['impm', 'm8b'], w=['sel'])
            chunks = []
            for h in range(4):
                hh = g * 4 + h
                for br in (1, 2):
                    kts = list(range(qt + 1)) if br == 1 else list(range(max(0, qt - 4), qt + 1))
                    parts = [kts[i:i + 4] for i in range(0, len(kts), 4)]
                    accb = 4 + cnt['j'] % 2
                    ji = cnt['j'] % 2
                    cnt['j'] += 1
                    npv = len(kts)
                    ipv = 0
                    for pi_, ch in enumerate(parts):
                        chunks.append(dict(h=h, hh=hh, br=br, ch=ch, accb=accb, ji=ji, ipv0=ipv, npv=npv,
                                           last=(pi_ == len(parts) - 1)))
                        ipv += len(ch)

            def stage1(c_):
                i = c_['idx']
                ch = c_['ch']
                n = 128 * len(ch)
                k0 = ch[0] * 128
                sb = i % 2
                kT, kkey = (ksr[:, g, :], ('ksr', g)) if c_['br'] == 1 else (kwr[:, g, :], ('kwr', g))
                kb.mm(bank(sb, n), [(qr[:, c_['hh'], qsl], kT[:, k0:k0 + n])],
                      r=[(('qr', c_['hh']), qt // 8), (kkey, 0), (kkey, 1)], w=[('ps', sb)])
                pb = pbuf[i % 2]
                pk = ('pbuf', i % 2)
                kb.op('act', lambda e: e.activation(out=pb[:, :n], in_=bank(sb, n), func=AF.Exp, scale=SCALE),
                      r=[('ps', sb)], w=[pk])
                if c_['br'] == 1:
                    nb = 2 * len(ch)
                    kb.op('dve', lambda e: e.tensor_tensor(
                        out=pb[:, :n].rearrange("p (b k) -> p b k", k=64), in0=pb[:, :n].rearrange("p (b k) -> p b k", k=64),
                        in1=sel[:, 2 * ch[0]:2 * ch[0] + nb].unsqueeze(2).broadcast_to([128, nb, 64]), op=ALU.mult),
                        r=[pk, 'sel'], w=[pk])
                if c_['br'] == 2 and qt >= 4 and ch[0] == qt - 4:
                    kb.op('dve', lambda e: e.tensor_tensor(out=pb[:, 0:128], in0=pb[:, 0:128], in1=tri2[:], op=ALU.mult),
                          r=[pk, 'tri2'], w=[pk])
                if ch[-1] == qt:
                    off = 128 * (len(ch) - 1)
                    kb.op('dve', lambda e: e.tensor_tensor(out=pb[:, off:off + 128], in0=pb[:, off:off + 128], in1=tri[:],
                                                           op=ALU.mult), r=[pk, 'tri'], w=[pk])

            def stage2(c_):
                i = c_['idx']
                ch = c_['ch']
                n = 128 * len(ch)
                pb = pbuf[i % 2]
                tb = 2 + i % 2
                tbk = bankbf(tb)
                pe_multi(kb, [(lambda e, j=j: e.transpose(out=tbk[:, j * 128:(j + 1) * 128], in_=pb[:, j * 128:(j + 1) * 128],
                                                          identity=idb[:])) for j in range(len(ch))],
                         r=[('pbuf', i % 2), 'idb'], w=[('ps', tb)])
                if i % 2 == 0:
                    kb.op('act', lambda e: e.activation(out=pT[0][:, :n], in_=tbk[:, :n], func=AF.Copy), r=[('ps', tb)],
                          w=[('pT', 0)])
                else:
                    kb.op('dve', lambda e: e.tensor_copy(out=pT[1][:, :n], in_=tbk[:, :n]), r=[('ps', tb)], w=[('pT', 1)])

            def stage3(c_):
                i = c_['idx']
                ch = c_['ch']
                accb = c_['accb']
                vaug, vkeys = (vs, [('vs', g), 'vs1']) if c_['br'] == 1 else (vw, [('vw', g), 'vw1'])
                fns = []
                for j, kt in enumerate(ch):
                    ip = c_['ipv0'] + j
                    fns.append(lambda e, j=j, kt=kt, first=(ip == 0), lastm=(ip == c_['npv'] - 1): e.matmul(
                        bank(accb, 129), pT[i % 2][:, j * 128:(j + 1) * 128], vaug[:, g, kt, 0:129], start=first, stop=lastm))
                pe_multi(kb, fns, r=[('pT', i % 2)] + vkeys, w=[('ps', accb)])
                if c_['last']:
                    h = c_['h']
                    gs_ = gsc[c_['ji']]
                    gk = ('gsc', c_['ji'])
                    col = h_col(c_['hh'], c_['br'])
                    kb.op('dve', lambda e: e.reciprocal(out=gs_[:], in_=bank(accb, 129)[:, 128:129]), r=[('ps', accb)], w=[gk])
                    kb.op('dve', lambda e: e.tensor_tensor(out=gs_[:], in0=gs_[:], in1=gl[:, qt, col:col + 1], op=ALU.mult),
                          r=[gk, 'gl'], w=[gk])
                    kb.op('dve', lambda e: e.scalar_tensor_tensor(out=oa[:, h, :], in0=bank(accb, 128), scalar=gs_[:, 0:1],
                                                                  in1=oa[:, h, :], op0=ALU.mult, op1=ALU.add),
                          r=[('ps', accb), gk, ('oacc', oi, h)], w=[('oacc', oi, h)])

            nchk = len(chunks)
            for i, c_ in enumerate(chunks):
                c_['idx'] = cnt['c'] + i
            for i in range(nchk + 2):
                if i < nchk:
                    stage1(chunks[i])
                if 0 <= i - 1 < nchk:
                    stage2(chunks[i - 1])
                if 0 <= i - 2 < nchk:
                    stage3(chunks[i - 2])
            cnt['c'] += nchk
            kb.dma('sp', o_o[qt * 128:(qt + 1) * 128, g * 512:(g + 1) * 512], oa[:].rearrange("p h d -> p (h d)"),
                   r=oakeys, w=[('o_o', g, qt)])
    if fused:
        kb.barrier()
        return nc
    kb.finish('sp')
    return nc


def h_col(hh, br):
    return hh * 3 + br


def l3_consts():
    i = np.arange(128)[:, None]
    j = np.arange(128)[None, :]
    tri = (j <= i).astype(np.float32)
    tri2 = (j > i).astype(np.float32)
    n = np.arange(128)
    cm = np.zeros((128, 16, 128), np.float32)
    fbv = np.zeros((128, 16, 32), np.float32)
    jb = np.arange(32)
    for qt in range(16):
        t = qt * 128 + np.arange(128)
        cm[:, qt, :] = ((16 * n[None, :] + 31 <= t[:, None]) & (n[None, :] < 127)).astype(np.float32)
        cur = (t // 64)[:, None]
        forced = (jb[None] == 0) | (jb[None] == cur) | (jb[None] == cur - 1)
        valid = jb[None] * 64 <= t[:, None]
        fbv[:, qt, :] = np.where(valid, np.where(forced, 1000.0, 0.0), -1e30)
    ovl = np.zeros((128, 32), np.float32)
    for nn in range(127):
        for jj in range(32):
            if 16 * nn < 64 * jj + 64 and 16 * nn + 31 >= 64 * jj:
                ovl[nn, jj] = 1.0
    d = np.arange(128)
    inv = np.where(d < 32, 500000.0 ** (-(2.0 * (d % 16)) / 32.0), 0.0).astype(np.float32)[:, None]
    sgn = np.where(d < 16, -1.0, np.where(d < 32, 1.0, 0.0)).astype(np.float32)[:, None]
    return dict(tri=tri, tri2=tri2, cm=cm, fbv=fbv, ovl=ovl, inv=inv, sgn=sgn, ident=np.eye(128, dtype=np.float32))


def swap_rot(xT):
    y = xT.copy()
    y[..., 0:16, :] = xT[..., 16:32, :]
    y[..., 16:32, :] = xT[..., 0:16, :]
    return y


def prep_L3(zT_b, pos_b, half, W, consts):
    c = np.ascontiguousarray
    gs = [2 * half, 2 * half + 1]
    qT = c(zT_b[half * 1024:(half + 1) * 1024].reshape(8, 128, S))

    def grp(base):
        return c(np.stack([zT_b[base + g * 128:base + (g + 1) * 128] for g in gs]))
    kc, vc, ks, vs_, kw, vw_ = [grp(2048 + i * 512) for i in range(6)]
    gl = c(zT_b[5120 + half * 24:5120 + (half + 1) * 24].T)
    m = dict(qT=qT, qPT=swap_rot(qT), ksT=ks, ksPT=swap_rot(ks), kwT=kw, kwPT=swap_rot(kw),
             kcT=c(kc.transpose(1, 0, 2)), vcT=c(vc.transpose(1, 0, 2)),
             vs=c(vs_.transpose(0, 2, 1)), vw=c(vw_.transpose(0, 2, 1)), gl=gl,
             pos=c(pos_b.reshape(1, S).astype(np.int32)))
    m.update(W)
    m.update(consts)
    return m


def prep_L3_weights(pe_k, w1_k, w2_k, pe_v, w1_v, w2_v):
    c = np.ascontiguousarray
    return dict(pekT=c(pe_k.T), w1k=c(w1_k), w2k=c(w2_k), pevT=c(pe_v.T), w1v=c(w1_v), w2v=c(w2_v))


_PROGS = {}


def _prog(name, fn):
    if name not in _PROGS:
        _PROGS[name] = fn()
    return _PROGS[name]


def _lay(g, n=16):
    return np.ascontiguousarray(np.asarray(g, np.float32).reshape(n, 128).T)


def kernel_unfused(**inp):
    c = np.ascontiguousarray
    f32 = lambda a: np.asarray(a, dtype=np.float32)
    x = f32(inp['x'])
    pos = np.asarray(inp['positions'])
    T = 1024
    cores = list(range(NCORES))
    tok = lambda ci: (ci // 2, slice((ci % 2) * T, (ci % 2 + 1) * T))
    tri_st = np.triu(np.ones((128, 128), np.float32))
    common = dict(g1=_lay(inp['l0_ffn1_norm']), g2=_lay(inp['l0_mix_norm']), wgd=f32(inp['l0_ffn1_w_gate']),
                  wud=f32(inp['l0_ffn1_w_up']), wdd=f32(inp['l0_ffn1_w_down']), w_in=f32(inp['l0_w_in']),
                  lng=c(f32(inp['l0_gmlp_ln_g']).reshape(1, 1024)), lnb=c(f32(inp['l0_gmlp_ln_b']).reshape(1, 1024)),
                  wsT=c(f32(inp['l0_gmlp_ws']).transpose(2, 0, 1)), tri=tri_st,
                  bs=c(f32(inp['l0_gmlp_bs']).reshape(1, 1024)))
    maps = []
    for ci in cores:
        b, sl = tok(ci)
        m = dict(common)
        m['xTd'] = c(x[b, sl].T)
        maps.append(m)
    r1 = run_bass_kernel_spmd(_prog('L1', build_L1), maps, core_ids=cores).results
    common = dict(conv_w=c(f32(inp['l0_conv_w'])[:, 0, :].reshape(31, 8, 128).transpose(2, 1, 0)),
                  conv_b=_lay(inp['l0_conv_b'], 8), cln_g=_lay(inp['l0_conv_ln_g'], 8), cln_b=_lay(inp['l0_conv_ln_b'], 8),
                  w_out=f32(inp['l0_w_out']),
                  gA=_lay(inp['l0_ffn2_norm']), wgA=f32(inp['l0_ffn2_w_gate']), wuA=f32(inp['l0_ffn2_w_up']),
                  wdA=f32(inp['l0_ffn2_w_down']),
                  gB=_lay(inp['l1_ffn1_norm']), wgB=f32(inp['l1_ffn1_w_gate']), wuB=f32(inp['l1_ffn1_w_up']),
                  wdB=f32(inp['l1_ffn1_w_down']),
                  gM=_lay(inp['l1_mix_norm']), w_in1=f32(inp['l1_w_in']))
    maps = []
    for ci in cores:
        m = dict(common)
        aT = r1[ci]['aT']
        halo = np.zeros((1024, 32), np.float32)
        if ci % 2 == 1:
            halo = r1[ci - 1]['aT'][:, T - 32:]
        m['aTh'] = c(np.concatenate([halo, aT], axis=1))
        m['boTd'] = r1[ci]['boT']
        m['x1Td'] = r1[ci]['x1T']
        maps.append(m)
    r2 = run_bass_kernel_spmd(_prog('L2', build_L2), maps, core_ids=cores).results
    W = prep_L3_weights(*[f32(inp[k]) for k in ('l1_cmp_pe_k', 'l1_cmp_w1_k', 'l1_cmp_w2_k',
                                                'l1_cmp_pe_v', 'l1_cmp_w1_v', 'l1_cmp_w2_v')])
    consts = l3_consts()
    maps = []
    for ci in cores:
        b, half = ci // 2, ci % 2
        zT_b = np.concatenate([r2[2 * b]['zT'], r2[2 * b + 1]['zT']], axis=1)
        maps.append(prep_L3(zT_b, pos[b], half, W, consts))
    r3 = run_bass_kernel_spmd(_prog('L3', build_L3), maps, core_ids=cores).results
    common = dict(w_out1=f32(inp['l1_w_out']), gA=_lay(inp['l1_ffn2_norm']), wgA=f32(inp['l1_ffn2_w_gate']),
                  wuA=f32(inp['l1_ffn2_w_up']), wdA=f32(inp['l1_ffn2_w_down']), gF=_lay(inp['final_norm']))
    maps = []
    for ci in cores:
        b, sl = tok(ci)
        o_b = np.concatenate([r3[2 * b]['o'], r3[2 * b + 1]['o']], axis=1)
        m = dict(common)
        m['oTd'] = c(o_b[sl].T)
        m['x4Td'] = r2[ci]['x4T']
        maps.append(m)
    r4 = run_bass_kernel_spmd(_prog('L4', build_L4), maps, core_ids=cores).results
    out = np.zeros((4, 2048, 2048), np.float32)
    for ci in cores:
        b, sl = tok(ci)
        out[b, sl] = r4[ci]['yT'].T
    return out


from contextlib import ExitStack

W_NAMES = [('l0_ffn1', 'f1'), ('l0_ffn2', 'f2'), ('l1_ffn1', 'f3'), ('l1_ffn2', 'f4')]


def build_fused(dff=DFF, nz=5168):
    nc = bass.Bass("TRN2", target_bir_lowering=False)
    T = 1024
    ext = lambda name, shape, dtype=F32: nc.dram_tensor(name, shape, dtype, kind="ExternalInput").ap()
    scr = lambda name, shape: nc.dram_tensor(name, shape, F32, kind="Internal").ap()
    I = {}
    I['xT'] = ext('xT', [2, D, T])
    for _, s in W_NAMES:
        I[s + '_g'] = ext(s + '_g', [128, 16])
        I[s + '_wg'] = ext(s + '_wg', [D, dff])
        I[s + '_wu'] = ext(s + '_wu', [D, dff])
        I[s + '_wd'] = ext(s + '_wd', [dff, D])
    for name, shape in [('g_mix0', [128, 16]), ('w_in0', [D, 4096]), ('lng', [1, 1024]), ('lnb', [1, 1024]),
                        ('wsT', [128, 8, 128]), ('tri_st', [128, 128]), ('bs', [1, 1024]),
                        ('conv_w', [128, 8, 31]), ('conv_b', [128, 8]), ('cln_g', [128, 8]), ('cln_b', [128, 8]),
                        ('w_out0', [D, D]), ('g_mix1', [128, 16]), ('w_in1', [D, nz]),
                        ('inv', [128, 1]), ('sgn', [128, 1]), ('pekT', [128, 32]), ('w1k', [4096, 256]),
                        ('w2k', [256, 128]), ('pevT', [128, 32]), ('w1v', [4096, 256]), ('w2v', [256, 128]),
                        ('ovl', [128, 32]), ('cm', [128, 16, 128]), ('fbv', [128, 16, 32]), ('tri', [128, 128]),
                        ('tri2', [128, 128]), ('ident', [128, 128]), ('w_out1', [D, D]), ('g_fin', [128, 16])]:
        I[name] = ext(name, shape)
    I['pos'] = ext('pos', [1, S], I32)
    yT = nc.dram_tensor('yT', [D, T], F32, kind="ExternalOutput").ap()
    I['flag'] = ext('flag', [128, 2])
    x1T = scr('x1T_s', [2, D, T]); aT = scr('aT_s', [2, 1024, T]); boT = scr('boT_s', [2, 1024, T])
    x4T = scr('x4T_s', [2, D, T]); zT = scr('zT_s', [nz, 2 * T]); o_s = scr('o_s', [2 * T, 2048])
    kb = KB(nc)
    ps = nc.alloc_psum_tensor('ps', [128, 8 * 512], F32)
    with ExitStack() as es_core:
        _ES[0] = es_core
        core = Core(nc, T, kb, ps)
        for h in range(2):
            with ExitStack() as es:
                _ES[0] = es
                build_L1(T, dff, nc, dict(xTd=I['xT'][h], g1=I['f1_g'], g2=I['g_mix0'], wgd=I['f1_wg'], wud=I['f1_wu'],
                                          wdd=I['f1_wd'], w_in=I['w_in0'], lng=I['lng'], lnb=I['lnb'], wsT=I['wsT'],
                                          tri=I['tri_st'], bs=I['bs'], x1T=x1T[h], aT=aT[h], boT=boT[h]), core)
            with ExitStack() as es:
                _ES[0] = es
                build_L2(T, dff, nz, nc, dict(x1Td=x1T[h], aT_cur=aT[h], aT_prev=(aT[0] if h == 1 else None),
                                              boTd=boT[h], conv_w=I['conv_w'], conv_b=I['conv_b'], cln_g=I['cln_g'],
                                              cln_b=I['cln_b'], w_out=I['w_out0'],
                                              gA=I['f2_g'], wgA=I['f2_wg'], wuA=I['f2_wu'], wdA=I['f2_wd'],
                                              gB=I['f3_g'], wgB=I['f3_wg'], wuB=I['f3_wu'], wdB=I['f3_wd'],
                                              gM=I['g_mix1'], w_in1=I['w_in1'], x4T=x4T[h],
                                              zT=zT[:, h * T:(h + 1) * T]), core)
            _ES[0] = es_core
    for gp in range(2):
        with ExitStack() as es:
            _ES[0] = es
            ov = {k: I[k] for k in ('inv', 'sgn', 'pekT', 'w1k', 'w2k', 'pevT', 'w1v', 'w2v', 'ovl', 'cm', 'fbv',
                                    'tri', 'tri2', 'ident', 'pos')}
            ov['zT'] = zT
            ov['o'] = o_s[:, gp * 1024:(gp + 1) * 1024]
            build_L3(nc, ov, kb, ps, gp)
    with ExitStack() as es_core:
        _ES[0] = es_core
        core = Core(nc, T, kb, ps)
        with ExitStack() as es:
            _ES[0] = es
            build_L4(T, dff, nc, dict(x4Td=x4T[0], x4T_1=x4T[1], o_tok=o_s[0:T, :], o_tok1=o_s[T:2 * T, :],
                                      flag=I['flag'], ident=I['ident'], w_out1=I['w_out1'], gA=I['f4_g'],
                                      wgA=I['f4_wg'], wuA=I['f4_wu'], wdA=I['f4_wd'], gF=I['g_fin'], yT=yT), core)
        _ES[0] = es_core
    _ES[0] = None
    kb.finish('sp')
    return nc


def fused_inputs(inp, b, r=0):
    c = np.ascontiguousarray
    f32 = lambda a: np.asarray(a, dtype=np.float32)
    x = f32(inp['x'])
    m = dict(xT=c(np.stack([x[b, 0:1024].T, x[b, 1024:2048].T])))
    for pre, s in W_NAMES:
        m[s + '_g'] = _lay(inp[pre + '_norm'])
        m[s + '_wg'] = f32(inp[pre + '_w_gate'])
        m[s + '_wu'] = f32(inp[pre + '_w_up'])
        m[s + '_wd'] = f32(inp[pre + '_w_down'])
    m.update(g_mix0=_lay(inp['l0_mix_norm']), w_in0=f32(inp['l0_w_in']),
             lng=c(f32(inp['l0_gmlp_ln_g']).reshape(1, 1024)), lnb=c(f32(inp['l0_gmlp_ln_b']).reshape(1, 1024)),
             wsT=c(f32(inp['l0_gmlp_ws']).transpose(2, 0, 1)), tri_st=np.triu(np.ones((128, 128), np.float32)),
             bs=c(f32(inp['l0_gmlp_bs']).reshape(1, 1024)),
             conv_w=c(f32(inp['l0_conv_w'])[:, 0, :].reshape(31, 8, 128).transpose(2, 1, 0)),
             conv_b=_lay(inp['l0_conv_b'], 8), cln_g=_lay(inp['l0_conv_ln_g'], 8), cln_b=_lay(inp['l0_conv_ln_b'], 8),
             w_out0=f32(inp['l0_w_out']), g_mix1=_lay(inp['l1_mix_norm']), w_in1=f32(inp['l1_w_in']),
             w_out1=f32(inp['l1_w_out']), g_fin=_lay(inp['final_norm']),
             pos=c(np.asarray(inp['positions'])[b].reshape(1, S).astype(np.int32)))
    m.update(prep_L3_weights(*[f32(inp[k]) for k in ('l1_cmp_pe_k', 'l1_cmp_w1_k', 'l1_cmp_w2_k',
                                                     'l1_cmp_pe_v', 'l1_cmp_w1_v', 'l1_cmp_w2_v')]))
    m.update(l3_consts())
    fl = np.zeros((128, 2), np.float32)
    fl[:, r] = 1.0
    m['flag'] = fl
    return m


def kernel(**inp):
    nc = _prog('fused', build_fused)
    maps = [fused_inputs(inp, ci // 2, ci % 2) for ci in range(NCORES)]
    res = run_bass_kernel_spmd(nc, maps, core_ids=list(range(NCORES))).results
    out = np.zeros((4, 2048, 2048), np.float32)
    for ci in range(NCORES):
        b, h = ci // 2, ci % 2
        out[b, h * 1024:(h + 1) * 1024] = res[ci]['yT'].T
    return out
```

```python
import os
import numpy as np
import concourse.bass as bass
import concourse.mybir as mybir
from concourse.bass_utils import run_bass_kernel_spmd

F32 = mybir.dt.float32
BF16 = mybir.dt.bfloat16
I32 = mybir.dt.int32
AF = mybir.ActivationFunctionType
ALU = mybir.AluOpType

_ES = [None]
_UID = [0]


def SB(nc, name, shape, dtype=None):
    dtype = F32 if dtype is None else dtype
    _UID[0] += 1
    nm = '%s_%d' % (name, _UID[0])
    if _ES[0] is None:
        return nc.alloc_sbuf_tensor(nm, list(shape), dtype)
    return _ES[0].enter_context(nc.sbuf_tensor(nm, list(shape), dtype))


def mk_dt(nc, over, pre=''):
    def dt(name, shape, kind="ExternalInput", dtype=F32):
        if over is not None and name in over:
            return over[name]
        return nc.dram_tensor(pre + name, shape, dtype, kind=kind).ap()
    return dt


D = 2048
DFF = 5632
NCORES = 8
EPS = 1e-6


class KB:
    NS = 6

    def __init__(self, nc):
        self.nc = nc
        self.eng = dict(pe=nc.tensor, dve=nc.vector, act=nc.scalar, pool=nc.gpsimd, sp=nc.sync)
        self.sem = {}
        self.cnt = {}
        for e in ('pe', 'dve', 'act', 'pool'):
            self.sem[e] = nc.alloc_semaphore('c_' + e)
            self.cnt[e] = 0
        self.nsq = {'sp': 6, 'pool': 6, 'act': 2}
        self.dsem = {q: [nc.alloc_semaphore('d_%s%d' % (q, i)) for i in range(self.nsq[q])]
                     for q in ('sp', 'pool', 'act')}
        self.dcnt = {q: 0 for q in self.dsem}
        self.seen = {e: {} for e in self.eng}
        self.st = {}
        self.semobj = {}
        for s in list(self.sem.values()) + [x for v in self.dsem.values() for x in v]:
            self.semobj[s.num] = s
        self.nwait = 0

    def _deps(self, r, w):
        deps = {}

        def add(tok):
            if tok is None:
                return
            s, v = tok
            if deps.get(s, 0) < v:
                deps[s] = v
        for k in r:
            st = self.st.get(k)
            if st:
                add(st[0])
        for k in w:
            st = self.st.get(k)
            if st:
                add(st[0])
                for s, v in st[1].items():
                    add((s, v))
        return deps

    def _emit_waits(self, e, deps, skip_sem=None):
        eng = self.eng[e]
        seen = self.seen[e]
        for s, v in deps.items():
            if skip_sem is not None and s == skip_sem:
                continue
            if seen.get(s, 0) >= v:
                continue
            eng.wait_ge(self.semobj[s], v)
            seen[s] = v
            self.nwait += 1

    def _commit(self, tok, r, w):
        for k in r:
            st = self.st.setdefault(k, [None, {}])
            if st[1].get(tok[0], 0) < tok[1]:
                st[1][tok[0]] = tok[1]
        for k in w:
            self.st[k] = [tok, {}]

    def op(self, e, fn, r=(), w=()):
        deps = self._deps(r, w)
        self._emit_waits(e, deps, skip_sem=(self.sem['pe'].num if e == 'pe' else None))
        inst = fn(self.eng[e])
        self.cnt[e] += 1
        inst.then_inc(self.sem[e], 1)
        tok = (self.sem[e].num, self.cnt[e])
        self._commit(tok, r, w)
        return tok

    def mm(self, out, pairs, r=(), w=()):
        deps = self._deps(r, w)
        self._emit_waits('pe', deps, skip_sem=self.sem['pe'].num)
        n = len(pairs)
        inst = None
        for i, (lhsT, rhs) in enumerate(pairs):
            inst = self.nc.tensor.matmul(out, lhsT, rhs, start=(i == 0), stop=(i == n - 1))
        self.cnt['pe'] += 1
        inst.then_inc(self.sem['pe'], 1)
        tok = (self.sem['pe'].num, self.cnt['pe'])
        self._commit(tok, r, w)
        return tok

    def mm1(self, out, lhsT, rhs, start, stop, r=(), w=()):
        return self.op('pe', lambda e: e.matmul(out, lhsT, rhs, start=start, stop=stop), r=r, w=w)

    def dma(self, q, out, in_, r=(), w=(), **kw):
        deps = self._deps(r, w)
        i = self.dcnt[q]
        self.dcnt[q] += 1
        s = self.dsem[q][i % self.nsq[q]]
        rnd = i // self.nsq[q]
        if rnd > 0:
            deps[s.num] = max(deps.get(s.num, 0), 16 * rnd)
        self._emit_waits(q, deps)
        inst = self.eng[q].dma_start(out=out, in_=in_, **kw)
        inst.then_inc(s, 16)
        tok = (s.num, 16 * (rnd + 1))
        self._commit(tok, r, w)
        return tok

    def barrier(self):
        deps = {}
        for e, sm in self.sem.items():
            if self.cnt[e] > 0:
                deps[sm.num] = self.cnt[e]
        for q, sl in self.dsem.items():
            n = self.dcnt[q]
            for i, sm in enumerate(sl):
                k = (n - 1 - i) // self.nsq[q] + 1 if n > i else 0
                if k > 0:
                    deps[sm.num] = 16 * k
        for e in self.eng:
            self._emit_waits(e, dict(deps))

    def finish(self, e='sp'):
        deps = {}
        for k, st in self.st.items():
            if st[0] is not None:
                s, v = st[0]
                if deps.get(s, 0) < v:
                    deps[s] = v
        self._emit_waits(e, deps)


class Core:
    def __init__(self, nc, T, kb=None, ps=None):
        self.nc = nc
        self.kb = KB(nc) if kb is None else kb
        self.T = T
        self.TB = [(i, min(512, T - i)) for i in range(0, T, 512)]
        self.xT = SB(nc, 'xT', [128, 16, T], F32)
        self.hT = SB(nc, 'hT', [128, 16, T], BF16)
        self.wg = [SB(nc, 'wg%d' % i, [128, 16, 256], BF16) for i in range(2)]
        self.wu = [SB(nc, 'wu%d' % i, [128, 16, 256], BF16) for i in range(2)]
        self.wd = [SB(nc, 'wd%d' % i, [128, 2, 2048], BF16) for i in range(4)]
        self.actT = [SB(nc, 'actT%d' % i, [128, 4, T], BF16) for i in range(2)]
        self.tmp = [SB(nc, 'tmp%d' % i, [128, 512], F32) for i in range(2)]
        self.sq = [SB(nc, 'sq%d' % i, [128, 512], BF16) for i in range(2)]
        self.rstd = SB(nc, 'rstd', [128, 512], F32)
        self.ones = SB(nc, 'ones', [128, 128], BF16)
        self.ps = nc.alloc_psum_tensor('ps', [128, 8 * 512], F32) if ps is None else ps
        self.ntmp = 0
        self.nsq = 0
        self.nwt = 0
        self.nwd = 0
        self.nact = 0
        self.kb.op('dve', lambda e: e.memset(self.ones[:], 1.0), w=['ones'])
        self.half_sb = SB(nc, 'half', [128, 1], F32)
        self.kb.op('dve', lambda e: e.memset(self.half_sb[:], 0.5), w=['half'])
        self.eps_sb = SB(nc, 'eps', [128, 1], F32)
        self.kb.op('dve', lambda e: e.memset(self.eps_sb[:], EPS), w=['eps'])

    def bank(self, b, n=512):
        return self.ps[:, b * 512:b * 512 + n]


def rmsnorm_T(c, g_sb, gkey, out=None, okey='hT', src=None, skey='xT', bank=6):
    kb = c.kb
    out = c.hT if out is None else out
    src = c.xT if src is None else src
    for (t0, tn) in c.TB:
        for kc in range(16):
            i = c.nsq % 2
            c.nsq += 1
            sq = c.sq[i]
            kb.op('act', lambda e, kc=kc, sq=sq: e.activation(out=sq[:, :tn], in_=src[:, kc, t0:t0 + tn],
                                                             func=AF.Square),
                  r=[(skey, kc, t0)], w=[('sq', i)])
            kb.mm1(c.bank(bank, tn), c.ones[:], sq[:, :tn], kc == 0, kc == 15,
                   r=['ones', ('sq', i)], w=([('ps', bank)] if kc in (0, 15) else []))
        kb.op('act', lambda e: e.activation(out=c.rstd[:, :tn], in_=c.bank(bank, tn), func=AF.Sqrt,
                                            scale=1.0 / D, bias=c.eps_sb[:, 0:1]),
              r=[('ps', bank), 'eps'], w=['rstd'])
        kb.op('dve', lambda e: e.reciprocal(out=c.rstd[:, :tn], in_=c.rstd[:, :tn]), r=['rstd'], w=['rstd'])
        for kc in range(16):
            kb.op('dve', lambda e, kc=kc: e.scalar_tensor_tensor(
                out=out[:, kc, t0:t0 + tn], in0=src[:, kc, t0:t0 + tn], scalar=g_sb[:, kc:kc + 1],
                in1=c.rstd[:, :tn], op0=ALU.mult, op1=ALU.mult),
                r=[(skey, kc, t0), 'rstd', gkey], w=[(okey, kc, t0)])


def ffn_T(c, wg_d, wu_d, wd_d, dff=DFF):
    kb = c.kb
    nc = c.nc
    wg_v = wg_d.rearrange("(kc p) f -> p kc f", p=128)
    wu_v = wu_d.rearrange("(kc p) f -> p kc f", p=128)
    NFB = dff // 512
    gbank = 0
    dbank = 0
    for fb in range(NFB):
        ab = c.nact % 2
        c.nact += 1
        actT = c.actT[ab]
        wd_tiles = []
        for half in range(2):
            wt = fb * 2 + half
            wb = c.nwt % 2
            c.nwt += 1
            kb.dma('pool', c.wg[wb][:], wg_v[:, :, wt * 256:(wt + 1) * 256], w=[('wg', wb)])
            kb.dma('pool', c.wu[wb][:], wu_v[:, :, wt * 256:(wt + 1) * 256], w=[('wu', wb)])
            db = c.nwd % 4
            c.nwd += 1
            kb.dma('pool', c.wd[db][:],
                   wd_d[wt * 256:(wt + 1) * 256, :].rearrange("(fc p) d -> p fc d", p=128), w=[('wd', db)])
            wd_tiles.append(db)
            for j in range(2):
                fcl = half * 2 + j
                for (t0, tn) in c.TB:
                    bg = gbank % 4
                    bu = (gbank + 1) % 4
                    gbank += 2
                    kb.mm(c.bank(bg, tn), [(c.wg[wb][:, kc, j * 128:(j + 1) * 128], c.hT[:, kc, t0:t0 + tn])
                                           for kc in range(16)],
                          r=[('wg', wb)] + [('hT', kc, t0) for kc in range(16)], w=[('ps', bg)])
                    kb.mm(c.bank(bu, tn), [(c.wu[wb][:, kc, j * 128:(j + 1) * 128], c.hT[:, kc, t0:t0 + tn])
                                           for kc in range(16)],
                          r=[('wu', wb)] + [('hT', kc, t0) for kc in range(16)], w=[('ps', bu)])
                    ti = c.ntmp % 2
                    c.ntmp += 1
                    tmp = c.tmp[ti]
                    kb.op('act', lambda e: e.activation(out=tmp[:, :tn], in_=c.bank(bg, tn), func=AF.Silu),
                          r=[('ps', bg)], w=[('tmp', ti)])
                    kb.op('dve', lambda e: e.tensor_tensor(out=actT[:, fcl, t0:t0 + tn], in0=c.bank(bu, tn),
                                                           in1=tmp[:, :tn], op=ALU.mult),
                          r=[('ps', bu), ('tmp', ti)], w=[('actT', ab, fcl, t0)])
        for dc in range(16):
            for (t0, tn) in c.TB:
                bd = 4 + dbank % 4
                dbank += 1
                kb.mm(c.bank(bd, tn),
                      [(c.wd[wd_tiles[fcl // 2]][:, fcl % 2, dc * 128:(dc + 1) * 128], actT[:, fcl, t0:t0 + tn])
                       for fcl in range(4)],
                      r=[('wd', wd_tiles[0]), ('wd', wd_tiles[1])] + [('actT', ab, fcl, t0) for fcl in range(4)],
                      w=[('ps', bd)])
                if True:
                    kb.op('dve', lambda e: e.scalar_tensor_tensor(
                        out=c.xT[:, dc, t0:t0 + tn], in0=c.bank(bd, tn), scalar=c.half_sb[:, 0:1], in1=c.xT[:, dc, t0:t0 + tn],
                        op0=ALU.mult, op1=ALU.add),
                        r=[('ps', bd), ('xT', dc, t0), 'half'], w=[('xT', dc, t0)])


def load_small(c, name, d_ap, shape, dtype=F32, q='sp'):
    t = SB(c.nc, name, list(shape), dtype)
    c.kb.dma(q, t[:], d_ap, w=[name])
    return t


def load_xT(c, x_d):
    v = x_d.rearrange("(kc p) t -> p kc t", p=128)
    for kc in range(0, 16, 4):
        c.kb.dma('sp', c.xT[:, kc:kc + 4, :], v[:, kc:kc + 4, :],
                 w=[('xT', k, t0) for k in range(kc, kc + 4) for (t0, _) in c.TB])


def store_T(c, out_d, src, skey):
    v = out_d.rearrange("(kc p) t -> p kc t", p=128)
    for kc in range(0, 16, 4):
        c.kb.dma('sp', v[:, kc:kc + 4, :], src[:, kc:kc + 4, :],
                 r=[(skey, k, t0) for k in range(kc, kc + 4) for (t0, _) in c.TB], w=[('out', kc)])


def build_ffn_test(T, dff=DFF, stage=2):
    nc = bass.Bass("TRN2", target_bir_lowering=False)
    x_d = nc.dram_tensor("xTd", [D, T], F32, kind="ExternalInput").ap()
    g1 = nc.dram_tensor("g1", [128, 16], F32, kind="ExternalInput").ap()
    g2 = nc.dram_tensor("g2", [128, 16], F32, kind="ExternalInput").ap()
    wg = nc.dram_tensor("wgd", [D, dff], F32, kind="ExternalInput").ap()
    wu = nc.dram_tensor("wud", [D, dff], F32, kind="ExternalInput").ap()
    wd = nc.dram_tensor("wdd", [dff, D], F32, kind="ExternalInput").ap()
    y_d = nc.dram_tensor("yTd", [D, T], BF16, kind="ExternalOutput").ap()
    c = Core(nc, T)
    g1s = load_small(c, 'g1s', g1, [128, 16])
    g2s = load_small(c, 'g2s', g2, [128, 16])
    load_xT(c, x_d)
    rmsnorm_T(c, g1s, 'g1s')
    if stage >= 1:
        ffn_T(c, wg, wu, wd, dff)
    if stage >= 2:
        rmsnorm_T(c, g2s, 'g2s')
    store_T(c, y_d, c.hT, 'hT')
    c.kb.finish('sp')
    return nc


GELU_C = 0.7978845608028654
RING = [('wg', 0), ('wu', 0), ('wg', 1), ('wu', 1)]


def load_wt(c, w_v, col0, ncols, nkc=16):
    i = getattr(c, 'nring', 0)
    c.nring = i + 1
    name, b = RING[i % 4]
    buf = c.wg[b] if name == 'wg' else c.wu[b]
    c.kb.dma('pool', buf[:, :nkc, :ncols], w_v[:, :, col0:col0 + ncols], w=[(name, b)])
    return buf, (name, b)


def extra_tiles(c):
    nc = c.nc
    c.stg = [SB(nc, 'stg%d' % i, [128, 512], F32) for i in range(2)]
    c.ga = c.tmp[0]
    c.gb = c.tmp[1]
    c.nstg = 0
    c.c1 = SB(nc, 'c1', [128, 1], F32)
    c.kb.op('dve', lambda e: e.memset(c.c1[:], 1.0), w=['c1'])
    c.cg = SB(nc, 'cg', [128, 1], F32)
    c.kb.op('dve', lambda e: e.memset(c.cg[:], 0.044715), w=['cg'])


def gelu_from(c, src, skey, out, okey, P, n):
    kb = c.kb
    kb.op('act', lambda e: e.activation(out=c.ga[:P, :n], in_=src, func=AF.Square), r=[skey], w=[('tmp', 0)])
    kb.op('dve', lambda e: e.tensor_scalar(out=c.ga[:P, :n], in0=c.ga[:P, :n], scalar1=c.cg[:P, 0:1],
                                           scalar2=c.c1[:P, 0:1], op0=ALU.mult, op1=ALU.add),
          r=[('tmp', 0), 'cg', 'c1'], w=[('tmp', 0)])
    kb.op('dve', lambda e: e.tensor_tensor(out=c.ga[:P, :n], in0=src, in1=c.ga[:P, :n], op=ALU.mult),
          r=[skey, ('tmp', 0)], w=[('tmp', 0)])
    kb.op('act', lambda e: e.activation(out=c.gb[:P, :n], in_=c.ga[:P, :n], func=AF.Sigmoid, scale=2.0 * GELU_C),
          r=[('tmp', 0)], w=[('tmp', 1)])
    kb.op('dve', lambda e: e.tensor_tensor(out=out, in0=src, in1=c.gb[:P, :n], op=ALU.mult),
          r=[skey, ('tmp', 1)], w=[okey])


def build_L1(T=1024, dff=DFF, nc=None, over=None, core=None):
    fused = nc is not None
    nc = bass.Bass("TRN2", target_bir_lowering=False) if nc is None else nc
    dt = mk_dt(nc, over)
    x_d = dt("xTd", [D, T])
    g1 = dt("g1", [128, 16])
    g2 = dt("g2", [128, 16])
    wg = dt("wgd", [D, dff])
    wu = dt("wud", [D, dff])
    wd = dt("wdd", [dff, D])
    w_in = dt("w_in", [D, 4096])
    lng = dt("lng", [1, 1024])
    lnb = dt("lnb", [1, 1024])
    wsT_d = dt("wsT", [128, 8, 128])
    tri_d = dt("tri", [128, 128])
    bs_d = dt("bs", [1, 1024])
    x1_o = dt("x1T", [D, T], "ExternalOutput")
    a_o = dt("aT", [1024, T], "ExternalOutput")
    bo_o = dt("boT", [1024, T], "ExternalOutput")
    c = Core(nc, T) if core is None else core
    kb = c.kb
    extra_tiles(c)
    g1s = load_small(c, 'g1s', g1, [128, 16])
    g2s = load_small(c, 'g2s', g2, [128, 16])
    lng_s = load_small(c, 'lng_s', lng.partition_broadcast(128), [128, 1024])
    lnb_s = load_small(c, 'lnb_s', lnb.partition_broadcast(128), [128, 1024])
    bs_s = load_small(c, 'bs_s', bs_d.partition_broadcast(128), [128, 1024])
    wsT_f = load_small(c, 'wsT_f', wsT_d, [128, 8, 128], BF16, q='pool')
    tri_s = load_small(c, 'tri_s', tri_d, [128, 128], BF16, q='pool')
    wsT_m = SB(nc, 'wsT_m', [128, 8, 128], BF16)
    for g in range(8):
        kb.op('dve', lambda e, g=g: e.tensor_tensor(out=wsT_m[:, g, :], in0=wsT_f[:, g, :], in1=tri_s[:],
                                                    op=ALU.mult), r=['wsT_f', 'tri_s'], w=[('wsT_m', g)])
    load_xT(c, x_d)
    rmsnorm_T(c, g1s, 'g1s')
    ffn_T(c, wg, wu, wd, dff)
    store_T(c, x1_o, c.xT, 'xT')
    rmsnorm_T(c, g2s, 'g2s')
    w_v = w_in.rearrange("(kc p) f -> p kc f", p=128)
    hkeys = lambda t0: [('hT', kc, t0) for kc in range(16)]
    gbank = 0
    for ip in range(4):
        bv, kv = load_wt(c, w_v, ip * 256, 256)
        bg_, kg = load_wt(c, w_v, 1024 + ip * 256, 256)
        for j in range(2):
            ch = ip * 2 + j
            for (t0, tn) in c.TB:
                b0 = gbank % 4
                b1 = (gbank + 1) % 4
                gbank += 2
                kb.mm(c.bank(b0, tn), [(bv[:, kc, j * 128:(j + 1) * 128], c.hT[:, kc, t0:t0 + tn]) for kc in range(16)],
                      r=[kv] + hkeys(t0), w=[('ps', b0)])
                kb.mm(c.bank(b1, tn), [(bg_[:, kc, j * 128:(j + 1) * 128], c.hT[:, kc, t0:t0 + tn]) for kc in range(16)],
                      r=[kg] + hkeys(t0), w=[('ps', b1)])
                ti = c.ntmp % 2
                c.ntmp += 1
                si = c.nstg % 2
                c.nstg += 1
                kb.op('act', lambda e: e.activation(out=c.tmp[ti][:, :tn], in_=c.bank(b1, tn), func=AF.Sigmoid),
                      r=[('ps', b1)], w=[('tmp', ti)])
                kb.op('dve', lambda e: e.tensor_tensor(out=c.stg[si][:, :tn], in0=c.bank(b0, tn), in1=c.tmp[ti][:, :tn],
                                                       op=ALU.mult), r=[('ps', b0), ('tmp', ti)], w=[('stg', si)])
                kb.dma('sp', a_o[ch * 128:(ch + 1) * 128, t0:t0 + tn], c.stg[si][:, :tn], r=[('stg', si)],
                       w=[('a_o', ch, t0)])
    for ip in range(4):
        bu_, ku = load_wt(c, w_v, 2048 + ip * 256, 256)
        for j in range(2):
            g = ip * 2 + j
            for (t0, tn) in c.TB:
                b0 = gbank % 4
                gbank += 1
                kb.mm(c.bank(b0, tn), [(bu_[:, kc, j * 128:(j + 1) * 128], c.hT[:, kc, t0:t0 + tn]) for kc in range(16)],
                      r=[ku] + hkeys(t0), w=[('ps', b0)])
                gelu_from(c, c.bank(b0, tn), ('ps', b0), c.actT[g // 4][:, g % 4, t0:t0 + tn], ('uT', g, t0), 128, tn)
    vg = SB(nc, 'vg', [128, 256], F32)
    vln = SB(nc, 'vln', [128, 256], BF16)
    stats = SB(nc, 'stats', [128, 6], F32)
    mv = SB(nc, 'mv', [128, 2], F32)
    NT = T // 128
    for ip in range(4):
        bw, kw_ = load_wt(c, w_v, 3072 + ip * 256, 256)
        for tt in range(NT):
            t0b = (tt * 128 // 512) * 512
            b0 = gbank % 4
            gbank += 1
            kb.mm(c.bank(b0, 256), [(c.hT[:, kc, tt * 128:(tt + 1) * 128], bw[:, kc, 0:256]) for kc in range(16)],
                  r=[kw_] + hkeys(t0b), w=[('ps', b0)])
            gelu_from(c, c.bank(b0, 256), ('ps', b0), vg[:, :], 'vg', 128, 256)
            for gg in range(2):
                g = ip * 2 + gg
                sl = slice(gg * 128, (gg + 1) * 128)
                gsl = slice(g * 128, (g + 1) * 128)
                kb.op('dve', lambda e: e.bn_stats(out=stats[:], in_=vg[:, sl]), r=['vg'], w=['stats'])
                kb.op('dve', lambda e: e.bn_aggr(out=mv[:], in_=stats[:]), r=['stats'], w=['mv'])
                kb.op('act', lambda e: e.activation(out=mv[:, 1:2], in_=mv[:, 1:2], func=AF.Sqrt, bias=c.eps_sb[:, 0:1]),
                      r=['mv', 'eps'], w=['mv'])
                kb.op('dve', lambda e: e.reciprocal(out=mv[:, 1:2], in_=mv[:, 1:2]), r=['mv'], w=['mv'])
                kb.op('dve', lambda e: e.tensor_scalar(out=vg[:, sl], in0=vg[:, sl], scalar1=mv[:, 0:1],
                                                       scalar2=mv[:, 1:2], op0=ALU.subtract, op1=ALU.mult),
                      r=['vg', 'mv'], w=['vg'])
                kb.op('dve', lambda e: e.tensor_tensor(out=vg[:, sl], in0=vg[:, sl], in1=lng_s[:, gsl], op=ALU.mult),
                      r=['vg', 'lng_s'], w=['vg'])
                kb.op('dve', lambda e: e.tensor_tensor(out=vln[:, sl], in0=vg[:, sl], in1=lnb_s[:, gsl], op=ALU.add),
                      r=['vg', 'lnb_s'], w=[('vln', gg)])
                b1 = 4 + (gbank % 2)
                gbank += 1
                kb.mm(c.bank(b1, 128), [(vln[:, sl], wsT_m[:, g, :])], r=[('vln', gg), ('wsT_m', g)], w=[('ps', b1)])
                si = c.nstg % 2
                c.nstg += 1
                kb.op('dve', lambda e: e.tensor_tensor(out=c.stg[si][:, :128], in0=c.bank(b1, 128), in1=bs_s[:, gsl],
                                                       op=ALU.add), r=[('ps', b1), 'bs_s'], w=[('stg', si)])
                kb.op('dve', lambda e: e.tensor_tensor(out=c.stg[si][:, :128], in0=c.stg[si][:, :128],
                                                       in1=c.actT[g // 4][:, g % 4, tt * 128:(tt + 1) * 128], op=ALU.mult),
                      r=[('stg', si), ('uT', g, t0b)], w=[('stg', si)])
                kb.dma('sp', bo_o[g * 128:(g + 1) * 128, tt * 128:(tt + 1) * 128], c.stg[si][:, :128], r=[('stg', si)],
                       w=[('bo_o', g, tt)])
    if fused:
        kb.barrier()
        return nc
    kb.finish('sp')
    return nc


def proj_out_T(c, w_d, ncols_total, out_d, gbank0=0):
    kb = c.kb
    w_v = w_d.rearrange("(kc p) f -> p kc f", p=128)
    gbank = gbank0
    col = 0
    while col < ncols_total:
        ncol_t = min(256, ncols_total - col)
        buf, key = load_wt(c, w_v, col, ncol_t)
        j0 = 0
        while j0 < ncol_t:
            m = min(128, ncol_t - j0)
            for (t0, tn) in c.TB:
                b0 = gbank % 4
                gbank += 1
                kb.mm(c.ps[:m, b0 * 512:b0 * 512 + tn],
                      [(buf[:, kc, j0:j0 + m], c.hT[:, kc, t0:t0 + tn]) for kc in range(16)],
                      r=[key] + [('hT', kc, t0) for kc in range(16)], w=[('ps', b0)])
                si = c.nstg % 2
                c.nstg += 1
                if si == 0:
                    kb.op('act', lambda e: e.activation(out=c.stg[si][:m, :tn], in_=c.ps[:m, b0 * 512:b0 * 512 + tn],
                                                        func=AF.Copy), r=[('ps', b0)], w=[('stg', si)])
                else:
                    kb.op('dve', lambda e: e.tensor_copy(out=c.stg[si][:m, :tn], in_=c.ps[:m, b0 * 512:b0 * 512 + tn]),
                          r=[('ps', b0)], w=[('stg', si)])
                kb.dma('sp', out_d[col + j0:col + j0 + m, t0:t0 + tn], c.stg[si][:m, :tn], r=[('stg', si)],
                       w=[('z_o', col + j0, t0)])
            j0 += m
        col += ncol_t
    return gbank


def proj_resid_T(c, w_d, gbank0=0):
    kb = c.kb
    w_v = w_d.rearrange("(kc p) f -> p kc f", p=128)
    gbank = gbank0
    for dcp in range(8):
        buf, key = load_wt(c, w_v, dcp * 256, 256)
        for j in range(2):
            dc = dcp * 2 + j
            for (t0, tn) in c.TB:
                b0 = gbank % 4
                gbank += 1
                kb.mm(c.bank(b0, tn), [(buf[:, kc, j * 128:(j + 1) * 128], c.hT[:, kc, t0:t0 + tn]) for kc in range(16)],
                      r=[key] + [('hT', kc, t0) for kc in range(16)], w=[('ps', b0)])
                kb.op('dve', lambda e: e.tensor_tensor(out=c.xT[:, dc, t0:t0 + tn], in0=c.bank(b0, tn),
                                                       in1=c.xT[:, dc, t0:t0 + tn], op=ALU.add),
                      r=[('ps', b0), ('xT', dc, t0)], w=[('xT', dc, t0)])
    return gbank


def build_L2(T=1024, dff=DFF, nz=5168, nc=None, over=None, core=None):
    fused = nc is not None
    nc = bass.Bass("TRN2", target_bir_lowering=False) if nc is None else nc
    dt = mk_dt(nc, over)
    HALO = 32
    x_d = dt("x1Td", [D, T])
    a_d = None if fused else dt("aTh", [1024, HALO + T])
    bo_d = dt("boTd", [1024, T])
    cw_d = dt("conv_w", [128, 8, 31])
    cb_d = dt("conv_b", [128, 8])
    cg_d = dt("cln_g", [128, 8])
    cbb_d = dt("cln_b", [128, 8])
    wout_d = dt("w_out", [D, D])
    gA = dt("gA", [128, 16]); wgA = dt("wgA", [D, dff]); wuA = dt("wuA", [D, dff]); wdA = dt("wdA", [dff, D])
    gB = dt("gB", [128, 16]); wgB = dt("wgB", [D, dff]); wuB = dt("wuB", [D, dff]); wdB = dt("wdB", [dff, D])
    gM = dt("gM", [128, 16])
    win_d = dt("w_in1", [D, nz])
    x4_o = dt("x4T", [D, T], "ExternalOutput")
    z_o = dt("zT", [nz, T], "ExternalOutput")
    c = Core(nc, T) if core is None else core
    kb = c.kb
    extra_tiles(c)
    cw = load_small(c, 'cw', cw_d, [128, 8, 31])
    cb = load_small(c, 'cb', cb_d, [128, 8])
    cg = load_small(c, 'cgn', cg_d, [128, 8])
    cbb = load_small(c, 'cbb', cbb_d, [128, 8])
    gAs = load_small(c, 'gAs', gA, [128, 16])
    gBs = load_small(c, 'gBs', gB, [128, 16])
    gMs = load_small(c, 'gMs', gM, [128, 16])
    kinv = SB(nc, 'kinv', [128, 1], F32)
    kb.op('dve', lambda e: e.memset(kinv[:], 1.0 / 1024), w=['kinv'])
    load_xT(c, x_d)
    kb.dma('pool', c.hT[:, 8:16, :], bo_d.rearrange("(g p) t -> p g t", p=128),
           w=[('hT', k, t0) for k in range(8, 16) for (t0, _) in c.TB])
    abuf = [c.wg[i][:].rearrange("p a b -> p (a b)").bitcast(F32) for i in range(2)]
    ybuf = [c.wd[i][:].rearrange("p a b -> p (a b)").bitcast(F32) for i in range(4)]
    actkeys = lambda i: [('actT', i, fcl, t0) for fcl in range(4) for (t0, _) in c.TB]
    mr = c.actT[0][:].rearrange("p a b -> p (a b)").bitcast(F32)
    mean = mr[:, 0:T]
    rstd = mr[:, T:2 * T]
    S1 = [6, 7]
    S2 = [4, 5]
    for ch in range(8):
        ab = ch % 2
        a_sb = abuf[ab]
        if not fused:
            kb.dma('sp', a_sb[:, 0:HALO + T], a_d[ch * 128:(ch + 1) * 128, :], w=[('wg', ab)])
        else:
            if over.get('aT_prev') is None:
                kb.op('dve', lambda e: e.memset(a_sb[:, 0:HALO], 0.0), w=[('wg', ab)])
            else:
                kb.dma('sp', a_sb[:, 0:HALO], over['aT_prev'][ch * 128:(ch + 1) * 128, T - HALO:T], w=[('wg', ab)])
            kb.dma('sp', a_sb[:, HALO:HALO + T], over['aT_cur'][ch * 128:(ch + 1) * 128, :], r=[('wg', ab)],
                   w=[('wg', ab)])
        y = ybuf[ch // 2][:, (ch % 2) * T:(ch % 2) * T + T]
        ykey = ('wd', ch // 2)
        for k in range(31):
            if k == 0:
                kb.op('dve', lambda e: e.tensor_scalar(out=y, in0=a_sb[:, 2:2 + T], scalar1=cw[:, ch, 0:1], scalar2=None,
                                                       op0=ALU.mult), r=[('wg', ab), 'cw'], w=[ykey])
            else:
                kb.op('dve', lambda e, k=k: e.scalar_tensor_tensor(out=y, in0=a_sb[:, 2 + k:2 + k + T],
                                                                  scalar=cw[:, ch, k:k + 1], in1=y,
                                                                  op0=ALU.mult, op1=ALU.add),
                      r=[('wg', ab), 'cw', ykey], w=[ykey])
        kb.op('dve', lambda e: e.tensor_scalar(out=y, in0=y, scalar1=cb[:, ch:ch + 1], scalar2=None, op0=ALU.add),
              r=[ykey, 'cb'], w=[ykey])
        for bi, (t0, tn) in enumerate(c.TB):
            i = c.nsq % 2
            c.nsq += 1
            kb.op('act', lambda e: e.activation(out=c.sq[i][:, :tn], in_=y[:, t0:t0 + tn], func=AF.Copy),
                  r=[ykey], w=[('sq', i)])
            kb.mm1(c.bank(S1[bi], tn), c.ones[:], c.sq[i][:, :tn], ch == 0, ch == 7,
                   r=['ones', ('sq', i)], w=([('ps', S1[bi])] if ch in (0, 7) else []))
            i = c.nsq % 2
            c.nsq += 1
            kb.op('act', lambda e: e.activation(out=c.sq[i][:, :tn], in_=y[:, t0:t0 + tn], func=AF.Square),
                  r=[ykey], w=[('sq', i)])
            kb.mm1(c.bank(S2[bi], tn), c.ones[:], c.sq[i][:, :tn], ch == 0, ch == 7,
                   r=['ones', ('sq', i)], w=([('ps', S2[bi])] if ch in (0, 7) else []))
    for bi, (t0, tn) in enumerate(c.TB):
        kb.op('act', lambda e: e.activation(out=mean[:, t0:t0 + tn], in_=c.bank(S1[bi], tn), func=AF.Copy,
                                            scale=1.0 / 1024), r=[('ps', S1[bi])], w=actkeys(0))
        kb.op('dve', lambda e: e.tensor_tensor(out=c.tmp[0][:, :tn], in0=mean[:, t0:t0 + tn], in1=mean[:, t0:t0 + tn],
                                               op=ALU.mult), r=actkeys(0), w=[('tmp', 0)])
        kb.op('dve', lambda e: e.scalar_tensor_tensor(out=rstd[:, t0:t0 + tn], in0=c.bank(S2[bi], tn), scalar=kinv[:, 0:1],
                                                      in1=c.tmp[0][:, :tn], op0=ALU.mult, op1=ALU.subtract),
              r=[('ps', S2[bi]), 'kinv', ('tmp', 0)], w=actkeys(0))
        kb.op('act', lambda e: e.activation(out=rstd[:, t0:t0 + tn], in_=rstd[:, t0:t0 + tn], func=AF.Sqrt,
                                            bias=c.eps_sb[:, 0:1]), r=actkeys(0) + ['eps'], w=actkeys(0))
        kb.op('dve', lambda e: e.reciprocal(out=rstd[:, t0:t0 + tn], in_=rstd[:, t0:t0 + tn]), r=actkeys(0), w=actkeys(0))
    for ch in range(8):
        y = ybuf[ch // 2][:, (ch % 2) * T:(ch % 2) * T + T]
        ykey = ('wd', ch // 2)
        for (t0, tn) in c.TB:
            kb.op('dve', lambda e: e.tensor_tensor(out=y[:, t0:t0 + tn], in0=y[:, t0:t0 + tn], in1=mean[:, t0:t0 + tn],
                                                   op=ALU.subtract), r=[ykey] + actkeys(0), w=[ykey])
            kb.op('dve', lambda e: e.tensor_tensor(out=y[:, t0:t0 + tn], in0=y[:, t0:t0 + tn], in1=rstd[:, t0:t0 + tn],
                                                   op=ALU.mult), r=[ykey] + actkeys(0), w=[ykey])
            kb.op('act', lambda e: e.activation(out=c.hT[:, ch, t0:t0 + tn], in_=y[:, t0:t0 + tn], func=AF.Silu,
                                                scale=cg[:, ch:ch + 1], bias=cbb[:, ch:ch + 1]),
                  r=[ykey, 'cgn', 'cbb'], w=[('hT', ch, t0)])
    gb_ = proj_resid_T(c, wout_d)
    rmsnorm_T(c, gAs, 'gAs')
    ffn_T(c, wgA, wuA, wdA, dff)
    rmsnorm_T(c, gBs, 'gBs')
    ffn_T(c, wgB, wuB, wdB, dff)
    store_T(c, x4_o, c.xT, 'xT')
    rmsnorm_T(c, gMs, 'gMs')
    proj_out_T(c, win_d, nz, z_o)
    if fused:
        kb.barrier()
        return nc
    kb.finish('sp')
    return nc


def build_L4(T=1024, dff=DFF, nc=None, over=None, core=None):
    fused = nc is not None
    nc = bass.Bass("TRN2", target_bir_lowering=False) if nc is None else nc
    dt = mk_dt(nc, over)
    x_d = dt("x4Td", [D, T])
    o_d = None if fused else dt("oTd", [D, T])
    wout_d = dt("w_out1", [D, D])
    gA = dt("gA", [128, 16]); wgA = dt("wgA", [D, dff]); wuA = dt("wuA", [D, dff]); wdA = dt("wdA", [dff, D])
    gF = dt("gF", [128, 16])
    y_o = dt("yT", [D, T], "ExternalOutput")
    c = Core(nc, T) if core is None else core
    kb = c.kb
    gAs = load_small(c, 'gAs', gA, [128, 16])
    gFs = load_small(c, 'gFs', gF, [128, 16])
    load_xT(c, x_d)
    if not fused:
        ov = o_d.rearrange("(kc p) t -> p kc t", p=128)
        for k0 in range(0, 16, 8):
            kb.dma('pool', c.hT[:, k0:k0 + 8, :], ov[:, k0:k0 + 8, :],
                   w=[('hT', k, t0) for k in range(k0, k0 + 8) for (t0, _) in c.TB])
    else:
        extra_tiles(c)
        fl = load_small(c, 'flag', over['flag'], [128, 2])
        x1v = over['x4T_1'].rearrange("(kc p) t -> p kc t", p=128)
        for kc in range(16):
            for (t0, tn) in c.TB:
                si = c.nstg % 2
                c.nstg += 1
                kb.dma('sp', c.stg[si][:, :tn], x1v[:, kc, t0:t0 + tn], w=[('stg', si)])
                kb.op('dve', lambda e: e.tensor_scalar(out=c.xT[:, kc, t0:t0 + tn], in0=c.xT[:, kc, t0:t0 + tn],
                                                       scalar1=fl[:, 0:1], scalar2=None, op0=ALU.mult),
                      r=[('xT', kc, t0), 'flag'], w=[('xT', kc, t0)])
                kb.op('dve', lambda e: e.scalar_tensor_tensor(out=c.xT[:, kc, t0:t0 + tn], in0=c.stg[si][:, :tn],
                                                              scalar=fl[:, 1:2], in1=c.xT[:, kc, t0:t0 + tn],
                                                              op0=ALU.mult, op1=ALU.add),
                      r=[('stg', si), ('xT', kc, t0), 'flag'], w=[('xT', kc, t0)])
        otm = [SB(nc, 'otm%d' % i, [128, 2048], BF16) for i in range(2)]
        otn = [SB(nc, 'otn%d' % i, [128, 2048], BF16) for i in range(2)]
        idb = load_small(c, 'idb4', over['ident'], [128, 128], BF16, q='pool')
        ntr = 0
        for tt in range(T // 128):
            i = tt % 2
            t0b = (tt * 128 // 512) * 512
            kb.dma('pool', otm[i][:], over['o_tok'][tt * 128:(tt + 1) * 128, :], w=[('otm', i)])
            kb.dma('pool', otn[i][:], over['o_tok1'][tt * 128:(tt + 1) * 128, :], w=[('otn', i)])
            kb.op('dve', lambda e: e.tensor_scalar(out=otm[i][:], in0=otm[i][:], scalar1=fl[:, 0:1], scalar2=None,
                                                   op0=ALU.mult), r=[('otm', i), 'flag'], w=[('otm', i)])
            kb.op('dve', lambda e: e.scalar_tensor_tensor(out=otm[i][:], in0=otn[i][:], scalar=fl[:, 1:2], in1=otm[i][:],
                                                          op0=ALU.mult, op1=ALU.add),
                  r=[('otn', i), ('otm', i), 'flag'], w=[('otm', i)])
            for k4 in range(4):
                tb = 2 + ntr % 2
                ntr += 1
                tbk = c.ps[:, tb * 512:(tb + 1) * 512].bitcast(BF16)
                pe_multi(kb, [(lambda e, j=j: e.transpose(out=tbk[:, j * 128:(j + 1) * 128],
                                                          in_=otm[i][:, (k4 * 4 + j) * 128:(k4 * 4 + j + 1) * 128],
                                                          identity=idb[:])) for j in range(4)],
                         r=[('otm', i), 'idb4'], w=[('ps', tb)])
                kb.op('dve', lambda e: e.tensor_copy(out=c.hT[:, k4 * 4:k4 * 4 + 4, tt * 128:(tt + 1) * 128],
                                                     in_=tbk[:, 0:512].rearrange("p (a b) -> p a b", b=128)),
                      r=[('ps', tb)], w=[('hT', k, t0b) for k in range(k4 * 4, k4 * 4 + 4)])
    proj_resid_T(c, wout_d)
    rmsnorm_T(c, gAs, 'gAs')
    ffn_T(c, wgA, wuA, wdA, dff)
    rmsnorm_T(c, gFs, 'gFs', out=c.xT, okey='xT')
    store_T(c, y_o, c.xT, 'xT')
    if fused:
        kb.barrier()
        return nc
    kb.finish('sp')
    return nc


S = 2048
NQT = 16
SCALE = 128 ** -0.5
MAGIC = 12582912.0
PI = 3.141592653589793


def pe_multi(kb, fns, r=(), w=()):
    deps = kb._deps(r, w)
    kb._emit_waits('pe', deps, skip_sem=kb.sem['pe'].num)
    inst = None
    for f in fns:
        inst = f(kb.nc.tensor)
    kb.cnt['pe'] += 1
    inst.then_inc(kb.sem['pe'], 1)
    tok = (kb.sem['pe'].num, kb.cnt['pe'])
    kb._commit(tok, r, w)
    return tok


def build_L3(nc=None, over=None, kb=None, ps=None, gp=0):
    fused = nc is not None
    nc = bass.Bass("TRN2", target_bir_lowering=False) if nc is None else nc
    dt = mk_dt(nc, over)
    if not fused:
        q_d = dt("qT", [8, 128, S]); qp_d = dt("qPT", [8, 128, S])
        ks_d = dt("ksT", [2, 128, S]); ksp_d = dt("ksPT", [2, 128, S])
        kw_d = dt("kwT", [2, 128, S]); kwp_d = dt("kwPT", [2, 128, S])
        kc_d = dt("kcT", [128, 2, S]); vc_d = dt("vcT", [128, 2, S])
        vs_d = dt("vs", [2, S, 128]); vw_d = dt("vw", [2, S, 128])
        gl_d = dt("gl", [S, 24])
    else:
        zT = over['zT']
        rows = lambda base, i: zT[base + i * 128:base + (i + 1) * 128, :]
        q_d = [rows(0, gp * 8 + hh) for hh in range(8)]
        qp_d = [None] * 8
        ks_d = [rows(3072, 2 * gp + g) for g in range(2)]; ksp_d = [None] * 2
        kw_d = [rows(4096, 2 * gp + g) for g in range(2)]; kwp_d = [None] * 2
        kc_d = zT[2048 + 2 * gp * 128:2048 + (2 * gp + 2) * 128, :].rearrange("(g p) s -> p g s", p=128)
        vc_d = zT[2560 + 2 * gp * 128:2560 + (2 * gp + 2) * 128, :].rearrange("(g p) s -> p g s", p=128)
        vsT_d = [rows(3584, 2 * gp + g) for g in range(2)]
        vwT_d = [rows(4608, 2 * gp + g) for g in range(2)]
        glT_d = zT[5120 + gp * 24:5120 + (gp + 1) * 24, :]
    pos_d = dt("pos", [1, S], dtype=I32)
    inv_d = dt("inv", [128, 1]); sgn_d = dt("sgn", [128, 1])
    pek_d = dt("pekT", [128, 32]); w1k_d = dt("w1k", [4096, 256]); w2k_d = dt("w2k", [256, 128])
    pev_d = dt("pevT", [128, 32]); w1v_d = dt("w1v", [4096, 256]); w2v_d = dt("w2v", [256, 128])
    ovl_d = dt("ovl", [128, 32]); cm_d = dt("cm", [128, 16, 128]); fbv_d = dt("fbv", [128, 16, 32])
    tri_d = dt("tri", [128, 128]); tri2_d = dt("tri2", [128, 128]); id_d = dt("ident", [128, 128])
    o_o = dt("o", [S, 1024], "ExternalOutput")
    kb = KB(nc) if kb is None else kb
    A = lambda name, shape, dtype=F32: SB(nc, 's_' + name, list(shape), dtype)

    def small(name, d_ap, shape, dtype=F32, q='sp'):
        t = A(name, shape, dtype)
        kb.dma(q, t[:], d_ap, w=[name])
        return t

    def const(name, val):
        t = A(name, [128, 1])
        kb.op('dve', lambda e: e.memset(t[:], val), w=[name])
        return t
    ps = nc.alloc_psum_tensor('ps', [128, 8 * 512], F32) if ps is None else ps
    bank = lambda b, n=512, p=128: ps[:p, b * 512:b * 512 + n]
    bankbf = lambda b: ps[:, b * 512:(b + 1) * 512].bitcast(BF16)
    H = S // 2
    xs = [A('xs%d' % i, [128, H]) for i in range(2)]
    xp = [A('xp%d' % i, [128, H]) for i in range(2)]
    inv = small('inv', inv_d, [128, 1]); sgn = small('sgn', sgn_d, [128, 1])
    ovl = small('ovl', ovl_d, [128, 32]); cm = small('cm', cm_d, [128, 16, 128], BF16, 'pool'); fbv = small('fbv', fbv_d, [128, 16, 32])
    tri = small('tri', tri_d, [128, 128], BF16, 'pool'); tri2 = small('tri2', tri2_d, [128, 128], BF16, 'pool')
    idf = small('idf', id_d, [128, 128]); idb = small('idb', id_d, [128, 128], BF16, 'pool')
    if not fused:
        gl = small('gl', gl_d.rearrange("(n p) c -> p n c", p=128), [128, 16, 24])
        kb.op('act', lambda e: e.activation(out=gl[:], in_=gl[:], func=AF.Sigmoid), r=['gl'], w=['gl'])
    else:
        gl = A('gl', [128, 16, 24])
        for hf in range(2):
            kb.dma('sp', xs[hf][:24, :], glT_d[:, hf * H:(hf + 1) * H], w=[('xs', hf)])
        for kt in range(16):
            pe_multi(kb, [lambda e: e.transpose(out=bank(7, 24), in_=xs[kt // 8][:24, (kt % 8) * 128:(kt % 8 + 1) * 128],
                                                identity=idf[:24, :24])], r=[('xs', kt // 8), 'idf'], w=[('ps', 7)])
            kb.op('act', lambda e: e.activation(out=gl[:, kt, :], in_=bank(7, 24), func=AF.Sigmoid),
                  r=[('ps', 7)], w=['gl'])
    c_i2p = const('c_i2p', 1.0 / (2 * PI)); c_mag = const('c_mag', MAGIC); c_nmag = const('c_nmag', -MAGIC)
    c_n2p = const('c_n2p', -2 * PI); c_pi = const('c_pi', PI); c_npi = const('c_npi', -PI); c_hpi = const('c_hpi', PI / 2)
    c_tiny = const('c_tiny', 1e-30); c_one = const('c_one', 1.0); c_cg = const('c_cg', 0.044715)
    c_nsc = const('c_nsc', -SCALE)
    posi = A('posi', [128, S], I32)
    kb.dma('sp', posi[:], pos_d.partition_broadcast(128), w=['posi'])
    ang = A('ang', [128, S]); cosT = A('cosT', [128, S]); sinT = A('sinT', [128, S])
    kb.op('dve', lambda e: e.tensor_copy(out=ang[:], in_=posi[:]), r=['posi'], w=['ang'])
    kb.op('dve', lambda e: e.tensor_scalar(out=ang[:], in0=ang[:], scalar1=inv[:, 0:1], scalar2=None, op0=ALU.mult),
          r=['ang', 'inv'], w=['ang'])

    def sin_of(dst, dkey, shift):
        wk = posi[:].bitcast(F32)
        src = ang
        if shift is not None:
            kb.op('dve', lambda e: e.tensor_scalar(out=dst[:], in0=ang[:], scalar1=shift[:, 0:1], scalar2=None, op0=ALU.add),
                  r=['ang'], w=[dkey])
            src = dst
        kb.op('dve', lambda e: e.tensor_scalar(out=wk, in0=src[:], scalar1=c_i2p[:, 0:1], scalar2=c_mag[:, 0:1],
                                               op0=ALU.mult, op1=ALU.add), r=[dkey, 'ang', 'posi'], w=['posi'])
        kb.op('dve', lambda e: e.tensor_scalar(out=wk, in0=wk, scalar1=c_nmag[:, 0:1], scalar2=None, op0=ALU.add),
              r=['posi'], w=['posi'])
        kb.op('dve', lambda e: e.scalar_tensor_tensor(out=dst[:], in0=wk, scalar=c_n2p[:, 0:1], in1=src[:],
                                                      op0=ALU.mult, op1=ALU.add), r=['posi', 'ang', dkey], w=[dkey])
        kb.op('dve', lambda e: e.tensor_scalar(out=dst[:], in0=dst[:], scalar1=c_pi[:, 0:1], scalar2=c_npi[:, 0:1],
                                               op0=ALU.min, op1=ALU.max), r=[dkey], w=[dkey])
        kb.op('act', lambda e: e.activation(out=dst[:], in_=dst[:], func=AF.Sin), r=[dkey], w=[dkey])
    sin_of(sinT, 'sinT', None)
    kb.op('dve', lambda e: e.tensor_scalar(out=sinT[:], in0=sinT[:], scalar1=sgn[:, 0:1], scalar2=None, op0=ALU.mult),
          r=['sinT', 'sgn'], w=['sinT'])
    sin_of(cosT, 'cosT', c_hpi)
    qT = A('qTb', [128, 8, S], BF16); qr = A('qr', [128, 8, S], BF16)
    ksr = A('ksr', [128, 2, S], BF16); kwr = A('kwr', [128, 2, S], BF16)
    kcT = A('kcTb', [128, 2, S], BF16); vcT = A('vcTb', [128, 2, S], BF16)
    vs = A('vsb', [128, 2, 16, 132], BF16); vw = A('vwb', [128, 2, 16, 132], BF16)
    kb.dma('pool', kcT[:], kc_d, w=['kcT'])
    kb.dma('pool', vcT[:], vc_d, w=['vcT'])
    kb.op('dve', lambda e: e.memset(vs[:, :, :, 128:129], 1.0), w=['vs1'])
    kb.op('dve', lambda e: e.memset(vw[:, :, :, 128:129], 1.0), w=['vw1'])
    if not fused:
        for g in range(2):
            kb.dma('pool', vs[:, g, :, 0:128], vs_d[g].rearrange("(n p) d -> p n d", p=128), w=[('vs', g)])
            kb.dma('pool', vw[:, g, :, 0:128], vw_d[g].rearrange("(n p) d -> p n d", p=128), w=[('vw', g)])
    else:
        vtmp = [xp[i][:].bitcast(BF16) for i in range(2)]
        nv = 0
        for g in range(2):
            for (srcs, dstt, key) in ((vsT_d, vs, 'vs'), (vwT_d, vw, 'vw')):
                vi = nv % 2
                nv += 1
                kb.dma('pool', vtmp[vi], srcs[g], w=[('xp', vi)])
                for k4 in range(4):
                    tb = 2 + k4 % 2
                    tbk = bankbf(tb)
                    pe_multi(kb, [(lambda e, j=j: e.transpose(out=tbk[:, j * 128:(j + 1) * 128],
                                                              in_=vtmp[vi][:, (k4 * 4 + j) * 128:(k4 * 4 + j + 1) * 128],
                                                              identity=idb[:])) for j in range(4)],
                             r=[('xp', vi), 'idb'], w=[('ps', tb)])
                    kb.op('dve', lambda e: e.tensor_copy(out=dstt[:, g, k4 * 4:k4 * 4 + 4, 0:128],
                                                         in_=tbk[:, 0:512].rearrange("p (a b) -> p a b", b=128)),
                          r=[('ps', tb)], w=[(key, g)])
    nst = [0]

    def rope(src_d, srcp_d, dst, dkey, plain=None):
        for hf in range(2):
            i = nst[0] % 2
            nst[0] += 1
            sl = slice(hf * H, (hf + 1) * H)
            kb.dma('sp', xs[i][:], src_d[:, sl], w=[('xs', i)])
            if srcp_d is not None:
                kb.dma('sp', xp[i][:], srcp_d[:, sl], w=[('xp', i)])
            else:
                kb.dma('sp', xp[i][0:16, :], src_d[16:32, sl], w=[('xp', i)])
                kb.dma('sp', xp[i][16:32, :], src_d[0:16, sl], r=[('xp', i)], w=[('xp', i)])
                kb.dma('sp', xp[i][32:128, :], src_d[32:128, sl], r=[('xp', i)], w=[('xp', i)])
            if plain is not None:
                kb.op('act', lambda e: e.activation(out=plain[:, sl], in_=xs[i][:], func=AF.Copy), r=[('xs', i)],
                      w=[(dkey, 'plain', hf)])
            kb.op('dve', lambda e: e.tensor_tensor(out=xs[i][:], in0=xs[i][:], in1=cosT[:, sl], op=ALU.mult),
                  r=[('xs', i), 'cosT'], w=[('xs', i)])
            kb.op('dve', lambda e: e.tensor_tensor(out=xp[i][:], in0=xp[i][:], in1=sinT[:, sl], op=ALU.mult),
                  r=[('xp', i), 'sinT'], w=[('xp', i)])
            kb.op('dve', lambda e: e.tensor_tensor(out=dst[:, sl], in0=xs[i][:], in1=xp[i][:], op=ALU.add),
                  r=[('xs', i), ('xp', i)], w=[(dkey, hf)])
    for hh in range(8):
        rope(q_d[hh], qp_d[hh], qr[:, hh, :], ('qr', hh), plain=qT[:, hh, :])
    for g in range(2):
        rope(ks_d[g], ksp_d[g], ksr[:, g, :], ('ksr', g))
        rope(kw_d[g], kwp_d[g], kwr[:, g, :], ('kwr', g))
    qkeys = lambda hh: [(('qr', hh), 0), (('qr', hh), 1), (('qr', hh), 'plain', 0), (('qr', hh), 'plain', 1)]
    w1 = A('w1', [128, 32, 256], BF16)
    w2 = A('w2', [128, 2, 128], BF16)
    hid = A('hid', [128, 2, 128], BF16)
    hx = A('hx', [128, 128]); ha = A('ha', [128, 128]); hb = A('hb', [128, 128])
    cpe = A('cpe', [128, 2])
    kcmpT = A('kcmpT', [128, 2, 128], BF16)
    vcmp = A('vcmp', [128, 2, 128])
    kb.op('dve', lambda e: e.memset(hid[:], 0.0), w=['hid'])
    kb.op('dve', lambda e: e.memset(kcmpT[:], 0.0), w=['kcmpT'])
    kb.op('dve', lambda e: e.memset(vcmp[:], 0.0), w=['vcmp'])
    for which, (pe_d, w1_d, w2_d, srcT, skey) in enumerate([(pek_d, w1k_d, w2k_d, kcT, 'kcT'), (pev_d, w1v_d, w2v_d, vcT, 'vcT')]):
        pe_b = small('pe_b%d' % which, pe_d, [128, 32], BF16, 'pool')
        for l0 in range(0, 32, 8):
            kb.dma('pool', w1[:, l0:l0 + 8, :], w1_d[l0 * 128:(l0 + 8) * 128, :].rearrange("(l p) h -> p l h", p=128),
                   w=[('w1', l0)])
        kb.dma('pool', w2[:], w2_d.rearrange("(c p) d -> p c d", p=128), w=['w2'])
        w1keys = [('w1', l0) for l0 in range(0, 32, 8)]
        src4 = srcT[:].rearrange("p g (n r) -> p g n r", r=16)
        for hc in range(2):
            kb.mm(bank(7, 1), [(w1[:, l, hc * 128:(hc + 1) * 128], pe_b[:, l:l + 1]) for l in range(32)],
                  r=w1keys + ['pe_b%d' % which], w=[('ps', 7)])
            kb.op('dve', lambda e: e.tensor_copy(out=cpe[:, hc:hc + 1], in_=bank(7, 1)), r=[('ps', 7)], w=[('cpe', hc)])
        for g in range(2):
            for hc in range(2):
                kb.mm(bank(6, 127), [(w1[:, l, hc * 128:(hc + 1) * 128], src4[:, g, (l // 16):(l // 16) + 127, l % 16])
                                     for l in range(32)], r=w1keys + [skey], w=[('ps', 6)])
                kb.op('dve', lambda e: e.tensor_scalar(out=hx[:, :127], in0=bank(6, 127), scalar1=cpe[:, hc:hc + 1],
                                                       scalar2=None, op0=ALU.add), r=[('ps', 6), ('cpe', hc)], w=['hx'])
                kb.op('act', lambda e: e.activation(out=ha[:, :127], in_=hx[:, :127], func=AF.Square), r=['hx'], w=['ha'])
                kb.op('dve', lambda e: e.tensor_scalar(out=ha[:, :127], in0=ha[:, :127], scalar1=c_cg[:, 0:1],
                                                       scalar2=c_one[:, 0:1], op0=ALU.mult, op1=ALU.add), r=['ha'], w=['ha'])
                kb.op('dve', lambda e: e.tensor_tensor(out=ha[:, :127], in0=hx[:, :127], in1=ha[:, :127], op=ALU.mult),
                      r=['hx', 'ha'], w=['ha'])
                kb.op('act', lambda e: e.activation(out=hb[:, :127], in_=ha[:, :127], func=AF.Sigmoid, scale=2.0 * GELU_C),
                      r=['ha'], w=['hb'])
                kb.op('dve', lambda e: e.tensor_tensor(out=hid[:, hc, :127], in0=hx[:, :127], in1=hb[:, :127], op=ALU.mult),
                      r=['hx', 'hb'], w=[('hid', hc)])
            if which == 0:
                kb.mm(bank(6, 127), [(w2[:, hc, :], hid[:, hc, :127]) for hc in range(2)],
                      r=['w2', ('hid', 0), ('hid', 1)], w=[('ps', 6)])
                kb.op('dve', lambda e: e.tensor_copy(out=kcmpT[:, g, :127], in_=bank(6, 127)), r=[('ps', 6)],
                      w=[('kcmpT', g)])
            else:
                kb.mm(bank(6, 128, 127), [(hid[:, hc, :127], w2[:, hc, :]) for hc in range(2)],
                      r=['w2', ('hid', 0), ('hid', 1)], w=[('ps', 6)])
                kb.op('dve', lambda e: e.tensor_copy(out=vcmp[:127, g, :], in_=bank(6, 128, 127)), r=[('ps', 6)],
                      w=[('vcmp', g)])
    sc4s = [A('sc4_%d' % i, [128, 4, 128]) for i in range(2)]
    mxs = [A('mx%d' % i, [128, 1]) for i in range(2)]
    rs4s = [A('rs4_%d' % i, [128, 4]) for i in range(2)]
    pcT4s = [A('pcT4_%d' % i, [128, 4, 128]) for i in range(2)]
    impms = [A('impm%d' % i, [128, 32]) for i in range(2)]
    wk32s = [A('wk32_%d' % i, [128, 32]) for i in range(2)]
    m8s = [A('m8_%d' % i, [128, 8]) for i in range(2)]
    m8bs = [A('m8b_%d' % i, [128, 8]) for i in range(2)]
    sels = [A('sel%d' % i, [128, 32], BF16) for i in range(2)]
    pbuf = [A('pbuf%d' % i, [128, 512], BF16) for i in range(2)]
    pT = [A('pT%d' % i, [128, 512], BF16) for i in range(2)]
    oacc = [A('oacc%d' % i, [128, 4, 128]) for i in range(2)]
    gsc = [A('gsc%d' % i, [128, 1]) for i in range(2)]
    cnt = dict(c=0, j=0)
    X = mybir.AxisListType.X

    def comp_ops(g, qt, p):
        sc4, mx, rs4, pcT4, impm, wk32, m8, m8b, sel = (sc4s[p], mxs[p], rs4s[p], pcT4s[p], impms[p], wk32s[p],
                                                        m8s[p], m8bs[p], sels[p])
        K = lambda n: (n, p)
        oa = oacc[p]
        oakeys = [('oacc', p, h) for h in range(4)]
        qsl = slice(qt * 128, (qt + 1) * 128)
        gview = gl[:, qt, g * 12:(g + 1) * 12].rearrange("p (h c) -> p h c", c=3)[:, :, 0:1]
        ops = []
        ops.append(lambda: pe_multi(kb, [(lambda e, h=h: e.matmul(bank(6, 512)[:, h * 128:(h + 1) * 128],
                                                                   qT[:, g * 4 + h, qsl], kcmpT[:, g, :], start=True,
                                                                   stop=True)) for h in range(4)],
                                    r=[(('qr', g * 4 + h), 'plain', qt // 8) for h in range(4)] + [('kcmpT', g)],
                                    w=[('ps', 6)]))
        ops.append(lambda: kb.op('dve', lambda e: e.reduce_max(out=mx[:], in_=bank(6, 512), axis=X), r=[('ps', 6)],
                                 w=[K('mx')]))
        ops.append(lambda: kb.op('dve', lambda e: e.tensor_scalar(out=mx[:], in0=mx[:], scalar1=c_nsc[:, 0:1],
                                                                  scalar2=None, op0=ALU.mult), r=[K('mx')], w=[K('mx')]))
        ops.append(lambda: kb.op('act', lambda e: e.activation(out=sc4[:].rearrange("p h n -> p (h n)"), in_=bank(6, 512),
                                                               func=AF.Exp, scale=SCALE, bias=mx[:, 0:1]),
                                 r=[('ps', 6), K('mx')], w=[K('sc4')]))
        ops.append(lambda: kb.op('dve', lambda e: e.tensor_tensor(out=sc4[:], in0=sc4[:],
                                                                  in1=cm[:, qt, :].unsqueeze(1).broadcast_to([128, 4, 128]),
                                                                  op=ALU.mult), r=[K('sc4'), 'cm'], w=[K('sc4')]))
        ops.append(lambda: kb.op('dve', lambda e: e.reduce_sum(out=rs4[:], in_=sc4[:], axis=X), r=[K('sc4')], w=[K('rs4')]))
        ops.append(lambda: kb.op('dve', lambda e: e.tensor_scalar(out=rs4[:], in0=rs4[:], scalar1=c_tiny[:, 0:1],
                                                                  scalar2=None, op0=ALU.max), r=[K('rs4')], w=[K('rs4')]))
        ops.append(lambda: kb.op('dve', lambda e: e.reciprocal(out=rs4[:], in_=rs4[:]), r=[K('rs4')], w=[K('rs4')]))
        ops.append(lambda: kb.op('dve', lambda e: e.tensor_tensor(out=sc4[:], in0=sc4[:],
                                                                  in1=rs4[:, :].unsqueeze(2).broadcast_to([128, 4, 128]),
                                                                  op=ALU.mult), r=[K('sc4'), K('rs4')], w=[K('sc4')]))
        ops.append(lambda: pe_multi(kb, [(lambda e, h=h: e.transpose(out=bank(6, 512)[:, h * 128:(h + 1) * 128],
                                                                      in_=sc4[:, h, :], identity=idf[:])) for h in range(4)],
                                    r=[K('sc4'), 'idf'], w=[('ps', 6)]))
        ops.append(lambda: kb.op('act', lambda e: e.activation(out=pcT4[:].rearrange("p h n -> p (h n)"), in_=bank(6, 512),
                                                               func=AF.Copy), r=[('ps', 6)], w=[K('pcT4')]))
        ops.append(lambda: kb.mm(bank(7, 32), [(pcT4[:, h, :], ovl[:, :]) for h in range(4)], r=[K('pcT4'), 'ovl'],
                                 w=[('ps', 7)]))
        ops.append(lambda: pe_multi(kb, [(lambda e, h=h: e.matmul(bank(6, 512)[:, h * 128:(h + 1) * 128], pcT4[:, h, :],
                                                                   vcmp[:, g, :], start=True, stop=True)) for h in range(4)],
                                    r=[K('pcT4'), ('vcmp', g)], w=[('ps', 6)]))
        ops.append(lambda: kb.op('dve', lambda e: e.tensor_tensor(out=oa[:],
                                                                  in0=bank(6, 512).rearrange("p (h d) -> p h d", d=128),
                                                                  in1=gview.broadcast_to([128, 4, 128]), op=ALU.mult),
                                 r=[('ps', 6), 'gl'], w=oakeys))
        ops.append(lambda: kb.op('dve', lambda e: e.tensor_tensor(out=impm[:], in0=bank(7, 32), in1=fbv[:, qt, :],
                                                                  op=ALU.add), r=[('ps', 7), 'fbv'], w=[K('impm')]))
        ops.append(lambda: kb.op('dve', lambda e: e.max(out=m8[:], in_=impm[:]), r=[K('impm')], w=[K('m8')]))
        ops.append(lambda: kb.op('dve', lambda e: e.match_replace(out=wk32[:], in_to_replace=m8[:], in_values=impm[:],
                                                                  imm_value=-3.0e38), r=[K('impm'), K('m8')], w=[K('wk32')]))
        ops.append(lambda: kb.op('dve', lambda e: e.max(out=m8b[:], in_=wk32[:]), r=[K('wk32')], w=[K('m8b')]))
        ops.append(lambda: kb.op('dve', lambda e: e.tensor_scalar(out=sel[:], in0=impm[:], scalar1=m8b[:, 7:8],
                                                                  scalar2=None, op0=ALU.is_ge), r=[K('impm'), K('m8b')],
                                 w=[K('sel')]))
        return ops

    tiles_ = [(g, qt) for g in range(2) for qt in range(NQT)]
    for f_ in comp_ops(0, 0, 0):
        f_()
    for ti_, (g, qt) in enumerate(tiles_):
        if True:
            oi = ti_ % 2
            oa = oacc[oi]
            sel = sels[oi]
            selkey = ('sel', oi)
            oakeys = [('oacc', oi, h) for h in range(4)]
            qsl = slice(qt * 128, (qt + 1) * 128)
            nxt = comp_ops(tiles_[ti_ + 1][0], tiles_[ti_ + 1][1], (ti_ + 1) % 2) if ti_ + 1 < len(tiles_) else []
            chunks = []
            for h in range(4):
                hh = g * 4 + h
                for br in (1, 2):
                    kts = list(range(qt + 1)) if br == 1 else list(range(max(0, qt - 4), qt + 1))
                    parts = [kts[i:i + 4] for i in range(0, len(kts), 4)]
                    accb = 4 + cnt['j'] % 2
                    ji = cnt['j'] % 2
                    cnt['j'] += 1
                    npv = len(kts)
                    ipv = 0
                    for pi_, ch in enumerate(parts):
                        chunks.append(dict(h=h, hh=hh, br=br, ch=ch, accb=accb, ji=ji, ipv0=ipv, npv=npv,
                                           last=(pi_ == len(parts) - 1)))
                        ipv += len(ch)

            def stage1(c_):
                i = c_['idx']
                ch = c_['ch']
                n = 128 * len(ch)
                k0 = ch[0] * 128
                sb = i % 2
                kT, kkey = (ksr[:, g, :], ('ksr', g)) if c_['br'] == 1 else (kwr[:, g, :], ('kwr', g))
                kb.mm(bank(sb, n), [(qr[:, c_['hh'], qsl], kT[:, k0:k0 + n])],
                      r=[(('qr', c_['hh']), qt // 8), (kkey, 0), (kkey, 1)], w=[('ps', sb)])
                pb = pbuf[i % 2]
                pk = ('pbuf', i % 2)
                kb.op('act', lambda e: e.activation(out=pb[:, :n], in_=bank(sb, n), func=AF.Exp, scale=SCALE),
                      r=[('ps', sb)], w=[pk])
                if c_['br'] == 1:
                    nb = 2 * len(ch)
                    kb.op('dve', lambda e: e.tensor_tensor(
                        out=pb[:, :n].rearrange("p (b k) -> p b k", k=64), in0=pb[:, :n].rearrange("p (b k) -> p b k", k=64),
                        in1=sel[:, 2 * ch[0]:2 * ch[0] + nb].unsqueeze(2).broadcast_to([128, nb, 64]), op=ALU.mult),
                        r=[pk, selkey], w=[pk])
                if c_['br'] == 2 and qt >= 4 and ch[0] == qt - 4:
                    kb.op('dve', lambda e: e.tensor_tensor(out=pb[:, 0:128], in0=pb[:, 0:128], in1=tri2[:], op=ALU.mult),
                          r=[pk, 'tri2'], w=[pk])
                if ch[-1] == qt:
                    off = 128 * (len(ch) - 1)
                    kb.op('dve', lambda e: e.tensor_tensor(out=pb[:, off:off + 128], in0=pb[:, off:off + 128], in1=tri[:],
                                                           op=ALU.mult), r=[pk, 'tri'], w=[pk])

            def stage2(c_):
                i = c_['idx']
                ch = c_['ch']
                n = 128 * len(ch)
                pb = pbuf[i % 2]
                tb = 2 + i % 2
                tbk = bankbf(tb)
                pe_multi(kb, [(lambda e, j=j: e.transpose(out=tbk[:, j * 128:(j + 1) * 128], in_=pb[:, j * 128:(j + 1) * 128],
                                                          identity=idb[:])) for j in range(len(ch))],
                         r=[('pbuf', i % 2), 'idb'], w=[('ps', tb)])
                if i % 2 == 0:
                    kb.op('act', lambda e: e.activation(out=pT[0][:, :n], in_=tbk[:, :n], func=AF.Copy), r=[('ps', tb)],
                          w=[('pT', 0)])
                else:
                    kb.op('dve', lambda e: e.tensor_copy(out=pT[1][:, :n], in_=tbk[:, :n]), r=[('ps', tb)], w=[('pT', 1)])

            def stage3(c_):
                i = c_['idx']
                ch = c_['ch']
                accb = c_['accb']
                vaug, vkeys = (vs, [('vs', g), 'vs1']) if c_['br'] == 1 else (vw, [('vw', g), 'vw1'])
                fns = []
                for j, kt in enumerate(ch):
                    ip = c_['ipv0'] + j
                    fns.append(lambda e, j=j, kt=kt, first=(ip == 0), lastm=(ip == c_['npv'] - 1): e.matmul(
                        bank(accb, 129), pT[i % 2][:, j * 128:(j + 1) * 128], vaug[:, g, kt, 0:129], start=first, stop=lastm))
                pe_multi(kb, fns, r=[('pT', i % 2)] + vkeys, w=[('ps', accb)])
                if c_['last']:
                    h = c_['h']
                    gs_ = gsc[c_['ji']]
                    gk = ('gsc', c_['ji'])
                    col = h_col(c_['hh'], c_['br'])
                    kb.op('dve', lambda e: e.reciprocal(out=gs_[:], in_=bank(accb, 129)[:, 128:129]), r=[('ps', accb)], w=[gk])
                    kb.op('dve', lambda e: e.tensor_tensor(out=gs_[:], in0=gs_[:], in1=gl[:, qt, col:col + 1], op=ALU.mult),
                          r=[gk, 'gl'], w=[gk])
                    kb.op('dve', lambda e: e.scalar_tensor_tensor(out=oa[:, h, :], in0=bank(accb, 128), scalar=gs_[:, 0:1],
                                                                  in1=oa[:, h, :], op0=ALU.mult, op1=ALU.add),
                          r=[('ps', accb), gk, ('oacc', oi, h)], w=[('oacc', oi, h)])

            nchk = len(chunks)
            for i, c_ in enumerate(chunks):
                c_['idx'] = cnt['c'] + i
            per = (len(nxt) + nchk - 1) // max(nchk, 1)
            for i in range(nchk + 2):
                if i < nchk:
                    stage1(chunks[i])
                if 0 <= i - 1 < nchk:
                    stage2(chunks[i - 1])
                if 0 <= i - 2 < nchk:
                    stage3(chunks[i - 2])
                for _ in range(per):
                    if nxt:
                        nxt.pop(0)()
            while nxt:
                nxt.pop(0)()
            cnt['c'] += nchk
            kb.dma('sp', o_o[qt * 128:(qt + 1) * 128, g * 512:(g + 1) * 512], oa[:].rearrange("p h d -> p (h d)"),
                   r=oakeys, w=[('o_o', g, qt)])
    if fused:
        kb.barrier()
        return nc
    kb.finish('sp')
    return nc


def h_col(hh, br):
    return hh * 3 + br


def l3_consts():
    i = np.arange(128)[:, None]
    j = np.arange(128)[None, :]
    tri = (j <= i).astype(np.float32)
    tri2 = (j > i).astype(np.float32)
    n = np.arange(128)
    cm = np.zeros((128, 16, 128), np.float32)
    fbv = np.zeros((128, 16, 32), np.float32)
    jb = np.arange(32)
    for qt in range(16):
        t = qt * 128 + np.arange(128)
        cm[:, qt, :] = ((16 * n[None, :] + 31 <= t[:, None]) & (n[None, :] < 127)).astype(np.float32)
        cur = (t // 64)[:, None]
        forced = (jb[None] == 0) | (jb[None] == cur) | (jb[None] == cur - 1)
        valid = jb[None] * 64 <= t[:, None]
        fbv[:, qt, :] = np.where(valid, np.where(forced, 1000.0, 0.0), -1e30)
    ovl = np.zeros((128, 32), np.float32)
    for nn in range(127):
        for jj in range(32):
            if 16 * nn < 64 * jj + 64 and 16 * nn + 31 >= 64 * jj:
                ovl[nn, jj] = 1.0
    d = np.arange(128)
    inv = np.where(d < 32, 500000.0 ** (-(2.0 * (d % 16)) / 32.0), 0.0).astype(np.float32)[:, None]
    sgn = np.where(d < 16, -1.0, np.where(d < 32, 1.0, 0.0)).astype(np.float32)[:, None]
    return dict(tri=tri, tri2=tri2, cm=cm, fbv=fbv, ovl=ovl, inv=inv, sgn=sgn, ident=np.eye(128, dtype=np.float32))


def swap_rot(xT):
    y = xT.copy()
    y[..., 0:16, :] = xT[..., 16:32, :]
    y[..., 16:32, :] = xT[..., 0:16, :]
    return y


def prep_L3(zT_b, pos_b, half, W, consts):
    c = np.ascontiguousarray
    gs = [2 * half, 2 * half + 1]
    qT = c(zT_b[half * 1024:(half + 1) * 1024].reshape(8, 128, S))

    def grp(base):
        return c(np.stack([zT_b[base + g * 128:base + (g + 1) * 128] for g in gs]))
    kc, vc, ks, vs_, kw, vw_ = [grp(2048 + i * 512) for i in range(6)]
    gl = c(zT_b[5120 + half * 24:5120 + (half + 1) * 24].T)
    m = dict(qT=qT, qPT=swap_rot(qT), ksT=ks, ksPT=swap_rot(ks), kwT=kw, kwPT=swap_rot(kw),
             kcT=c(kc.transpose(1, 0, 2)), vcT=c(vc.transpose(1, 0, 2)),
             vs=c(vs_.transpose(0, 2, 1)), vw=c(vw_.transpose(0, 2, 1)), gl=gl,
             pos=c(pos_b.reshape(1, S).astype(np.int32)))
    m.update(W)
    m.update(consts)
    return m


def prep_L3_weights(pe_k, w1_k, w2_k, pe_v, w1_v, w2_v):
    c = np.ascontiguousarray
    return dict(pekT=c(pe_k.T), w1k=c(w1_k), w2k=c(w2_k), pevT=c(pe_v.T), w1v=c(w1_v), w2v=c(w2_v))


_PROGS = {}


def _prog(name, fn):
    if name not in _PROGS:
        _PROGS[name] = fn()
    return _PROGS[name]


def _lay(g, n=16):
    return np.ascontiguousarray(np.asarray(g, np.float32).reshape(n, 128).T)


def kernel_unfused(**inp):
    c = np.ascontiguousarray
    f32 = lambda a: np.asarray(a, dtype=np.float32)
    x = f32(inp['x'])
    pos = np.asarray(inp['positions'])
    T = 1024
    cores = list(range(NCORES))
    tok = lambda ci: (ci // 2, slice((ci % 2) * T, (ci % 2 + 1) * T))
    tri_st = np.triu(np.ones((128, 128), np.float32))
    common = dict(g1=_lay(inp['l0_ffn1_norm']), g2=_lay(inp['l0_mix_norm']), wgd=f32(inp['l0_ffn1_w_gate']),
                  wud=f32(inp['l0_ffn1_w_up']), wdd=f32(inp['l0_ffn1_w_down']), w_in=f32(inp['l0_w_in']),
                  lng=c(f32(inp['l0_gmlp_ln_g']).reshape(1, 1024)), lnb=c(f32(inp['l0_gmlp_ln_b']).reshape(1, 1024)),
                  wsT=c(f32(inp['l0_gmlp_ws']).transpose(2, 0, 1)), tri=tri_st,
                  bs=c(f32(inp['l0_gmlp_bs']).reshape(1, 1024)))
    maps = []
    for ci in cores:
        b, sl = tok(ci)
        m = dict(common)
        m['xTd'] = c(x[b, sl].T)
        maps.append(m)
    r1 = run_bass_kernel_spmd(_prog('L1', build_L1), maps, core_ids=cores).results
    common = dict(conv_w=c(f32(inp['l0_conv_w'])[:, 0, :].reshape(31, 8, 128).transpose(2, 1, 0)),
                  conv_b=_lay(inp['l0_conv_b'], 8), cln_g=_lay(inp['l0_conv_ln_g'], 8), cln_b=_lay(inp['l0_conv_ln_b'], 8),
                  w_out=f32(inp['l0_w_out']),
                  gA=_lay(inp['l0_ffn2_norm']), wgA=f32(inp['l0_ffn2_w_gate']), wuA=f32(inp['l0_ffn2_w_up']),
                  wdA=f32(inp['l0_ffn2_w_down']),
                  gB=_lay(inp['l1_ffn1_norm']), wgB=f32(inp['l1_ffn1_w_gate']), wuB=f32(inp['l1_ffn1_w_up']),
                  wdB=f32(inp['l1_ffn1_w_down']),
                  gM=_lay(inp['l1_mix_norm']), w_in1=f32(inp['l1_w_in']))
    maps = []
    for ci in cores:
        m = dict(common)
        aT = r1[ci]['aT']
        halo = np.zeros((1024, 32), np.float32)
        if ci % 2 == 1:
            halo = r1[ci - 1]['aT'][:, T - 32:]
        m['aTh'] = c(np.concatenate([halo, aT], axis=1))
        m['boTd'] = r1[ci]['boT']
        m['x1Td'] = r1[ci]['x1T']
        maps.append(m)
    r2 = run_bass_kernel_spmd(_prog('L2', build_L2), maps, core_ids=cores).results
    W = prep_L3_weights(*[f32(inp[k]) for k in ('l1_cmp_pe_k', 'l1_cmp_w1_k', 'l1_cmp_w2_k',
                                                'l1_cmp_pe_v', 'l1_cmp_w1_v', 'l1_cmp_w2_v')])
    consts = l3_consts()
    maps = []
    for ci in cores:
        b, half = ci // 2, ci % 2
        zT_b = np.concatenate([r2[2 * b]['zT'], r2[2 * b + 1]['zT']], axis=1)
        maps.append(prep_L3(zT_b, pos[b], half, W, consts))
    r3 = run_bass_kernel_spmd(_prog('L3', build_L3), maps, core_ids=cores).results
    common = dict(w_out1=f32(inp['l1_w_out']), gA=_lay(inp['l1_ffn2_norm']), wgA=f32(inp['l1_ffn2_w_gate']),
                  wuA=f32(inp['l1_ffn2_w_up']), wdA=f32(inp['l1_ffn2_w_down']), gF=_lay(inp['final_norm']))
    maps = []
    for ci in cores:
        b, sl = tok(ci)
        o_b = np.concatenate([r3[2 * b]['o'], r3[2 * b + 1]['o']], axis=1)
        m = dict(common)
        m['oTd'] = c(o_b[sl].T)
        m['x4Td'] = r2[ci]['x4T']
        maps.append(m)
    r4 = run_bass_kernel_spmd(_prog('L4', build_L4), maps, core_ids=cores).results
    out = np.zeros((4, 2048, 2048), np.float32)
    for ci in cores:
        b, sl = tok(ci)
        out[b, sl] = r4[ci]['yT'].T
    return out


from contextlib import ExitStack

W_NAMES = [('l0_ffn1', 'f1'), ('l0_ffn2', 'f2'), ('l1_ffn1', 'f3'), ('l1_ffn2', 'f4')]


def build_fused(dff=DFF, nz=5168):
    nc = bass.Bass("TRN2", target_bir_lowering=False)
    T = 1024
    ext = lambda name, shape, dtype=F32: nc.dram_tensor(name, shape, dtype, kind="ExternalInput").ap()
    scr = lambda name, shape: nc.dram_tensor(name, shape, F32, kind="Internal").ap()
    I = {}
    I['xT'] = ext('xT', [2, D, T])
    for _, s in W_NAMES:
        I[s + '_g'] = ext(s + '_g', [128, 16])
        I[s + '_wg'] = ext(s + '_wg', [D, dff])
        I[s + '_wu'] = ext(s + '_wu', [D, dff])
        I[s + '_wd'] = ext(s + '_wd', [dff, D])
    for name, shape in [('g_mix0', [128, 16]), ('w_in0', [D, 4096]), ('lng', [1, 1024]), ('lnb', [1, 1024]),
                        ('wsT', [128, 8, 128]), ('tri_st', [128, 128]), ('bs', [1, 1024]),
                        ('conv_w', [128, 8, 31]), ('conv_b', [128, 8]), ('cln_g', [128, 8]), ('cln_b', [128, 8]),
                        ('w_out0', [D, D]), ('g_mix1', [128, 16]), ('w_in1', [D, nz]),
                        ('inv', [128, 1]), ('sgn', [128, 1]), ('pekT', [128, 32]), ('w1k', [4096, 256]),
                        ('w2k', [256, 128]), ('pevT', [128, 32]), ('w1v', [4096, 256]), ('w2v', [256, 128]),
                        ('ovl', [128, 32]), ('cm', [128, 16, 128]), ('fbv', [128, 16, 32]), ('tri', [128, 128]),
                        ('tri2', [128, 128]), ('ident', [128, 128]), ('w_out1', [D, D]), ('g_fin', [128, 16])]:
        I[name] = ext(name, shape)
    I['pos'] = ext('pos', [1, S], I32)
    yT = nc.dram_tensor('yT', [D, T], F32, kind="ExternalOutput").ap()
    I['flag'] = ext('flag', [128, 2])
    x1T = scr('x1T_s', [2, D, T]); aT = scr('aT_s', [2, 1024, T]); boT = scr('boT_s', [2, 1024, T])
    x4T = scr('x4T_s', [2, D, T]); zT = scr('zT_s', [nz, 2 * T]); o_s = scr('o_s', [2 * T, 2048])
    kb = KB(nc)
    ps = nc.alloc_psum_tensor('ps', [128, 8 * 512], F32)
    with ExitStack() as es_core:
        _ES[0] = es_core
        core = Core(nc, T, kb, ps)
        for h in range(2):
            with ExitStack() as es:
                _ES[0] = es
                build_L1(T, dff, nc, dict(xTd=I['xT'][h], g1=I['f1_g'], g2=I['g_mix0'], wgd=I['f1_wg'], wud=I['f1_wu'],
                                          wdd=I['f1_wd'], w_in=I['w_in0'], lng=I['lng'], lnb=I['lnb'], wsT=I['wsT'],
                                          tri=I['tri_st'], bs=I['bs'], x1T=x1T[h], aT=aT[h], boT=boT[h]), core)
            with ExitStack() as es:
                _ES[0] = es
                build_L2(T, dff, nz, nc, dict(x1Td=x1T[h], aT_cur=aT[h], aT_prev=(aT[0] if h == 1 else None),
                                              boTd=boT[h], conv_w=I['conv_w'], conv_b=I['conv_b'], cln_g=I['cln_g'],
                                              cln_b=I['cln_b'], w_out=I['w_out0'],
                                              gA=I['f2_g'], wgA=I['f2_wg'], wuA=I['f2_wu'], wdA=I['f2_wd'],
                                              gB=I['f3_g'], wgB=I['f3_wg'], wuB=I['f3_wu'], wdB=I['f3_wd'],
                                              gM=I['g_mix1'], w_in1=I['w_in1'], x4T=x4T[h],
                                              zT=zT[:, h * T:(h + 1) * T]), core)
            _ES[0] = es_core
    for gp in range(2):
        with ExitStack() as es:
            _ES[0] = es
            ov = {k: I[k] for k in ('inv', 'sgn', 'pekT', 'w1k', 'w2k', 'pevT', 'w1v', 'w2v', 'ovl', 'cm', 'fbv',
                                    'tri', 'tri2', 'ident', 'pos')}
            ov['zT'] = zT
            ov['o'] = o_s[:, gp * 1024:(gp + 1) * 1024]
            build_L3(nc, ov, kb, ps, gp)
    with ExitStack() as es_core:
        _ES[0] = es_core
        core = Core(nc, T, kb, ps)
        with ExitStack() as es:
            _ES[0] = es
            build_L4(T, dff, nc, dict(x4Td=x4T[0], x4T_1=x4T[1], o_tok=o_s[0:T, :], o_tok1=o_s[T:2 * T, :],
                                      flag=I['flag'], ident=I['ident'], w_out1=I['w_out1'], gA=I['f4_g'],
                                      wgA=I['f4_wg'], wuA=I['f4_wu'], wdA=I['f4_wd'], gF=I['g_fin'], yT=yT), core)
        _ES[0] = es_core
    _ES[0] = None
    kb.finish('sp')
    return nc


def fused_inputs(inp, b, r=0):
    c = np.ascontiguousarray
    f32 = lambda a: np.asarray(a, dtype=np.float32)
    x = f32(inp['x'])
    m = dict(xT=c(np.stack([x[b, 0:1024].T, x[b, 1024:2048].T])))
    for pre, s in W_NAMES:
        m[s + '_g'] = _lay(inp[pre + '_norm'])
        m[s + '_wg'] = f32(inp[pre + '_w_gate'])
        m[s + '_wu'] = f32(inp[pre + '_w_up'])
        m[s + '_wd'] = f32(inp[pre + '_w_down'])
    m.update(g_mix0=_lay(inp['l0_mix_norm']), w_in0=f32(inp['l0_w_in']),
             lng=c(f32(inp['l0_gmlp_ln_g']).reshape(1, 1024)), lnb=c(f32(inp['l0_gmlp_ln_b']).reshape(1, 1024)),
             wsT=c(f32(inp['l0_gmlp_ws']).transpose(2, 0, 1)), tri_st=np.triu(np.ones((128, 128), np.float32)),
             bs=c(f32(inp['l0_gmlp_bs']).reshape(1, 1024)),
             conv_w=c(f32(inp['l0_conv_w'])[:, 0, :].reshape(31, 8, 128).transpose(2, 1, 0)),
             conv_b=_lay(inp['l0_conv_b'], 8), cln_g=_lay(inp['l0_conv_ln_g'], 8), cln_b=_lay(inp['l0_conv_ln_b'], 8),
             w_out0=f32(inp['l0_w_out']), g_mix1=_lay(inp['l1_mix_norm']), w_in1=f32(inp['l1_w_in']),
             w_out1=f32(inp['l1_w_out']), g_fin=_lay(inp['final_norm']),
             pos=c(np.asarray(inp['positions'])[b].reshape(1, S).astype(np.int32)))
    m.update(prep_L3_weights(*[f32(inp[k]) for k in ('l1_cmp_pe_k', 'l1_cmp_w1_k', 'l1_cmp_w2_k',
                                                     'l1_cmp_pe_v', 'l1_cmp_w1_v', 'l1_cmp_w2_v')]))
    m.update(l3_consts())
    fl = np.zeros((128, 2), np.float32)
    fl[:, r] = 1.0
    m['flag'] = fl
    return m


def kernel(**inp):
    nc = _prog('fused', build_fused)
    maps = [fused_inputs(inp, ci // 2, ci % 2) for ci in range(NCORES)]
    res = run_bass_kernel_spmd(nc, maps, core_ids=list(range(NCORES))).results
    out = np.zeros((4, 2048, 2048), np.float32)
    for ci in range(NCORES):
        b, h = ci // 2, ci % 2
        out[b, h * 1024:(h + 1) * 1024] = res[ci]['yT'].T
    return out
```

```python
import os
import numpy as np
import concourse.bass as bass
import concourse.mybir as mybir
from concourse.bass_utils import run_bass_kernel_spmd

F32 = mybir.dt.float32
BF16 = mybir.dt.bfloat16
I32 = mybir.dt.int32
AF = mybir.ActivationFunctionType
ALU = mybir.AluOpType

_ES = [None]
_UID = [0]


def SB(nc, name, shape, dtype=None):
    dtype = F32 if dtype is None else dtype
    _UID[0] += 1
    nm = '%s_%d' % (name, _UID[0])
    if _ES[0] is None:
        return nc.alloc_sbuf_tensor(nm, list(shape), dtype)
    return _ES[0].enter_context(nc.sbuf_tensor(nm, list(shape), dtype))


def mk_dt(nc, over, pre=''):
    def dt(name, shape, kind="ExternalInput", dtype=F32):
        if over is not None and name in over:
            return over[name]
        return nc.dram_tensor(pre + name, shape, dtype, kind=kind).ap()
    return dt


D = 2048
DFF = 5632
NCORES = 8
EPS = 1e-6


class KB:
    NS = 6

    def __init__(self, nc):
        self.nc = nc
        self.eng = dict(pe=nc.tensor, dve=nc.vector, act=nc.scalar, pool=nc.gpsimd, sp=nc.sync)
        self.sem = {}
        self.cnt = {}
        for e in ('pe', 'dve', 'act', 'pool'):
            self.sem[e] = nc.alloc_semaphore('c_' + e)
            self.cnt[e] = 0
        self.nsq = {'sp': 6, 'pool': 8, 'act': 2}
        self.dsem = {q: [nc.alloc_semaphore('d_%s%d' % (q, i)) for i in range(self.nsq[q])]
                     for q in ('sp', 'pool', 'act')}
        self.dcnt = {q: 0 for q in self.dsem}
        self.seen = {e: {} for e in self.eng}
        self.st = {}
        self.semobj = {}
        for s in list(self.sem.values()) + [x for v in self.dsem.values() for x in v]:
            self.semobj[s.num] = s
        self.nwait = 0

    def _deps(self, r, w):
        deps = {}

        def add(tok):
            if tok is None:
                return
            s, v = tok
            if deps.get(s, 0) < v:
                deps[s] = v
        for k in r:
            st = self.st.get(k)
            if st:
                add(st[0])
        for k in w:
            st = self.st.get(k)
            if st:
                add(st[0])
                for s, v in st[1].items():
                    add((s, v))
        return deps

    def _emit_waits(self, e, deps, skip_sem=None):
        eng = self.eng[e]
        seen = self.seen[e]
        for s, v in deps.items():
            if skip_sem is not None and s == skip_sem:
                continue
            if seen.get(s, 0) >= v:
                continue
            eng.wait_ge(self.semobj[s], v)
            seen[s] = v
            self.nwait += 1

    def _commit(self, tok, r, w):
        for k in r:
            st = self.st.setdefault(k, [None, {}])
            if st[1].get(tok[0], 0) < tok[1]:
                st[1][tok[0]] = tok[1]
        for k in w:
            self.st[k] = [tok, {}]

    def op(self, e, fn, r=(), w=()):
        deps = self._deps(r, w)
        self._emit_waits(e, deps, skip_sem=(self.sem['pe'].num if e == 'pe' else None))
        inst = fn(self.eng[e])
        self.cnt[e] += 1
        inst.then_inc(self.sem[e], 1)
        tok = (self.sem[e].num, self.cnt[e])
        self._commit(tok, r, w)
        return tok

    def mm(self, out, pairs, r=(), w=()):
        deps = self._deps(r, w)
        self._emit_waits('pe', deps, skip_sem=self.sem['pe'].num)
        n = len(pairs)
        inst = None
        for i, (lhsT, rhs) in enumerate(pairs):
            inst = self.nc.tensor.matmul(out, lhsT, rhs, start=(i == 0), stop=(i == n - 1))
        self.cnt['pe'] += 1
        inst.then_inc(self.sem['pe'], 1)
        tok = (self.sem['pe'].num, self.cnt['pe'])
        self._commit(tok, r, w)
        return tok

    def mm1(self, out, lhsT, rhs, start, stop, r=(), w=()):
        return self.op('pe', lambda e: e.matmul(out, lhsT, rhs, start=start, stop=stop), r=r, w=w)

    def dma(self, q, out, in_, r=(), w=(), **kw):
        deps = self._deps(r, w)
        i = self.dcnt[q]
        self.dcnt[q] += 1
        s = self.dsem[q][i % self.nsq[q]]
        rnd = i // self.nsq[q]
        if rnd > 0:
            deps[s.num] = max(deps.get(s.num, 0), 16 * rnd)
        self._emit_waits(q, deps)
        inst = self.eng[q].dma_start(out=out, in_=in_, **kw)
        inst.then_inc(s, 16)
        tok = (s.num, 16 * (rnd + 1))
        self._commit(tok, r, w)
        return tok

    def barrier(self):
        deps = {}
        for e, sm in self.sem.items():
            if self.cnt[e] > 0:
                deps[sm.num] = self.cnt[e]
        for q, sl in self.dsem.items():
            n = self.dcnt[q]
            for i, sm in enumerate(sl):
                k = (n - 1 - i) // self.nsq[q] + 1 if n > i else 0
                if k > 0:
                    deps[sm.num] = 16 * k
        for e in self.eng:
            self._emit_waits(e, dict(deps))

    def finish(self, e='sp'):
        deps = {}
        for k, st in self.st.items():
            if st[0] is not None:
                s, v = st[0]
                if deps.get(s, 0) < v:
                    deps[s] = v
        self._emit_waits(e, deps)


class Core:
    def __init__(self, nc, T, kb=None, ps=None):
        self.nc = nc
        self.kb = KB(nc) if kb is None else kb
        self.T = T
        self.TB = [(i, min(512, T - i)) for i in range(0, T, 512)]
        self.xT = SB(nc, 'xT', [128, 16, T], F32)
        self.hT = SB(nc, 'hT', [128, 16, T], BF16)
        self.wg = [SB(nc, 'wg%d' % i, [128, 16, 256], BF16) for i in range(2)]
        self.wu = [SB(nc, 'wu%d' % i, [128, 16, 256], BF16) for i in range(2)]
        self.wd = [SB(nc, 'wd%d' % i, [128, 2, 2048], BF16) for i in range(4)]
        self.actT = [SB(nc, 'actT%d' % i, [128, 4, T], BF16) for i in range(2)]
        self.tmp = [SB(nc, 'tmp%d' % i, [128, 512], F32) for i in range(2)]
        self.sq = [SB(nc, 'sq%d' % i, [128, 512], BF16) for i in range(2)]
        self.rstd = SB(nc, 'rstd', [128, 512], F32)
        self.ones = SB(nc, 'ones', [128, 128], BF16)
        self.ps = nc.alloc_psum_tensor('ps', [128, 8 * 512], F32) if ps is None else ps
        self.ntmp = 0
        self.nsq = 0
        self.nwt = 0
        self.nwd = 0
        self.nact = 0
        self.kb.op('dve', lambda e: e.memset(self.ones[:], 1.0), w=['ones'])
        self.half_sb = SB(nc, 'half', [128, 1], F32)
        self.kb.op('dve', lambda e: e.memset(self.half_sb[:], 0.5), w=['half'])
        self.eps_sb = SB(nc, 'eps', [128, 1], F32)
        self.kb.op('dve', lambda e: e.memset(self.eps_sb[:], EPS), w=['eps'])

    def bank(self, b, n=512):
        return self.ps[:, b * 512:b * 512 + n]


def rmsnorm_T(c, g_sb, gkey, out=None, okey='hT', src=None, skey='xT', bank=6):
    kb = c.kb
    out = c.hT if out is None else out
    src = c.xT if src is None else src
    for (t0, tn) in c.TB:
        for kc in range(16):
            i = c.nsq % 2
            c.nsq += 1
            sq = c.sq[i]
            kb.op('act', lambda e, kc=kc, sq=sq: e.activation(out=sq[:, :tn], in_=src[:, kc, t0:t0 + tn],
                                                             func=AF.Square),
                  r=[(skey, kc, t0)], w=[('sq', i)])
            kb.mm1(c.bank(bank, tn), c.ones[:], sq[:, :tn], kc == 0, kc == 15,
                   r=['ones', ('sq', i)], w=([('ps', bank)] if kc in (0, 15) else []))
        kb.op('act', lambda e: e.activation(out=c.rstd[:, :tn], in_=c.bank(bank, tn), func=AF.Sqrt,
                                            scale=1.0 / D, bias=c.eps_sb[:, 0:1]),
              r=[('ps', bank), 'eps'], w=['rstd'])
        kb.op('dve', lambda e: e.reciprocal(out=c.rstd[:, :tn], in_=c.rstd[:, :tn]), r=['rstd'], w=['rstd'])
        for kc in range(16):
            kb.op('dve', lambda e, kc=kc: e.scalar_tensor_tensor(
                out=out[:, kc, t0:t0 + tn], in0=src[:, kc, t0:t0 + tn], scalar=g_sb[:, kc:kc + 1],
                in1=c.rstd[:, :tn], op0=ALU.mult, op1=ALU.mult),
                r=[(skey, kc, t0), 'rstd', gkey], w=[(okey, kc, t0)])


def ffn_T(c, wg_d, wu_d, wd_d, dff=DFF):
    kb = c.kb
    nc = c.nc
    wg_v = wg_d.rearrange("(kc p) f -> p kc f", p=128)
    wu_v = wu_d.rearrange("(kc p) f -> p kc f", p=128)
    NFB = dff // 512
    gbank = 0
    dbank = 0
    for fb in range(NFB):
        ab = c.nact % 2
        c.nact += 1
        actT = c.actT[ab]
        wd_tiles = []
        for half in range(2):
            wt = fb * 2 + half
            wb = c.nwt % 2
            c.nwt += 1
            kb.dma('pool', c.wg[wb][:], wg_v[:, :, wt * 256:(wt + 1) * 256], w=[('wg', wb)])
            kb.dma('pool', c.wu[wb][:], wu_v[:, :, wt * 256:(wt + 1) * 256], w=[('wu', wb)])
            db = c.nwd % 4
            c.nwd += 1
            kb.dma('pool', c.wd[db][:],
                   wd_d[wt * 256:(wt + 1) * 256, :].rearrange("(fc p) d -> p fc d", p=128), w=[('wd', db)])
            wd_tiles.append(db)
            for j in range(2):
                fcl = half * 2 + j
                for (t0, tn) in c.TB:
                    bg = gbank % 4
                    bu = (gbank + 1) % 4
                    gbank += 2
                    kb.mm(c.bank(bg, tn), [(c.wg[wb][:, kc, j * 128:(j + 1) * 128], c.hT[:, kc, t0:t0 + tn])
                                           for kc in range(16)],
                          r=[('wg', wb)] + [('hT', kc, t0) for kc in range(16)], w=[('ps', bg)])
                    kb.mm(c.bank(bu, tn), [(c.wu[wb][:, kc, j * 128:(j + 1) * 128], c.hT[:, kc, t0:t0 + tn])
                                           for kc in range(16)],
                          r=[('wu', wb)] + [('hT', kc, t0) for kc in range(16)], w=[('ps', bu)])
                    ti = c.ntmp % 2
                    c.ntmp += 1
                    tmp = c.tmp[ti]
                    kb.op('act', lambda e: e.activation(out=tmp[:, :tn], in_=c.bank(bg, tn), func=AF.Silu),
                          r=[('ps', bg)], w=[('tmp', ti)])
                    kb.op('dve', lambda e: e.tensor_tensor(out=actT[:, fcl, t0:t0 + tn], in0=c.bank(bu, tn),
                                                           in1=tmp[:, :tn], op=ALU.mult),
                          r=[('ps', bu), ('tmp', ti)], w=[('actT', ab, fcl, t0)])
        for dc in range(16):
            for (t0, tn) in c.TB:
                bd = 4 + dbank % 4
                dbank += 1
                kb.mm(c.bank(bd, tn),
                      [(c.wd[wd_tiles[fcl // 2]][:, fcl % 2, dc * 128:(dc + 1) * 128], actT[:, fcl, t0:t0 + tn])
                       for fcl in range(4)],
                      r=[('wd', wd_tiles[0]), ('wd', wd_tiles[1])] + [('actT', ab, fcl, t0) for fcl in range(4)],
                      w=[('ps', bd)])
                if True:
                    kb.op('dve', lambda e: e.scalar_tensor_tensor(
                        out=c.xT[:, dc, t0:t0 + tn], in0=c.bank(bd, tn), scalar=c.half_sb[:, 0:1], in1=c.xT[:, dc, t0:t0 + tn],
                        op0=ALU.mult, op1=ALU.add),
                        r=[('ps', bd), ('xT', dc, t0), 'half'], w=[('xT', dc, t0)])


def load_small(c, name, d_ap, shape, dtype=F32, q='sp'):
    t = SB(c.nc, name, list(shape), dtype)
    c.kb.dma(q, t[:], d_ap, w=[name])
    return t


def load_xT(c, x_d):
    v = x_d.rearrange("(kc p) t -> p kc t", p=128)
    for kc in range(0, 16, 4):
        c.kb.dma('sp', c.xT[:, kc:kc + 4, :], v[:, kc:kc + 4, :],
                 w=[('xT', k, t0) for k in range(kc, kc + 4) for (t0, _) in c.TB])


def store_T(c, out_d, src, skey):
    v = out_d.rearrange("(kc p) t -> p kc t", p=128)
    for kc in range(0, 16, 4):
        c.kb.dma('sp', v[:, kc:kc + 4, :], src[:, kc:kc + 4, :],
                 r=[(skey, k, t0) for k in range(kc, kc + 4) for (t0, _) in c.TB], w=[('out', kc)])


def build_ffn_test(T, dff=DFF, stage=2):
    nc = bass.Bass("TRN2", target_bir_lowering=False)
    x_d = nc.dram_tensor("xTd", [D, T], F32, kind="ExternalInput").ap()
    g1 = nc.dram_tensor("g1", [128, 16], F32, kind="ExternalInput").ap()
    g2 = nc.dram_tensor("g2", [128, 16], F32, kind="ExternalInput").ap()
    wg = nc.dram_tensor("wgd", [D, dff], F32, kind="ExternalInput").ap()
    wu = nc.dram_tensor("wud", [D, dff], F32, kind="ExternalInput").ap()
    wd = nc.dram_tensor("wdd", [dff, D], F32, kind="ExternalInput").ap()
    y_d = nc.dram_tensor("yTd", [D, T], BF16, kind="ExternalOutput").ap()
    c = Core(nc, T)
    g1s = load_small(c, 'g1s', g1, [128, 16])
    g2s = load_small(c, 'g2s', g2, [128, 16])
    load_xT(c, x_d)
    rmsnorm_T(c, g1s, 'g1s')
    if stage >= 1:
        ffn_T(c, wg, wu, wd, dff)
    if stage >= 2:
        rmsnorm_T(c, g2s, 'g2s')
    store_T(c, y_d, c.hT, 'hT')
    c.kb.finish('sp')
    return nc


GELU_C = 0.7978845608028654
RING = [('wg', 0), ('wu', 0), ('wg', 1), ('wu', 1)]


def load_wt(c, w_v, col0, ncols, nkc=16):
    i = getattr(c, 'nring', 0)
    c.nring = i + 1
    name, b = RING[i % 4]
    buf = c.wg[b] if name == 'wg' else c.wu[b]
    c.kb.dma('pool', buf[:, :nkc, :ncols], w_v[:, :, col0:col0 + ncols], w=[(name, b)])
    return buf, (name, b)


def extra_tiles(c):
    nc = c.nc
    c.stg = [SB(nc, 'stg%d' % i, [128, 512], F32) for i in range(2)]
    c.ga = c.tmp[0]
    c.gb = c.tmp[1]
    c.nstg = 0
    c.c1 = SB(nc, 'c1', [128, 1], F32)
    c.kb.op('dve', lambda e: e.memset(c.c1[:], 1.0), w=['c1'])
    c.cg = SB(nc, 'cg', [128, 1], F32)
    c.kb.op('dve', lambda e: e.memset(c.cg[:], 0.044715), w=['cg'])


def gelu_from(c, src, skey, out, okey, P, n):
    kb = c.kb
    kb.op('act', lambda e: e.activation(out=c.ga[:P, :n], in_=src, func=AF.Square), r=[skey], w=[('tmp', 0)])
    kb.op('dve', lambda e: e.tensor_scalar(out=c.ga[:P, :n], in0=c.ga[:P, :n], scalar1=c.cg[:P, 0:1],
                                           scalar2=c.c1[:P, 0:1], op0=ALU.mult, op1=ALU.add),
          r=[('tmp', 0), 'cg', 'c1'], w=[('tmp', 0)])
    kb.op('dve', lambda e: e.tensor_tensor(out=c.ga[:P, :n], in0=src, in1=c.ga[:P, :n], op=ALU.mult),
          r=[skey, ('tmp', 0)], w=[('tmp', 0)])
    kb.op('act', lambda e: e.activation(out=c.gb[:P, :n], in_=c.ga[:P, :n], func=AF.Sigmoid, scale=2.0 * GELU_C),
          r=[('tmp', 0)], w=[('tmp', 1)])
    kb.op('dve', lambda e: e.tensor_tensor(out=out, in0=src, in1=c.gb[:P, :n], op=ALU.mult),
          r=[skey, ('tmp', 1)], w=[okey])


def build_L1(T=1024, dff=DFF, nc=None, over=None, core=None):
    fused = nc is not None
    nc = bass.Bass("TRN2", target_bir_lowering=False) if nc is None else nc
    dt = mk_dt(nc, over)
    x_d = dt("xTd", [D, T])
    g1 = dt("g1", [128, 16])
    g2 = dt("g2", [128, 16])
    wg = dt("wgd", [D, dff])
    wu = dt("wud", [D, dff])
    wd = dt("wdd", [dff, D])
    w_in = dt("w_in", [D, 4096])
    lng = dt("lng", [1, 1024])
    lnb = dt("lnb", [1, 1024])
    wsT_d = dt("wsT", [128, 8, 128])
    tri_d = dt("tri", [128, 128])
    bs_d = dt("bs", [1, 1024])
    x1_o = dt("x1T", [D, T], "ExternalOutput")
    a_o = dt("aT", [1024, T], "ExternalOutput")
    bo_o = dt("boT", [1024, T], "ExternalOutput")
    c = Core(nc, T) if core is None else core
    kb = c.kb
    extra_tiles(c)
    g1s = load_small(c, 'g1s', g1, [128, 16])
    g2s = load_small(c, 'g2s', g2, [128, 16])
    lng_s = load_small(c, 'lng_s', lng.partition_broadcast(128), [128, 1024])
    lnb_s = load_small(c, 'lnb_s', lnb.partition_broadcast(128), [128, 1024])
    bs_s = load_small(c, 'bs_s', bs_d.partition_broadcast(128), [128, 1024])
    wsT_f = load_small(c, 'wsT_f', wsT_d, [128, 8, 128], BF16, q='pool')
    tri_s = load_small(c, 'tri_s', tri_d, [128, 128], BF16, q='pool')
    wsT_m = SB(nc, 'wsT_m', [128, 8, 128], BF16)
    for g in range(8):
        kb.op('dve', lambda e, g=g: e.tensor_tensor(out=wsT_m[:, g, :], in0=wsT_f[:, g, :], in1=tri_s[:],
                                                    op=ALU.mult), r=['wsT_f', 'tri_s'], w=[('wsT_m', g)])
    load_xT(c, x_d)
    rmsnorm_T(c, g1s, 'g1s')
    ffn_T(c, wg, wu, wd, dff)
    store_T(c, x1_o, c.xT, 'xT')
    rmsnorm_T(c, g2s, 'g2s')
    w_v = w_in.rearrange("(kc p) f -> p kc f", p=128)
    hkeys = lambda t0: [('hT', kc, t0) for kc in range(16)]
    gbank = 0
    for ip in range(4):
        bv, kv = load_wt(c, w_v, ip * 256, 256)
        bg_, kg = load_wt(c, w_v, 1024 + ip * 256, 256)
        for j in range(2):
            ch = ip * 2 + j
            for (t0, tn) in c.TB:
                b0 = gbank % 4
                b1 = (gbank + 1) % 4
                gbank += 2
                kb.mm(c.bank(b0, tn), [(bv[:, kc, j * 128:(j + 1) * 128], c.hT[:, kc, t0:t0 + tn]) for kc in range(16)],
                      r=[kv] + hkeys(t0), w=[('ps', b0)])
                kb.mm(c.bank(b1, tn), [(bg_[:, kc, j * 128:(j + 1) * 128], c.hT[:, kc, t0:t0 + tn]) for kc in range(16)],
                      r=[kg] + hkeys(t0), w=[('ps', b1)])
                ti = c.ntmp % 2
                c.ntmp += 1
                si = c.nstg % 2
                c.nstg += 1
                kb.op('act', lambda e: e.activation(out=c.tmp[ti][:, :tn], in_=c.bank(b1, tn), func=AF.Sigmoid),
                      r=[('ps', b1)], w=[('tmp', ti)])
                kb.op('dve', lambda e: e.tensor_tensor(out=c.stg[si][:, :tn], in0=c.bank(b0, tn), in1=c.tmp[ti][:, :tn],
                                                       op=ALU.mult), r=[('ps', b0), ('tmp', ti)], w=[('stg', si)])
                kb.dma('sp', a_o[ch * 128:(ch + 1) * 128, t0:t0 + tn], c.stg[si][:, :tn], r=[('stg', si)],
                       w=[('a_o', ch, t0)])
    for ip in range(4):
        bu_, ku = load_wt(c, w_v, 2048 + ip * 256, 256)
        for j in range(2):
            g = ip * 2 + j
            for (t0, tn) in c.TB:
                b0 = gbank % 4
                gbank += 1
                kb.mm(c.bank(b0, tn), [(bu_[:, kc, j * 128:(j + 1) * 128], c.hT[:, kc, t0:t0 + tn]) for kc in range(16)],
                      r=[ku] + hkeys(t0), w=[('ps', b0)])
                gelu_from(c, c.bank(b0, tn), ('ps', b0), c.actT[g // 4][:, g % 4, t0:t0 + tn], ('uT', g, t0), 128, tn)
    vg = SB(nc, 'vg', [128, 256], F32)
    vln = SB(nc, 'vln', [128, 256], BF16)
    stats = SB(nc, 'stats', [128, 6], F32)
    mv = SB(nc, 'mv', [128, 2], F32)
    NT = T // 128
    for ip in range(4):
        bw, kw_ = load_wt(c, w_v, 3072 + ip * 256, 256)
        for tt in range(NT):
            t0b = (tt * 128 // 512) * 512
            b0 = gbank % 4
            gbank += 1
            kb.mm(c.bank(b0, 256), [(c.hT[:, kc, tt * 128:(tt + 1) * 128], bw[:, kc, 0:256]) for kc in range(16)],
                  r=[kw_] + hkeys(t0b), w=[('ps', b0)])
            gelu_from(c, c.bank(b0, 256), ('ps', b0), vg[:, :], 'vg', 128, 256)
            for gg in range(2):
                g = ip * 2 + gg
                sl = slice(gg * 128, (gg + 1) * 128)
                gsl = slice(g * 128, (g + 1) * 128)
                kb.op('dve', lambda e: e.bn_stats(out=stats[:], in_=vg[:, sl]), r=['vg'], w=['stats'])
                kb.op('dve', lambda e: e.bn_aggr(out=mv[:], in_=stats[:]), r=['stats'], w=['mv'])
                kb.op('act', lambda e: e.activation(out=mv[:, 1:2], in_=mv[:, 1:2], func=AF.Sqrt, bias=c.eps_sb[:, 0:1]),
                      r=['mv', 'eps'], w=['mv'])
                kb.op('dve', lambda e: e.reciprocal(out=mv[:, 1:2], in_=mv[:, 1:2]), r=['mv'], w=['mv'])
                kb.op('dve', lambda e: e.tensor_scalar(out=vg[:, sl], in0=vg[:, sl], scalar1=mv[:, 0:1],
                                                       scalar2=mv[:, 1:2], op0=ALU.subtract, op1=ALU.mult),
                      r=['vg', 'mv'], w=['vg'])
                kb.op('dve', lambda e: e.tensor_tensor(out=vg[:, sl], in0=vg[:, sl], in1=lng_s[:, gsl], op=ALU.mult),
                      r=['vg', 'lng_s'], w=['vg'])
                kb.op('dve', lambda e: e.tensor_tensor(out=vln[:, sl], in0=vg[:, sl], in1=lnb_s[:, gsl], op=ALU.add),
                      r=['vg', 'lnb_s'], w=[('vln', gg)])
                b1 = 4 + (gbank % 2)
                gbank += 1
                kb.mm(c.bank(b1, 128), [(vln[:, sl], wsT_m[:, g, :])], r=[('vln', gg), ('wsT_m', g)], w=[('ps', b1)])
                si = c.nstg % 2
                c.nstg += 1
                kb.op('dve', lambda e: e.tensor_tensor(out=c.stg[si][:, :128], in0=c.bank(b1, 128), in1=bs_s[:, gsl],
                                                       op=ALU.add), r=[('ps', b1), 'bs_s'], w=[('stg', si)])
                kb.op('dve', lambda e: e.tensor_tensor(out=c.stg[si][:, :128], in0=c.stg[si][:, :128],
                                                       in1=c.actT[g // 4][:, g % 4, tt * 128:(tt + 1) * 128], op=ALU.mult),
                      r=[('stg', si), ('uT', g, t0b)], w=[('stg', si)])
                kb.dma('sp', bo_o[g * 128:(g + 1) * 128, tt * 128:(tt + 1) * 128], c.stg[si][:, :128], r=[('stg', si)],
                       w=[('bo_o', g, tt)])
    if fused:
        kb.barrier()
        return nc
    kb.finish('sp')
    return nc


def proj_out_T(c, w_d, ncols_total, out_d, gbank0=0):
    kb = c.kb
    w_v = w_d.rearrange("(kc p) f -> p kc f", p=128)
    gbank = gbank0
    col = 0
    while col < ncols_total:
        ncol_t = min(256, ncols_total - col)
        buf, key = load_wt(c, w_v, col, ncol_t)
        j0 = 0
        while j0 < ncol_t:
            m = min(128, ncol_t - j0)
            for (t0, tn) in c.TB:
                b0 = gbank % 4
                gbank += 1
                kb.mm(c.ps[:m, b0 * 512:b0 * 512 + tn],
                      [(buf[:, kc, j0:j0 + m], c.hT[:, kc, t0:t0 + tn]) for kc in range(16)],
                      r=[key] + [('hT', kc, t0) for kc in range(16)], w=[('ps', b0)])
                si = c.nstg % 2
                c.nstg += 1
                if si == 0:
                    kb.op('act', lambda e: e.activation(out=c.stg[si][:m, :tn], in_=c.ps[:m, b0 * 512:b0 * 512 + tn],
                                                        func=AF.Copy), r=[('ps', b0)], w=[('stg', si)])
                else:
                    kb.op('dve', lambda e: e.tensor_copy(out=c.stg[si][:m, :tn], in_=c.ps[:m, b0 * 512:b0 * 512 + tn]),
                          r=[('ps', b0)], w=[('stg', si)])
                kb.dma('sp', out_d[col + j0:col + j0 + m, t0:t0 + tn], c.stg[si][:m, :tn], r=[('stg', si)],
                       w=[('z_o', col + j0, t0)])
            j0 += m
        col += ncol_t
    return gbank


def proj_resid_T(c, w_d, gbank0=0):
    kb = c.kb
    w_v = w_d.rearrange("(kc p) f -> p kc f", p=128)
    gbank = gbank0
    for dcp in range(8):
        buf, key = load_wt(c, w_v, dcp * 256, 256)
        for j in range(2):
            dc = dcp * 2 + j
            for (t0, tn) in c.TB:
                b0 = gbank % 4
                gbank += 1
                kb.mm(c.bank(b0, tn), [(buf[:, kc, j * 128:(j + 1) * 128], c.hT[:, kc, t0:t0 + tn]) for kc in range(16)],
                      r=[key] + [('hT', kc, t0) for kc in range(16)], w=[('ps', b0)])
                kb.op('dve', lambda e: e.tensor_tensor(out=c.xT[:, dc, t0:t0 + tn], in0=c.bank(b0, tn),
                                                       in1=c.xT[:, dc, t0:t0 + tn], op=ALU.add),
                      r=[('ps', b0), ('xT', dc, t0)], w=[('xT', dc, t0)])
    return gbank


def build_L2(T=1024, dff=DFF, nz=5168, nc=None, over=None, core=None):
    fused = nc is not None
    nc = bass.Bass("TRN2", target_bir_lowering=False) if nc is None else nc
    dt = mk_dt(nc, over)
    HALO = 32
    x_d = dt("x1Td", [D, T])
    a_d = None if fused else dt("aTh", [1024, HALO + T])
    bo_d = dt("boTd", [1024, T])
    cw_d = dt("conv_w", [128, 8, 31])
    cb_d = dt("conv_b", [128, 8])
    cg_d = dt("cln_g", [128, 8])
    cbb_d = dt("cln_b", [128, 8])
    wout_d = dt("w_out", [D, D])
    gA = dt("gA", [128, 16]); wgA = dt("wgA", [D, dff]); wuA = dt("wuA", [D, dff]); wdA = dt("wdA", [dff, D])
    gB = dt("gB", [128, 16]); wgB = dt("wgB", [D, dff]); wuB = dt("wuB", [D, dff]); wdB = dt("wdB", [dff, D])
    gM = dt("gM", [128, 16])
    win_d = dt("w_in1", [D, nz])
    x4_o = dt("x4T", [D, T], "ExternalOutput")
    z_o = dt("zT", [nz, T], "ExternalOutput")
    c = Core(nc, T) if core is None else core
    kb = c.kb
    extra_tiles(c)
    cw = load_small(c, 'cw', cw_d, [128, 8, 31])
    cb = load_small(c, 'cb', cb_d, [128, 8])
    cg = load_small(c, 'cgn', cg_d, [128, 8])
    cbb = load_small(c, 'cbb', cbb_d, [128, 8])
    gAs = load_small(c, 'gAs', gA, [128, 16])
    gBs = load_small(c, 'gBs', gB, [128, 16])
    gMs = load_small(c, 'gMs', gM, [128, 16])
    kinv = SB(nc, 'kinv', [128, 1], F32)
    kb.op('dve', lambda e: e.memset(kinv[:], 1.0 / 1024), w=['kinv'])
    load_xT(c, x_d)
    kb.dma('pool', c.hT[:, 8:16, :], bo_d.rearrange("(g p) t -> p g t", p=128),
           w=[('hT', k, t0) for k in range(8, 16) for (t0, _) in c.TB])
    idbc = load_small(c, 'idbc', dt("ident", [128, 128]), [128, 128], BF16, q='pool')
    abuf = [c.wg[i][:].rearrange("p a b -> p (a b)") for i in range(2)]
    dgb = [c.wu[i][:].rearrange("p a b -> p (a b)")[:, 0:31 * 128].rearrange("p (k d) -> p k d", d=128)
           for i in range(2)]
    ybuf = [c.wd[i][:].rearrange("p a b -> p (a b)").bitcast(F32) for i in range(4)]
    actkeys = lambda i: [('actT', i, fcl, t0) for fcl in range(4) for (t0, _) in c.TB]
    mr = c.actT[0][:].rearrange("p a b -> p (a b)").bitcast(F32)
    mean = mr[:, 0:T]
    rstd = mr[:, T:2 * T]
    S1 = [6, 7]
    S2 = [4, 5]
    cbank = 0
    for ch in range(8):
        ab = ch % 2
        a_sb = abuf[ab]
        if not fused:
            kb.dma('pool', a_sb[:, 0:HALO + T], a_d[ch * 128:(ch + 1) * 128, :], w=[('wg', ab)])
        else:
            if over.get('aT_prev') is None:
                kb.op('dve', lambda e: e.memset(a_sb[:, 0:HALO], 0.0), w=[('wg', ab)])
            else:
                kb.dma('pool', a_sb[:, 0:HALO], over['aT_prev'][ch * 128:(ch + 1) * 128, T - HALO:T], w=[('wg', ab)])
            kb.dma('pool', a_sb[:, HALO:HALO + T], over['aT_cur'][ch * 128:(ch + 1) * 128, :], r=[('wg', ab)],
                   w=[('wg', ab)])
        dg = dgb[ab]
        kb.op('dve', lambda e: e.tensor_tensor(out=dg, in0=idbc[:].unsqueeze(1).broadcast_to([128, 31, 128]),
                                               in1=cw[:, ch, :].unsqueeze(2).broadcast_to([128, 31, 128]), op=ALU.mult),
              r=['idbc', 'cw'], w=[('wu', ab)])
        y = ybuf[ch // 2][:, (ch % 2) * T:(ch % 2) * T + T]
        ykey = ('wd', ch // 2)
        for (t0, tn) in c.TB:
            b0 = cbank % 4
            cbank += 1
            kb.mm(c.bank(b0, tn), [(dg[:, k, :], a_sb[:, 2 + k + t0:2 + k + t0 + tn]) for k in range(31)],
                  r=[('wg', ab), ('wu', ab)], w=[('ps', b0)])
            kb.op('dve', lambda e: e.tensor_scalar(out=y[:, t0:t0 + tn], in0=c.bank(b0, tn), scalar1=cb[:, ch:ch + 1],
                                                   scalar2=None, op0=ALU.add), r=[('ps', b0), 'cb'], w=[ykey])
        for bi, (t0, tn) in enumerate(c.TB):
            i = c.nsq % 2
            c.nsq += 1
            kb.op('act', lambda e: e.activation(out=c.sq[i][:, :tn], in_=y[:, t0:t0 + tn], func=AF.Copy),
                  r=[ykey], w=[('sq', i)])
            kb.mm1(c.bank(S1[bi], tn), c.ones[:], c.sq[i][:, :tn], ch == 0, ch == 7,
                   r=['ones', ('sq', i)], w=([('ps', S1[bi])] if ch in (0, 7) else []))
            i = c.nsq % 2
            c.nsq += 1
            kb.op('act', lambda e: e.activation(out=c.sq[i][:, :tn], in_=y[:, t0:t0 + tn], func=AF.Square),
                  r=[ykey], w=[('sq', i)])
            kb.mm1(c.bank(S2[bi], tn), c.ones[:], c.sq[i][:, :tn], ch == 0, ch == 7,
                   r=['ones', ('sq', i)], w=([('ps', S2[bi])] if ch in (0, 7) else []))
    for bi, (t0, tn) in enumerate(c.TB):
        kb.op('act', lambda e: e.activation(out=mean[:, t0:t0 + tn], in_=c.bank(S1[bi], tn), func=AF.Copy,
                                            scale=1.0 / 1024), r=[('ps', S1[bi])], w=actkeys(0))
        kb.op('dve', lambda e: e.tensor_tensor(out=c.tmp[0][:, :tn], in0=mean[:, t0:t0 + tn], in1=mean[:, t0:t0 + tn],
                                               op=ALU.mult), r=actkeys(0), w=[('tmp', 0)])
        kb.op('dve', lambda e: e.scalar_tensor_tensor(out=rstd[:, t0:t0 + tn], in0=c.bank(S2[bi], tn), scalar=kinv[:, 0:1],
                                                      in1=c.tmp[0][:, :tn], op0=ALU.mult, op1=ALU.subtract),
              r=[('ps', S2[bi]), 'kinv', ('tmp', 0)], w=actkeys(0))
        kb.op('act', lambda e: e.activation(out=rstd[:, t0:t0 + tn], in_=rstd[:, t0:t0 + tn], func=AF.Sqrt,
                                            bias=c.eps_sb[:, 0:1]), r=actkeys(0) + ['eps'], w=actkeys(0))
        kb.op('dve', lambda e: e.reciprocal(out=rstd[:, t0:t0 + tn], in_=rstd[:, t0:t0 + tn]), r=actkeys(0), w=actkeys(0))
    for ch in range(8):
        y = ybuf[ch // 2][:, (ch % 2) * T:(ch % 2) * T + T]
        ykey = ('wd', ch // 2)
        for (t0, tn) in c.TB:
            kb.op('dve', lambda e: e.tensor_tensor(out=y[:, t0:t0 + tn], in0=y[:, t0:t0 + tn], in1=mean[:, t0:t0 + tn],
                                                   op=ALU.subtract), r=[ykey] + actkeys(0), w=[ykey])
            kb.op('dve', lambda e: e.tensor_tensor(out=y[:, t0:t0 + tn], in0=y[:, t0:t0 + tn], in1=rstd[:, t0:t0 + tn],
                                                   op=ALU.mult), r=[ykey] + actkeys(0), w=[ykey])
            kb.op('act', lambda e: e.activation(out=c.hT[:, ch, t0:t0 + tn], in_=y[:, t0:t0 + tn], func=AF.Silu,
                                                scale=cg[:, ch:ch + 1], bias=cbb[:, ch:ch + 1]),
                  r=[ykey, 'cgn', 'cbb'], w=[('hT', ch, t0)])
    gb_ = proj_resid_T(c, wout_d)
    rmsnorm_T(c, gAs, 'gAs')
    ffn_T(c, wgA, wuA, wdA, dff)
    rmsnorm_T(c, gBs, 'gBs')
    ffn_T(c, wgB, wuB, wdB, dff)
    store_T(c, x4_o, c.xT, 'xT')
    rmsnorm_T(c, gMs, 'gMs')
    proj_out_T(c, win_d, nz, z_o)
    if fused:
        kb.barrier()
        return nc
    kb.finish('sp')
    return nc


def build_L4(T=1024, dff=DFF, nc=None, over=None, core=None):
    fused = nc is not None
    nc = bass.Bass("TRN2", target_bir_lowering=False) if nc is None else nc
    dt = mk_dt(nc, over)
    x_d = dt("x4Td", [D, T])
    o_d = None if fused else dt("oTd", [D, T])
    wout_d = dt("w_out1", [D, D])
    gA = dt("gA", [128, 16]); wgA = dt("wgA", [D, dff]); wuA = dt("wuA", [D, dff]); wdA = dt("wdA", [dff, D])
    gF = dt("gF", [128, 16])
    y_o = dt("yT", [D, T], "ExternalOutput")
    c = Core(nc, T) if core is None else core
    kb = c.kb
    gAs = load_small(c, 'gAs', gA, [128, 16])
    gFs = load_small(c, 'gFs', gF, [128, 16])
    load_xT(c, x_d)
    if not fused:
        ov = o_d.rearrange("(kc p) t -> p kc t", p=128)
        for k0 in range(0, 16, 8):
            kb.dma('pool', c.hT[:, k0:k0 + 8, :], ov[:, k0:k0 + 8, :],
                   w=[('hT', k, t0) for k in range(k0, k0 + 8) for (t0, _) in c.TB])
    else:
        extra_tiles(c)
        fl = load_small(c, 'flag', over['flag'], [128, 2])
        x1v = over['x4T_1'].rearrange("(kc p) t -> p kc t", p=128)
        for kc in range(16):
            for (t0, tn) in c.TB:
                si = c.nstg % 2
                c.nstg += 1
                kb.dma('sp', c.stg[si][:, :tn], x1v[:, kc, t0:t0 + tn], w=[('stg', si)])
                kb.op('dve', lambda e: e.tensor_scalar(out=c.xT[:, kc, t0:t0 + tn], in0=c.xT[:, kc, t0:t0 + tn],
                                                       scalar1=fl[:, 0:1], scalar2=None, op0=ALU.mult),
                      r=[('xT', kc, t0), 'flag'], w=[('xT', kc, t0)])
                kb.op('dve', lambda e: e.scalar_tensor_tensor(out=c.xT[:, kc, t0:t0 + tn], in0=c.stg[si][:, :tn],
                                                              scalar=fl[:, 1:2], in1=c.xT[:, kc, t0:t0 + tn],
                                                              op0=ALU.mult, op1=ALU.add),
                      r=[('stg', si), ('xT', kc, t0), 'flag'], w=[('xT', kc, t0)])
        otm = [SB(nc, 'otm%d' % i, [128, 2048], BF16) for i in range(2)]
        otn = [SB(nc, 'otn%d' % i, [128, 2048], BF16) for i in range(2)]
        idb = load_small(c, 'idb4', over['ident'], [128, 128], BF16, q='pool')
        ntr = 0
        for tt in range(T // 128):
            i = tt % 2
            t0b = (tt * 128 // 512) * 512
            kb.dma('pool', otm[i][:], over['o_tok'][tt * 128:(tt + 1) * 128, :], w=[('otm', i)])
            kb.dma('pool', otn[i][:], over['o_tok1'][tt * 128:(tt + 1) * 128, :], w=[('otn', i)])
            kb.op('dve', lambda e: e.tensor_scalar(out=otm[i][:], in0=otm[i][:], scalar1=fl[:, 0:1], scalar2=None,
                                                   op0=ALU.mult), r=[('otm', i), 'flag'], w=[('otm', i)])
            kb.op('dve', lambda e: e.scalar_tensor_tensor(out=otm[i][:], in0=otn[i][:], scalar=fl[:, 1:2], in1=otm[i][:],
                                                          op0=ALU.mult, op1=ALU.add),
                  r=[('otn', i), ('otm', i), 'flag'], w=[('otm', i)])
            for k4 in range(4):
                tb = 2 + ntr % 2
                ntr += 1
                tbk = c.ps[:, tb * 512:(tb + 1) * 512].bitcast(BF16)
                pe_multi(kb, [(lambda e, j=j: e.transpose(out=tbk[:, j * 128:(j + 1) * 128],
                                                          in_=otm[i][:, (k4 * 4 + j) * 128:(k4 * 4 + j + 1) * 128],
                                                          identity=idb[:])) for j in range(4)],
                         r=[('otm', i), 'idb4'], w=[('ps', tb)])
                kb.op('dve', lambda e: e.tensor_copy(out=c.hT[:, k4 * 4:k4 * 4 + 4, tt * 128:(tt + 1) * 128],
                                                     in_=tbk[:, 0:512].rearrange("p (a b) -> p a b", b=128)),
                      r=[('ps', tb)], w=[('hT', k, t0b) for k in range(k4 * 4, k4 * 4 + 4)])
    proj_resid_T(c, wout_d)
    rmsnorm_T(c, gAs, 'gAs')
    ffn_T(c, wgA, wuA, wdA, dff)
    rmsnorm_T(c, gFs, 'gFs', out=c.xT, okey='xT')
    store_T(c, y_o, c.xT, 'xT')
    if fused:
        kb.barrier()
        return nc
    kb.finish('sp')
    return nc


S = 2048
NQT = 16
SCALE = 128 ** -0.5
MAGIC = 12582912.0
PI = 3.141592653589793


def pe_multi(kb, fns, r=(), w=()):
    deps = kb._deps(r, w)
    kb._emit_waits('pe', deps, skip_sem=kb.sem['pe'].num)
    inst = None
    for f in fns:
        inst = f(kb.nc.tensor)
    kb.cnt['pe'] += 1
    inst.then_inc(kb.sem['pe'], 1)
    tok = (kb.sem['pe'].num, kb.cnt['pe'])
    kb._commit(tok, r, w)
    return tok


def build_L3(nc=None, over=None, kb=None, ps=None, gp=0):
    fused = nc is not None
    nc = bass.Bass("TRN2", target_bir_lowering=False) if nc is None else nc
    dt = mk_dt(nc, over)
    if not fused:
        q_d = dt("qT", [8, 128, S]); qp_d = dt("qPT", [8, 128, S])
        ks_d = dt("ksT", [2, 128, S]); ksp_d = dt("ksPT", [2, 128, S])
        kw_d = dt("kwT", [2, 128, S]); kwp_d = dt("kwPT", [2, 128, S])
        kc_d = dt("kcT", [128, 2, S]); vc_d = dt("vcT", [128, 2, S])
        vs_d = dt("vs", [2, S, 128]); vw_d = dt("vw", [2, S, 128])
        gl_d = dt("gl", [S, 24])
    else:
        zT = over['zT']
        rows = lambda base, i: zT[base + i * 128:base + (i + 1) * 128, :]
        q_d = [rows(0, gp * 8 + hh) for hh in range(8)]
        qp_d = [None] * 8
        ks_d = [rows(3072, 2 * gp + g) for g in range(2)]; ksp_d = [None] * 2
        kw_d = [rows(4096, 2 * gp + g) for g in range(2)]; kwp_d = [None] * 2
        kc_d = zT[2048 + 2 * gp * 128:2048 + (2 * gp + 2) * 128, :].rearrange("(g p) s -> p g s", p=128)
        vc_d = zT[2560 + 2 * gp * 128:2560 + (2 * gp + 2) * 128, :].rearrange("(g p) s -> p g s", p=128)
        vsT_d = [rows(3584, 2 * gp + g) for g in range(2)]
        vwT_d = [rows(4608, 2 * gp + g) for g in range(2)]
        glT_d = zT[5120 + gp * 24:5120 + (gp + 1) * 24, :]
    pos_d = dt("pos", [1, S], dtype=I32)
    inv_d = dt("inv", [128, 1]); sgn_d = dt("sgn", [128, 1])
    pek_d = dt("pekT", [128, 32]); w1k_d = dt("w1k", [4096, 256]); w2k_d = dt("w2k", [256, 128])
    pev_d = dt("pevT", [128, 32]); w1v_d = dt("w1v", [4096, 256]); w2v_d = dt("w2v", [256, 128])
    ovl_d = dt("ovl", [128, 32]); cm_d = dt("cm", [128, 16, 128]); fbv_d = dt("fbv", [128, 16, 32])
    tri_d = dt("tri", [128, 128]); tri2_d = dt("tri2", [128, 128]); id_d = dt("ident", [128, 128])
    o_o = dt("o", [S, 1024], "ExternalOutput")
    kb = KB(nc) if kb is None else kb
    A = lambda name, shape, dtype=F32: SB(nc, 's_' + name, list(shape), dtype)

    def small(name, d_ap, shape, dtype=F32, q='sp'):
        t = A(name, shape, dtype)
        kb.dma(q, t[:], d_ap, w=[name])
        return t

    def const(name, val):
        t = A(name, [128, 1])
        kb.op('dve', lambda e: e.memset(t[:], val), w=[name])
        return t
    ps = nc.alloc_psum_tensor('ps', [128, 8 * 512], F32) if ps is None else ps
    bank = lambda b, n=512, p=128: ps[:p, b * 512:b * 512 + n]
    bankbf = lambda b: ps[:, b * 512:(b + 1) * 512].bitcast(BF16)
    H = S // 2
    xs = [A('xs%d' % i, [128, H]) for i in range(2)]
    xp = [A('xp%d' % i, [128, H]) for i in range(2)]
    inv = small('inv', inv_d, [128, 1]); sgn = small('sgn', sgn_d, [128, 1])
    ovl = small('ovl', ovl_d, [128, 32]); cm = small('cm', cm_d, [128, 16, 128], BF16, 'pool'); fbv = small('fbv', fbv_d, [128, 16, 32])
    tri = small('tri', tri_d, [128, 128], BF16, 'pool'); tri2 = small('tri2', tri2_d, [128, 128], BF16, 'pool')
    idf = small('idf', id_d, [128, 128]); idb = small('idb', id_d, [128, 128], BF16, 'pool')
    if not fused:
        gl = small('gl', gl_d.rearrange("(n p) c -> p n c", p=128), [128, 16, 24])
        kb.op('act', lambda e: e.activation(out=gl[:], in_=gl[:], func=AF.Sigmoid), r=['gl'], w=['gl'])
    else:
        gl = A('gl', [128, 16, 24])
        for hf in range(2):
            kb.dma('sp', xs[hf][:24, :], glT_d[:, hf * H:(hf + 1) * H], w=[('xs', hf)])
        for kt in range(16):
            pe_multi(kb, [lambda e: e.transpose(out=bank(7, 24), in_=xs[kt // 8][:24, (kt % 8) * 128:(kt % 8 + 1) * 128],
                                                identity=idf[:24, :24])], r=[('xs', kt // 8), 'idf'], w=[('ps', 7)])
            kb.op('act', lambda e: e.activation(out=gl[:, kt, :], in_=bank(7, 24), func=AF.Sigmoid),
                  r=[('ps', 7)], w=['gl'])
    c_i2p = const('c_i2p', 1.0 / (2 * PI)); c_mag = const('c_mag', MAGIC); c_nmag = const('c_nmag', -MAGIC)
    c_n2p = const('c_n2p', -2 * PI); c_pi = const('c_pi', PI); c_npi = const('c_npi', -PI); c_hpi = const('c_hpi', PI / 2)
    c_tiny = const('c_tiny', 1e-30); c_one = const('c_one', 1.0); c_cg = const('c_cg', 0.044715)
    c_nsc = const('c_nsc', -SCALE)
    posi = A('posi', [128, S], I32)
    kb.dma('sp', posi[:], pos_d.partition_broadcast(128), w=['posi'])
    ang = A('ang', [128, S]); cosT = A('cosT', [128, S]); sinT = A('sinT', [128, S])
    kb.op('dve', lambda e: e.tensor_copy(out=ang[:], in_=posi[:]), r=['posi'], w=['ang'])
    kb.op('dve', lambda e: e.tensor_scalar(out=ang[:], in0=ang[:], scalar1=inv[:, 0:1], scalar2=None, op0=ALU.mult),
          r=['ang', 'inv'], w=['ang'])

    def sin_of(dst, dkey, shift):
        wk = posi[:].bitcast(F32)
        src = ang
        if shift is not None:
            kb.op('dve', lambda e: e.tensor_scalar(out=dst[:], in0=ang[:], scalar1=shift[:, 0:1], scalar2=None, op0=ALU.add),
                  r=['ang'], w=[dkey])
            src = dst
        kb.op('dve', lambda e: e.tensor_scalar(out=wk, in0=src[:], scalar1=c_i2p[:, 0:1], scalar2=c_mag[:, 0:1],
                                               op0=ALU.mult, op1=ALU.add), r=[dkey, 'ang', 'posi'], w=['posi'])
        kb.op('dve', lambda e: e.tensor_scalar(out=wk, in0=wk, scalar1=c_nmag[:, 0:1], scalar2=None, op0=ALU.add),
              r=['posi'], w=['posi'])
        kb.op('dve', lambda e: e.scalar_tensor_tensor(out=dst[:], in0=wk, scalar=c_n2p[:, 0:1], in1=src[:],
                                                      op0=ALU.mult, op1=ALU.add), r=['posi', 'ang', dkey], w=[dkey])
        kb.op('dve', lambda e: e.tensor_scalar(out=dst[:], in0=dst[:], scalar1=c_pi[:, 0:1], scalar2=c_npi[:, 0:1],
                                               op0=ALU.min, op1=ALU.max), r=[dkey], w=[dkey])
        kb.op('act', lambda e: e.activation(out=dst[:], in_=dst[:], func=AF.Sin), r=[dkey], w=[dkey])
    sin_of(sinT, 'sinT', None)
    kb.op('dve', lambda e: e.tensor_scalar(out=sinT[:], in0=sinT[:], scalar1=sgn[:, 0:1], scalar2=None, op0=ALU.mult),
          r=['sinT', 'sgn'], w=['sinT'])
    sin_of(cosT, 'cosT', c_hpi)
    qT = A('qTb', [128, 8, S], BF16); qr = A('qr', [128, 8, S], BF16)
    ksr = A('ksr', [128, 2, S], BF16); kwr = A('kwr', [128, 2, S], BF16)
    kcT = A('kcTb', [128, 2, S], BF16); vcT = A('vcTb', [128, 2, S], BF16)
    vs = A('vsb', [128, 2, 16, 132], BF16); vw = A('vwb', [128, 2, 16, 132], BF16)
    kb.dma('pool', kcT[:], kc_d, w=['kcT'])
    kb.dma('pool', vcT[:], vc_d, w=['vcT'])
    kb.op('dve', lambda e: e.memset(vs[:, :, :, 128:129], 1.0), w=['vs1'])
    kb.op('dve', lambda e: e.memset(vw[:, :, :, 128:129], 1.0), w=['vw1'])
    if not fused:
        for g in range(2):
            kb.dma('pool', vs[:, g, :, 0:128], vs_d[g].rearrange("(n p) d -> p n d", p=128), w=[('vs', g)])
            kb.dma('pool', vw[:, g, :, 0:128], vw_d[g].rearrange("(n p) d -> p n d", p=128), w=[('vw', g)])
    else:
        vtmp = [xp[i][:].bitcast(BF16) for i in range(2)]
        nv = 0
        for g in range(2):
            for (srcs, dstt, key) in ((vsT_d, vs, 'vs'), (vwT_d, vw, 'vw')):
                vi = nv % 2
                nv += 1
                kb.dma('pool', vtmp[vi], srcs[g], w=[('xp', vi)])
                for k4 in range(4):
                    tb = 2 + k4 % 2
                    tbk = bankbf(tb)
                    pe_multi(kb, [(lambda e, j=j: e.transpose(out=tbk[:, j * 128:(j + 1) * 128],
                                                              in_=vtmp[vi][:, (k4 * 4 + j) * 128:(k4 * 4 + j + 1) * 128],
                                                              identity=idb[:])) for j in range(4)],
                             r=[('xp', vi), 'idb'], w=[('ps', tb)])
                    kb.op('dve', lambda e: e.tensor_copy(out=dstt[:, g, k4 * 4:k4 * 4 + 4, 0:128],
                                                         in_=tbk[:, 0:512].rearrange("p (a b) -> p a b", b=128)),
                          r=[('ps', tb)], w=[(key, g)])
    nst = [0]

    def rope(src_d, srcp_d, dst, dkey, plain=None):
        for hf in range(2):
            i = nst[0] % 2
            nst[0] += 1
            sl = slice(hf * H, (hf + 1) * H)
            kb.dma('sp', xs[i][:], src_d[:, sl], w=[('xs', i)])
            if srcp_d is not None:
                kb.dma('sp', xp[i][:], srcp_d[:, sl], w=[('xp', i)])
            else:
                kb.dma('sp', xp[i][0:16, :], src_d[16:32, sl], w=[('xp', i)])
                kb.dma('sp', xp[i][16:32, :], src_d[0:16, sl], r=[('xp', i)], w=[('xp', i)])
                kb.dma('sp', xp[i][32:128, :], src_d[32:128, sl], r=[('xp', i)], w=[('xp', i)])
            if plain is not None:
                kb.op('act', lambda e: e.activation(out=plain[:, sl], in_=xs[i][:], func=AF.Copy), r=[('xs', i)],
                      w=[(dkey, 'plain', hf)])
            kb.op('dve', lambda e: e.tensor_tensor(out=xs[i][:], in0=xs[i][:], in1=cosT[:, sl], op=ALU.mult),
                  r=[('xs', i), 'cosT'], w=[('xs', i)])
            kb.op('dve', lambda e: e.tensor_tensor(out=xp[i][:], in0=xp[i][:], in1=sinT[:, sl], op=ALU.mult),
                  r=[('xp', i), 'sinT'], w=[('xp', i)])
            kb.op('dve', lambda e: e.tensor_tensor(out=dst[:, sl], in0=xs[i][:], in1=xp[i][:], op=ALU.add),
                  r=[('xs', i), ('xp', i)], w=[(dkey, hf)])
    for hh in range(8):
        rope(q_d[hh], qp_d[hh], qr[:, hh, :], ('qr', hh), plain=qT[:, hh, :])
    for g in range(2):
        rope(ks_d[g], ksp_d[g], ksr[:, g, :], ('ksr', g))
        rope(kw_d[g], kwp_d[g], kwr[:, g, :], ('kwr', g))
    qkeys = lambda hh: [(('qr', hh), 0), (('qr', hh), 1), (('qr', hh), 'plain', 0), (('qr', hh), 'plain', 1)]
    w1 = A('w1', [128, 32, 256], BF16)
    w2 = A('w2', [128, 2, 128], BF16)
    hid = A('hid', [128, 2, 128], BF16)
    hx = A('hx', [128, 128]); ha = A('ha', [128, 128]); hb = A('hb', [128, 128])
    cpe = A('cpe', [128, 2])
    kcmpT = A('kcmpT', [128, 2, 128], BF16)
    vcmp = A('vcmp', [128, 2, 128])
    kb.op('dve', lambda e: e.memset(hid[:], 0.0), w=['hid'])
    kb.op('dve', lambda e: e.memset(kcmpT[:], 0.0), w=['kcmpT'])
    kb.op('dve', lambda e: e.memset(vcmp[:], 0.0), w=['vcmp'])
    for which, (pe_d, w1_d, w2_d, srcT, skey) in enumerate([(pek_d, w1k_d, w2k_d, kcT, 'kcT'), (pev_d, w1v_d, w2v_d, vcT, 'vcT')]):
        pe_b = small('pe_b%d' % which, pe_d, [128, 32], BF16, 'pool')
        for l0 in range(0, 32, 8):
            kb.dma('pool', w1[:, l0:l0 + 8, :], w1_d[l0 * 128:(l0 + 8) * 128, :].rearrange("(l p) h -> p l h", p=128),
                   w=[('w1', l0)])
        kb.dma('pool', w2[:], w2_d.rearrange("(c p) d -> p c d", p=128), w=['w2'])
        w1keys = [('w1', l0) for l0 in range(0, 32, 8)]
        src4 = srcT[:].rearrange("p g (n r) -> p g n r", r=16)
        for hc in range(2):
            kb.mm(bank(7, 1), [(w1[:, l, hc * 128:(hc + 1) * 128], pe_b[:, l:l + 1]) for l in range(32)],
                  r=w1keys + ['pe_b%d' % which], w=[('ps', 7)])
            kb.op('dve', lambda e: e.tensor_copy(out=cpe[:, hc:hc + 1], in_=bank(7, 1)), r=[('ps', 7)], w=[('cpe', hc)])
        for g in range(2):
            for hc in range(2):
                kb.mm(bank(6, 127), [(w1[:, l, hc * 128:(hc + 1) * 128], src4[:, g, (l // 16):(l // 16) + 127, l % 16])
                                     for l in range(32)], r=w1keys + [skey], w=[('ps', 6)])
                kb.op('dve', lambda e: e.tensor_scalar(out=hx[:, :127], in0=bank(6, 127), scalar1=cpe[:, hc:hc + 1],
                                                       scalar2=None, op0=ALU.add), r=[('ps', 6), ('cpe', hc)], w=['hx'])
                kb.op('act', lambda e: e.activation(out=ha[:, :127], in_=hx[:, :127], func=AF.Square), r=['hx'], w=['ha'])
                kb.op('dve', lambda e: e.tensor_scalar(out=ha[:, :127], in0=ha[:, :127], scalar1=c_cg[:, 0:1],
                                                       scalar2=c_one[:, 0:1], op0=ALU.mult, op1=ALU.add), r=['ha'], w=['ha'])
                kb.op('dve', lambda e: e.tensor_tensor(out=ha[:, :127], in0=hx[:, :127], in1=ha[:, :127], op=ALU.mult),
                      r=['hx', 'ha'], w=['ha'])
                kb.op('act', lambda e: e.activation(out=hb[:, :127], in_=ha[:, :127], func=AF.Sigmoid, scale=2.0 * GELU_C),
                      r=['ha'], w=['hb'])
                kb.op('dve', lambda e: e.tensor_tensor(out=hid[:, hc, :127], in0=hx[:, :127], in1=hb[:, :127], op=ALU.mult),
                      r=['hx', 'hb'], w=[('hid', hc)])
            if which == 0:
                kb.mm(bank(6, 127), [(w2[:, hc, :], hid[:, hc, :127]) for hc in range(2)],
                      r=['w2', ('hid', 0), ('hid', 1)], w=[('ps', 6)])
                kb.op('dve', lambda e: e.tensor_copy(out=kcmpT[:, g, :127], in_=bank(6, 127)), r=[('ps', 6)],
                      w=[('kcmpT', g)])
            else:
                kb.mm(bank(6, 128, 127), [(hid[:, hc, :127], w2[:, hc, :]) for hc in range(2)],
                      r=['w2', ('hid', 0), ('hid', 1)], w=[('ps', 6)])
                kb.op('dve', lambda e: e.tensor_copy(out=vcmp[:127, g, :], in_=bank(6, 128, 127)), r=[('ps', 6)],
                      w=[('vcmp', g)])
    sc4s = [A('sc4_%d' % i, [128, 4, 128]) for i in range(2)]
    mxs = [A('mx%d' % i, [128, 1]) for i in range(2)]
    rs4s = [A('rs4_%d' % i, [128, 4]) for i in range(2)]
    pcT4s = [A('pcT4_%d' % i, [128, 4, 128]) for i in range(2)]
    impms = [A('impm%d' % i, [128, 32]) for i in range(2)]
    wk32s = [A('wk32_%d' % i, [128, 32]) for i in range(2)]
    m8s = [A('m8_%d' % i, [128, 8]) for i in range(2)]
    m8bs = [A('m8b_%d' % i, [128, 8]) for i in range(2)]
    sels = [A('sel%d' % i, [128, 32], BF16) for i in range(2)]
    pbuf = [A('pbuf%d' % i, [128, 512], BF16) for i in range(2)]
    pT = [A('pT%d' % i, [128, 512], BF16) for i in range(2)]
    oacc = [A('oacc%d' % i, [128, 4, 128]) for i in range(2)]
    gsc = [A('gsc%d' % i, [128, 1]) for i in range(2)]
    cnt = dict(c=0, j=0)
    X = mybir.AxisListType.X

    def comp_ops(g, qt, p):
        sc4, mx, rs4, pcT4, impm, wk32, m8, m8b, sel = (sc4s[p], mxs[p], rs4s[p], pcT4s[p], impms[p], wk32s[p],
                                                        m8s[p], m8bs[p], sels[p])
        K = lambda n: (n, p)
        oa = oacc[p]
        oakeys = [('oacc', p, h) for h in range(4)]
        qsl = slice(qt * 128, (qt + 1) * 128)
        gview = gl[:, qt, g * 12:(g + 1) * 12].rearrange("p (h c) -> p h c", c=3)[:, :, 0:1]
        ops = []
        ops.append(lambda: pe_multi(kb, [(lambda e, h=h: e.matmul(bank(6, 512)[:, h * 128:(h + 1) * 128],
                                                                   qT[:, g * 4 + h, qsl], kcmpT[:, g, :], start=True,
                                                                   stop=True)) for h in range(4)],
                                    r=[(('qr', g * 4 + h), 'plain', qt // 8) for h in range(4)] + [('kcmpT', g)],
                                    w=[('ps', 6)]))
        ops.append(lambda: kb.op('dve', lambda e: e.reduce_max(out=mx[:], in_=bank(6, 512), axis=X), r=[('ps', 6)],
                                 w=[K('mx')]))
        ops.append(lambda: kb.op('dve', lambda e: e.tensor_scalar(out=mx[:], in0=mx[:], scalar1=c_nsc[:, 0:1],
                                                                  scalar2=None, op0=ALU.mult), r=[K('mx')], w=[K('mx')]))
        ops.append(lambda: kb.op('act', lambda e: e.activation(out=sc4[:].rearrange("p h n -> p (h n)"), in_=bank(6, 512),
                                                               func=AF.Exp, scale=SCALE, bias=mx[:, 0:1]),
                                 r=[('ps', 6), K('mx')], w=[K('sc4')]))
        ops.append(lambda: kb.op('dve', lambda e: e.tensor_tensor(out=sc4[:], in0=sc4[:],
                                                                  in1=cm[:, qt, :].unsqueeze(1).broadcast_to([128, 4, 128]),
                                                                  op=ALU.mult), r=[K('sc4'), 'cm'], w=[K('sc4')]))
        ops.append(lambda: kb.op('dve', lambda e: e.reduce_sum(out=rs4[:], in_=sc4[:], axis=X), r=[K('sc4')], w=[K('rs4')]))
        ops.append(lambda: kb.op('dve', lambda e: e.tensor_scalar(out=rs4[:], in0=rs4[:], scalar1=c_tiny[:, 0:1],
                                                                  scalar2=None, op0=ALU.max), r=[K('rs4')], w=[K('rs4')]))
        ops.append(lambda: kb.op('dve', lambda e: e.reciprocal(out=rs4[:], in_=rs4[:]), r=[K('rs4')], w=[K('rs4')]))
        ops.append(lambda: kb.op('dve', lambda e: e.tensor_tensor(out=sc4[:], in0=sc4[:],
                                                                  in1=rs4[:, :].unsqueeze(2).broadcast_to([128, 4, 128]),
                                                                  op=ALU.mult), r=[K('sc4'), K('rs4')], w=[K('sc4')]))
        ops.append(lambda: pe_multi(kb, [(lambda e, h=h: e.transpose(out=bank(6, 512)[:, h * 128:(h + 1) * 128],
                                                                      in_=sc4[:, h, :], identity=idf[:])) for h in range(4)],
                                    r=[K('sc4'), 'idf'], w=[('ps', 6)]))
        ops.append(lambda: kb.op('act', lambda e: e.activation(out=pcT4[:].rearrange("p h n -> p (h n)"), in_=bank(6, 512),
                                                               func=AF.Copy), r=[('ps', 6)], w=[K('pcT4')]))
        ops.append(lambda: kb.mm(bank(7, 32), [(pcT4[:, h, :], ovl[:, :]) for h in range(4)], r=[K('pcT4'), 'ovl'],
                                 w=[('ps', 7)]))
        ops.append(lambda: pe_multi(kb, [(lambda e, h=h: e.matmul(bank(6, 512)[:, h * 128:(h + 1) * 128], pcT4[:, h, :],
                                                                   vcmp[:, g, :], start=True, stop=True)) for h in range(4)],
                                    r=[K('pcT4'), ('vcmp', g)], w=[('ps', 6)]))
        ops.append(lambda: kb.op('dve', lambda e: e.tensor_tensor(out=oa[:],
                                                                  in0=bank(6, 512).rearrange("p (h d) -> p h d", d=128),
                                                                  in1=gview.broadcast_to([128, 4, 128]), op=ALU.mult),
                                 r=[('ps', 6), 'gl'], w=oakeys))
        ops.append(lambda: kb.op('dve', lambda e: e.tensor_tensor(out=impm[:], in0=bank(7, 32), in1=fbv[:, qt, :],
                                                                  op=ALU.add), r=[('ps', 7), 'fbv'], w=[K('impm')]))
        ops.append(lambda: kb.op('dve', lambda e: e.max(out=m8[:], in_=impm[:]), r=[K('impm')], w=[K('m8')]))
        ops.append(lambda: kb.op('dve', lambda e: e.match_replace(out=wk32[:], in_to_replace=m8[:], in_values=impm[:],
                                                                  imm_value=-3.0e38), r=[K('impm'), K('m8')], w=[K('wk32')]))
        ops.append(lambda: kb.op('dve', lambda e: e.max(out=m8b[:], in_=wk32[:]), r=[K('wk32')], w=[K('m8b')]))
        ops.append(lambda: kb.op('dve', lambda e: e.tensor_scalar(out=sel[:], in0=impm[:], scalar1=m8b[:, 7:8],
                                                                  scalar2=None, op0=ALU.is_ge), r=[K('impm'), K('m8b')],
                                 w=[K('sel')]))
        return ops

    tiles_ = [(g, qt) for g in range(2) for qt in range(NQT)]
    for f_ in comp_ops(0, 0, 0):
        f_()
    for ti_, (g, qt) in enumerate(tiles_):
        if True:
            oi = ti_ % 2
            oa = oacc[oi]
            sel = sels[oi]
            selkey = ('sel', oi)
            oakeys = [('oacc', oi, h) for h in range(4)]
            qsl = slice(qt * 128, (qt + 1) * 128)
            nxt = comp_ops(tiles_[ti_ + 1][0], tiles_[ti_ + 1][1], (ti_ + 1) % 2) if ti_ + 1 < len(tiles_) else []
            chunks = []
            for h in range(4):
                hh = g * 4 + h
                for br in (1, 2):
                    kts = list(range(qt + 1)) if br == 1 else list(range(max(0, qt - 4), qt + 1))
                    parts = [kts[i:i + 4] for i in range(0, len(kts), 4)]
                    accb = 4 + cnt['j'] % 2
                    ji = cnt['j'] % 2
                    cnt['j'] += 1
                    npv = len(kts)
                    ipv = 0
                    for pi_, ch in enumerate(parts):
                        chunks.append(dict(h=h, hh=hh, br=br, ch=ch, accb=accb, ji=ji, ipv0=ipv, npv=npv,
                                           last=(pi_ == len(parts) - 1)))
                        ipv += len(ch)

            def stage1(c_):
                i = c_['idx']
                ch = c_['ch']
                n = 128 * len(ch)
                k0 = ch[0] * 128
                sb = i % 2
                kT, kkey = (ksr[:, g, :], ('ksr', g)) if c_['br'] == 1 else (kwr[:, g, :], ('kwr', g))
                kb.mm(bank(sb, n), [(qr[:, c_['hh'], qsl], kT[:, k0:k0 + n])],
                      r=[(('qr', c_['hh']), qt // 8), (kkey, 0), (kkey, 1)], w=[('ps', sb)])
                pb = pbuf[i % 2]
                pk = ('pbuf', i % 2)
                kb.op('act', lambda e: e.activation(out=pb[:, :n], in_=bank(sb, n), func=AF.Exp, scale=SCALE),
                      r=[('ps', sb)], w=[pk])
                if c_['br'] == 1:
                    nb = 2 * len(ch)
                    kb.op('dve', lambda e: e.tensor_tensor(
                        out=pb[:, :n].rearrange("p (b k) -> p b k", k=64), in0=pb[:, :n].rearrange("p (b k) -> p b k", k=64),
                        in1=sel[:, 2 * ch[0]:2 * ch[0] + nb].unsqueeze(2).broadcast_to([128, nb, 64]), op=ALU.mult),
                        r=[pk, selkey], w=[pk])
                if c_['br'] == 2 and qt >= 4 and ch[0] == qt - 4:
                    kb.op('dve', lambda e: e.tensor_tensor(out=pb[:, 0:128], in0=pb[:, 0:128], in1=tri2[:], op=ALU.mult),
                          r=[pk, 'tri2'], w=[pk])
                if ch[-1] == qt:
                    off = 128 * (len(ch) - 1)
                    kb.op('dve', lambda e: e.tensor_tensor(out=pb[:, off:off + 128], in0=pb[:, off:off + 128], in1=tri[:],
                                                           op=ALU.mult), r=[pk, 'tri'], w=[pk])

            def stage2(c_):
                i = c_['idx']
                ch = c_['ch']
                n = 128 * len(ch)
                pb = pbuf[i % 2]
                tb = 2 + i % 2
                tbk = bankbf(tb)
                pe_multi(kb, [(lambda e, j=j: e.transpose(out=tbk[:, j * 128:(j + 1) * 128], in_=pb[:, j * 128:(j + 1) * 128],
                                                          identity=idb[:])) for j in range(len(ch))],
                         r=[('pbuf', i % 2), 'idb'], w=[('ps', tb)])
                if i % 2 == 0:
                    kb.op('act', lambda e: e.activation(out=pT[0][:, :n], in_=tbk[:, :n], func=AF.Copy), r=[('ps', tb)],
                          w=[('pT', 0)])
                else:
                    kb.op('dve', lambda e: e.tensor_copy(out=pT[1][:, :n], in_=tbk[:, :n]), r=[('ps', tb)], w=[('pT', 1)])

            def stage3(c_):
                i = c_['idx']
                ch = c_['ch']
                accb = c_['accb']
                vaug, vkeys = (vs, [('vs', g), 'vs1']) if c_['br'] == 1 else (vw, [('vw', g), 'vw1'])
                fns = []
                for j, kt in enumerate(ch):
                    ip = c_['ipv0'] + j
                    fns.append(lambda e, j=j, kt=kt, first=(ip == 0), lastm=(ip == c_['npv'] - 1): e.matmul(
                        bank(accb, 129), pT[i % 2][:, j * 128:(j + 1) * 128], vaug[:, g, kt, 0:129], start=first, stop=lastm))
                pe_multi(kb, fns, r=[('pT', i % 2)] + vkeys, w=[('ps', accb)])
                if c_['last']:
                    h = c_['h']
                    gs_ = gsc[c_['ji']]
                    gk = ('gsc', c_['ji'])
                    col = h_col(c_['hh'], c_['br'])
                    kb.op('dve', lambda e: e.reciprocal(out=gs_[:], in_=bank(accb, 129)[:, 128:129]), r=[('ps', accb)], w=[gk])
                    kb.op('dve', lambda e: e.tensor_tensor(out=gs_[:], in0=gs_[:], in1=gl[:, qt, col:col + 1], op=ALU.mult),
                          r=[gk, 'gl'], w=[gk])
                    kb.op('dve', lambda e: e.scalar_tensor_tensor(out=oa[:, h, :], in0=bank(accb, 128), scalar=gs_[:, 0:1],
                                                                  in1=oa[:, h, :], op0=ALU.mult, op1=ALU.add),
                          r=[('ps', accb), gk, ('oacc', oi, h)], w=[('oacc', oi, h)])

            nchk = len(chunks)
            for i, c_ in enumerate(chunks):
                c_['idx'] = cnt['c'] + i
            per = (len(nxt) + nchk - 1) // max(nchk, 1)
            for i in range(nchk + 2):
                if i < nchk:
                    stage1(chunks[i])
                if 0 <= i - 1 < nchk:
                    stage2(chunks[i - 1])
                if 0 <= i - 2 < nchk:
                    stage3(chunks[i - 2])
                for _ in range(per):
                    if nxt:
                        nxt.pop(0)()
            while nxt:
                nxt.pop(0)()
            cnt['c'] += nchk
            kb.dma('sp', o_o[qt * 128:(qt + 1) * 128, g * 512:(g + 1) * 512], oa[:].rearrange("p h d -> p (h d)"),
                   r=oakeys, w=[('o_o', g, qt)])
    if fused:
        kb.barrier()
        return nc
    kb.finish('sp')
    return nc


def h_col(hh, br):
    return hh * 3 + br


def l3_consts():
    i = np.arange(128)[:, None]
    j = np.arange(128)[None, :]
    tri = (j <= i).astype(np.float32)
    tri2 = (j > i).astype(np.float32)
    n = np.arange(128)
    cm = np.zeros((128, 16, 128), np.float32)
    fbv = np.zeros((128, 16, 32), np.float32)
    jb = np.arange(32)
    for qt in range(16):
        t = qt * 128 + np.arange(128)
        cm[:, qt, :] = ((16 * n[None, :] + 31 <= t[:, None]) & (n[None, :] < 127)).astype(np.float32)
        cur = (t // 64)[:, None]
        forced = (jb[None] == 0) | (jb[None] == cur) | (jb[None] == cur - 1)
        valid = jb[None] * 64 <= t[:, None]
        fbv[:, qt, :] = np.where(valid, np.where(forced, 1000.0, 0.0), -1e30)
    ovl = np.zeros((128, 32), np.float32)
    for nn in range(127):
        for jj in range(32):
            if 16 * nn < 64 * jj + 64 and 16 * nn + 31 >= 64 * jj:
                ovl[nn, jj] = 1.0
    d = np.arange(128)
    inv = np.where(d < 32, 500000.0 ** (-(2.0 * (d % 16)) / 32.0), 0.0).astype(np.float32)[:, None]
    sgn = np.where(d < 16, -1.0, np.where(d < 32, 1.0, 0.0)).astype(np.float32)[:, None]
    return dict(tri=tri, tri2=tri2, cm=cm, fbv=fbv, ovl=ovl, inv=inv, sgn=sgn, ident=np.eye(128, dtype=np.float32))


def swap_rot(xT):
    y = xT.copy()
    y[..., 0:16, :] = xT[..., 16:32, :]
    y[..., 16:32, :] = xT[..., 0:16, :]
    return y


def prep_L3(zT_b, pos_b, half, W, consts):
    c = np.ascontiguousarray
    gs = [2 * half, 2 * half + 1]
    qT = c(zT_b[half * 1024:(half + 1) * 1024].reshape(8, 128, S))

    def grp(base):
        return c(np.stack([zT_b[base + g * 128:base + (g + 1) * 128] for g in gs]))
    kc, vc, ks, vs_, kw, vw_ = [grp(2048 + i * 512) for i in range(6)]
    gl = c(zT_b[5120 + half * 24:5120 + (half + 1) * 24].T)
    m = dict(qT=qT, qPT=swap_rot(qT), ksT=ks, ksPT=swap_rot(ks), kwT=kw, kwPT=swap_rot(kw),
             kcT=c(kc.transpose(1, 0, 2)), vcT=c(vc.transpose(1, 0, 2)),
             vs=c(vs_.transpose(0, 2, 1)), vw=c(vw_.transpose(0, 2, 1)), gl=gl,
             pos=c(pos_b.reshape(1, S).astype(np.int32)))
    m.update(W)
    m.update(consts)
    return m


def prep_L3_weights(pe_k, w1_k, w2_k, pe_v, w1_v, w2_v):
    c = np.ascontiguousarray
    return dict(pekT=c(pe_k.T), w1k=c(w1_k), w2k=c(w2_k), pevT=c(pe_v.T), w1v=c(w1_v), w2v=c(w2_v))


_PROGS = {}


def _prog(name, fn):
    if name not in _PROGS:
        _PROGS[name] = fn()
    return _PROGS[name]


def _lay(g, n=16):
    return np.ascontiguousarray(np.asarray(g, np.float32).reshape(n, 128).T)


def kernel_unfused(**inp):
    c = np.ascontiguousarray
    f32 = lambda a: np.asarray(a, dtype=np.float32)
    x = f32(inp['x'])
    pos = np.asarray(inp['positions'])
    T = 1024
    cores = list(range(NCORES))
    tok = lambda ci: (ci // 2, slice((ci % 2) * T, (ci % 2 + 1) * T))
    tri_st = np.triu(np.ones((128, 128), np.float32))
    common = dict(g1=_lay(inp['l0_ffn1_norm']), g2=_lay(inp['l0_mix_norm']), wgd=f32(inp['l0_ffn1_w_gate']),
                  wud=f32(inp['l0_ffn1_w_up']), wdd=f32(inp['l0_ffn1_w_down']), w_in=f32(inp['l0_w_in']),
                  lng=c(f32(inp['l0_gmlp_ln_g']).reshape(1, 1024)), lnb=c(f32(inp['l0_gmlp_ln_b']).reshape(1, 1024)),
                  wsT=c(f32(inp['l0_gmlp_ws']).transpose(2, 0, 1)), tri=tri_st,
                  bs=c(f32(inp['l0_gmlp_bs']).reshape(1, 1024)))
    maps = []
    for ci in cores:
        b, sl = tok(ci)
        m = dict(common)
        m['xTd'] = c(x[b, sl].T)
        maps.append(m)
    r1 = run_bass_kernel_spmd(_prog('L1', build_L1), maps, core_ids=cores).results
    common = dict(conv_w=c(f32(inp['l0_conv_w'])[:, 0, :].reshape(31, 8, 128).transpose(2, 1, 0)),
                  conv_b=_lay(inp['l0_conv_b'], 8), cln_g=_lay(inp['l0_conv_ln_g'], 8), cln_b=_lay(inp['l0_conv_ln_b'], 8),
                  ident=np.eye(128, dtype=np.float32), w_out=f32(inp['l0_w_out']),
                  gA=_lay(inp['l0_ffn2_norm']), wgA=f32(inp['l0_ffn2_w_gate']), wuA=f32(inp['l0_ffn2_w_up']),
                  wdA=f32(inp['l0_ffn2_w_down']),
                  gB=_lay(inp['l1_ffn1_norm']), wgB=f32(inp['l1_ffn1_w_gate']), wuB=f32(inp['l1_ffn1_w_up']),
                  wdB=f32(inp['l1_ffn1_w_down']),
                  gM=_lay(inp['l1_mix_norm']), w_in1=f32(inp['l1_w_in']))
    maps = []
    for ci in cores:
        m = dict(common)
        aT = r1[ci]['aT']
        halo = np.zeros((1024, 32), np.float32)
        if ci % 2 == 1:
            halo = r1[ci - 1]['aT'][:, T - 32:]
        m['aTh'] = c(np.concatenate([halo, aT], axis=1))
        m['boTd'] = r1[ci]['boT']
        m['x1Td'] = r1[ci]['x1T']
        maps.append(m)
    r2 = run_bass_kernel_spmd(_prog('L2', build_L2), maps, core_ids=cores).results
    W = prep_L3_weights(*[f32(inp[k]) for k in ('l1_cmp_pe_k', 'l1_cmp_w1_k', 'l1_cmp_w2_k',
                                                'l1_cmp_pe_v', 'l1_cmp_w1_v', 'l1_cmp_w2_v')])
    consts = l3_consts()
    maps = []
    for ci in cores:
        b, half = ci // 2, ci % 2
        zT_b = np.concatenate([r2[2 * b]['zT'], r2[2 * b + 1]['zT']], axis=1)
        maps.append(prep_L3(zT_b, pos[b], half, W, consts))
    r3 = run_bass_kernel_spmd(_prog('L3', build_L3), maps, core_ids=cores).results
    common = dict(w_out1=f32(inp['l1_w_out']), gA=_lay(inp['l1_ffn2_norm']), wgA=f32(inp['l1_ffn2_w_gate']),
                  wuA=f32(inp['l1_ffn2_w_up']), wdA=f32(inp['l1_ffn2_w_down']), gF=_lay(inp['final_norm']))
    maps = []
    for ci in cores:
        b, sl = tok(ci)
        o_b = np.concatenate([r3[2 * b]['o'], r3[2 * b + 1]['o']], axis=1)
        m = dict(common)
        m['oTd'] = c(o_b[sl].T)
        m['x4Td'] = r2[ci]['x4T']
        maps.append(m)
    r4 = run_bass_kernel_spmd(_prog('L4', build_L4), maps, core_ids=cores).results
    out = np.zeros((4, 2048, 2048), np.float32)
    for ci in cores:
        b, sl = tok(ci)
        out[b, sl] = r4[ci]['yT'].T
    return out


from contextlib import ExitStack

W_NAMES = [('l0_ffn1', 'f1'), ('l0_ffn2', 'f2'), ('l1_ffn1', 'f3'), ('l1_ffn2', 'f4')]


def build_fused(dff=DFF, nz=5168):
    nc = bass.Bass("TRN2", target_bir_lowering=False)
    T = 1024
    ext = lambda name, shape, dtype=F32: nc.dram_tensor(name, shape, dtype, kind="ExternalInput").ap()
    scr = lambda name, shape: nc.dram_tensor(name, shape, F32, kind="Internal").ap()
    I = {}
    I['xT'] = ext('xT', [2, D, T])
    for _, s in W_NAMES:
        I[s + '_g'] = ext(s + '_g', [128, 16])
        I[s + '_wg'] = ext(s + '_wg', [D, dff])
        I[s + '_wu'] = ext(s + '_wu', [D, dff])
        I[s + '_wd'] = ext(s + '_wd', [dff, D])
    for name, shape in [('g_mix0', [128, 16]), ('w_in0', [D, 4096]), ('lng', [1, 1024]), ('lnb', [1, 1024]),
                        ('wsT', [128, 8, 128]), ('tri_st', [128, 128]), ('bs', [1, 1024]),
                        ('conv_w', [128, 8, 31]), ('conv_b', [128, 8]), ('cln_g', [128, 8]), ('cln_b', [128, 8]),
                        ('w_out0', [D, D]), ('g_mix1', [128, 16]), ('w_in1', [D, nz]),
                        ('inv', [128, 1]), ('sgn', [128, 1]), ('pekT', [128, 32]), ('w1k', [4096, 256]),
                        ('w2k', [256, 128]), ('pevT', [128, 32]), ('w1v', [4096, 256]), ('w2v', [256, 128]),
                        ('ovl', [128, 32]), ('cm', [128, 16, 128]), ('fbv', [128, 16, 32]), ('tri', [128, 128]),
                        ('tri2', [128, 128]), ('ident', [128, 128]), ('w_out1', [D, D]), ('g_fin', [128, 16])]:
        I[name] = ext(name, shape)
    I['pos'] = ext('pos', [1, S], I32)
    yT = nc.dram_tensor('yT', [D, T], F32, kind="ExternalOutput").ap()
    I['flag'] = ext('flag', [128, 2])
    x1T = scr('x1T_s', [2, D, T]); aT = scr('aT_s', [2, 1024, T]); boT = scr('boT_s', [2, 1024, T])
    x4T = scr('x4T_s', [2, D, T]); zT = scr('zT_s', [nz, 2 * T]); o_s = scr('o_s', [2 * T, 2048])
    kb = KB(nc)
    ps = nc.alloc_psum_tensor('ps', [128, 8 * 512], F32)
    with ExitStack() as es_core:
        _ES[0] = es_core
        core = Core(nc, T, kb, ps)
        for h in range(2):
            with ExitStack() as es:
                _ES[0] = es
                build_L1(T, dff, nc, dict(xTd=I['xT'][h], g1=I['f1_g'], g2=I['g_mix0'], wgd=I['f1_wg'], wud=I['f1_wu'],
                                          wdd=I['f1_wd'], w_in=I['w_in0'], lng=I['lng'], lnb=I['lnb'], wsT=I['wsT'],
                                          tri=I['tri_st'], bs=I['bs'], x1T=x1T[h], aT=aT[h], boT=boT[h]), core)
            with ExitStack() as es:
                _ES[0] = es
                build_L2(T, dff, nz, nc, dict(x1Td=x1T[h], aT_cur=aT[h], aT_prev=(aT[0] if h == 1 else None),
                                              boTd=boT[h], ident=I['ident'], conv_w=I['conv_w'], conv_b=I['conv_b'], cln_g=I['cln_g'],
                                              cln_b=I['cln_b'], w_out=I['w_out0'],
                                              gA=I['f2_g'], wgA=I['f2_wg'], wuA=I['f2_wu'], wdA=I['f2_wd'],
                                              gB=I['f3_g'], wgB=I['f3_wg'], wuB=I['f3_wu'], wdB=I['f3_wd'],
                                              gM=I['g_mix1'], w_in1=I['w_in1'], x4T=x4T[h],
                                              zT=zT[:, h * T:(h + 1) * T]), core)
            _ES[0] = es_core
    for gp in range(2):
        with ExitStack() as es:
            _ES[0] = es
            ov = {k: I[k] for k in ('inv', 'sgn', 'pekT', 'w1k', 'w2k', 'pevT', 'w1v', 'w2v', 'ovl', 'cm', 'fbv',
                                    'tri', 'tri2', 'ident', 'pos')}
            ov['zT'] = zT
            ov['o'] = o_s[:, gp * 1024:(gp + 1) * 1024]
            build_L3(nc, ov, kb, ps, gp)
    with ExitStack() as es_core:
        _ES[0] = es_core
        core = Core(nc, T, kb, ps)
        with ExitStack() as es:
            _ES[0] = es
            build_L4(T, dff, nc, dict(x4Td=x4T[0], x4T_1=x4T[1], o_tok=o_s[0:T, :], o_tok1=o_s[T:2 * T, :],
                                      flag=I['flag'], ident=I['ident'], w_out1=I['w_out1'], gA=I['f4_g'],
                                      wgA=I['f4_wg'], wuA=I['f4_wu'], wdA=I['f4_wd'], gF=I['g_fin'], yT=yT), core)
        _ES[0] = es_core
    _ES[0] = None
    kb.finish('sp')
    return nc


def fused_inputs(inp, b, r=0):
    c = np.ascontiguousarray
    f32 = lambda a: np.asarray(a, dtype=np.float32)
    x = f32(inp['x'])
    m = dict(xT=c(np.stack([x[b, 0:1024].T, x[b, 1024:2048].T])))
    for pre, s in W_NAMES:
        m[s + '_g'] = _lay(inp[pre + '_norm'])
        m[s + '_wg'] = f32(inp[pre + '_w_gate'])
        m[s + '_wu'] = f32(inp[pre + '_w_up'])
        m[s + '_wd'] = f32(inp[pre + '_w_down'])
    m.update(g_mix0=_lay(inp['l0_mix_norm']), w_in0=f32(inp['l0_w_in']),
             lng=c(f32(inp['l0_gmlp_ln_g']).reshape(1, 1024)), lnb=c(f32(inp['l0_gmlp_ln_b']).reshape(1, 1024)),
             wsT=c(f32(inp['l0_gmlp_ws']).transpose(2, 0, 1)), tri_st=np.triu(np.ones((128, 128), np.float32)),
             bs=c(f32(inp['l0_gmlp_bs']).reshape(1, 1024)),
             conv_w=c(f32(inp['l0_conv_w'])[:, 0, :].reshape(31, 8, 128).transpose(2, 1, 0)),
             conv_b=_lay(inp['l0_conv_b'], 8), cln_g=_lay(inp['l0_conv_ln_g'], 8), cln_b=_lay(inp['l0_conv_ln_b'], 8),
             w_out0=f32(inp['l0_w_out']), g_mix1=_lay(inp['l1_mix_norm']), w_in1=f32(inp['l1_w_in']),
             w_out1=f32(inp['l1_w_out']), g_fin=_lay(inp['final_norm']),
             pos=c(np.asarray(inp['positions'])[b].reshape(1, S).astype(np.int32)))
    m.update(prep_L3_weights(*[f32(inp[k]) for k in ('l1_cmp_pe_k', 'l1_cmp_w1_k', 'l1_cmp_w2_k',
                                                     'l1_cmp_pe_v', 'l1_cmp_w1_v', 'l1_cmp_w2_v')]))
    m.update(l3_consts())
    fl = np.zeros((128, 2), np.float32)
    fl[:, r] = 1.0
    m['flag'] = fl
    return m


def kernel(**inp):
    nc = _prog('fused', build_fused)
    maps = [fused_inputs(inp, ci // 2, ci % 2) for ci in range(NCORES)]
    res = run_bass_kernel_spmd(nc, maps, core_ids=list(range(NCORES))).results
    out = np.zeros((4, 2048, 2048), np.float32)
    for ci in range(NCORES):
        b, h = ci // 2, ci % 2
        out[b, h * 1024:(h + 1) * 1024] = res[ci]['yT'].T
    return out
```

```python
import os
import numpy as np
import concourse.bass as bass
import concourse.mybir as mybir
from concourse.bass_utils import run_bass_kernel_spmd

F32 = mybir.dt.float32
BF16 = mybir.dt.bfloat16
I32 = mybir.dt.int32
AF = mybir.ActivationFunctionType
ALU = mybir.AluOpType

_ES = [None]
_UID = [0]


def SB(nc, name, shape, dtype=None):
    dtype = F32 if dtype is None else dtype
    _UID[0] += 1
    nm = '%s_%d' % (name, _UID[0])
    if _ES[0] is None:
        return nc.alloc_sbuf_tensor(nm, list(shape), dtype)
    return _ES[0].enter_context(nc.sbuf_tensor(nm, list(shape), dtype))


def mk_dt(nc, over, pre=''):
    def dt(name, shape, kind="ExternalInput", dtype=F32):
        if over is not None and name in over:
            return over[name]
        return nc.dram_tensor(pre + name, shape, dtype, kind=kind).ap()
    return dt


D = 2048
DFF = 5632
NCORES = 8
EPS = 1e-6


class KB:
    NS = 6

    def __init__(self, nc):
        self.nc = nc
        self.eng = dict(pe=nc.tensor, dve=nc.vector, act=nc.scalar, pool=nc.gpsimd, sp=nc.sync)
        self.sem = {}
        self.cnt = {}
        for e in ('pe', 'dve', 'act', 'pool'):
            self.sem[e] = nc.alloc_semaphore('c_' + e)
            self.cnt[e] = 0
        self.nsq = {'sp': 6, 'pool': 8, 'act': 2}
        self.dsem = {q: [nc.alloc_semaphore('d_%s%d' % (q, i)) for i in range(self.nsq[q])]
                     for q in ('sp', 'pool', 'act')}
        self.dcnt = {q: 0 for q in self.dsem}
        self.seen = {e: {} for e in self.eng}
        self.st = {}
        self.semobj = {}
        for s in list(self.sem.values()) + [x for v in self.dsem.values() for x in v]:
            self.semobj[s.num] = s
        self.nwait = 0

    def _deps(self, r, w):
        deps = {}

        def add(tok):
            if tok is None:
                return
            s, v = tok
            if deps.get(s, 0) < v:
                deps[s] = v
        for k in r:
            st = self.st.get(k)
            if st:
                add(st[0])
        for k in w:
            st = self.st.get(k)
            if st:
                add(st[0])
                for s, v in st[1].items():
                    add((s, v))
        return deps

    def _emit_waits(self, e, deps, skip_sem=None):
        eng = self.eng[e]
        seen = self.seen[e]
        for s, v in deps.items():
            if skip_sem is not None and s == skip_sem:
                continue
            if seen.get(s, 0) >= v:
                continue
            eng.wait_ge(self.semobj[s], v)
            seen[s] = v
            self.nwait += 1

    def _commit(self, tok, r, w):
        for k in r:
            st = self.st.setdefault(k, [None, {}])
            if st[1].get(tok[0], 0) < tok[1]:
                st[1][tok[0]] = tok[1]
        for k in w:
            self.st[k] = [tok, {}]

    def op(self, e, fn, r=(), w=()):
        deps = self._deps(r, w)
        self._emit_waits(e, deps, skip_sem=(self.sem['pe'].num if e == 'pe' else None))
        inst = fn(self.eng[e])
        self.cnt[e] += 1
        inst.then_inc(self.sem[e], 1)
        tok = (self.sem[e].num, self.cnt[e])
        self._commit(tok, r, w)
        return tok

    def mm(self, out, pairs, r=(), w=()):
        deps = self._deps(r, w)
        self._emit_waits('pe', deps, skip_sem=self.sem['pe'].num)
        n = len(pairs)
        inst = None
        for i, (lhsT, rhs) in enumerate(pairs):
            inst = self.nc.tensor.matmul(out, lhsT, rhs, start=(i == 0), stop=(i == n - 1))
        self.cnt['pe'] += 1
        inst.then_inc(self.sem['pe'], 1)
        tok = (self.sem['pe'].num, self.cnt['pe'])
        self._commit(tok, r, w)
        return tok

    def mm1(self, out, lhsT, rhs, start, stop, r=(), w=()):
        return self.op('pe', lambda e: e.matmul(out, lhsT, rhs, start=start, stop=stop), r=r, w=w)

    def dma(self, q, out, in_, r=(), w=(), **kw):
        deps = self._deps(r, w)
        i = self.dcnt[q]
        self.dcnt[q] += 1
        s = self.dsem[q][i % self.nsq[q]]
        rnd = i // self.nsq[q]
        if rnd > 0:
            deps[s.num] = max(deps.get(s.num, 0), 16 * rnd)
        self._emit_waits(q, deps)
        inst = self.eng[q].dma_start(out=out, in_=in_, **kw)
        inst.then_inc(s, 16)
        tok = (s.num, 16 * (rnd + 1))
        self._commit(tok, r, w)
        return tok

    def barrier(self):
        deps = {}
        for e, sm in self.sem.items():
            if self.cnt[e] > 0:
                deps[sm.num] = self.cnt[e]
        for q, sl in self.dsem.items():
            n = self.dcnt[q]
            for i, sm in enumerate(sl):
                k = (n - 1 - i) // self.nsq[q] + 1 if n > i else 0
                if k > 0:
                    deps[sm.num] = 16 * k
        for e in self.eng:
            self._emit_waits(e, dict(deps))

    def finish(self, e='sp'):
        deps = {}
        for k, st in self.st.items():
            if st[0] is not None:
                s, v = st[0]
                if deps.get(s, 0) < v:
                    deps[s] = v
        self._emit_waits(e, deps)


class Core:
    def __init__(self, nc, T, kb=None, ps=None):
        self.nc = nc
        self.kb = KB(nc) if kb is None else kb
        self.T = T
        self.TB = [(i, min(512, T - i)) for i in range(0, T, 512)]
        self.xT = SB(nc, 'xT', [128, 16, T], F32)
        self.hT = SB(nc, 'hT', [128, 16, T], BF16)
        self.wg = [SB(nc, 'wg%d' % i, [128, 16, 256], BF16) for i in range(2)]
        self.wu = [SB(nc, 'wu%d' % i, [128, 16, 256], BF16) for i in range(2)]
        self.wd = [SB(nc, 'wd%d' % i, [128, 2, 2048], BF16) for i in range(4)]
        self.actT = [SB(nc, 'actT%d' % i, [128, 4, T], BF16) for i in range(2)]
        self.tmp = [SB(nc, 'tmp%d' % i, [128, 512], F32) for i in range(2)]
        self.sq = [SB(nc, 'sq%d' % i, [128, 512], BF16) for i in range(2)]
        self.rstd = SB(nc, 'rstd', [128, 512], F32)
        self.ones = SB(nc, 'ones', [128, 128], BF16)
        self.ps = nc.alloc_psum_tensor('ps', [128, 8 * 512], F32) if ps is None else ps
        self.ntmp = 0
        self.nsq = 0
        self.nwt = 0
        self.nwd = 0
        self.nact = 0
        self.kb.op('dve', lambda e: e.memset(self.ones[:], 1.0), w=['ones'])
        self.half_sb = SB(nc, 'half', [128, 1], F32)
        self.kb.op('dve', lambda e: e.memset(self.half_sb[:], 0.5), w=['half'])
        self.eps_sb = SB(nc, 'eps', [128, 1], F32)
        self.kb.op('dve', lambda e: e.memset(self.eps_sb[:], EPS), w=['eps'])

    def bank(self, b, n=512):
        return self.ps[:, b * 512:b * 512 + n]


def rmsnorm_T(c, g_sb, gkey, out=None, okey='hT', src=None, skey='xT', bank=6):
    kb = c.kb
    out = c.hT if out is None else out
    src = c.xT if src is None else src
    for (t0, tn) in c.TB:
        for kc in range(16):
            i = c.nsq % 2
            c.nsq += 1
            sq = c.sq[i]
            kb.op('act', lambda e, kc=kc, sq=sq: e.activation(out=sq[:, :tn], in_=src[:, kc, t0:t0 + tn],
                                                             func=AF.Square),
                  r=[(skey, kc, t0)], w=[('sq', i)])
            kb.mm1(c.bank(bank, tn), c.ones[:], sq[:, :tn], kc == 0, kc == 15,
                   r=['ones', ('sq', i)], w=([('ps', bank)] if kc in (0, 15) else []))
        kb.op('act', lambda e: e.activation(out=c.rstd[:, :tn], in_=c.bank(bank, tn), func=AF.Sqrt,
                                            scale=1.0 / D, bias=c.eps_sb[:, 0:1]),
              r=[('ps', bank), 'eps'], w=['rstd'])
        kb.op('dve', lambda e: e.reciprocal(out=c.rstd[:, :tn], in_=c.rstd[:, :tn]), r=['rstd'], w=['rstd'])
        for kc in range(16):
            kb.op('dve', lambda e, kc=kc: e.scalar_tensor_tensor(
                out=out[:, kc, t0:t0 + tn], in0=src[:, kc, t0:t0 + tn], scalar=g_sb[:, kc:kc + 1],
                in1=c.rstd[:, :tn], op0=ALU.mult, op1=ALU.mult),
                r=[(skey, kc, t0), 'rstd', gkey], w=[(okey, kc, t0)])


def ffn_T(c, wg_d, wu_d, wd_d, dff=DFF):
    kb = c.kb
    nc = c.nc
    wg_v = wg_d.rearrange("(kc p) f -> p kc f", p=128)
    wu_v = wu_d.rearrange("(kc p) f -> p kc f", p=128)
    NFB = dff // 512
    gbank = 0
    dbank = 0
    for fb in range(NFB):
        ab = c.nact % 2
        c.nact += 1
        actT = c.actT[ab]
        wd_tiles = []
        for half in range(2):
            wt = fb * 2 + half
            wb = c.nwt % 2
            c.nwt += 1
            kb.dma('pool', c.wg[wb][:], wg_v[:, :, wt * 256:(wt + 1) * 256], w=[('wg', wb)])
            kb.dma('pool', c.wu[wb][:], wu_v[:, :, wt * 256:(wt + 1) * 256], w=[('wu', wb)])
            db = c.nwd % 4
            c.nwd += 1
            kb.dma('pool', c.wd[db][:],
                   wd_d[wt * 256:(wt + 1) * 256, :].rearrange("(fc p) d -> p fc d", p=128), w=[('wd', db)])
            wd_tiles.append(db)
            for j in range(2):
                fcl = half * 2 + j
                for (t0, tn) in c.TB:
                    bg = gbank % 4
                    bu = (gbank + 1) % 4
                    gbank += 2
                    kb.mm(c.bank(bg, tn), [(c.wg[wb][:, kc, j * 128:(j + 1) * 128], c.hT[:, kc, t0:t0 + tn])
                                           for kc in range(16)],
                          r=[('wg', wb)] + [('hT', kc, t0) for kc in range(16)], w=[('ps', bg)])
                    kb.mm(c.bank(bu, tn), [(c.wu[wb][:, kc, j * 128:(j + 1) * 128], c.hT[:, kc, t0:t0 + tn])
                                           for kc in range(16)],
                          r=[('wu', wb)] + [('hT', kc, t0) for kc in range(16)], w=[('ps', bu)])
                    ti = c.ntmp % 2
                    c.ntmp += 1
                    tmp = c.tmp[ti]
                    kb.op('act', lambda e: e.activation(out=tmp[:, :tn], in_=c.bank(bg, tn), func=AF.Silu),
                          r=[('ps', bg)], w=[('tmp', ti)])
                    kb.op('dve', lambda e: e.tensor_tensor(out=actT[:, fcl, t0:t0 + tn], in0=c.bank(bu, tn),
                                                           in1=tmp[:, :tn], op=ALU.mult),
                          r=[('ps', bu), ('tmp', ti)], w=[('actT', ab, fcl, t0)])
        for dc in range(16):
            for (t0, tn) in c.TB:
                bd = 4 + dbank % 4
                dbank += 1
                kb.mm(c.bank(bd, tn),
                      [(c.wd[wd_tiles[fcl // 2]][:, fcl % 2, dc * 128:(dc + 1) * 128], actT[:, fcl, t0:t0 + tn])
                       for fcl in range(4)],
                      r=[('wd', wd_tiles[0]), ('wd', wd_tiles[1])] + [('actT', ab, fcl, t0) for fcl in range(4)],
                      w=[('ps', bd)])
                if True:
                    kb.op('dve', lambda e: e.scalar_tensor_tensor(
                        out=c.xT[:, dc, t0:t0 + tn], in0=c.bank(bd, tn), scalar=c.half_sb[:, 0:1], in1=c.xT[:, dc, t0:t0 + tn],
                        op0=ALU.mult, op1=ALU.add),
                        r=[('ps', bd), ('xT', dc, t0), 'half'], w=[('xT', dc, t0)])


def load_small(c, name, d_ap, shape, dtype=F32, q='sp'):
    t = SB(c.nc, name, list(shape), dtype)
    c.kb.dma(q, t[:], d_ap, w=[name])
    return t


def load_xT(c, x_d):
    v = x_d.rearrange("(kc p) t -> p kc t", p=128)
    for kc in range(0, 16, 4):
        c.kb.dma('sp', c.xT[:, kc:kc + 4, :], v[:, kc:kc + 4, :],
                 w=[('xT', k, t0) for k in range(kc, kc + 4) for (t0, _) in c.TB])


def store_T(c, out_d, src, skey):
    v = out_d.rearrange("(kc p) t -> p kc t", p=128)
    for kc in range(0, 16, 4):
        c.kb.dma('sp', v[:, kc:kc + 4, :], src[:, kc:kc + 4, :],
                 r=[(skey, k, t0) for k in range(kc, kc + 4) for (t0, _) in c.TB], w=[('out', kc)])


def build_ffn_test(T, dff=DFF, stage=2):
    nc = bass.Bass("TRN2", target_bir_lowering=False)
    x_d = nc.dram_tensor("xTd", [D, T], F32, kind="ExternalInput").ap()
    g1 = nc.dram_tensor("g1", [128, 16], F32, kind="ExternalInput").ap()
    g2 = nc.dram_tensor("g2", [128, 16], F32, kind="ExternalInput").ap()
    wg = nc.dram_tensor("wgd", [D, dff], F32, kind="ExternalInput").ap()
    wu = nc.dram_tensor("wud", [D, dff], F32, kind="ExternalInput").ap()
    wd = nc.dram_tensor("wdd", [dff, D], F32, kind="ExternalInput").ap()
    y_d = nc.dram_tensor("yTd", [D, T], BF16, kind="ExternalOutput").ap()
    c = Core(nc, T)
    g1s = load_small(c, 'g1s', g1, [128, 16])
    g2s = load_small(c, 'g2s', g2, [128, 16])
    load_xT(c, x_d)
    rmsnorm_T(c, g1s, 'g1s')
    if stage >= 1:
        ffn_T(c, wg, wu, wd, dff)
    if stage >= 2:
        rmsnorm_T(c, g2s, 'g2s')
    store_T(c, y_d, c.hT, 'hT')
    c.kb.finish('sp')
    return nc


GELU_C = 0.7978845608028654
RING = [('wg', 0), ('wu', 0), ('wg', 1), ('wu', 1)]


def load_wt(c, w_v, col0, ncols, nkc=16):
    i = getattr(c, 'nring', 0)
    c.nring = i + 1
    name, b = RING[i % 4]
    buf = c.wg[b] if name == 'wg' else c.wu[b]
    c.kb.dma('pool', buf[:, :nkc, :ncols], w_v[:, :, col0:col0 + ncols], w=[(name, b)])
    return buf, (name, b)


def extra_tiles(c):
    nc = c.nc
    c.stg = [SB(nc, 'stg%d' % i, [128, 512], F32) for i in range(2)]
    c.ga = c.tmp[0]
    c.gb = c.tmp[1]
    c.nstg = 0
    c.c1 = SB(nc, 'c1', [128, 1], F32)
    c.kb.op('dve', lambda e: e.memset(c.c1[:], 1.0), w=['c1'])
    c.cg = SB(nc, 'cg', [128, 1], F32)
    c.kb.op('dve', lambda e: e.memset(c.cg[:], 0.044715), w=['cg'])


def gelu_from(c, src, skey, out, okey, P, n):
    kb = c.kb
    kb.op('act', lambda e: e.activation(out=c.ga[:P, :n], in_=src, func=AF.Square), r=[skey], w=[('tmp', 0)])
    kb.op('dve', lambda e: e.tensor_scalar(out=c.ga[:P, :n], in0=c.ga[:P, :n], scalar1=c.cg[:P, 0:1],
                                           scalar2=c.c1[:P, 0:1], op0=ALU.mult, op1=ALU.add),
          r=[('tmp', 0), 'cg', 'c1'], w=[('tmp', 0)])
    kb.op('dve', lambda e: e.tensor_tensor(out=c.ga[:P, :n], in0=src, in1=c.ga[:P, :n], op=ALU.mult),
          r=[skey, ('tmp', 0)], w=[('tmp', 0)])
    kb.op('act', lambda e: e.activation(out=c.gb[:P, :n], in_=c.ga[:P, :n], func=AF.Sigmoid, scale=2.0 * GELU_C),
          r=[('tmp', 0)], w=[('tmp', 1)])
    kb.op('dve', lambda e: e.tensor_tensor(out=out, in0=src, in1=c.gb[:P, :n], op=ALU.mult),
          r=[skey, ('tmp', 1)], w=[okey])


def build_L1(T=1024, dff=DFF, nc=None, over=None, core=None):
    fused = nc is not None
    nc = bass.Bass("TRN2", target_bir_lowering=False) if nc is None else nc
    dt = mk_dt(nc, over)
    x_d = dt("xTd", [D, T])
    g1 = dt("g1", [128, 16])
    g2 = dt("g2", [128, 16])
    wg = dt("wgd", [D, dff])
    wu = dt("wud", [D, dff])
    wd = dt("wdd", [dff, D])
    w_in = dt("w_in", [D, 4096])
    lng = dt("lng", [1, 1024])
    lnb = dt("lnb", [1, 1024])
    wsT_d = dt("wsT", [128, 8, 128])
    tri_d = dt("tri", [128, 128])
    bs_d = dt("bs", [1, 1024])
    x1_o = dt("x1T", [D, T], "ExternalOutput")
    a_o = dt("aT", [1024, T], "ExternalOutput")
    bo_o = dt("boT", [1024, T], "ExternalOutput")
    c = Core(nc, T) if core is None else core
    kb = c.kb
    extra_tiles(c)
    g1s = load_small(c, 'g1s', g1, [128, 16])
    g2s = load_small(c, 'g2s', g2, [128, 16])
    lng_s = load_small(c, 'lng_s', lng.partition_broadcast(128), [128, 1024])
    lnb_s = load_small(c, 'lnb_s', lnb.partition_broadcast(128), [128, 1024])
    bs_s = load_small(c, 'bs_s', bs_d.partition_broadcast(128), [128, 1024])
    wsT_f = load_small(c, 'wsT_f', wsT_d, [128, 8, 128], BF16, q='pool')
    tri_s = load_small(c, 'tri_s', tri_d, [128, 128], BF16, q='pool')
    wsT_m = SB(nc, 'wsT_m', [128, 8, 128], BF16)
    for g in range(8):
        kb.op('dve', lambda e, g=g: e.tensor_tensor(out=wsT_m[:, g, :], in0=wsT_f[:, g, :], in1=tri_s[:],
                                                    op=ALU.mult), r=['wsT_f', 'tri_s'], w=[('wsT_m', g)])
    load_xT(c, x_d)
    rmsnorm_T(c, g1s, 'g1s')
    ffn_T(c, wg, wu, wd, dff)
    store_T(c, x1_o, c.xT, 'xT')
    rmsnorm_T(c, g2s, 'g2s')
    w_v = w_in.rearrange("(kc p) f -> p kc f", p=128)
    hkeys = lambda t0: [('hT', kc, t0) for kc in range(16)]
    gbank = 0
    for ip in range(4):
        bv, kv = load_wt(c, w_v, ip * 256, 256)
        bg_, kg = load_wt(c, w_v, 1024 + ip * 256, 256)
        for j in range(2):
            ch = ip * 2 + j
            for (t0, tn) in c.TB:
                b0 = gbank % 4
                b1 = (gbank + 1) % 4
                gbank += 2
                kb.mm(c.bank(b0, tn), [(bv[:, kc, j * 128:(j + 1) * 128], c.hT[:, kc, t0:t0 + tn]) for kc in range(16)],
                      r=[kv] + hkeys(t0), w=[('ps', b0)])
                kb.mm(c.bank(b1, tn), [(bg_[:, kc, j * 128:(j + 1) * 128], c.hT[:, kc, t0:t0 + tn]) for kc in range(16)],
                      r=[kg] + hkeys(t0), w=[('ps', b1)])
                ti = c.ntmp % 2
                c.ntmp += 1
                si = c.nstg % 2
                c.nstg += 1
                kb.op('act', lambda e: e.activation(out=c.tmp[ti][:, :tn], in_=c.bank(b1, tn), func=AF.Sigmoid),
                      r=[('ps', b1)], w=[('tmp', ti)])
                kb.op('dve', lambda e: e.tensor_tensor(out=c.stg[si][:, :tn], in0=c.bank(b0, tn), in1=c.tmp[ti][:, :tn],
                                                       op=ALU.mult), r=[('ps', b0), ('tmp', ti)], w=[('stg', si)])
                kb.dma('sp', a_o[ch * 128:(ch + 1) * 128, t0:t0 + tn], c.stg[si][:, :tn], r=[('stg', si)],
                       w=[('a_o', ch, t0)])
    for ip in range(4):
        bu_, ku = load_wt(c, w_v, 2048 + ip * 256, 256)
        for j in range(2):
            g = ip * 2 + j
            for (t0, tn) in c.TB:
                b0 = gbank % 4
                gbank += 1
                kb.mm(c.bank(b0, tn), [(bu_[:, kc, j * 128:(j + 1) * 128], c.hT[:, kc, t0:t0 + tn]) for kc in range(16)],
                      r=[ku] + hkeys(t0), w=[('ps', b0)])
                gelu_from(c, c.bank(b0, tn), ('ps', b0), c.actT[g // 4][:, g % 4, t0:t0 + tn], ('uT', g, t0), 128, tn)
    vg = SB(nc, 'vg', [128, 256], F32)
    vln = SB(nc, 'vln', [128, 256], BF16)
    stats = SB(nc, 'stats', [128, 6], F32)
    mv = SB(nc, 'mv', [128, 2], F32)
    NT = T // 128
    for ip in range(4):
        bw, kw_ = load_wt(c, w_v, 3072 + ip * 256, 256)
        for tt in range(NT):
            t0b = (tt * 128 // 512) * 512
            b0 = gbank % 4
            gbank += 1
            kb.mm(c.bank(b0, 256), [(c.hT[:, kc, tt * 128:(tt + 1) * 128], bw[:, kc, 0:256]) for kc in range(16)],
                  r=[kw_] + hkeys(t0b), w=[('ps', b0)])
            gelu_from(c, c.bank(b0, 256), ('ps', b0), vg[:, :], 'vg', 128, 256)
            for gg in range(2):
                g = ip * 2 + gg
                sl = slice(gg * 128, (gg + 1) * 128)
                gsl = slice(g * 128, (g + 1) * 128)
                kb.op('dve', lambda e: e.bn_stats(out=stats[:], in_=vg[:, sl]), r=['vg'], w=['stats'])
                kb.op('dve', lambda e: e.bn_aggr(out=mv[:], in_=stats[:]), r=['stats'], w=['mv'])
                kb.op('act', lambda e: e.activation(out=mv[:, 1:2], in_=mv[:, 1:2], func=AF.Sqrt, bias=c.eps_sb[:, 0:1]),
                      r=['mv', 'eps'], w=['mv'])
                kb.op('dve', lambda e: e.reciprocal(out=mv[:, 1:2], in_=mv[:, 1:2]), r=['mv'], w=['mv'])
                kb.op('dve', lambda e: e.tensor_scalar(out=vg[:, sl], in0=vg[:, sl], scalar1=mv[:, 0:1],
                                                       scalar2=mv[:, 1:2], op0=ALU.subtract, op1=ALU.mult),
                      r=['vg', 'mv'], w=['vg'])
                kb.op('dve', lambda e: e.tensor_tensor(out=vg[:, sl], in0=vg[:, sl], in1=lng_s[:, gsl], op=ALU.mult),
                      r=['vg', 'lng_s'], w=['vg'])
                kb.op('dve', lambda e: e.tensor_tensor(out=vln[:, sl], in0=vg[:, sl], in1=lnb_s[:, gsl], op=ALU.add),
                      r=['vg', 'lnb_s'], w=[('vln', gg)])
                b1 = 4 + (gbank % 2)
                gbank += 1
                kb.mm(c.bank(b1, 128), [(vln[:, sl], wsT_m[:, g, :])], r=[('vln', gg), ('wsT_m', g)], w=[('ps', b1)])
                si = c.nstg % 2
                c.nstg += 1
                kb.op('dve', lambda e: e.tensor_tensor(out=c.stg[si][:, :128], in0=c.bank(b1, 128), in1=bs_s[:, gsl],
                                                       op=ALU.add), r=[('ps', b1), 'bs_s'], w=[('stg', si)])
                kb.op('dve', lambda e: e.tensor_tensor(out=c.stg[si][:, :128], in0=c.stg[si][:, :128],
                                                       in1=c.actT[g // 4][:, g % 4, tt * 128:(tt + 1) * 128], op=ALU.mult),
                      r=[('stg', si), ('uT', g, t0b)], w=[('stg', si)])
                kb.dma('sp', bo_o[g * 128:(g + 1) * 128, tt * 128:(tt + 1) * 128], c.stg[si][:, :128], r=[('stg', si)],
                       w=[('bo_o', g, tt)])
    if fused:
        kb.barrier()
        return nc
    kb.finish('sp')
    return nc


def proj_out_T(c, w_d, ncols_total, out_d, gbank0=0):
    kb = c.kb
    w_v = w_d.rearrange("(kc p) f -> p kc f", p=128)
    gbank = gbank0
    col = 0
    while col < ncols_total:
        ncol_t = min(256, ncols_total - col)
        buf, key = load_wt(c, w_v, col, ncol_t)
        j0 = 0
        while j0 < ncol_t:
            m = min(128, ncol_t - j0)
            for (t0, tn) in c.TB:
                b0 = gbank % 4
                gbank += 1
                kb.mm(c.ps[:m, b0 * 512:b0 * 512 + tn],
                      [(buf[:, kc, j0:j0 + m], c.hT[:, kc, t0:t0 + tn]) for kc in range(16)],
                      r=[key] + [('hT', kc, t0) for kc in range(16)], w=[('ps', b0)])
                si = c.nstg % len(c.stg)
                c.nstg += 1
                if si % 2 == 0:
                    kb.op('act', lambda e: e.activation(out=c.stg[si][:m, :tn], in_=c.ps[:m, b0 * 512:b0 * 512 + tn],
                                                        func=AF.Copy), r=[('ps', b0)], w=[('stg', si)])
                else:
                    kb.op('dve', lambda e: e.tensor_copy(out=c.stg[si][:m, :tn], in_=c.ps[:m, b0 * 512:b0 * 512 + tn]),
                          r=[('ps', b0)], w=[('stg', si)])
                kb.dma('sp', out_d[col + j0:col + j0 + m, t0:t0 + tn], c.stg[si][:m, :tn], r=[('stg', si)],
                       w=[('z_o', col + j0, t0)])
            j0 += m
        col += ncol_t
    return gbank


def proj_resid_T(c, w_d, gbank0=0):
    kb = c.kb
    w_v = w_d.rearrange("(kc p) f -> p kc f", p=128)
    gbank = gbank0
    for dcp in range(8):
        buf, key = load_wt(c, w_v, dcp * 256, 256)
        for j in range(2):
            dc = dcp * 2 + j
            for (t0, tn) in c.TB:
                b0 = gbank % 4
                gbank += 1
                kb.mm(c.bank(b0, tn), [(buf[:, kc, j * 128:(j + 1) * 128], c.hT[:, kc, t0:t0 + tn]) for kc in range(16)],
                      r=[key] + [('hT', kc, t0) for kc in range(16)], w=[('ps', b0)])
                kb.op('dve', lambda e: e.tensor_tensor(out=c.xT[:, dc, t0:t0 + tn], in0=c.bank(b0, tn),
                                                       in1=c.xT[:, dc, t0:t0 + tn], op=ALU.add),
                      r=[('ps', b0), ('xT', dc, t0)], w=[('xT', dc, t0)])
    return gbank


def build_L2(T=1024, dff=DFF, nz=5168, nc=None, over=None, core=None):
    fused = nc is not None
    nc = bass.Bass("TRN2", target_bir_lowering=False) if nc is None else nc
    dt = mk_dt(nc, over)
    HALO = 32
    x_d = dt("x1Td", [D, T])
    a_d = None if fused else dt("aTh", [1024, HALO + T])
    bo_d = dt("boTd", [1024, T])
    cw_d = dt("conv_w", [128, 8, 31])
    cb_d = dt("conv_b", [128, 8])
    cg_d = dt("cln_g", [128, 8])
    cbb_d = dt("cln_b", [128, 8])
    wout_d = dt("w_out", [D, D])
    gA = dt("gA", [128, 16]); wgA = dt("wgA", [D, dff]); wuA = dt("wuA", [D, dff]); wdA = dt("wdA", [dff, D])
    gB = dt("gB", [128, 16]); wgB = dt("wgB", [D, dff]); wuB = dt("wuB", [D, dff]); wdB = dt("wdB", [dff, D])
    gM = dt("gM", [128, 16])
    win_d = dt("w_in1", [D, nz])
    x4_o = dt("x4T", [D, T], "ExternalOutput")
    z_o = dt("zT", [nz, T], "ExternalOutput")
    c = Core(nc, T) if core is None else core
    kb = c.kb
    extra_tiles(c)
    c.stg = c.stg + [SB(nc, 'stgx%d' % i, [128, 512], F32) for i in range(2)]
    cw = load_small(c, 'cw', cw_d, [128, 8, 31])
    cb = load_small(c, 'cb', cb_d, [128, 8])
    cg = load_small(c, 'cgn', cg_d, [128, 8])
    cbb = load_small(c, 'cbb', cbb_d, [128, 8])
    gAs = load_small(c, 'gAs', gA, [128, 16])
    gBs = load_small(c, 'gBs', gB, [128, 16])
    gMs = load_small(c, 'gMs', gM, [128, 16])
    kinv = SB(nc, 'kinv', [128, 1], F32)
    kb.op('dve', lambda e: e.memset(kinv[:], 1.0 / 1024), w=['kinv'])
    load_xT(c, x_d)
    kb.dma('pool', c.hT[:, 8:16, :], bo_d.rearrange("(g p) t -> p g t", p=128),
           w=[('hT', k, t0) for k in range(8, 16) for (t0, _) in c.TB])
    idbc = load_small(c, 'idbc', dt("ident", [128, 128]), [128, 128], BF16, q='pool')
    abuf = [c.wg[i][:].rearrange("p a b -> p (a b)") for i in range(2)]
    dgb = [c.wu[i][:].rearrange("p a b -> p (a b)")[:, 0:31 * 128].rearrange("p (k d) -> p k d", d=128)
           for i in range(2)]
    ybuf = [c.wd[i][:].rearrange("p a b -> p (a b)").bitcast(F32) for i in range(4)]
    actkeys = lambda i: [('actT', i, fcl, t0) for fcl in range(4) for (t0, _) in c.TB]
    mr = c.actT[0][:].rearrange("p a b -> p (a b)").bitcast(F32)
    mean = mr[:, 0:T]
    rstd = mr[:, T:2 * T]
    S1 = [6, 7]
    S2 = [4, 5]
    cbank = 0
    for ch in range(8):
        ab = ch % 2
        a_sb = abuf[ab]
        if not fused:
            kb.dma('pool', a_sb[:, 0:HALO + T], a_d[ch * 128:(ch + 1) * 128, :], w=[('wg', ab)])
        else:
            if over.get('aT_prev') is None:
                kb.op('dve', lambda e: e.memset(a_sb[:, 0:HALO], 0.0), w=[('wg', ab)])
            else:
                kb.dma('pool', a_sb[:, 0:HALO], over['aT_prev'][ch * 128:(ch + 1) * 128, T - HALO:T], w=[('wg', ab)])
            kb.dma('pool', a_sb[:, HALO:HALO + T], over['aT_cur'][ch * 128:(ch + 1) * 128, :], r=[('wg', ab)],
                   w=[('wg', ab)])
        dg = dgb[ab]
        kb.op('dve', lambda e: e.tensor_tensor(out=dg, in0=idbc[:].unsqueeze(1).broadcast_to([128, 31, 128]),
                                               in1=cw[:, ch, :].unsqueeze(2).broadcast_to([128, 31, 128]), op=ALU.mult),
              r=['idbc', 'cw'], w=[('wu', ab)])
        y = ybuf[ch // 2][:, (ch % 2) * T:(ch % 2) * T + T]
        ykey = ('wd', ch // 2)
        for (t0, tn) in c.TB:
            b0 = cbank % 4
            cbank += 1
            kb.mm(c.bank(b0, tn), [(dg[:, k, :], a_sb[:, 2 + k + t0:2 + k + t0 + tn]) for k in range(31)],
                  r=[('wg', ab), ('wu', ab)], w=[('ps', b0)])
            kb.op('dve', lambda e: e.tensor_scalar(out=y[:, t0:t0 + tn], in0=c.bank(b0, tn), scalar1=cb[:, ch:ch + 1],
                                                   scalar2=None, op0=ALU.add), r=[('ps', b0), 'cb'], w=[ykey])
        for bi, (t0, tn) in enumerate(c.TB):
            i = c.nsq % 2
            c.nsq += 1
            kb.op('act', lambda e: e.activation(out=c.sq[i][:, :tn], in_=y[:, t0:t0 + tn], func=AF.Copy),
                  r=[ykey], w=[('sq', i)])
            kb.mm1(c.bank(S1[bi], tn), c.ones[:], c.sq[i][:, :tn], ch == 0, ch == 7,
                   r=['ones', ('sq', i)], w=([('ps', S1[bi])] if ch in (0, 7) else []))
            i = c.nsq % 2
            c.nsq += 1
            kb.op('act', lambda e: e.activation(out=c.sq[i][:, :tn], in_=y[:, t0:t0 + tn], func=AF.Square),
                  r=[ykey], w=[('sq', i)])
            kb.mm1(c.bank(S2[bi], tn), c.ones[:], c.sq[i][:, :tn], ch == 0, ch == 7,
                   r=['ones', ('sq', i)], w=([('ps', S2[bi])] if ch in (0, 7) else []))
    for bi, (t0, tn) in enumerate(c.TB):
        kb.op('act', lambda e: e.activation(out=mean[:, t0:t0 + tn], in_=c.bank(S1[bi], tn), func=AF.Copy,
                                            scale=1.0 / 1024), r=[('ps', S1[bi])], w=actkeys(0))
        kb.op('dve', lambda e: e.tensor_tensor(out=c.tmp[0][:, :tn], in0=mean[:, t0:t0 + tn], in1=mean[:, t0:t0 + tn],
                                               op=ALU.mult), r=actkeys(0), w=[('tmp', 0)])
        kb.op('dve', lambda e: e.scalar_tensor_tensor(out=rstd[:, t0:t0 + tn], in0=c.bank(S2[bi], tn), scalar=kinv[:, 0:1],
                                                      in1=c.tmp[0][:, :tn], op0=ALU.mult, op1=ALU.subtract),
              r=[('ps', S2[bi]), 'kinv', ('tmp', 0)], w=actkeys(0))
        kb.op('act', lambda e: e.activation(out=rstd[:, t0:t0 + tn], in_=rstd[:, t0:t0 + tn], func=AF.Sqrt,
                                            bias=c.eps_sb[:, 0:1]), r=actkeys(0) + ['eps'], w=actkeys(0))
        kb.op('dve', lambda e: e.reciprocal(out=rstd[:, t0:t0 + tn], in_=rstd[:, t0:t0 + tn]), r=actkeys(0), w=actkeys(0))
    for ch in range(8):
        y = ybuf[ch // 2][:, (ch % 2) * T:(ch % 2) * T + T]
        ykey = ('wd', ch // 2)
        for (t0, tn) in c.TB:
            kb.op('dve', lambda e: e.tensor_tensor(out=y[:, t0:t0 + tn], in0=y[:, t0:t0 + tn], in1=mean[:, t0:t0 + tn],
                                                   op=ALU.subtract), r=[ykey] + actkeys(0), w=[ykey])
            kb.op('dve', lambda e: e.tensor_tensor(out=y[:, t0:t0 + tn], in0=y[:, t0:t0 + tn], in1=rstd[:, t0:t0 + tn],
                                                   op=ALU.mult), r=[ykey] + actkeys(0), w=[ykey])
            kb.op('act', lambda e: e.activation(out=c.hT[:, ch, t0:t0 + tn], in_=y[:, t0:t0 + tn], func=AF.Silu,
                                                scale=cg[:, ch:ch + 1], bias=cbb[:, ch:ch + 1]),
                  r=[ykey, 'cgn', 'cbb'], w=[('hT', ch, t0)])
    gb_ = proj_resid_T(c, wout_d)
    rmsnorm_T(c, gAs, 'gAs')
    ffn_T(c, wgA, wuA, wdA, dff)
    rmsnorm_T(c, gBs, 'gBs')
    ffn_T(c, wgB, wuB, wdB, dff)
    store_T(c, x4_o, c.xT, 'xT')
    rmsnorm_T(c, gMs, 'gMs')
    proj_out_T(c, win_d, nz, z_o)
    if fused:
        kb.barrier()
        return nc
    kb.finish('sp')
    return nc


def build_L4(T=1024, dff=DFF, nc=None, over=None, core=None):
    fused = nc is not None
    nc = bass.Bass("TRN2", target_bir_lowering=False) if nc is None else nc
    dt = mk_dt(nc, over)
    x_d = dt("x4Td", [D, T])
    o_d = None if fused else dt("oTd", [D, T])
    wout_d = dt("w_out1", [D, D])
    gA = dt("gA", [128, 16]); wgA = dt("wgA", [D, dff]); wuA = dt("wuA", [D, dff]); wdA = dt("wdA", [dff, D])
    gF = dt("gF", [128, 16])
    y_o = dt("yT", [D, T], "ExternalOutput")
    c = Core(nc, T) if core is None else core
    kb = c.kb
    gAs = load_small(c, 'gAs', gA, [128, 16])
    gFs = load_small(c, 'gFs', gF, [128, 16])
    load_xT(c, x_d)
    if not fused:
        ov = o_d.rearrange("(kc p) t -> p kc t", p=128)
        for k0 in range(0, 16, 8):
            kb.dma('pool', c.hT[:, k0:k0 + 8, :], ov[:, k0:k0 + 8, :],
                   w=[('hT', k, t0) for k in range(k0, k0 + 8) for (t0, _) in c.TB])
    else:
        extra_tiles(c)
        fl = load_small(c, 'flag', over['flag'], [128, 2])
        x1v = over['x4T_1'].rearrange("(kc p) t -> p kc t", p=128)
        for kc in range(16):
            for (t0, tn) in c.TB:
                si = c.nstg % 2
                c.nstg += 1
                kb.dma('sp', c.stg[si][:, :tn], x1v[:, kc, t0:t0 + tn], w=[('stg', si)])
                kb.op('dve', lambda e: e.tensor_scalar(out=c.xT[:, kc, t0:t0 + tn], in0=c.xT[:, kc, t0:t0 + tn],
                                                       scalar1=fl[:, 0:1], scalar2=None, op0=ALU.mult),
                      r=[('xT', kc, t0), 'flag'], w=[('xT', kc, t0)])
                kb.op('dve', lambda e: e.scalar_tensor_tensor(out=c.xT[:, kc, t0:t0 + tn], in0=c.stg[si][:, :tn],
                                                              scalar=fl[:, 1:2], in1=c.xT[:, kc, t0:t0 + tn],
                                                              op0=ALU.mult, op1=ALU.add),
                      r=[('stg', si), ('xT', kc, t0), 'flag'], w=[('xT', kc, t0)])
        otm = [SB(nc, 'otm%d' % i, [128, 2048], BF16) for i in range(2)]
        otn = [SB(nc, 'otn%d' % i, [128, 2048], BF16) for i in range(2)]
        idb = load_small(c, 'idb4', over['ident'], [128, 128], BF16, q='pool')
        ntr = 0
        for tt in range(T // 128):
            i = tt % 2
            t0b = (tt * 128 // 512) * 512
            kb.dma('pool', otm[i][:], over['o_tok'][tt * 128:(tt + 1) * 128, :], w=[('otm', i)])
            kb.dma('pool', otn[i][:], over['o_tok1'][tt * 128:(tt + 1) * 128, :], w=[('otn', i)])
            kb.op('dve', lambda e: e.tensor_scalar(out=otm[i][:], in0=otm[i][:], scalar1=fl[:, 0:1], scalar2=None,
                                                   op0=ALU.mult), r=[('otm', i), 'flag'], w=[('otm', i)])
            kb.op('dve', lambda e: e.scalar_tensor_tensor(out=otm[i][:], in0=otn[i][:], scalar=fl[:, 1:2], in1=otm[i][:],
                                                          op0=ALU.mult, op1=ALU.add),
                  r=[('otn', i), ('otm', i), 'flag'], w=[('otm', i)])
            for k4 in range(4):
                tb = 2 + ntr % 2
                ntr += 1
                tbk = c.ps[:, tb * 512:(tb + 1) * 512].bitcast(BF16)
                pe_multi(kb, [(lambda e, j=j: e.transpose(out=tbk[:, j * 128:(j + 1) * 128],
                                                          in_=otm[i][:, (k4 * 4 + j) * 128:(k4 * 4 + j + 1) * 128],
                                                          identity=idb[:])) for j in range(4)],
                         r=[('otm', i), 'idb4'], w=[('ps', tb)])
                kb.op('dve', lambda e: e.tensor_copy(out=c.hT[:, k4 * 4:k4 * 4 + 4, tt * 128:(tt + 1) * 128],
                                                     in_=tbk[:, 0:512].rearrange("p (a b) -> p a b", b=128)),
                      r=[('ps', tb)], w=[('hT', k, t0b) for k in range(k4 * 4, k4 * 4 + 4)])
    proj_resid_T(c, wout_d)
    rmsnorm_T(c, gAs, 'gAs')
    ffn_T(c, wgA, wuA, wdA, dff)
    rmsnorm_T(c, gFs, 'gFs', out=c.xT, okey='xT')
    store_T(c, y_o, c.xT, 'xT')
    if fused:
        kb.barrier()
        return nc
    kb.finish('sp')
    return nc


S = 2048
NQT = 16
SCALE = 128 ** -0.5
MAGIC = 12582912.0
PI = 3.141592653589793


def pe_multi(kb, fns, r=(), w=()):
    deps = kb._deps(r, w)
    kb._emit_waits('pe', deps, skip_sem=kb.sem['pe'].num)
    inst = None
    for f in fns:
        inst = f(kb.nc.tensor)
    kb.cnt['pe'] += 1
    inst.then_inc(kb.sem['pe'], 1)
    tok = (kb.sem['pe'].num, kb.cnt['pe'])
    kb._commit(tok, r, w)
    return tok


def build_L3(nc=None, over=None, kb=None, ps=None, gp=0):
    fused = nc is not None
    nc = bass.Bass("TRN2", target_bir_lowering=False) if nc is None else nc
    dt = mk_dt(nc, over)
    if not fused:
        q_d = dt("qT", [8, 128, S]); qp_d = dt("qPT", [8, 128, S])
        ks_d = dt("ksT", [2, 128, S]); ksp_d = dt("ksPT", [2, 128, S])
        kw_d = dt("kwT", [2, 128, S]); kwp_d = dt("kwPT", [2, 128, S])
        kc_d = dt("kcT", [128, 2, S]); vc_d = dt("vcT", [128, 2, S])
        vs_d = dt("vs", [2, S, 128]); vw_d = dt("vw", [2, S, 128])
        gl_d = dt("gl", [S, 24])
    else:
        zT = over['zT']
        rows = lambda base, i: zT[base + i * 128:base + (i + 1) * 128, :]
        q_d = [rows(0, gp * 8 + hh) for hh in range(8)]
        qp_d = [None] * 8
        ks_d = [rows(3072, 2 * gp + g) for g in range(2)]; ksp_d = [None] * 2
        kw_d = [rows(4096, 2 * gp + g) for g in range(2)]; kwp_d = [None] * 2
        kc_d = zT[2048 + 2 * gp * 128:2048 + (2 * gp + 2) * 128, :].rearrange("(g p) s -> p g s", p=128)
        vc_d = zT[2560 + 2 * gp * 128:2560 + (2 * gp + 2) * 128, :].rearrange("(g p) s -> p g s", p=128)
        vsT_d = [rows(3584, 2 * gp + g) for g in range(2)]
        vwT_d = [rows(4608, 2 * gp + g) for g in range(2)]
        glT_d = zT[5120 + gp * 24:5120 + (gp + 1) * 24, :]
    pos_d = dt("pos", [1, S], dtype=I32)
    inv_d = dt("inv", [128, 1]); sgn_d = dt("sgn", [128, 1])
    pek_d = dt("pekT", [128, 32]); w1k_d = dt("w1k", [4096, 256]); w2k_d = dt("w2k", [256, 128])
    pev_d = dt("pevT", [128, 32]); w1v_d = dt("w1v", [4096, 256]); w2v_d = dt("w2v", [256, 128])
    ovl_d = dt("ovl", [128, 32]); cm_d = dt("cm", [128, 16, 128]); fbv_d = dt("fbv", [128, 16, 32])
    tri_d = dt("tri", [128, 128]); tri2_d = dt("tri2", [128, 128]); id_d = dt("ident", [128, 128])
    o_o = dt("o", [S, 1024], "ExternalOutput")
    kb = KB(nc) if kb is None else kb
    A = lambda name, shape, dtype=F32: SB(nc, 's_' + name, list(shape), dtype)

    def small(name, d_ap, shape, dtype=F32, q='sp'):
        t = A(name, shape, dtype)
        kb.dma(q, t[:], d_ap, w=[name])
        return t

    def const(name, val):
        t = A(name, [128, 1])
        kb.op('dve', lambda e: e.memset(t[:], val), w=[name])
        return t
    ps = nc.alloc_psum_tensor('ps', [128, 8 * 512], F32) if ps is None else ps
    bank = lambda b, n=512, p=128: ps[:p, b * 512:b * 512 + n]
    bankbf = lambda b: ps[:, b * 512:(b + 1) * 512].bitcast(BF16)
    H = S // 2
    xs = [A('xs%d' % i, [128, H]) for i in range(2)]
    xp = [A('xp%d' % i, [128, H]) for i in range(2)]
    inv = small('inv', inv_d, [128, 1]); sgn = small('sgn', sgn_d, [128, 1])
    ovl = small('ovl', ovl_d, [128, 32]); cm = small('cm', cm_d, [128, 16, 128], BF16, 'pool'); fbv = small('fbv', fbv_d, [128, 16, 32])
    tri = small('tri', tri_d, [128, 128], BF16, 'pool'); tri2 = small('tri2', tri2_d, [128, 128], BF16, 'pool')
    idf = small('idf', id_d, [128, 128]); idb = small('idb', id_d, [128, 128], BF16, 'pool')
    if not fused:
        gl = small('gl', gl_d.rearrange("(n p) c -> p n c", p=128), [128, 16, 24])
        kb.op('act', lambda e: e.activation(out=gl[:], in_=gl[:], func=AF.Sigmoid), r=['gl'], w=['gl'])
    else:
        gl = A('gl', [128, 16, 24])
        for hf in range(2):
            kb.dma('sp', xs[hf][:24, :], glT_d[:, hf * H:(hf + 1) * H], w=[('xs', hf)])
        for kt in range(16):
            pe_multi(kb, [lambda e: e.transpose(out=bank(7, 24), in_=xs[kt // 8][:24, (kt % 8) * 128:(kt % 8 + 1) * 128],
                                                identity=idf[:24, :24])], r=[('xs', kt // 8), 'idf'], w=[('ps', 7)])
            kb.op('act', lambda e: e.activation(out=gl[:, kt, :], in_=bank(7, 24), func=AF.Sigmoid),
                  r=[('ps', 7)], w=['gl'])
    c_i2p = const('c_i2p', 1.0 / (2 * PI)); c_mag = const('c_mag', MAGIC); c_nmag = const('c_nmag', -MAGIC)
    c_n2p = const('c_n2p', -2 * PI); c_pi = const('c_pi', PI); c_npi = const('c_npi', -PI); c_hpi = const('c_hpi', PI / 2)
    c_tiny = const('c_tiny', 1e-30); c_one = const('c_one', 1.0); c_cg = const('c_cg', 0.044715)
    c_nsc = const('c_nsc', -SCALE)
    posi = A('posi', [128, S], I32)
    kb.dma('sp', posi[:], pos_d.partition_broadcast(128), w=['posi'])
    ang = A('ang', [128, S]); cosT = A('cosT', [128, S]); sinT = A('sinT', [128, S])
    kb.op('dve', lambda e: e.tensor_copy(out=ang[:], in_=posi[:]), r=['posi'], w=['ang'])
    kb.op('dve', lambda e: e.tensor_scalar(out=ang[:], in0=ang[:], scalar1=inv[:, 0:1], scalar2=None, op0=ALU.mult),
          r=['ang', 'inv'], w=['ang'])

    def sin_of(dst, dkey, shift):
        wk = posi[:].bitcast(F32)
        src = ang
        if shift is not None:
            kb.op('dve', lambda e: e.tensor_scalar(out=dst[:], in0=ang[:], scalar1=shift[:, 0:1], scalar2=None, op0=ALU.add),
                  r=['ang'], w=[dkey])
            src = dst
        kb.op('dve', lambda e: e.tensor_scalar(out=wk, in0=src[:], scalar1=c_i2p[:, 0:1], scalar2=c_mag[:, 0:1],
                                               op0=ALU.mult, op1=ALU.add), r=[dkey, 'ang', 'posi'], w=['posi'])
        kb.op('dve', lambda e: e.tensor_scalar(out=wk, in0=wk, scalar1=c_nmag[:, 0:1], scalar2=None, op0=ALU.add),
              r=['posi'], w=['posi'])
        kb.op('dve', lambda e: e.scalar_tensor_tensor(out=dst[:], in0=wk, scalar=c_n2p[:, 0:1], in1=src[:],
                                                      op0=ALU.mult, op1=ALU.add), r=['posi', 'ang', dkey], w=[dkey])
        kb.op('dve', lambda e: e.tensor_scalar(out=dst[:], in0=dst[:], scalar1=c_pi[:, 0:1], scalar2=c_npi[:, 0:1],
                                               op0=ALU.min, op1=ALU.max), r=[dkey], w=[dkey])
        kb.op('act', lambda e: e.activation(out=dst[:], in_=dst[:], func=AF.Sin), r=[dkey], w=[dkey])
    sin_of(sinT, 'sinT', None)
    kb.op('dve', lambda e: e.tensor_scalar(out=sinT[:], in0=sinT[:], scalar1=sgn[:, 0:1], scalar2=None, op0=ALU.mult),
          r=['sinT', 'sgn'], w=['sinT'])
    sin_of(cosT, 'cosT', c_hpi)
    qT = A('qTb', [128, 8, S], BF16); qr = A('qr', [128, 8, S], BF16)
    ksr = A('ksr', [128, 2, S], BF16); kwr = A('kwr', [128, 2, S], BF16)
    kcT = A('kcTb', [128, 2, S], BF16); vcT = A('vcTb', [128, 2, S], BF16)
    vs = A('vsb', [128, 2, 16, 132], BF16); vw = A('vwb', [128, 2, 16, 132], BF16)
    kb.dma('pool', kcT[:], kc_d, w=['kcT'])
    kb.dma('pool', vcT[:], vc_d, w=['vcT'])
    kb.op('dve', lambda e: e.memset(vs[:, :, :, 128:129], 1.0), w=['vs1'])
    kb.op('dve', lambda e: e.memset(vw[:, :, :, 128:129], 1.0), w=['vw1'])
    if not fused:
        for g in range(2):
            kb.dma('pool', vs[:, g, :, 0:128], vs_d[g].rearrange("(n p) d -> p n d", p=128), w=[('vs', g)])
            kb.dma('pool', vw[:, g, :, 0:128], vw_d[g].rearrange("(n p) d -> p n d", p=128), w=[('vw', g)])
    else:
        vtmp = [xp[i][:].bitcast(BF16) for i in range(2)]
        nv = 0
        for g in range(2):
            for (srcs, dstt, key) in ((vsT_d, vs, 'vs'), (vwT_d, vw, 'vw')):
                vi = nv % 2
                nv += 1
                kb.dma('pool', vtmp[vi], srcs[g], w=[('xp', vi)])
                for k4 in range(4):
                    tb = 2 + k4 % 2
                    tbk = bankbf(tb)
                    pe_multi(kb, [(lambda e, j=j: e.transpose(out=tbk[:, j * 128:(j + 1) * 128],
                                                              in_=vtmp[vi][:, (k4 * 4 + j) * 128:(k4 * 4 + j + 1) * 128],
                                                              identity=idb[:])) for j in range(4)],
                             r=[('xp', vi), 'idb'], w=[('ps', tb)])
                    kb.op('dve', lambda e: e.tensor_copy(out=dstt[:, g, k4 * 4:k4 * 4 + 4, 0:128],
                                                         in_=tbk[:, 0:512].rearrange("p (a b) -> p a b", b=128)),
                          r=[('ps', tb)], w=[(key, g)])
    nst = [0]

    def rope(src_d, srcp_d, dst, dkey, plain=None):
        for hf in range(2):
            i = nst[0] % 2
            nst[0] += 1
            sl = slice(hf * H, (hf + 1) * H)
            kb.dma('sp', xs[i][:], src_d[:, sl], w=[('xs', i)])
            if srcp_d is not None:
                kb.dma('sp', xp[i][:], srcp_d[:, sl], w=[('xp', i)])
            else:
                kb.dma('sp', xp[i][0:16, :], src_d[16:32, sl], w=[('xp', i)])
                kb.dma('sp', xp[i][16:32, :], src_d[0:16, sl], r=[('xp', i)], w=[('xp', i)])
                kb.dma('sp', xp[i][32:128, :], src_d[32:128, sl], r=[('xp', i)], w=[('xp', i)])
            if plain is not None:
                kb.op('act', lambda e: e.activation(out=plain[:, sl], in_=xs[i][:], func=AF.Copy), r=[('xs', i)],
                      w=[(dkey, 'plain', hf)])
            kb.op('dve', lambda e: e.tensor_tensor(out=xs[i][:], in0=xs[i][:], in1=cosT[:, sl], op=ALU.mult),
                  r=[('xs', i), 'cosT'], w=[('xs', i)])
            kb.op('dve', lambda e: e.tensor_tensor(out=xp[i][:], in0=xp[i][:], in1=sinT[:, sl], op=ALU.mult),
                  r=[('xp', i), 'sinT'], w=[('xp', i)])
            kb.op('dve', lambda e: e.tensor_tensor(out=dst[:, sl], in0=xs[i][:], in1=xp[i][:], op=ALU.add),
                  r=[('xs', i), ('xp', i)], w=[(dkey, hf)])
    for hh in range(8):
        rope(q_d[hh], qp_d[hh], qr[:, hh, :], ('qr', hh), plain=qT[:, hh, :])
    for g in range(2):
        rope(ks_d[g], ksp_d[g], ksr[:, g, :], ('ksr', g))
        rope(kw_d[g], kwp_d[g], kwr[:, g, :], ('kwr', g))
    qkeys = lambda hh: [(('qr', hh), 0), (('qr', hh), 1), (('qr', hh), 'plain', 0), (('qr', hh), 'plain', 1)]
    w1 = A('w1', [128, 32, 256], BF16)
    w2 = A('w2', [128, 2, 128], BF16)
    hid = A('hid', [128, 2, 128], BF16)
    hx = A('hx', [128, 128]); ha = A('ha', [128, 128]); hb = A('hb', [128, 128])
    cpe = A('cpe', [128, 2])
    kcmpT = A('kcmpT', [128, 2, 128], BF16)
    vcmp = A('vcmp', [128, 2, 128])
    kb.op('dve', lambda e: e.memset(hid[:], 0.0), w=['hid'])
    kb.op('dve', lambda e: e.memset(kcmpT[:], 0.0), w=['kcmpT'])
    kb.op('dve', lambda e: e.memset(vcmp[:], 0.0), w=['vcmp'])
    for which, (pe_d, w1_d, w2_d, srcT, skey) in enumerate([(pek_d, w1k_d, w2k_d, kcT, 'kcT'), (pev_d, w1v_d, w2v_d, vcT, 'vcT')]):
        pe_b = small('pe_b%d' % which, pe_d, [128, 32], BF16, 'pool')
        for l0 in range(0, 32, 8):
            kb.dma('pool', w1[:, l0:l0 + 8, :], w1_d[l0 * 128:(l0 + 8) * 128, :].rearrange("(l p) h -> p l h", p=128),
                   w=[('w1', l0)])
        kb.dma('pool', w2[:], w2_d.rearrange("(c p) d -> p c d", p=128), w=['w2'])
        w1keys = [('w1', l0) for l0 in range(0, 32, 8)]
        src4 = srcT[:].rearrange("p g (n r) -> p g n r", r=16)
        for hc in range(2):
            kb.mm(bank(7, 1), [(w1[:, l, hc * 128:(hc + 1) * 128], pe_b[:, l:l + 1]) for l in range(32)],
                  r=w1keys + ['pe_b%d' % which], w=[('ps', 7)])
            kb.op('dve', lambda e: e.tensor_copy(out=cpe[:, hc:hc + 1], in_=bank(7, 1)), r=[('ps', 7)], w=[('cpe', hc)])
        for g in range(2):
            for hc in range(2):
                kb.mm(bank(6, 127), [(w1[:, l, hc * 128:(hc + 1) * 128], src4[:, g, (l // 16):(l // 16) + 127, l % 16])
                                     for l in range(32)], r=w1keys + [skey], w=[('ps', 6)])
                kb.op('dve', lambda e: e.tensor_scalar(out=hx[:, :127], in0=bank(6, 127), scalar1=cpe[:, hc:hc + 1],
                                                       scalar2=None, op0=ALU.add), r=[('ps', 6), ('cpe', hc)], w=['hx'])
                kb.op('act', lambda e: e.activation(out=ha[:, :127], in_=hx[:, :127], func=AF.Square), r=['hx'], w=['ha'])
                kb.op('dve', lambda e: e.tensor_scalar(out=ha[:, :127], in0=ha[:, :127], scalar1=c_cg[:, 0:1],
                                                       scalar2=c_one[:, 0:1], op0=ALU.mult, op1=ALU.add), r=['ha'], w=['ha'])
                kb.op('dve', lambda e: e.tensor_tensor(out=ha[:, :127], in0=hx[:, :127], in1=ha[:, :127], op=ALU.mult),
                      r=['hx', 'ha'], w=['ha'])
                kb.op('act', lambda e: e.activation(out=hb[:, :127], in_=ha[:, :127], func=AF.Sigmoid, scale=2.0 * GELU_C),
                      r=['ha'], w=['hb'])
                kb.op('dve', lambda e: e.tensor_tensor(out=hid[:, hc, :127], in0=hx[:, :127], in1=hb[:, :127], op=ALU.mult),
                      r=['hx', 'hb'], w=[('hid', hc)])
            if which == 0:
                kb.mm(bank(6, 127), [(w2[:, hc, :], hid[:, hc, :127]) for hc in range(2)],
                      r=['w2', ('hid', 0), ('hid', 1)], w=[('ps', 6)])
                kb.op('dve', lambda e: e.tensor_copy(out=kcmpT[:, g, :127], in_=bank(6, 127)), r=[('ps', 6)],
                      w=[('kcmpT', g)])
            else:
                kb.mm(bank(6, 128, 127), [(hid[:, hc, :127], w2[:, hc, :]) for hc in range(2)],
                      r=['w2', ('hid', 0), ('hid', 1)], w=[('ps', 6)])
                kb.op('dve', lambda e: e.tensor_copy(out=vcmp[:127, g, :], in_=bank(6, 128, 127)), r=[('ps', 6)],
                      w=[('vcmp', g)])
    sc4s = [A('sc4_%d' % i, [128, 4, 128]) for i in range(2)]
    mxs = [A('mx%d' % i, [128, 1]) for i in range(2)]
    rs4s = [A('rs4_%d' % i, [128, 4]) for i in range(2)]
    pcT4s = [A('pcT4_%d' % i, [128, 4, 128]) for i in range(2)]
    impms = [A('impm%d' % i, [128, 32]) for i in range(2)]
    wk32s = [A('wk32_%d' % i, [128, 32]) for i in range(2)]
    m8s = [A('m8_%d' % i, [128, 8]) for i in range(2)]
    m8bs = [A('m8b_%d' % i, [128, 8]) for i in range(2)]
    sels = [A('sel%d' % i, [128, 32], BF16) for i in range(2)]
    pbuf = [A('pbuf%d' % i, [128, 512], BF16) for i in range(2)]
    pT = [A('pT%d' % i, [128, 512], BF16) for i in range(2)]
    oacc = [A('oacc%d' % i, [128, 4, 128]) for i in range(2)]
    gsc = [A('gsc%d' % i, [128, 1]) for i in range(2)]
    cnt = dict(c=0, j=0)
    X = mybir.AxisListType.X

    def comp_ops(g, qt, p):
        sc4, mx, rs4, pcT4, impm, wk32, m8, m8b, sel = (sc4s[p], mxs[p], rs4s[p], pcT4s[p], impms[p], wk32s[p],
                                                        m8s[p], m8bs[p], sels[p])
        K = lambda n: (n, p)
        oa = oacc[p]
        oakeys = [('oacc', p, h) for h in range(4)]
        qsl = slice(qt * 128, (qt + 1) * 128)
        gview = gl[:, qt, g * 12:(g + 1) * 12].rearrange("p (h c) -> p h c", c=3)[:, :, 0:1]
        ops = []
        ops.append(lambda: pe_multi(kb, [(lambda e, h=h: e.matmul(bank(6, 512)[:, h * 128:(h + 1) * 128],
                                                                   qT[:, g * 4 + h, qsl], kcmpT[:, g, :], start=True,
                                                                   stop=True)) for h in range(4)],
                                    r=[(('qr', g * 4 + h), 'plain', qt // 8) for h in range(4)] + [('kcmpT', g)],
                                    w=[('ps', 6)]))
        ops.append(lambda: kb.op('dve', lambda e: e.reduce_max(out=mx[:], in_=bank(6, 512), axis=X), r=[('ps', 6)],
                                 w=[K('mx')]))
        ops.append(lambda: kb.op('dve', lambda e: e.tensor_scalar(out=mx[:], in0=mx[:], scalar1=c_nsc[:, 0:1],
                                                                  scalar2=None, op0=ALU.mult), r=[K('mx')], w=[K('mx')]))
        ops.append(lambda: kb.op('act', lambda e: e.activation(out=sc4[:].rearrange("p h n -> p (h n)"), in_=bank(6, 512),
                                                               func=AF.Exp, scale=SCALE, bias=mx[:, 0:1]),
                                 r=[('ps', 6), K('mx')], w=[K('sc4')]))
        ops.append(lambda: kb.op('dve', lambda e: e.tensor_tensor(out=sc4[:], in0=sc4[:],
                                                                  in1=cm[:, qt, :].unsqueeze(1).broadcast_to([128, 4, 128]),
                                                                  op=ALU.mult), r=[K('sc4'), 'cm'], w=[K('sc4')]))
        ops.append(lambda: kb.op('dve', lambda e: e.reduce_sum(out=rs4[:], in_=sc4[:], axis=X), r=[K('sc4')], w=[K('rs4')]))
        ops.append(lambda: kb.op('dve', lambda e: e.tensor_scalar(out=rs4[:], in0=rs4[:], scalar1=c_tiny[:, 0:1],
                                                                  scalar2=None, op0=ALU.max), r=[K('rs4')], w=[K('rs4')]))
        ops.append(lambda: kb.op('dve', lambda e: e.reciprocal(out=rs4[:], in_=rs4[:]), r=[K('rs4')], w=[K('rs4')]))
        ops.append(lambda: kb.op('dve', lambda e: e.tensor_tensor(out=sc4[:], in0=sc4[:],
                                                                  in1=rs4[:, :].unsqueeze(2).broadcast_to([128, 4, 128]),
                                                                  op=ALU.mult), r=[K('sc4'), K('rs4')], w=[K('sc4')]))
        ops.append(lambda: pe_multi(kb, [(lambda e, h=h: e.transpose(out=bank(6, 512)[:, h * 128:(h + 1) * 128],
                                                                      in_=sc4[:, h, :], identity=idf[:])) for h in range(4)],
                                    r=[K('sc4'), 'idf'], w=[('ps', 6)]))
        ops.append(lambda: kb.op('act', lambda e: e.activation(out=pcT4[:].rearrange("p h n -> p (h n)"), in_=bank(6, 512),
                                                               func=AF.Copy), r=[('ps', 6)], w=[K('pcT4')]))
        ops.append(lambda: kb.mm(bank(7, 32), [(pcT4[:, h, :], ovl[:, :]) for h in range(4)], r=[K('pcT4'), 'ovl'],
                                 w=[('ps', 7)]))
        ops.append(lambda: pe_multi(kb, [(lambda e, h=h: e.matmul(bank(6, 512)[:, h * 128:(h + 1) * 128], pcT4[:, h, :],
                                                                   vcmp[:, g, :], start=True, stop=True)) for h in range(4)],
                                    r=[K('pcT4'), ('vcmp', g)], w=[('ps', 6)]))
        ops.append(lambda: kb.op('dve', lambda e: e.tensor_tensor(out=oa[:],
                                                                  in0=bank(6, 512).rearrange("p (h d) -> p h d", d=128),
                                                                  in1=gview.broadcast_to([128, 4, 128]), op=ALU.mult),
                                 r=[('ps', 6), 'gl'], w=oakeys))
        ops.append(lambda: kb.op('dve', lambda e: e.tensor_tensor(out=impm[:], in0=bank(7, 32), in1=fbv[:, qt, :],
                                                                  op=ALU.add), r=[('ps', 7), 'fbv'], w=[K('impm')]))
        ops.append(lambda: kb.op('dve', lambda e: e.max(out=m8[:], in_=impm[:]), r=[K('impm')], w=[K('m8')]))
        ops.append(lambda: kb.op('dve', lambda e: e.match_replace(out=wk32[:], in_to_replace=m8[:], in_values=impm[:],
                                                                  imm_value=-3.0e38), r=[K('impm'), K('m8')], w=[K('wk32')]))
        ops.append(lambda: kb.op('dve', lambda e: e.max(out=m8b[:], in_=wk32[:]), r=[K('wk32')], w=[K('m8b')]))
        ops.append(lambda: kb.op('dve', lambda e: e.tensor_scalar(out=sel[:], in0=impm[:], scalar1=m8b[:, 7:8],
                                                                  scalar2=None, op0=ALU.is_ge), r=[K('impm'), K('m8b')],
                                 w=[K('sel')]))
        return ops

    tiles_ = [(g, qt) for g in range(2) for qt in range(NQT)]
    for f_ in comp_ops(0, 0, 0):
        f_()
    for ti_, (g, qt) in enumerate(tiles_):
        if True:
            oi = ti_ % 2
            oa = oacc[oi]
            sel = sels[oi]
            selkey = ('sel', oi)
            oakeys = [('oacc', oi, h) for h in range(4)]
            qsl = slice(qt * 128, (qt + 1) * 128)
            nxt = comp_ops(tiles_[ti_ + 1][0], tiles_[ti_ + 1][1], (ti_ + 1) % 2) if ti_ + 1 < len(tiles_) else []
            chunks = []
            for h in range(4):
                hh = g * 4 + h
                for br in (1, 2):
                    kts = list(range(qt + 1)) if br == 1 else list(range(max(0, qt - 4), qt + 1))
                    parts = [kts[i:i + 4] for i in range(0, len(kts), 4)]
                    accb = 4 + cnt['j'] % 2
                    ji = cnt['j'] % 2
                    cnt['j'] += 1
                    npv = len(kts)
                    ipv = 0
                    for pi_, ch in enumerate(parts):
                        chunks.append(dict(h=h, hh=hh, br=br, ch=ch, accb=accb, ji=ji, ipv0=ipv, npv=npv,
                                           last=(pi_ == len(parts) - 1)))
                        ipv += len(ch)

            def stage1(c_):
                i = c_['idx']
                ch = c_['ch']
                n = 128 * len(ch)
                k0 = ch[0] * 128
                sb = i % 2
                kT, kkey = (ksr[:, g, :], ('ksr', g)) if c_['br'] == 1 else (kwr[:, g, :], ('kwr', g))
                kb.mm(bank(sb, n), [(qr[:, c_['hh'], qsl], kT[:, k0:k0 + n])],
                      r=[(('qr', c_['hh']), qt // 8), (kkey, 0), (kkey, 1)], w=[('ps', sb)])
                pb = pbuf[i % 2]
                pk = ('pbuf', i % 2)
                kb.op('act', lambda e: e.activation(out=pb[:, :n], in_=bank(sb, n), func=AF.Exp, scale=SCALE),
                      r=[('ps', sb)], w=[pk])
                if c_['br'] == 1:
                    nb = 2 * len(ch)
                    kb.op('dve', lambda e: e.tensor_tensor(
                        out=pb[:, :n].rearrange("p (b k) -> p b k", k=64), in0=pb[:, :n].rearrange("p (b k) -> p b k", k=64),
                        in1=sel[:, 2 * ch[0]:2 * ch[0] + nb].unsqueeze(2).broadcast_to([128, nb, 64]), op=ALU.mult),
                        r=[pk, selkey], w=[pk])
                if c_['br'] == 2 and qt >= 4 and ch[0] == qt - 4:
                    kb.op('dve', lambda e: e.tensor_tensor(out=pb[:, 0:128], in0=pb[:, 0:128], in1=tri2[:], op=ALU.mult),
                          r=[pk, 'tri2'], w=[pk])
                if ch[-1] == qt:
                    off = 128 * (len(ch) - 1)
                    kb.op('dve', lambda e: e.tensor_tensor(out=pb[:, off:off + 128], in0=pb[:, off:off + 128], in1=tri[:],
                                                           op=ALU.mult), r=[pk, 'tri'], w=[pk])

            def stage2(c_):
                i = c_['idx']
                ch = c_['ch']
                n = 128 * len(ch)
                pb = pbuf[i % 2]
                tb = 2 + i % 2
                tbk = bankbf(tb)
                pe_multi(kb, [(lambda e, j=j: e.transpose(out=tbk[:, j * 128:(j + 1) * 128], in_=pb[:, j * 128:(j + 1) * 128],
                                                          identity=idb[:])) for j in range(len(ch))],
                         r=[('pbuf', i % 2), 'idb'], w=[('ps', tb)])
                if i % 2 == 0:
                    kb.op('act', lambda e: e.activation(out=pT[0][:, :n], in_=tbk[:, :n], func=AF.Copy), r=[('ps', tb)],
                          w=[('pT', 0)])
                else:
                    kb.op('dve', lambda e: e.tensor_copy(out=pT[1][:, :n], in_=tbk[:, :n]), r=[('ps', tb)], w=[('pT', 1)])

            def stage3(c_):
                i = c_['idx']
                ch = c_['ch']
                accb = c_['accb']
                vaug, vkeys = (vs, [('vs', g), 'vs1']) if c_['br'] == 1 else (vw, [('vw', g), 'vw1'])
                fns = []
                for j, kt in enumerate(ch):
                    ip = c_['ipv0'] + j
                    fns.append(lambda e, j=j, kt=kt, first=(ip == 0), lastm=(ip == c_['npv'] - 1): e.matmul(
                        bank(accb, 129), pT[i % 2][:, j * 128:(j + 1) * 128], vaug[:, g, kt, 0:129], start=first, stop=lastm))
                pe_multi(kb, fns, r=[('pT', i % 2)] + vkeys, w=[('ps', accb)])
                if c_['last']:
                    h = c_['h']
                    gs_ = gsc[c_['ji']]
                    gk = ('gsc', c_['ji'])
                    col = h_col(c_['hh'], c_['br'])
                    kb.op('dve', lambda e: e.reciprocal(out=gs_[:], in_=bank(accb, 129)[:, 128:129]), r=[('ps', accb)], w=[gk])
                    kb.op('dve', lambda e: e.tensor_tensor(out=gs_[:], in0=gs_[:], in1=gl[:, qt, col:col + 1], op=ALU.mult),
                          r=[gk, 'gl'], w=[gk])
                    kb.op('dve', lambda e: e.scalar_tensor_tensor(out=oa[:, h, :], in0=bank(accb, 128), scalar=gs_[:, 0:1],
                                                                  in1=oa[:, h, :], op0=ALU.mult, op1=ALU.add),
                          r=[('ps', accb), gk, ('oacc', oi, h)], w=[('oacc', oi, h)])

            nchk = len(chunks)
            for i, c_ in enumerate(chunks):
                c_['idx'] = cnt['c'] + i
            per = (len(nxt) + nchk - 1) // max(nchk, 1)
            for i in range(nchk + 2):
                if i < nchk:
                    stage1(chunks[i])
                if 0 <= i - 1 < nchk:
                    stage2(chunks[i - 1])
                if 0 <= i - 2 < nchk:
                    stage3(chunks[i - 2])
                for _ in range(per):
                    if nxt:
                        nxt.pop(0)()
            while nxt:
                nxt.pop(0)()
            cnt['c'] += nchk
            kb.dma('sp', o_o[qt * 128:(qt + 1) * 128, g * 512:(g + 1) * 512], oa[:].rearrange("p h d -> p (h d)"),
                   r=oakeys, w=[('o_o', g, qt)])
    if fused:
        kb.barrier()
        return nc
    kb.finish('sp')
    return nc


def h_col(hh, br):
    return hh * 3 + br


def l3_consts():
    i = np.arange(128)[:, None]
    j = np.arange(128)[None, :]
    tri = (j <= i).astype(np.float32)
    tri2 = (j > i).astype(np.float32)
    n = np.arange(128)
    cm = np.zeros((128, 16, 128), np.float32)
    fbv = np.zeros((128, 16, 32), np.float32)
    jb = np.arange(32)
    for qt in range(16):
        t = qt * 128 + np.arange(128)
        cm[:, qt, :] = ((16 * n[None, :] + 31 <= t[:, None]) & (n[None, :] < 127)).astype(np.float32)
        cur = (t // 64)[:, None]
        forced = (jb[None] == 0) | (jb[None] == cur) | (jb[None] == cur - 1)
        valid = jb[None] * 64 <= t[:, None]
        fbv[:, qt, :] = np.where(valid, np.where(forced, 1000.0, 0.0), -1e30)
    ovl = np.zeros((128, 32), np.float32)
    for nn in range(127):
        for jj in range(32):
            if 16 * nn < 64 * jj + 64 and 16 * nn + 31 >= 64 * jj:
                ovl[nn, jj] = 1.0
    d = np.arange(128)
    inv = np.where(d < 32, 500000.0 ** (-(2.0 * (d % 16)) / 32.0), 0.0).astype(np.float32)[:, None]
    sgn = np.where(d < 16, -1.0, np.where(d < 32, 1.0, 0.0)).astype(np.float32)[:, None]
    return dict(tri=tri, tri2=tri2, cm=cm, fbv=fbv, ovl=ovl, inv=inv, sgn=sgn, ident=np.eye(128, dtype=np.float32))


def swap_rot(xT):
    y = xT.copy()
    y[..., 0:16, :] = xT[..., 16:32, :]
    y[..., 16:32, :] = xT[..., 0:16, :]
    return y


def prep_L3(zT_b, pos_b, half, W, consts):
    c = np.ascontiguousarray
    gs = [2 * half, 2 * half + 1]
    qT = c(zT_b[half * 1024:(half + 1) * 1024].reshape(8, 128, S))

    def grp(base):
        return c(np.stack([zT_b[base + g * 128:base + (g + 1) * 128] for g in gs]))
    kc, vc, ks, vs_, kw, vw_ = [grp(2048 + i * 512) for i in range(6)]
    gl = c(zT_b[5120 + half * 24:5120 + (half + 1) * 24].T)
    m = dict(qT=qT, qPT=swap_rot(qT), ksT=ks, ksPT=swap_rot(ks), kwT=kw, kwPT=swap_rot(kw),
             kcT=c(kc.transpose(1, 0, 2)), vcT=c(vc.transpose(1, 0, 2)),
             vs=c(vs_.transpose(0, 2, 1)), vw=c(vw_.transpose(0, 2, 1)), gl=gl,
             pos=c(pos_b.reshape(1, S).astype(np.int32)))
    m.update(W)
    m.update(consts)
    return m


def prep_L3_weights(pe_k, w1_k, w2_k, pe_v, w1_v, w2_v):
    c = np.ascontiguousarray
    return dict(pekT=c(pe_k.T), w1k=c(w1_k), w2k=c(w2_k), pevT=c(pe_v.T), w1v=c(w1_v), w2v=c(w2_v))


_PROGS = {}


def _prog(name, fn):
    if name not in _PROGS:
        _PROGS[name] = fn()
    return _PROGS[name]


def _lay(g, n=16):
    return np.ascontiguousarray(np.asarray(g, np.float32).reshape(n, 128).T)


def kernel_unfused(**inp):
    c = np.ascontiguousarray
    f32 = lambda a: np.asarray(a, dtype=np.float32)
    x = f32(inp['x'])
    pos = np.asarray(inp['positions'])
    T = 1024
    cores = list(range(NCORES))
    tok = lambda ci: (ci // 2, slice((ci % 2) * T, (ci % 2 + 1) * T))
    tri_st = np.triu(np.ones((128, 128), np.float32))
    common = dict(g1=_lay(inp['l0_ffn1_norm']), g2=_lay(inp['l0_mix_norm']), wgd=f32(inp['l0_ffn1_w_gate']),
                  wud=f32(inp['l0_ffn1_w_up']), wdd=f32(inp['l0_ffn1_w_down']), w_in=f32(inp['l0_w_in']),
                  lng=c(f32(inp['l0_gmlp_ln_g']).reshape(1, 1024)), lnb=c(f32(inp['l0_gmlp_ln_b']).reshape(1, 1024)),
                  wsT=c(f32(inp['l0_gmlp_ws']).transpose(2, 0, 1)), tri=tri_st,
                  bs=c(f32(inp['l0_gmlp_bs']).reshape(1, 1024)))
    maps = []
    for ci in cores:
        b, sl = tok(ci)
        m = dict(common)
        m['xTd'] = c(x[b, sl].T)
        maps.append(m)
    r1 = run_bass_kernel_spmd(_prog('L1', build_L1), maps, core_ids=cores).results
    common = dict(conv_w=c(f32(inp['l0_conv_w'])[:, 0, :].reshape(31, 8, 128).transpose(2, 1, 0)),
                  conv_b=_lay(inp['l0_conv_b'], 8), cln_g=_lay(inp['l0_conv_ln_g'], 8), cln_b=_lay(inp['l0_conv_ln_b'], 8),
                  ident=np.eye(128, dtype=np.float32), w_out=f32(inp['l0_w_out']),
                  gA=_lay(inp['l0_ffn2_norm']), wgA=f32(inp['l0_ffn2_w_gate']), wuA=f32(inp['l0_ffn2_w_up']),
                  wdA=f32(inp['l0_ffn2_w_down']),
                  gB=_lay(inp['l1_ffn1_norm']), wgB=f32(inp['l1_ffn1_w_gate']), wuB=f32(inp['l1_ffn1_w_up']),
                  wdB=f32(inp['l1_ffn1_w_down']),
                  gM=_lay(inp['l1_mix_norm']), w_in1=f32(inp['l1_w_in']))
    maps = []
    for ci in cores:
        m = dict(common)
        aT = r1[ci]['aT']
        halo = np.zeros((1024, 32), np.float32)
        if ci % 2 == 1:
            halo = r1[ci - 1]['aT'][:, T - 32:]
        m['aTh'] = c(np.concatenate([halo, aT], axis=1))
        m['boTd'] = r1[ci]['boT']
        m['x1Td'] = r1[ci]['x1T']
        maps.append(m)
    r2 = run_bass_kernel_spmd(_prog('L2', build_L2), maps, core_ids=cores).results
    W = prep_L3_weights(*[f32(inp[k]) for k in ('l1_cmp_pe_k', 'l1_cmp_w1_k', 'l1_cmp_w2_k',
                                                'l1_cmp_pe_v', 'l1_cmp_w1_v', 'l1_cmp_w2_v')])
    consts = l3_consts()
    maps = []
    for ci in cores:
        b, half = ci // 2, ci % 2
        zT_b = np.concatenate([r2[2 * b]['zT'], r2[2 * b + 1]['zT']], axis=1)
        maps.append(prep_L3(zT_b, pos[b], half, W, consts))
    r3 = run_bass_kernel_spmd(_prog('L3', build_L3), maps, core_ids=cores).results
    common = dict(w_out1=f32(inp['l1_w_out']), gA=_lay(inp['l1_ffn2_norm']), wgA=f32(inp['l1_ffn2_w_gate']),
                  wuA=f32(inp['l1_ffn2_w_up']), wdA=f32(inp['l1_ffn2_w_down']), gF=_lay(inp['final_norm']))
    maps = []
    for ci in cores:
        b, sl = tok(ci)
        o_b = np.concatenate([r3[2 * b]['o'], r3[2 * b + 1]['o']], axis=1)
        m = dict(common)
        m['oTd'] = c(o_b[sl].T)
        m['x4Td'] = r2[ci]['x4T']
        maps.append(m)
    r4 = run_bass_kernel_spmd(_prog('L4', build_L4), maps, core_ids=cores).results
    out = np.zeros((4, 2048, 2048), np.float32)
    for ci in cores:
        b, sl = tok(ci)
        out[b, sl] = r4[ci]['yT'].T
    return out


from contextlib import ExitStack

W_NAMES = [('l0_ffn1', 'f1'), ('l0_ffn2', 'f2'), ('l1_ffn1', 'f3'), ('l1_ffn2', 'f4')]


def build_fused(dff=DFF, nz=5168):
    nc = bass.Bass("TRN2", target_bir_lowering=False)
    T = 1024
    ext = lambda name, shape, dtype=F32: nc.dram_tensor(name, shape, dtype, kind="ExternalInput").ap()
    scr = lambda name, shape: nc.dram_tensor(name, shape, F32, kind="Internal").ap()
    I = {}
    I['xT'] = ext('xT', [2, D, T])
    for _, s in W_NAMES:
        I[s + '_g'] = ext(s + '_g', [128, 16])
        I[s + '_wg'] = ext(s + '_wg', [D, dff])
        I[s + '_wu'] = ext(s + '_wu', [D, dff])
        I[s + '_wd'] = ext(s + '_wd', [dff, D])
    for name, shape in [('g_mix0', [128, 16]), ('w_in0', [D, 4096]), ('lng', [1, 1024]), ('lnb', [1, 1024]),
                        ('wsT', [128, 8, 128]), ('tri_st', [128, 128]), ('bs', [1, 1024]),
                        ('conv_w', [128, 8, 31]), ('conv_b', [128, 8]), ('cln_g', [128, 8]), ('cln_b', [128, 8]),
                        ('w_out0', [D, D]), ('g_mix1', [128, 16]), ('w_in1', [D, nz]),
                        ('inv', [128, 1]), ('sgn', [128, 1]), ('pekT', [128, 32]), ('w1k', [4096, 256]),
                        ('w2k', [256, 128]), ('pevT', [128, 32]), ('w1v', [4096, 256]), ('w2v', [256, 128]),
                        ('ovl', [128, 32]), ('cm', [128, 16, 128]), ('fbv', [128, 16, 32]), ('tri', [128, 128]),
                        ('tri2', [128, 128]), ('ident', [128, 128]), ('w_out1', [D, D]), ('g_fin', [128, 16])]:
        I[name] = ext(name, shape)
    I['pos'] = ext('pos', [1, S], I32)
    yT = nc.dram_tensor('yT', [D, T], F32, kind="ExternalOutput").ap()
    I['flag'] = ext('flag', [128, 2])
    x1T = scr('x1T_s', [2, D, T]); aT = scr('aT_s', [2, 1024, T]); boT = scr('boT_s', [2, 1024, T])
    x4T = scr('x4T_s', [2, D, T]); zT = scr('zT_s', [nz, 2 * T]); o_s = scr('o_s', [2 * T, 2048])
    kb = KB(nc)
    ps = nc.alloc_psum_tensor('ps', [128, 8 * 512], F32)
    with ExitStack() as es_core:
        _ES[0] = es_core
        core = Core(nc, T, kb, ps)
        for h in range(2):
            with ExitStack() as es:
                _ES[0] = es
                build_L1(T, dff, nc, dict(xTd=I['xT'][h], g1=I['f1_g'], g2=I['g_mix0'], wgd=I['f1_wg'], wud=I['f1_wu'],
                                          wdd=I['f1_wd'], w_in=I['w_in0'], lng=I['lng'], lnb=I['lnb'], wsT=I['wsT'],
                                          tri=I['tri_st'], bs=I['bs'], x1T=x1T[h], aT=aT[h], boT=boT[h]), core)
            with ExitStack() as es:
                _ES[0] = es
                build_L2(T, dff, nz, nc, dict(x1Td=x1T[h], aT_cur=aT[h], aT_prev=(aT[0] if h == 1 else None),
                                              boTd=boT[h], ident=I['ident'], conv_w=I['conv_w'], conv_b=I['conv_b'], cln_g=I['cln_g'],
                                              cln_b=I['cln_b'], w_out=I['w_out0'],
                                              gA=I['f2_g'], wgA=I['f2_wg'], wuA=I['f2_wu'], wdA=I['f2_wd'],
                                              gB=I['f3_g'], wgB=I['f3_wg'], wuB=I['f3_wu'], wdB=I['f3_wd'],
                                              gM=I['g_mix1'], w_in1=I['w_in1'], x4T=x4T[h],
                                              zT=zT[:, h * T:(h + 1) * T]), core)
            _ES[0] = es_core
    for gp in range(2):
        with ExitStack() as es:
            _ES[0] = es
            ov = {k: I[k] for k in ('inv', 'sgn', 'pekT', 'w1k', 'w2k', 'pevT', 'w1v', 'w2v', 'ovl', 'cm', 'fbv',
                                    'tri', 'tri2', 'ident', 'pos')}
            ov['zT'] = zT
            ov['o'] = o_s[:, gp * 1024:(gp + 1) * 1024]
            build_L3(nc, ov, kb, ps, gp)
    with ExitStack() as es_core:
        _ES[0] = es_core
        core = Core(nc, T, kb, ps)
        with ExitStack() as es:
            _ES[0] = es
            build_L4(T, dff, nc, dict(x4Td=x4T[0], x4T_1=x4T[1], o_tok=o_s[0:T, :], o_tok1=o_s[T:2 * T, :],
                                      flag=I['flag'], ident=I['ident'], w_out1=I['w_out1'], gA=I['f4_g'],
                                      wgA=I['f4_wg'], wuA=I['f4_wu'], wdA=I['f4_wd'], gF=I['g_fin'], yT=yT), core)
        _ES[0] = es_core
    _ES[0] = None
    kb.finish('sp')
    return nc


def fused_inputs(inp, b, r=0):
    c = np.ascontiguousarray
    f32 = lambda a: np.asarray(a, dtype=np.float32)
    x = f32(inp['x'])
    m = dict(xT=c(np.stack([x[b, 0:1024].T, x[b, 1024:2048].T])))
    for pre, s in W_NAMES:
        m[s + '_g'] = _lay(inp[pre + '_norm'])
        m[s + '_wg'] = f32(inp[pre + '_w_gate'])
        m[s + '_wu'] = f32(inp[pre + '_w_up'])
        m[s + '_wd'] = f32(inp[pre + '_w_down'])
    m.update(g_mix0=_lay(inp['l0_mix_norm']), w_in0=f32(inp['l0_w_in']),
             lng=c(f32(inp['l0_gmlp_ln_g']).reshape(1, 1024)), lnb=c(f32(inp['l0_gmlp_ln_b']).reshape(1, 1024)),
             wsT=c(f32(inp['l0_gmlp_ws']).transpose(2, 0, 1)), tri_st=np.triu(np.ones((128, 128), np.float32)),
             bs=c(f32(inp['l0_gmlp_bs']).reshape(1, 1024)),
             conv_w=c(f32(inp['l0_conv_w'])[:, 0, :].reshape(31, 8, 128).transpose(2, 1, 0)),
             conv_b=_lay(inp['l0_conv_b'], 8), cln_g=_lay(inp['l0_conv_ln_g'], 8), cln_b=_lay(inp['l0_conv_ln_b'], 8),
             w_out0=f32(inp['l0_w_out']), g_mix1=_lay(inp['l1_mix_norm']), w_in1=f32(inp['l1_w_in']),
             w_out1=f32(inp['l1_w_out']), g_fin=_lay(inp['final_norm']),
             pos=c(np.asarray(inp['positions'])[b].reshape(1, S).astype(np.int32)))
    m.update(prep_L3_weights(*[f32(inp[k]) for k in ('l1_cmp_pe_k', 'l1_cmp_w1_k', 'l1_cmp_w2_k',
                                                     'l1_cmp_pe_v', 'l1_cmp_w1_v', 'l1_cmp_w2_v')]))
    m.update(l3_consts())
    fl = np.zeros((128, 2), np.float32)
    fl[:, r] = 1.0
    m['flag'] = fl
    return m


def kernel(**inp):
    nc = _prog('fused', build_fused)
    maps = [fused_inputs(inp, ci // 2, ci % 2) for ci in range(NCORES)]
    res = run_bass_kernel_spmd(nc, maps, core_ids=list(range(NCORES))).results
    out = np.zeros((4, 2048, 2048), np.float32)
    for ci in range(NCORES):
        b, h = ci // 2, ci % 2
        out[b, h * 1024:(h + 1) * 1024] = res[ci]['yT'].T
    return out
```
